# Optimizing a Trainium2 kernel written in Bass

```python
import math
import jax, jax.numpy as jnp
from jax import lax
import numpy as np

D_MODEL = 1024
BATCH = 8
SEQ = 4096
DEPTH = 4

A_HEADS = 8
A_HEAD_DIM = 64
A_WIDTH = A_HEADS * A_HEAD_DIM
MOBA_BLOCK = 256
MOBA_TOPK = 3
B_HEADS = 4
B_HEAD_DIM = 128
B_WIDTH = B_HEADS * B_HEAD_DIM
CONV_WIDTH = 4
GDN_CHUNK = 64
C_HEADS = 8
C_HEAD_DIM = 64
C_WIDTH = C_HEADS * C_HEAD_DIM
KV_RANK = 256
IDX_HEADS = 8
IDX_DIM = 64
DSA_TOPK_MAX = 256
IDX_WEIGHT_SCALE = IDX_HEADS ** -0.5 * IDX_DIM ** -0.5
N_BUCKETS = 32
MAX_DISTANCE = 128
Q_BLOCK = 128
D_FF = 4 * D_MODEL
DEEPNORM_ALPHA = (2 * DEPTH) ** 0.25
DEEPNORM_BETA = (8 * DEPTH) ** -0.25
LN_EPS = 1e-5
RMS_EPS = 1e-6

IN_SPLIT = (A_WIDTH, A_WIDTH, A_WIDTH,
            B_WIDTH, B_WIDTH, B_WIDTH, B_WIDTH, B_HEADS, B_HEADS,
            C_WIDTH, KV_RANK, IDX_HEADS * IDX_DIM, IDX_DIM, IDX_HEADS,
            D_MODEL, D_MODEL, D_MODEL)
IN_WIDTH = sum(IN_SPLIT)

kernel_name = 'hybrid_moba_gdn_dsa_block'


def split_cols(x, sizes):
    return jnp.split(x, np.cumsum(sizes)[:-1].tolist(), axis=-1)


def layer_norm(x, g, b):
    xf = x.astype(jnp.float32)
    mu = jnp.mean(xf, axis=-1, keepdims=True)
    var = jnp.mean(jnp.square(xf - mu), axis=-1, keepdims=True)
    return ((xf - mu) * lax.rsqrt(var + LN_EPS) * g + b).astype(x.dtype)


def rms_norm(x, g):
    xf = x.astype(jnp.float32)
    return (xf * lax.rsqrt(jnp.mean(xf * xf, axis=-1, keepdims=True) + RMS_EPS) * g).astype(x.dtype)


def l2_normalize(x):
    xf = x.astype(jnp.float32)
    return xf * lax.rsqrt(jnp.sum(xf * xf, axis=-1, keepdims=True) + RMS_EPS)


def t5_bucket(dist):
    max_exact = N_BUCKETS // 2
    d = jnp.maximum(dist, 0)
    large = max_exact + (jnp.log(jnp.maximum(d, max_exact).astype(jnp.float32) / max_exact)
                         / math.log(MAX_DISTANCE / max_exact) * (N_BUCKETS - max_exact)).astype(jnp.int32)
    return jnp.where(d < max_exact, d, jnp.minimum(large, N_BUCKETS - 1))


def causal_conv_silu(x, w):
    y = lax.conv_general_dilated(x, w[:, None, :].astype(x.dtype), window_strides=(1,),
                                 padding=((CONV_WIDTH - 1, 0),),
                                 dimension_numbers=('NWC', 'WIO', 'NWC'),
                                 feature_group_count=x.shape[-1])
    return jax.nn.silu(y)


def moba_attention(q, k, v, bias_tab):
    B, T, H, dh = q.shape
    nb = -(-T // MOBA_BLOCK)
    pad = nb * MOBA_BLOCK - T
    nq = T // Q_BLOCK
    kk = min(MOBA_TOPK, nb)
    scale = dh ** -0.5

    def to_blocks(t):
        t = jnp.pad(t, ((0, 0), (0, pad), (0, 0), (0, 0)))
        return t.reshape(B, nb, MOBA_BLOCK, H, dh).transpose(0, 3, 1, 2, 4)

    k_blk = to_blocks(k)
    v_blk = to_blocks(v)
    k_mean = jnp.mean(k_blk.astype(jnp.float32), axis=3).astype(q.dtype)
    q_chunks = q.reshape(B, nq, Q_BLOCK, H, dh).transpose(1, 0, 3, 2, 4)
    bi = jnp.arange(B)[:, None, None, None]
    hi = jnp.arange(H)[None, :, None, None]
    blk_pos = jnp.arange(MOBA_BLOCK)

    def one_chunk(args):
        c, qc = args
        t = c * Q_BLOCK + jnp.arange(Q_BLOCK)
        cur = (c * Q_BLOCK) // MOBA_BLOCK
        gate = jnp.einsum('bhqd,bhnd->bhqn', qc, k_mean).astype(jnp.float32)
        gate = jnp.where(jnp.arange(nb) < cur, gate, -jnp.inf)
        _, sel = lax.top_k(gate, kk)
        k_sel = k_blk[bi, hi, sel]
        v_sel = v_blk[bi, hi, sel]
        dist_p = t[None, None, :, None, None] - (sel[..., None] * MOBA_BLOCK + blk_pos)
        s_p = jnp.einsum('bhqd,bhqjpd->bhqjp', qc, k_sel).astype(jnp.float32) * scale
        s_p = s_p + bias_tab[hi[..., None], t5_bucket(dist_p)]
        s_p = jnp.where((sel < cur)[..., None], s_p, -jnp.inf).reshape(B, H, Q_BLOCK, kk * MOBA_BLOCK)
        k_own = lax.dynamic_index_in_dim(k_blk, cur, axis=2, keepdims=False)
        v_own = lax.dynamic_index_in_dim(v_blk, cur, axis=2, keepdims=False)
        dist_o = t[:, None] - (cur * MOBA_BLOCK + blk_pos)[None, :]
        s_o = jnp.einsum('bhqd,bhpd->bhqp', qc, k_own).astype(jnp.float32) * scale
        s_o = jnp.where(dist_o >= 0, s_o + bias_tab[:, t5_bucket(dist_o)], -jnp.inf)
        p = jax.nn.softmax(jnp.concatenate([s_p, s_o], axis=-1), axis=-1).astype(v.dtype)
        p_p = p[..., :kk * MOBA_BLOCK].reshape(B, H, Q_BLOCK, kk, MOBA_BLOCK)
        return (jnp.einsum('bhqjp,bhqjpd->bhqd', p_p, v_sel)
                + jnp.einsum('bhqp,bhpd->bhqd', p[..., kk * MOBA_BLOCK:], v_own))

    o = lax.map(one_chunk, (jnp.arange(nq), q_chunks))
    return o.transpose(1, 0, 3, 2, 4).reshape(B, T, H, dh)


def gated_delta_rule(q, k, v, log_a, beta):
    B, T, H, dk = q.shape
    dv = v.shape[-1]
    C = GDN_CHUNK
    n = T // C

    def to_chunks(t):
        t = jnp.moveaxis(t.astype(jnp.float32), 2, 1)
        return t.reshape(B, H, n, C, *t.shape[3:])

    q = to_chunks(q) * dk ** -0.5
    k = to_chunks(k)
    v = to_chunks(v)
    beta = to_chunks(beta)
    g = jnp.cumsum(to_chunks(log_a), axis=-1)
    causal = jnp.tril(jnp.ones((C, C), dtype=bool))
    strict = jnp.tril(jnp.ones((C, C), dtype=bool), -1)
    diff = g[..., :, None] - g[..., None, :]
    decay = jnp.where(causal, jnp.exp(jnp.where(causal, diff, 0.0)), 0.0)
    kb = k * beta[..., None]
    m = jnp.where(strict, -jnp.einsum('bhncd,bhnsd->bhncs', kb, k) * decay, 0.0)
    eye = jnp.eye(C, dtype=jnp.float32)
    tinv = lax.linalg.triangular_solve(eye - m, jnp.broadcast_to(eye, m.shape),
                                       left_side=True, lower=True, unit_diagonal=True)
    u = jnp.einsum('bhncs,bhnse->bhnce', tinv, v * beta[..., None])
    w = jnp.einsum('bhncs,bhnsd->bhncd', tinv, kb * jnp.exp(g)[..., None])
    a_intra = jnp.einsum('bhncd,bhnsd->bhncs', q, k) * decay
    xs = tuple(jnp.moveaxis(t, 2, 0) for t in (q, k, u, w, g, a_intra))

    def step(S, inp):
        qc, kc, uc, wc, gc, ac = inp
        v_new = uc - jnp.einsum('bhcd,bhde->bhce', wc, S)
        o = (jnp.einsum('bhcd,bhde->bhce', qc * jnp.exp(gc)[..., None], S)
             + jnp.einsum('bhcs,bhse->bhce', ac, v_new))
        g_last = gc[..., -1]
        S = (S * jnp.exp(g_last)[..., None, None]
             + jnp.einsum('bhcd,bhce->bhde', kc * jnp.exp(g_last[..., None] - gc)[..., None], v_new))
        return S, o

    _, o = lax.scan(step, jnp.zeros((B, H, dk, dv), jnp.float32), xs)
    return o.transpose(1, 0, 3, 2, 4).reshape(B, T, H, dv)


def dsa_attention(q, ckv, qi, ki, wi, w_uk, w_uv, bias_tab):
    B, T, H, dh = q.shape
    R = ckv.shape[-1]
    nq = T // Q_BLOCK
    topk = min(DSA_TOPK_MAX, T // 4)
    q_lat = jnp.einsum('bthd,rhd->bthr', q, w_uk) * dh ** -0.5
    ql_c = q_lat.reshape(B, nq, Q_BLOCK, H, R).swapaxes(0, 1)
    qi_c = qi.reshape(B, nq, Q_BLOCK, IDX_HEADS, IDX_DIM).swapaxes(0, 1)
    wi_c = wi.reshape(B, nq, Q_BLOCK, IDX_HEADS).swapaxes(0, 1)
    bi = jnp.arange(B)[:, None, None]
    key_pos = jnp.arange(T)

    def one_chunk(args):
        c, qlc, qic, wic = args
        t = c * Q_BLOCK + jnp.arange(Q_BLOCK)
        score = jax.nn.relu(jnp.einsum('bqhd,bsd->bqhs', qic, ki))
        score = jnp.einsum('bqhs,bqh->bqs', score, wic).astype(jnp.float32)
        score = jnp.where(key_pos[None, None, :] <= t[None, :, None], score, -jnp.inf)
        _, sel = lax.top_k(score, topk)
        c_sel = ckv[bi, sel]
        s = jnp.einsum('bqhr,bqkr->bqhk', qlc, c_sel).astype(jnp.float32)
        s = s + jnp.moveaxis(bias_tab[:, t5_bucket(t[None, :, None] - sel)], 0, 2)
        s = jnp.where((sel <= t[None, :, None])[:, :, None, :], s, -jnp.inf)
        p = jax.nn.softmax(s, axis=-1).astype(ckv.dtype)
        return jnp.einsum('bqhk,bqkr->bqhr', p, c_sel)

    o_lat = lax.map(one_chunk, (jnp.arange(nq), ql_c, qi_c, wi_c))
    o_lat = o_lat.swapaxes(0, 1).reshape(B, T, H, R)
    return jnp.einsum('bthr,rhd->bthd', o_lat, w_uv)


def hybrid_mixer(x, rel_bias, w_in, conv_w, a_log, dt_bias, gdn_norm, kv_norm, idx_k_ln_g,
                 idx_k_ln_b, w_uk, w_uv, w_branch_a, w_branch_b, w_branch_c, w_out):
    B, T, _ = x.shape
    (q_a, k_a, v_a, q_b, k_b, v_b, z_b, a_b, b_b, q_c, ckv, qi, ki, wi,
     g_a, g_b, g_c) = split_cols(x @ w_in, IN_SPLIT)

    def heads(t, h):
        return t.reshape(B, T, h, -1)

    o_a = moba_attention(heads(q_a, A_HEADS), heads(k_a, A_HEADS), heads(v_a, A_HEADS),
                         rel_bias[:, :A_HEADS].T).reshape(B, T, A_WIDTH)
    q_b, k_b, v_b = jnp.split(causal_conv_silu(jnp.concatenate([q_b, k_b, v_b], axis=-1), conv_w), 3, axis=-1)
    log_a = -jnp.exp(a_log.astype(jnp.float32)) * jax.nn.softplus(a_b.astype(jnp.float32) + dt_bias)
    beta = jax.nn.sigmoid(b_b.astype(jnp.float32))
    o_b = gated_delta_rule(l2_normalize(heads(q_b, B_HEADS)), l2_normalize(heads(k_b, B_HEADS)),
                           heads(v_b, B_HEADS), log_a, beta)
    o_b = (rms_norm(o_b, gdn_norm) * jax.nn.silu(heads(z_b, B_HEADS).astype(jnp.float32)))
    o_b = o_b.astype(x.dtype).reshape(B, T, B_WIDTH)
    o_c = dsa_attention(heads(q_c, C_HEADS), rms_norm(ckv, kv_norm), heads(qi, IDX_HEADS),
                        layer_norm(ki, idx_k_ln_g, idx_k_ln_b), wi * IDX_WEIGHT_SCALE,
                        w_uk, w_uv, rel_bias[:, A_HEADS:].T).reshape(B, T, C_WIDTH)
    y = (jax.nn.sigmoid(g_a) * (o_a @ w_branch_a)
         + jax.nn.sigmoid(g_b) * (o_b @ w_branch_b)
         + jax.nn.sigmoid(g_c) * (o_c @ w_branch_c))
    return y @ w_out


def setup_inputs(seed: int = 0) -> dict:
    key = jax.random.key(seed)
    ks = jax.random.split(key, 24)
    f32 = jnp.float32
    L = DEPTH

    def nrm(k, shape, scale):
        return jax.random.normal(k, shape, f32) * scale

    dt = jnp.exp(jax.random.uniform(ks[4], (L, B_HEADS), f32, math.log(1e-3), math.log(1e-1)))
    return {
        'x': nrm(ks[0], (BATCH, SEQ, D_MODEL), 1.0),
        'rel_bias': nrm(ks[1], (N_BUCKETS, A_HEADS + C_HEADS), 0.2),
        'w_in': nrm(ks[2], (L, D_MODEL, IN_WIDTH), D_MODEL ** -0.5),
        'conv_w': nrm(ks[3], (L, CONV_WIDTH, 3 * B_WIDTH), CONV_WIDTH ** -0.5),
        'a_log': jnp.log(jax.random.uniform(ks[5], (L, B_HEADS), f32, 1.0, 16.0)),
        'dt_bias': dt + jnp.log(-jnp.expm1(-dt)),
        'gdn_norm': 1.0 + nrm(ks[6], (L, B_HEAD_DIM), 0.02),
        'kv_norm': 1.0 + nrm(ks[7], (L, KV_RANK), 0.02),
        'idx_k_ln_g': 1.0 + nrm(ks[8], (L, IDX_DIM), 0.02),
        'idx_k_ln_b': nrm(ks[9], (L, IDX_DIM), 0.02),
        'w_uk': nrm(ks[10], (L, KV_RANK, C_HEADS, C_HEAD_DIM), KV_RANK ** -0.5),
        'w_uv': nrm(ks[11], (L, KV_RANK, C_HEADS, C_HEAD_DIM), KV_RANK ** -0.5),
        'w_branch_a': nrm(ks[12], (L, A_WIDTH, D_MODEL), A_WIDTH ** -0.5),
        'w_branch_b': nrm(ks[13], (L, B_WIDTH, D_MODEL), B_WIDTH ** -0.5),
        'w_branch_c': nrm(ks[14], (L, C_WIDTH, D_MODEL), C_WIDTH ** -0.5),
        'w_out': nrm(ks[15], (L, D_MODEL, D_MODEL), D_MODEL ** -0.5 * DEEPNORM_BETA),
        'ln1_g': 1.0 + nrm(ks[16], (L, D_MODEL), 0.02),
        'ln1_b': nrm(ks[17], (L, D_MODEL), 0.02),
        'w_up': nrm(ks[18], (L, D_MODEL, D_FF), D_MODEL ** -0.5),
        'w_down': nrm(ks[19], (L, D_FF, D_MODEL), D_FF ** -0.5 * DEEPNORM_BETA),
        'ln2_g': 1.0 + nrm(ks[20], (L, D_MODEL), 0.02),
        'ln2_b': nrm(ks[21], (L, D_MODEL), 0.02),
    }


def reference(x, rel_bias, w_in, conv_w, a_log, dt_bias, gdn_norm, kv_norm, idx_k_ln_g, idx_k_ln_b,
              w_uk, w_uv, w_branch_a, w_branch_b, w_branch_c, w_out, ln1_g, ln1_b, w_up, w_down,
              ln2_g, ln2_b):
    for i in range(DEPTH):
        mix = hybrid_mixer(x, rel_bias, w_in[i], conv_w[i], a_log[i], dt_bias[i], gdn_norm[i],
                           kv_norm[i], idx_k_ln_g[i], idx_k_ln_b[i], w_uk[i], w_uv[i],
                           w_branch_a[i], w_branch_b[i], w_branch_c[i], w_out[i])
        x = layer_norm(DEEPNORM_ALPHA * x + mix, ln1_g[i], ln1_b[i])
        h = jnp.square(jax.nn.relu(x @ w_up[i]))
        x = layer_norm(DEEPNORM_ALPHA * x + h @ w_down[i], ln2_g[i], ln2_b[i])
    return x
```

```python
from contextlib import ExitStack
import math
import numpy as np
import ml_dtypes
import concourse.bass as bass
import concourse.mybir as mybir
from concourse.bass_utils import run_bass_kernel_spmd

F32 = mybir.dt.float32
BF16 = mybir.dt.bfloat16
AF = mybir.ActivationFunctionType
ALU = mybir.AluOpType
AX = mybir.AxisListType

T = 4096
D = 1024
NT = T // 128
DEPTH = 4
INW = 8016
DFF = 4096
ALPHA = (2 * DEPTH) ** 0.25
LN_EPS = 1e-5
RMS_EPS = 1e-6
NEG = -30000.0
IDX_WEIGHT_SCALE = 8 ** -0.5 * 64 ** -0.5

C_QA, C_KA, C_VA, C_QB, C_ZB, C_AB, C_QC, C_CKV, C_QI, C_KI, C_WI, C_GA = (
    0, 512, 1024, 1536, 3072, 3584, 3592, 4104, 4360, 4872, 4936, 4944)


class Sched:
    ENGS = ("pe", "act", "dve", "pool", "sp")
    NDS = 48

    def __init__(self, nc, es):
        self.nc = nc
        self.es = es
        self.epoch = 1
        self.eng = {"pe": nc.tensor, "act": nc.scalar, "dve": nc.vector, "pool": nc.gpsimd, "sp": nc.sync}
        self.sem = {e: es.enter_context(nc.semaphore("s_" + e)) for e in self.ENGS}
        self.dsem = [es.enter_context(nc.semaphore("d_%d" % i)) for i in range(self.NDS)]
        self.cnt = {e: 0 for e in self.ENGS}
        self.seen = {e: {} for e in self.ENGS}
        self.lastw = {}
        self.readers = {}
        self.ndma = 0
        self.ndq = {}
        half = self.NDS // 2
        self.dpool = {"sp": (0, half), "pool": (half, self.NDS - half)}
        self.dlast = {}
        self.ninst = 0
        self.scratch = es.enter_context(nc.sbuf_tensor("sched_scratch", [128, 1], F32))

    @staticmethod
    def _key(a):
        if isinstance(a, (str, tuple)):
            return a
        return a.name

    def _semof(self, sk):
        if isinstance(sk, tuple):
            return self.dsem[sk[1]]
        return self.sem[sk]

    def _deps(self, rk, wk):
        toks = []
        for k in rk:
            t = self.lastw.get(k)
            if t:
                toks.append(t)
        for k in wk:
            t = self.lastw.get(k)
            if t:
                toks.append(t)
            toks.extend(self.readers.get(k, {}).items())
        return toks

    def _wait(self, eng, toks):
        need = {}
        for (sk, v) in toks:
            if sk == "pe" and eng == "pe":
                continue
            if self.seen[eng].get(sk, 0) >= v:
                continue
            if need.get(sk, 0) < v:
                need[sk] = v
        e = self.eng[eng]
        for sk, v in need.items():
            self.seen[eng][sk] = v
            e.wait_ge(self._semof(sk), v)
            self.ninst += 1

    def _record(self, tok, rk, wk):
        for k in rk:
            d = self.readers.setdefault(k, {})
            if d.get(tok[0], 0) < tok[1]:
                d[tok[0]] = tok[1]
        for k in wk:
            self.lastw[k] = tok
            self.readers[k] = {}

    def op(self, eng, fn, r=(), w=()):
        rk = [self._key(a) for a in r]
        wk = [self._key(a) for a in w]
        self._wait(eng, self._deps(rk, wk))
        ins = fn(self.eng[eng])
        self.cnt[eng] += 1
        ins.then_inc(self.sem[eng], 1)
        self.ninst += 1
        tok = (eng, self.cnt[eng])
        self._record(tok, rk, wk)
        return tok

    def dma(self, qeng, out, in_, r=(), w=(), **kw):
        rk = [self._key(a) for a in r]
        wk = [self._key(a) for a in w]
        lo, n = self.dpool[qeng]
        k = self.ndq.get(qeng, 0)
        self.ndq[qeng] = k + 1
        idx = lo + k % n
        val = 16 * (k // n + 1)
        self.ndma += 1
        toks = self._deps(rk, wk)
        if val > 16:
            toks.append((("d", idx), val - 16))
        self._wait(qeng, toks)
        self.eng[qeng].dma_start(out=out, in_=in_, **kw).then_inc(self.dsem[idx], 16)
        self.ninst += 1
        tok = (("d", idx), val)
        self.dlast[idx] = val
        self._record(tok, rk, wk)
        return tok

    def barrier(self, new_epoch=False):
        toks = [(e, self.cnt[e]) for e in self.ENGS if self.cnt[e] > 0]
        toks += [(("d", i), v) for i, v in self.dlast.items()]
        for e in self.ENGS:
            self._wait(e, toks)
        self.lastw.clear()
        self.readers.clear()
        assert max(self.cnt.values()) < 30000, self.cnt
        if new_epoch:
            self.sem = {e: self.es.enter_context(self.nc.semaphore("s%d_%s" % (self.epoch, e))) for e in self.ENGS}
            self.epoch += 1
            self.cnt = {e: 0 for e in self.ENGS}
            for e in self.ENGS:
                for k in self.ENGS:
                    self.seen[e].pop(k, None)


def host_constants(rel_bias):
    def bucket(d):
        d = np.maximum(d, 0)
        large = 16 + (np.log(np.maximum(d, 16).astype(np.float32) / 16) / math.log(128 / 16) * 16).astype(np.int32)
        return np.where(d < 16, d, np.minimum(large, 31))
    j = np.arange(128)[:, None]
    c = np.arange(1280)[None, :]
    dist = c - j - 512
    b = bucket(dist)
    W = np.empty((128, 16, 1280), np.float32)
    for h in range(16):
        W[:, h, :] = np.where(dist >= 0, rel_bias[b, h], NEG)
    ident = np.eye(128, dtype=np.float32)
    ltri = np.tril(np.ones((128, 128), np.float32))
    s_ = np.arange(128)[:, None]
    c_ = np.arange(128)[None, :]
    same = (s_ // 64) == (c_ // 64)
    gd = np.stack([same & (s_ <= c_), same & (s_ < c_), same, np.broadcast_to(s_ < 64, (128, 128)),
                   np.broadcast_to(s_ >= 64, (128, 128))], axis=1).astype(np.float32)
    return {"c_toep": W.astype(ml_dtypes.bfloat16), "c_ident": ident, "c_ltri": ltri, "c_gdn": np.ascontiguousarray(gd)}


class Prog:
    def __init__(self, nlayers=DEPTH, debug=(), stop_after=None, inject=(), phases="ABCDEF"):
        self.nlayers = nlayers
        self.inject = set(inject)
        self.phases = phases
        self.debug = set(debug)
        self.stop_after = stop_after
        self.nc = bass.Bass("TRN2", target_bir_lowering=False)
        self.out_names = []

    def dram_in(self, name, shape, dt=F32):
        return self.nc.dram_tensor(name, list(shape), dt, kind="ExternalInput").ap()

    def dram_scr(self, name, shape, dt=F32):
        if name in self.inject:
            return self.nc.dram_tensor(name, list(shape), dt, kind="ExternalInput").ap()
        if name in self.debug:
            self.out_names.append(name)
            return self.nc.dram_tensor(name, list(shape), dt, kind="ExternalOutput").ap()
        return self.nc.dram_tensor(name, list(shape), dt).ap()

    def build(self):
        nc = self.nc
        L = self.nlayers
        I = {}
        I["x"] = self.dram_in("x", [T, D])
        I["w_in"] = self.dram_in("w_in", [L, D, INW])
        I["conv_w"] = self.dram_in("conv_w", [L, 4, 1536])
        I["a_log"] = self.dram_in("a_log", [L, 4])
        I["dt_bias"] = self.dram_in("dt_bias", [L, 4])
        I["gdn_norm"] = self.dram_in("gdn_norm", [L, 128])
        I["kv_norm"] = self.dram_in("kv_norm", [L, 256])
        I["idx_k_ln_g"] = self.dram_in("idx_k_ln_g", [L, 64])
        I["idx_k_ln_b"] = self.dram_in("idx_k_ln_b", [L, 64])
        I["w_uk"] = self.dram_in("w_uk", [L, 256, 512])
        I["w_uv"] = self.dram_in("w_uv", [L, 256, 512])
        I["w_branch_a"] = self.dram_in("w_branch_a", [L, 512, D])
        I["w_branch_b"] = self.dram_in("w_branch_b", [L, 512, D])
        I["w_branch_c"] = self.dram_in("w_branch_c", [L, 512, D])
        I["w_out"] = self.dram_in("w_out", [L, D, D])
        I["ln1_g"] = self.dram_in("ln1_g", [L, D])
        I["ln1_b"] = self.dram_in("ln1_b", [L, D])
        I["w_up"] = self.dram_in("w_up", [L, D, DFF])
        I["w_down"] = self.dram_in("w_down", [L, DFF, D])
        I["ln2_g"] = self.dram_in("ln2_g", [L, D])
        I["ln2_b"] = self.dram_in("ln2_b", [L, D])
        I["c_toep"] = self.dram_in("c_toep", [128, 16, 1280], BF16)
        I["c_ident"] = self.dram_in("c_ident", [128, 128])
        I["c_ltri"] = self.dram_in("c_ltri", [128, 128])
        I["c_gdn"] = self.dram_in("c_gdn", [128, 5, 128])
        self.I = I
        self.out = nc.dram_tensor("out", [T, D], F32, kind="ExternalOutput").ap()
        X = {}
        X["QA"] = self.dram_scr("QA", [512, T], BF16)
        X["KA"] = self.dram_scr("KA", [512, T], BF16)
        X["QKVB"] = self.dram_scr("QKVB", [1536, T], F32)
        X["QC"] = self.dram_scr("QC", [512, T], BF16)
        X["QI"] = self.dram_scr("QI", [512, T], BF16)
        X["VA"] = self.dram_scr("VA", [T, 520], BF16)
        X["ZB"] = self.dram_scr("ZB", [T, 512], F32)
        X["AB"] = self.dram_scr("AB", [T, 8], F32)
        X["CKV"] = self.dram_scr("CKV", [T, 256], F32)
        X["KIWI"] = self.dram_scr("KIWI", [T, 72], F32)
        X["GATES"] = self.dram_scr("GATES", [T, 3072], F32)
        X["OA"] = self.dram_scr("OA", [T, 512], BF16)
        X["OB"] = self.dram_scr("OB", [T, 512], BF16)
        X["OC"] = self.dram_scr("OC", [T, 512], BF16)
        X["X1"] = self.dram_scr("X1", [T, D], F32)
        X["XS"] = self.dram_scr("XS", [T, D], F32)
        self.X = X

        with ExitStack() as top:
            S = Sched(nc, top)
            self.S = S
            self.ident = top.enter_context(nc.sbuf_tensor("ident", [128, 128], F32))
            self.identb = top.enter_context(nc.sbuf_tensor("identb", [128, 128], BF16))
            self.i30k = top.enter_context(nc.sbuf_tensor("i30k", [128, 128], BF16))
            S.dma("sp", self.ident[:], I["c_ident"], w=[self.ident])
            S.op("dve", lambda e: e.tensor_copy(out=self.identb[:], in_=self.ident[:]), r=[self.ident], w=[self.identb])
            S.op("dve", lambda e: e.tensor_scalar(out=self.i30k[:], in0=self.ident[:], scalar1=-NEG, scalar2=None,
                                                   op0=ALU.mult), r=[self.ident], w=[self.i30k])
            self.ltri = top.enter_context(nc.sbuf_tensor("ltri", [128, 128], F32))
            S.dma("sp", self.ltri[:], I["c_ltri"], w=[self.ltri])
            self.gdc = top.enter_context(nc.sbuf_tensor("gdc", [128, 5, 128], F32))
            S.dma("sp", self.gdc[:], I["c_gdn"], w=[self.gdc])
            self.onesf = top.enter_context(nc.sbuf_tensor("onesf", [128, 128], F32))
            S.op("pool", lambda e: e.memset(self.onesf[:], 1.0), w=[self.onesf])
            self.onesb = top.enter_context(nc.sbuf_tensor("onesb", [128, 128], BF16))
            S.op("pool", lambda e: e.memset(self.onesb[:], 1.0), w=[self.onesb])
            self.epsln = top.enter_context(nc.sbuf_tensor("epsln", [128, 2], F32))
            S.op("pool", lambda e: e.memset(self.epsln[:, 0:1], LN_EPS), w=[self.epsln])
            S.op("pool", lambda e: e.memset(self.epsln[:, 1:2], RMS_EPS), w=[self.epsln])
            for l in range(self.nlayers):
                xin = I["x"] if l == 0 else X["XS"]
                xout = self.out if l == self.nlayers - 1 else X["XS"]
                for ph in "ABCDEF":
                    if ph not in self.phases:
                        continue
                    if ph == "A":
                        self.phase_a(l, xin)
                    elif ph == "B":
                        self.phase_b(l)
                    elif ph == "C":
                        self.phase_c(l)
                    elif ph == "D":
                        self.phase_d(l)
                    elif ph == "E":
                        self.phase_e(l, xin)
                    elif ph == "F":
                        self.phase_f(l, xout)
                    S.barrier(new_epoch=(ph in "CF"))
            S.barrier()
        return nc

    def phase_a(self, l, xin):
        nc, S, I, X = self.nc, self.S, self.I, self.X
        with ExitStack() as es:
            sb = lambda n, s, d: es.enter_context(nc.sbuf_tensor("%s_L%d" % (n, l), s, d))
            ps = lambda n, s, d: es.enter_context(nc.psum_tensor("%s_L%d" % (n, l), s, d))
            xT = sb("a_xT", [128, 8, T], BF16)
            xs = [sb("a_xs%d" % i, [128, D], F32) for i in range(2)]
            xb = [sb("a_xb%d" % i, [128, D], BF16) for i in range(2)]
            wst = [sb("a_wst%d" % i, [128, 8, 512], F32) for i in range(2)]
            wb = [sb("a_wb%d" % i, [128, 8, 512], BF16) for i in range(2)]
            stf = [sb("a_stf%d" % i, [128, T], F32) for i in range(2)]
            stb = [sb("a_stb%d" % i, [128, T], BF16) for i in range(2)]
            ptr = [ps("a_ptr%d" % i, [128, 8, 128], BF16) for i in range(2)]
            pacc = [ps("a_pacc%d" % i, [128, 512], F32) for i in range(4)]
            stv = [sb("a_stv%d" % i, [128, 8, 65], BF16) for i in range(2)]
            for i in range(2):
                S.op("pool", lambda e: e.memset(stv[i][:], 1.0), w=[stv[i]])

            for tt in range(NT):
                b = tt % 2
                S.dma("sp", xs[b][:], xin[tt * 128:(tt + 1) * 128, :], w=[xs[b]])
                S.op("pool", lambda e: e.tensor_copy(out=xb[b][:], in_=xs[b][:]), r=[xs[b]], w=[xb[b]])
                for kc in range(8):
                    S.op("pe", lambda e: e.transpose(out=ptr[b][:, kc, :], in_=xb[b][:, kc * 128:(kc + 1) * 128],
                                                     identity=self.identb[:]), r=[xb[b], self.identb], w=[ptr[b]])
                S.op("dve", lambda e: e.tensor_copy(out=xT[:, :, tt * 128:(tt + 1) * 128], in_=ptr[b][:]),
                     r=[ptr[b]], w=[xT])

            wsrc = I["w_in"][l].rearrange("(kc p) n -> p kc n", p=128)
            blocks = [
                ("FM", C_QA, 512, "QA", 0, 0.125), ("FM", C_KA, 512, "KA", 0, 1.0),
                ("FM", C_QB, 512, "QKVB", 0, 1.0), ("FM", C_QB + 512, 512, "QKVB", 512, 1.0),
                ("FM", C_QB + 1024, 512, "QKVB", 1024, 1.0),
                ("FM", C_QC, 512, "QC", 0, 0.125), ("FM", C_QI, 512, "QI", 0, 1.0),
                ("TM", C_VA, 512, "VA", 0, 1.0), ("TM", C_ZB, 512, "ZB", 0, 1.0), ("TM", C_AB, 8, "AB", 0, 1.0),
                ("TM", C_CKV, 256, "CKV", 0, 1.0), ("TM", C_KI, 72, "KIWI", 0, 1.0),
            ] + [("TM", C_GA + 512 * i, 512, "GATES", 512 * i, 1.0) for i in range(6)]
            ev = 0
            na = 0
            nst = 0
            def load_block(bi):
                _, c0_, ncol_, _, _, _ = blocks[bi]
                b_ = bi % 2
                S.dma("sp", wst[b_][:, :, 0:ncol_], wsrc[:, :, c0_:c0_ + ncol_], w=[wst[b_]])
                S.op("pool", lambda e: e.tensor_copy(out=wb[b_][:, :, 0:ncol_], in_=wst[b_][:, :, 0:ncol_]),
                     r=[wst[b_]], w=[wb[b_]])

            load_block(0)
            for bi, (kind, c0, ncol, dname, doff, scale) in enumerate(blocks):
                b = bi % 2
                dst = X[dname]
                isbf = dname in ("QA", "KA", "QC", "QI", "VA")
                if bi + 1 < len(blocks):
                    load_block(bi + 1)
                if kind == "FM":
                    for j in range(ncol // 128):
                        st = (stb if isbf else stf)[nst % 2]
                        nst += 1
                        for tg in range(8):
                            pa = pacc[na % 4]
                            na += 1
                            for kc in range(8):
                                S.op("pe", lambda e: e.matmul(pa[:], wb[b][:, kc, j * 128:(j + 1) * 128],
                                                              xT[:, kc, tg * 512:(tg + 1) * 512],
                                                              start=(kc == 0), stop=(kc == 7)),
                                     r=[wb[b], xT], w=[pa])
                            eng = "act" if ev % 2 == 0 else "dve"
                            ev += 1
                            if eng == "act":
                                S.op("act", lambda e: e.activation(out=st[:, tg * 512:(tg + 1) * 512], in_=pa[:],
                                                                   func=AF.Copy, scale=scale), r=[pa], w=[st])
                            else:
                                S.op("dve", lambda e: e.tensor_scalar(out=st[:, tg * 512:(tg + 1) * 512], in0=pa[:],
                                                                      scalar1=scale, scalar2=None, op0=ALU.mult),
                                     r=[pa], w=[st])
                        r0 = doff + j * 128
                        S.dma("pool", dst[r0:r0 + 128, :], st[:], r=[st], w=[dname])
                else:
                    for tt in range(NT):
                        pa = pacc[na % 4]
                        na += 1
                        for kc in range(8):
                            S.op("pe", lambda e: e.matmul(pa[:, 0:ncol], xT[:, kc, tt * 128:(tt + 1) * 128],
                                                          wb[b][:, kc, 0:ncol], start=(kc == 0), stop=(kc == 7)),
                                 r=[wb[b], xT], w=[pa])
                        if dname == "VA":
                            sv = stv[nst % 2]
                            nst += 1
                            S.op("act", lambda e: e.activation(out=sv[:, :, 0:64], in_=pa[:].rearrange("p (h d) -> p h d", h=8),
                                                               func=AF.Copy), r=[pa], w=[sv])
                            S.dma("pool", dst[tt * 128:(tt + 1) * 128, :], sv[:].rearrange("p h d -> p (h d)"), r=[sv], w=[dname])
                            continue
                        st = (stb if isbf else stf)[nst % 2]
                        nst += 1
                        eng = "act" if ev % 2 == 0 else "dve"
                        ev += 1
                        if eng == "act":
                            S.op("act", lambda e: e.activation(out=st[:, 0:ncol], in_=pa[:, 0:ncol], func=AF.Copy),
                                 r=[pa], w=[st])
                        else:
                            S.op("dve", lambda e: e.tensor_copy(out=st[:, 0:ncol], in_=pa[:, 0:ncol]), r=[pa], w=[st])
                        S.dma("pool", dst[tt * 128:(tt + 1) * 128, doff:doff + ncol], st[:, 0:ncol], r=[st], w=[dname])


    def layer_norm_tile(self, z, gt, bt, st1, junk, out):
        S = self.S
        S.op("dve", lambda e: e.tensor_reduce(out=st1[:, 0:1], in_=z[:], axis=AX.X, op=ALU.add), r=[z], w=[st1])
        S.op("dve", lambda e: e.tensor_scalar(out=st1[:, 1:2], in0=st1[:, 0:1], scalar1=-1.0 / D, scalar2=None,
                                               op0=ALU.mult), r=[st1], w=[st1])
        S.op("act", lambda e: e.activation(out=junk[:], in_=z[:], func=AF.Square, bias=st1[:, 1:2],
                                           accum_out=st1[:, 2:3]), r=[z, st1], w=[junk, st1])
        S.op("act", lambda e: e.activation(out=st1[:, 3:4], in_=st1[:, 2:3], func=AF.Sqrt, scale=1.0 / D,
                                           bias=self.epsln[:, 0:1]), r=[st1, self.epsln], w=[st1])
        S.op("dve", lambda e: e.reciprocal(out=st1[:, 4:5], in_=st1[:, 3:4]), r=[st1], w=[st1])
        S.op("dve", lambda e: e.tensor_scalar(out=z[:], in0=z[:], scalar1=st1[:, 1:2], scalar2=st1[:, 4:5],
                                               op0=ALU.add, op1=ALU.mult), r=[z, st1], w=[z])
        S.op("pool", lambda e: e.tensor_tensor(out=z[:], in0=z[:], in1=gt[:], op=ALU.mult), r=[z, gt], w=[z])
        S.op("pool", lambda e: e.tensor_tensor(out=out[:], in0=z[:], in1=bt[:], op=ALU.add), r=[z, bt], w=[out])

    def phase_e(self, l, xin):
        nc, S, I, X = self.nc, self.S, self.I, self.X
        with ExitStack() as es:
            sb = lambda n, s, d: es.enter_context(nc.sbuf_tensor("%s_L%d" % (n, l), s, d))
            ps = lambda n, s, d: es.enter_context(nc.psum_tensor("%s_L%d" % (n, l), s, d))
            wbr = sb("e_wbr", [128, 12, D], BF16)
            wout = sb("e_wout", [128, 8, D], BF16)
            wst = [sb("e_wst%d" % i, [128, 4, D], F32) for i in range(2)]
            gt = sb("e_g", [128, D], F32)
            bt = sb("e_b", [128, D], F32)
            obr = [sb("e_obr%d" % i, [128, 3, 512], BF16) for i in range(2)]
            obT = sb("e_obT", [128, 12, 128], BF16)
            sg = [sb("e_sg%d" % i, [128, 3072], F32) for i in range(2)]
            xs = [sb("e_xs%d" % i, [128, D], F32) for i in range(2)]
            y = sb("e_y", [128, D], F32)
            tmp = sb("e_tmp", [128, 512], F32)
            yb = sb("e_yb", [128, D], BF16)
            yT = sb("e_yT", [128, 8, 128], BF16)
            z = sb("e_z", [128, D], F32)
            junk = sb("e_junk", [128, D], F32)
            xo = [sb("e_xo%d" % i, [128, D], F32) for i in range(2)]
            st1 = sb("e_st1", [128, 8], F32)
            ptr = [ps("e_ptr%d" % i, [128, 8, 128], BF16) for i in range(2)]
            pacc = [ps("e_pacc%d" % i, [128, 512], F32) for i in range(4)]
            k = 0
            for bi, wn in enumerate(("w_branch_a", "w_branch_b", "w_branch_c")):
                S.dma("sp", wst[k % 2][:], I[wn][l].rearrange("(kc p) n -> p kc n", p=128), w=[wst[k % 2]])
                S.op("pool", lambda e: e.tensor_copy(out=wbr[:, bi * 4:(bi + 1) * 4, :], in_=wst[k % 2][:]),
                     r=[wst[k % 2]], w=[wbr])
                k += 1
            wo = I["w_out"][l].rearrange("(kc p) n -> p kc n", p=128)
            for hh in range(2):
                S.dma("sp", wst[k % 2][:], wo[:, hh * 4:(hh + 1) * 4, :], w=[wst[k % 2]])
                S.op("pool", lambda e: e.tensor_copy(out=wout[:, hh * 4:(hh + 1) * 4, :], in_=wst[k % 2][:]),
                     r=[wst[k % 2]], w=[wout])
                k += 1
            S.dma("sp", gt[:], I["ln1_g"][l:l + 1, :].partition_broadcast(128), w=[gt])
            S.dma("sp", bt[:], I["ln1_b"][l:l + 1, :].partition_broadcast(128), w=[bt])
            na = 0
            for tt in range(NT):
                b = tt % 2
                rows = slice(tt * 128, (tt + 1) * 128)
                for bi, nm in enumerate(("OA", "OB", "OC")):
                    S.dma("sp", obr[b][:, bi, :], X[nm][rows, :], r=[nm], w=[obr[b]])
                S.dma("sp", sg[b][:], X["GATES"][rows, :], r=["GATES"], w=[sg[b]])
                S.dma("sp", xs[b][:], xin[rows, :], r=["XS"], w=[xs[b]])
                S.op("act", lambda e: e.activation(out=sg[b][:], in_=sg[b][:], func=AF.Sigmoid), r=[sg[b]], w=[sg[b]])
                for c in range(12):
                    p = ptr[0] if c < 8 else ptr[1]
                    S.op("pe", lambda e: e.transpose(out=p[:, c % 8, :], in_=obr[b][:, c // 4, (c % 4) * 128:(c % 4 + 1) * 128],
                                                     identity=self.identb[:]), r=[obr[b], self.identb], w=[p])
                S.op("dve", lambda e: e.tensor_copy(out=obT[:, 0:8, :], in_=ptr[0][:]), r=[ptr[0]], w=[obT])
                S.op("dve", lambda e: e.tensor_copy(out=obT[:, 8:12, :], in_=ptr[1][:, 0:4, :]), r=[ptr[1]], w=[obT])
                for nh in range(2):
                    cs = slice(nh * 512, (nh + 1) * 512)
                    for bi in range(3):
                        pa = pacc[na % 4]
                        na += 1
                        for kc in range(4):
                            S.op("pe", lambda e: e.matmul(pa[:], obT[:, bi * 4 + kc, :], wbr[:, bi * 4 + kc, cs],
                                                          start=(kc == 0), stop=(kc == 3)), r=[obT, wbr], w=[pa])
                        gsl = sg[b][:, bi * 1024 + nh * 512: bi * 1024 + (nh + 1) * 512]
                        if bi == 0:
                            S.op("dve", lambda e: e.tensor_tensor(out=y[:, cs], in0=pa[:], in1=gsl, op=ALU.mult),
                                 r=[pa, sg[b]], w=[y])
                        else:
                            S.op("dve", lambda e: e.tensor_tensor(out=tmp[:], in0=pa[:], in1=gsl, op=ALU.mult),
                                 r=[pa, sg[b]], w=[tmp])
                            S.op("pool", lambda e: e.tensor_tensor(out=y[:, cs], in0=y[:, cs], in1=tmp[:], op=ALU.add),
                                 r=[y, tmp], w=[y])
                S.op("act", lambda e: e.activation(out=yb[:], in_=y[:], func=AF.Copy), r=[y], w=[yb])
                for kc in range(8):
                    S.op("pe", lambda e: e.transpose(out=ptr[0][:, kc, :], in_=yb[:, kc * 128:(kc + 1) * 128],
                                                     identity=self.identb[:]), r=[yb, self.identb], w=[ptr[0]])
                S.op("dve", lambda e: e.tensor_copy(out=yT[:], in_=ptr[0][:]), r=[ptr[0]], w=[yT])
                for nh in range(2):
                    cs = slice(nh * 512, (nh + 1) * 512)
                    pa = pacc[na % 4]
                    na += 1
                    for kc in range(8):
                        S.op("pe", lambda e: e.matmul(pa[:], yT[:, kc, :], wout[:, kc, cs], start=(kc == 0), stop=(kc == 7)),
                             r=[yT, wout], w=[pa])
                    S.op("dve", lambda e: e.scalar_tensor_tensor(out=z[:, cs], in0=xs[b][:, cs], scalar=ALPHA, in1=pa[:],
                                                                 op0=ALU.mult, op1=ALU.add), r=[xs[b], pa], w=[z])
                self.layer_norm_tile(z, gt, bt, st1, junk, xo[b])
                S.dma("pool", X["X1"][rows, :], xo[b][:], r=[xo[b]], w=["X1"])

    def phase_f(self, l, xout):
        nc, S, I, X = self.nc, self.S, self.I, self.X
        TG = 256
        with ExitStack() as es:
            sb = lambda n, s, d: es.enter_context(nc.sbuf_tensor("%s_L%d" % (n, l), s, d))
            ps = lambda n, s, d: es.enter_context(nc.psum_tensor("%s_L%d" % (n, l), s, d))
            wup = sb("f_wup", [128, 8, DFF], BF16)
            wdn = sb("f_wdn", [128, 32, D], BF16)
            wst = sb("f_wst", [128, 8, 512], F32)
            gt = sb("f_g", [128, D], F32)
            bt = sb("f_b", [128, D], F32)
            x1 = sb("f_x1", [128, 2, D], F32)
            x1b = sb("f_x1b", [128, D], BF16)
            x1T = sb("f_x1T", [128, 8, TG], BF16)
            hT = sb("f_hT", [128, 32, TG], BF16)
            rl = [sb("f_rl%d" % i, [128, TG], F32) for i in range(2)]
            z = sb("f_z", [128, D], F32)
            junk = sb("f_junk", [128, D], F32)
            xo = [sb("f_xo%d" % i, [128, D], F32) for i in range(2)]
            st1 = sb("f_st1", [128, 8], F32)
            ptr = [ps("f_ptr%d" % i, [128, 8, 128], BF16) for i in range(2)]
            pup = [ps("f_pup%d" % i, [128, TG], F32) for i in range(2)]
            pdn = [ps("f_pdn%d" % i, [128, 512], F32) for i in range(2)]
            wu = I["w_up"][l].rearrange("(kc p) n -> p kc n", p=128)
            for c in range(8):
                S.dma("sp", wst[:], wu[:, :, c * 512:(c + 1) * 512], w=[wst])
                S.op("pool", lambda e: e.tensor_copy(out=wup[:, :, c * 512:(c + 1) * 512], in_=wst[:]), r=[wst], w=[wup])
            wd = I["w_down"][l].rearrange("(fc p) n -> p fc n", p=128)

            def load_wdown():
                for c in range(8):
                    S.dma("sp", wst[:].rearrange("p a n -> p (a n)").rearrange("p (f n) -> p f n", f=4),
                          wd[:, c * 4:(c + 1) * 4, :], w=[wst])
                    S.op("pool", lambda e: e.tensor_copy(out=wdn[:, c * 4:(c + 1) * 4, :],
                                                         in_=wst[:].rearrange("p a n -> p (a n)").rearrange("p (f n) -> p f n", f=4)),
                         r=[wst], w=[wdn])
            S.dma("sp", gt[:], I["ln2_g"][l:l + 1, :].partition_broadcast(128), w=[gt])
            S.dma("sp", bt[:], I["ln2_b"][l:l + 1, :].partition_broadcast(128), w=[bt])
            nu = 0
            nd = 0
            no = 0
            for tg in range(T // TG):
                for u in range(2):
                    tt = tg * 2 + u
                    S.dma("sp", x1[:, u, :], X["X1"][tt * 128:(tt + 1) * 128, :], r=["X1"], w=[x1])
                for u in range(2):
                    S.op("act", lambda e: e.activation(out=x1b[:], in_=x1[:, u, :], func=AF.Copy), r=[x1], w=[x1b])
                    for kc in range(8):
                        S.op("pe", lambda e: e.transpose(out=ptr[u][:, kc, :], in_=x1b[:, kc * 128:(kc + 1) * 128],
                                                         identity=self.identb[:]), r=[x1b, self.identb], w=[ptr[u]])
                    S.op("dve", lambda e: e.tensor_copy(out=x1T[:, :, u * 128:(u + 1) * 128], in_=ptr[u][:]),
                         r=[ptr[u]], w=[x1T])
                for fc in range(32):
                    pu = pup[nu % 2]
                    r_ = rl[nu % 2]
                    nu += 1
                    for kc in range(8):
                        S.op("pe", lambda e: e.matmul(pu[:], wup[:, kc, fc * 128:(fc + 1) * 128], x1T[:, kc, :],
                                                      start=(kc == 0), stop=(kc == 7)), r=[wup, x1T], w=[pu])
                    S.op("act", lambda e: e.activation(out=r_[:], in_=pu[:], func=AF.Relu), r=[pu], w=[r_])
                    S.op("dve", lambda e: e.tensor_tensor(out=hT[:, fc, :], in0=r_[:], in1=r_[:], op=ALU.mult), r=[r_], w=[hT])
                if tg == 0:
                    load_wdown()
                for u in range(2):
                    tt = tg * 2 + u
                    for nh in range(2):
                        cs = slice(nh * 512, (nh + 1) * 512)
                        pd = pdn[nd % 2]
                        nd += 1
                        for fc in range(32):
                            S.op("pe", lambda e: e.matmul(pd[:], hT[:, fc, u * 128:(u + 1) * 128], wdn[:, fc, cs],
                                                          start=(fc == 0), stop=(fc == 31)), r=[hT, wdn], w=[pd])
                        S.op("dve", lambda e: e.scalar_tensor_tensor(out=z[:, cs], in0=x1[:, u, cs], scalar=ALPHA, in1=pd[:],
                                                                     op0=ALU.mult, op1=ALU.add), r=[x1, pd], w=[z])
                    o = xo[no % 2]
                    no += 1
                    self.layer_norm_tile(z, gt, bt, st1, junk, o)
                    oname = "OUT" if xout is self.out else "XS"
                    S.dma("pool", xout[tt * 128:(tt + 1) * 128, :], o[:], r=[o], w=[oname])


    def phase_b(self, l):
        nc, S, I, X = self.nc, self.S, self.I, self.X
        with ExitStack() as es:
            sb = lambda n, s, d: es.enter_context(nc.sbuf_tensor("%s_L%d" % (n, l), s, d))
            ps = lambda n, s, d: es.enter_context(nc.psum_tensor("%s_L%d" % (n, l), s, d))
            kT = sb("b_kT", [128, 4, T], BF16)
            qT = sb("b_qT", [128, 4, T], BF16)
            va = sb("b_va", [128, NT, 520], BF16)
            toep = sb("b_toep", [128, 8, 1280], BF16)
            nsT = sb("b_nsT", [128, T], BF16)
            ksum = sb("b_ksum", [128, 4, 16], F32)
            kmT = sb("b_kmT", [128, 4, 32], BF16)
            gm = sb("b_gm", [128, 8, 16], F32)
            m8 = sb("b_m8", [128, 8, 8], F32)
            ns = sb("b_ns", [128, 8, 16], BF16)
            PT = [sb("b_PT%d" % i, [128, 512], BF16) for i in range(4)]
            oa = [sb("b_oa%d" % i, [128, 4, 512], BF16) for i in range(2)]
            rden = sb("b_rden", [128, 4], F32)
            E = sb("b_E", [128, 128, 128], BF16)
            S.op("dve", lambda e: e.tensor_copy(out=E[:], in_=self.i30k[:].unsqueeze(2).to_broadcast([128, 128, 128])),
                 r=[self.i30k], w=[E])
            pS = [ps("b_pS%d" % i, [128, 512], F32) for i in range(2)]
            pO = [ps("b_pO%d" % i, [128, 65], F32) for i in range(4)]
            es_sel = ExitStack()
            pg = es_sel.enter_context(nc.psum_tensor("b_pg_L%d" % l, [128, 8, 16], F32))
            ptr = es_sel.enter_context(nc.psum_tensor("b_ptr_L%d" % l, [128, 128], BF16))
            for a in range(4):
                S.dma("sp", kT[:, a, :], X["KA"][a * 128:(a + 1) * 128, :], r=["KA"], w=[kT])
                S.dma("sp", qT[:, a, :], X["QA"][a * 128:(a + 1) * 128, :], r=["QA"], w=[qT])
            vsrc = X["VA"].rearrange("(tt p) c -> p tt c", p=128)
            for c in range(4):
                S.dma("sp", va[:, c * 8:(c + 1) * 8, :], vsrc[:, c * 8:(c + 1) * 8, :], r=["VA"], w=[va])
            S.dma("sp", toep[:], I["c_toep"][:, 0:8, :], w=[toep])
            S.op("pool", lambda e: e.memset(nsT[:], 0.0), w=[nsT])
            S.op("pool", lambda e: e.memset(gm[:], -1e30), w=[gm])
            S.op("pool", lambda e: e.memset(ns[:], 0.0), w=[ns])
            import os
            bstop = int(os.environ.get("BSTOP", "99"))
            if bstop <= 1:
                es_sel.close()
                return
            S.op("dve", lambda e: e.tensor_reduce(out=ksum[:], in_=kT[:].rearrange("p a (n s) -> p a n s", s=256),
                                                   axis=AX.X, op=ALU.add), r=[kT], w=[ksum])
            S.op("pool", lambda e: e.memset(kmT[:], 0.0), w=[kmT])
            S.op("dve", lambda e: e.tensor_scalar(out=kmT[0:64, :, 0:16], in0=ksum[0:64, :, :], scalar1=1.0 / 256, scalar2=None,
                                                   op0=ALU.mult), r=[ksum], w=[kmT])
            S.op("dve", lambda e: e.tensor_scalar(out=kmT[64:128, :, 16:32], in0=ksum[64:128, :, :], scalar1=1.0 / 256, scalar2=None,
                                                   op0=ALU.mult), r=[ksum], w=[kmT])
            if bstop <= 2:
                es_sel.close()
                return
            for tt in range(NT):
                cur = tt // 2
                if cur <= 3:
                    continue
                for a in range(4):
                    S.op("pe", lambda e: e.matmul(pg[:, 2 * a:2 * a + 2, :], qT[:, a, tt * 128:(tt + 1) * 128],
                                                  kmT[:, a, :].rearrange("p (h n) -> p h n", h=2), start=True, stop=True),
                         r=[qT, kmT], w=[pg])
                bsub = int(os.environ.get("BSUB", "99"))
                S.op("act", lambda e: e.activation(out=gm[:, :, 0:cur], in_=pg[:, :, 0:cur], func=AF.Copy), r=[pg], w=[gm])
                if bsub <= 1:
                    continue
                for h in range(8):
                    S.op("dve", lambda e: e.max(out=m8[:, h, :], in_=gm[:, h, :]), r=[gm], w=[m8])
                if bsub <= 2:
                    continue
                for h in range(8):
                    S.op("dve", lambda e: e.tensor_scalar(out=ns[:, h, 0:cur], in0=gm[:, h, 0:cur], scalar1=m8[:, h, 2:3],
                                                           scalar2=1.0, op0=ALU.is_ge, op1=ALU.subtract), r=[gm, m8], w=[ns])
                if bsub <= 3:
                    continue
                S.op("pe", lambda e: e.transpose(out=ptr[:], in_=ns[:].rearrange("p h n -> p (h n)"), identity=self.identb[:]),
                     r=[ns, self.identb], w=[ptr])
                S.op("act", lambda e: e.activation(out=nsT[:, tt * 128:(tt + 1) * 128], in_=ptr[:], func=AF.Copy),
                     r=[ptr], w=[nsT])
            S.barrier()
            es_sel.close()
            pS = pS + [ps("b_pS%d" % i, [128, 512], F32) for i in (2, 3)]
            if bstop <= 3:
                return
            nS = [0]
            for qg in range(8):
                if bstop <= 4 and qg >= 1:
                    break
                ob = oa[qg % 2]
                qs = slice(qg * 512, (qg + 1) * 512)
                nkt = 4 * (qg + 1)

                def emit_S(h, kt):
                    hb, a_ = 64 * (h % 2), h // 2
                    m = kt - 4 * qg
                    off = 512 - 128 * m if m >= -1 else 768
                    n = kt // 2
                    need_sel = (qg >= 2) and (n <= 2 * qg)
                    p, pt = pS[nS[0] % 4], PT[nS[0] % 4]
                    nS[0] += 1
                    S.op("pe", lambda e: e.matmul(p[:], kT[hb:hb + 64, a_, kt * 128:(kt + 1) * 128], qT[hb:hb + 64, a_, qs],
                                                  start=True, stop=False), r=[kT, qT], w=[p])
                    S.op("pe", lambda e: e.matmul(p[:], self.identb[:], toep[:, h, off:off + 512],
                                                  start=False, stop=not need_sel), r=[self.identb, toep], w=[p])
                    if need_sel:
                        rr = h * 16 + n
                        S.op("pe", lambda e: e.matmul(p[:], E[:, rr, :], nsT[:, qs], start=False, stop=True), r=[E, nsT], w=[p])
                    return p, pt

                def emit_PV(h, kt, pt):
                    for u in range(4):
                        last = 4 * qg + u
                        if kt <= last:
                            S.op("pe", lambda e: e.matmul(pO[u][:], pt[:, u * 128:(u + 1) * 128], va[:, kt, h * 65:(h + 1) * 65],
                                                          start=(kt == 0), stop=(kt == last)), r=[pt, va], w=[pO[u]])

                def finalize(h):
                    for u in range(4):
                        S.op("dve", lambda e: e.reciprocal(out=rden[:, u:u + 1], in_=pO[u][:, 64:65]), r=[pO[u]], w=[rden])
                        S.op("dve", lambda e: e.tensor_scalar(out=ob[:, u, h * 64:(h + 1) * 64], in0=pO[u][:, 0:64],
                                                               scalar1=rden[:, u:u + 1], scalar2=None, op0=ALU.mult),
                             r=[pO[u], rden], w=[ob])

                steps = [(h, kt) for h in range(8) for kt in range(nkt)]
                pending = emit_S(*steps[0])
                for i, (h, kt) in enumerate(steps):
                    p, pt = pending
                    S.op("act", lambda e: e.activation(out=pt[:], in_=p[:], func=AF.Exp), r=[p], w=[pt])
                    if i + 1 < len(steps):
                        pending = emit_S(*steps[i + 1])
                    emit_PV(h, kt, pt)
                    if kt == nkt - 1:
                        finalize(h)
                S.dma("pool", X["OA"][qg * 512:(qg + 1) * 512, :].rearrange("(u p) c -> p u c", p=128), ob[:],
                      r=[ob], w=["OA"])

    def phase_d(self, l):
        nc, S, I, X = self.nc, self.S, self.I, self.X
        import os
        dstop = int(os.environ.get("DSTOP", "99"))
        with ExitStack() as es:
            sb = lambda n, s, d: es.enter_context(nc.sbuf_tensor("%s_L%d" % (n, l), s, d))
            ps = lambda n, s, d: es.enter_context(nc.psum_tensor("%s_L%d" % (n, l), s, d))
            c = sb("d_c", [128, NT, 256], BF16)
            cT = sb("d_cT", [128, 2, T], BF16)
            kiT2 = sb("d_kiT2", [128, T], BF16)
            toep = sb("d_toep", [128, 8, 1280], BF16)
            absw = sb("d_absw", [128, NT, 8], F32)
            sgn = sb("d_sgn", [128, NT, 8], F32)
            wukT = sb("d_wukT", [128, 4, 256], BF16)
            WBD = sb("d_WBD", [128, 4, 2, 256], BF16)
            wuvb = sb("d_wuvb", [128, 2, 512], BF16)
            pI = [ps("d_pI%d" % i, [128, 512], F32) for i in range(2)]
            pAcc = ps("d_pAcc", [128, 512], F32)
            pS = [ps("d_pS%d" % i, [128, 4, 128], F32) for i in range(2)]
            pOT = [ps("d_pOT%d" % i, [128, 512], F32) for i in range(2)]
            pDen = ps("d_pDen", [128, 512], F32)
            pIb = [p[:].bitcast(BF16).rearrange("p (a n) -> p a n", a=8) for p in pI]

            S.dma("sp", toep[:], I["c_toep"][:, 8:16, :], w=[toep])
            with ExitStack() as es2:
                sb2 = lambda n, s, d: es2.enter_context(nc.sbuf_tensor("%s_L%d" % (n, l), s, d))
                ckv = sb2("d_ckv", [128, NT, 256], F32)
                kiwi = sb2("d_kiwi", [128, NT, 72], F32)
                kin = sb2("d_kin", [128, NT, 128], BF16)
                kif = sb2("d_kif", [128, NT, 64], F32)
                junk = sb2("d_junk", [128, 256], F32)
                wst = sb2("d_wst", [128, 2, 512], F32)
                wukb = sb2("d_wukb", [128, 2, 512], BF16)
                kvn = sb2("d_kvn", [128, 256], F32)
                lng = sb2("d_lng", [128, 64], F32)
                lnb = sb2("d_lnb", [128, 64], F32)
                ss = sb2("d_ss", [128, NT], F32)
                rs = sb2("d_rs", [128, NT], F32)
                mu = sb2("d_mu", [128, NT], F32)
                S.dma("sp", wst[:], I["w_uk"][l].rearrange("(rc p) n -> p rc n", p=128), w=[wst])
                S.op("pool", lambda e: e.tensor_copy(out=wukb[:], in_=wst[:]), r=[wst], w=[wukb])
                for rc in range(2):
                    for a in range(4):
                        S.op("pe", lambda e: e.transpose(out=pIb[0][:, rc * 4 + a, :], in_=wukb[:, rc, a * 128:(a + 1) * 128],
                                                         identity=self.identb[:]), r=[wukb, self.identb], w=[pI[0]])
                S.op("dve", lambda e: e.tensor_copy(out=wukT[:].rearrange("p a (rc r) -> p rc a r", rc=2),
                                                     in_=pIb[0].rearrange("p (rc a) r -> p rc a r", rc=2)), r=[pI[0]], w=[wukT])
                S.op("pool", lambda e: e.memset(WBD[:], 0.0), w=[WBD])
                S.op("dve", lambda e: e.tensor_copy(out=WBD[0:64, :, 0, :], in_=wukT[0:64, :, :]), r=[wukT], w=[WBD])
                S.op("dve", lambda e: e.tensor_copy(out=WBD[64:128, :, 1, :], in_=wukT[64:128, :, :]), r=[wukT], w=[WBD])
                S.dma("sp", wst[:], I["w_uv"][l].rearrange("(rc p) n -> p rc n", p=128), r=[], w=[wst])
                S.op("pool", lambda e: e.tensor_copy(out=wuvb[:], in_=wst[:]), r=[wst], w=[wuvb])
                S.dma("sp", kvn[:], I["kv_norm"][l:l + 1, :].partition_broadcast(128), w=[kvn])
                S.dma("sp", lng[:], I["idx_k_ln_g"][l:l + 1, :].partition_broadcast(128), w=[lng])
                S.dma("sp", lnb[:], I["idx_k_ln_b"][l:l + 1, :].partition_broadcast(128), w=[lnb])
                csrc = X["CKV"].rearrange("(tt p) c -> p tt c", p=128)
                for q4 in range(4):
                    S.dma("sp", ckv[:, q4 * 8:(q4 + 1) * 8, :], csrc[:, q4 * 8:(q4 + 1) * 8, :], r=["CKV"], w=[ckv])
                for tt in range(NT):
                    S.op("act", lambda e: e.activation(out=junk[:], in_=ckv[:, tt, :], func=AF.Square, accum_out=ss[:, tt:tt + 1]),
                         r=[ckv], w=[junk, ss])
                S.op("act", lambda e: e.activation(out=rs[:], in_=ss[:], func=AF.Sqrt, scale=1.0 / 256, bias=self.epsln[:, 1:2]),
                     r=[ss, self.epsln], w=[rs])
                S.op("dve", lambda e: e.reciprocal(out=rs[:], in_=rs[:]), r=[rs], w=[rs])
                S.op("dve", lambda e: e.tensor_tensor(out=ckv[:], in0=ckv[:], in1=rs[:].unsqueeze(2).to_broadcast([128, NT, 256]),
                                                       op=ALU.mult), r=[ckv, rs], w=[ckv])
                S.op("dve", lambda e: e.tensor_tensor(out=c[:], in0=ckv[:], in1=kvn[:].unsqueeze(1).to_broadcast([128, NT, 256]),
                                                       op=ALU.mult), r=[ckv, kvn], w=[c])
                for g in range(NT // 4):
                    pb = pIb[g % 2]
                    for t4 in range(4):
                        for rc in range(2):
                            S.op("pe", lambda e: e.transpose(out=pb[:, t4 * 2 + rc, :], in_=c[:, g * 4 + t4, rc * 128:(rc + 1) * 128],
                                                             identity=self.identb[:]), r=[c, self.identb], w=[pI[g % 2]])
                    S.op("act", lambda e: e.activation(out=cT[:, :, g * 512:(g + 1) * 512].rearrange("p rc (t q) -> p t rc q", t=4),
                                                       in_=pb.rearrange("p (t rc) q -> p t rc q", t=4), func=AF.Copy),
                         r=[pI[g % 2]], w=[cT])
                S.dma("sp", kiwi[:], X["KIWI"].rearrange("(tt p) c -> p tt c", p=128), r=["KIWI"], w=[kiwi])
                S.op("dve", lambda e: e.tensor_reduce(out=mu[:], in_=kiwi[:, :, 0:64], axis=AX.X, op=ALU.add), r=[kiwi], w=[mu])
                S.op("dve", lambda e: e.tensor_scalar(out=mu[:], in0=mu[:], scalar1=-1.0 / 64, scalar2=None, op0=ALU.mult),
                     r=[mu], w=[mu])
                S.op("dve", lambda e: e.tensor_tensor(out=kif[:], in0=kiwi[:, :, 0:64], in1=mu[:].unsqueeze(2).to_broadcast([128, NT, 64]),
                                                       op=ALU.add), r=[kiwi, mu], w=[kif])
                S.op("dve", lambda e: e.tensor_tensor(out=ckv[:, :, 0:64], in0=kif[:], in1=kif[:], op=ALU.mult), r=[kif], w=[ckv])
                S.op("dve", lambda e: e.tensor_reduce(out=ss[:], in_=ckv[:, :, 0:64], axis=AX.X, op=ALU.add), r=[ckv], w=[ss])
                S.op("act", lambda e: e.activation(out=rs[:], in_=ss[:], func=AF.Sqrt, scale=1.0 / 64, bias=self.epsln[:, 0:1]),
                     r=[ss, self.epsln], w=[rs])
                S.op("dve", lambda e: e.reciprocal(out=rs[:], in_=rs[:]), r=[rs], w=[rs])
                S.op("dve", lambda e: e.tensor_tensor(out=kif[:], in0=kif[:], in1=rs[:].unsqueeze(2).to_broadcast([128, NT, 64]),
                                                       op=ALU.mult), r=[kif, rs], w=[kif])
                S.op("dve", lambda e: e.tensor_tensor(out=kif[:], in0=kif[:], in1=lng[:].unsqueeze(1).to_broadcast([128, NT, 64]),
                                                       op=ALU.mult), r=[kif, lng], w=[kif])
                for hf in range(2):
                    S.op("dve", lambda e: e.tensor_tensor(out=kin[:, :, hf * 64:(hf + 1) * 64], in0=kif[:],
                                                           in1=lnb[:].unsqueeze(1).to_broadcast([128, NT, 64]), op=ALU.add),
                         r=[kif, lnb], w=[kin])
                for g in range(NT // 8):
                    pb = pIb[g % 2]
                    for t8 in range(8):
                        S.op("pe", lambda e: e.transpose(out=pb[:, t8, :], in_=kin[:, g * 8 + t8, :], identity=self.identb[:]),
                             r=[kin, self.identb], w=[pI[g % 2]])
                    S.op("act", lambda e: e.activation(out=kiT2[:, g * 1024:(g + 1) * 1024].rearrange("p (t q) -> p t q", t=8),
                                                       in_=pb, func=AF.Copy), r=[pI[g % 2]], w=[kiT2])
                S.op("act", lambda e: e.activation(out=absw[:], in_=kiwi[:, :, 64:72], func=AF.Abs, scale=IDX_WEIGHT_SCALE),
                     r=[kiwi], w=[absw])
                S.op("dve", lambda e: e.tensor_scalar(out=sgn[:], in0=kiwi[:, :, 64:72], scalar1=0.0, scalar2=2.0,
                                                       op0=ALU.is_ge, op1=ALU.mult), r=[kiwi], w=[sgn])
                S.op("dve", lambda e: e.tensor_scalar(out=sgn[:], in0=sgn[:], scalar1=-1.0, scalar2=None, op0=ALU.add),
                     r=[sgn], w=[sgn])
                S.barrier()
            Isc = [sb("d_Isc%d" % i, [128, T], F32) for i in range(2)]
            work = sb("d_work", [128, T], F32)
            negm = [sb("d_negm%d" % i, [128, T], BF16) for i in range(2)]
            nmT = [sb("d_nmT%d" % i, [128, NT, 128], BF16) for i in range(2)]
            qit = [sb("d_qit%d" % i, [128, 4, 128], BF16) for i in range(2)]
            qct = [sb("d_qct%d" % i, [128, 4, 128], BF16) for i in range(2)]
            qlT = [sb("d_qlT%d" % i, [128, 2, 8, 128], BF16) for i in range(3)]
            Dsg = sb("d_Dsg", [128, 8, 128], BF16)
            Ph = [sb("d_Ph%d" % i, [128, 512], BF16) for i in range(2)]
            PT = [sb("d_PT%d" % i, [128, 4, 128], BF16) for i in range(2)]
            rdn = sb("d_rdn", [128, 512], F32)
            OTn = sb("d_OTn", [128, 2, 4, 128], BF16)
            oc = [sb("d_oc%d" % i, [128, 512], BF16) for i in range(2)]
            m8 = sb("d_m8", [128, 8], F32)
            st = sb("d_st", [128, 4], F32)
            if dstop <= 1:
                return
            qisrc = X["QI"].rearrange("(a p) t -> p a t", p=128)
            qcsrc = X["QC"].rearrange("(a p) t -> p a t", p=128)
            nS = [0]

            def stage1(qt):
                L, b = (qt + 1) * 128, qt % 2
                ts = slice(qt * 128, (qt + 1) * 128)
                S.dma("sp", qit[b][:], qisrc[:, :, ts], r=["QI"], w=[qit[b]])
                S.dma("sp", qct[b][:], qcsrc[:, :, ts], r=["QC"], w=[qct[b]])
                if qt >= 2:
                    S.op("pool", lambda e: e.tensor_tensor(out=Dsg[:], in0=self.identb[:].unsqueeze(1).to_broadcast([128, 8, 128]),
                                                           in1=sgn[:, qt, :].unsqueeze(2).to_broadcast([128, 8, 128]), op=ALU.mult),
                         r=[self.identb, sgn], w=[Dsg])
                    for kg in range((L + 511) // 512):
                        w_ = min(512, L - 512 * kg)
                        for h in range(8):
                            hb, a_ = 64 * (h % 2), h // 2
                            pi, ph = pI[h % 2], Ph[h % 2]
                            S.op("pe", lambda e: e.matmul(pi[:, 0:w_], qit[b][hb:hb + 64, a_, :], kiT2[hb:hb + 64, kg * 512:kg * 512 + w_],
                                                          start=True, stop=True), r=[qit[b], kiT2], w=[pi])
                            S.op("act", lambda e: e.activation(out=ph[:, 0:w_], in_=pi[:, 0:w_], func=AF.Relu,
                                                               scale=absw[:, qt, h:h + 1]), r=[pi, absw], w=[ph])
                            S.op("pe", lambda e: e.matmul(pAcc[:, 0:w_], Dsg[:, h, :], ph[:, 0:w_], start=(h == 0), stop=(h == 7)),
                                 r=[Dsg, ph], w=[pAcc])
                        S.op("act", lambda e: e.activation(out=Isc[b][:, kg * 512:kg * 512 + w_], in_=pAcc[:, 0:w_], func=AF.Copy),
                             r=[pAcc], w=[Isc[b]])
                for rc in range(2):
                    for hq in range(2):
                        pq = pI[(rc * 2 + hq) % 2]
                        for hh in range(4):
                            h = hq * 4 + hh
                            S.op("pe", lambda e: e.matmul(pq[:, hh * 128:(hh + 1) * 128], WBD[:, h // 2, h % 2, rc * 128:(rc + 1) * 128],
                                                          qct[b][:, h // 2, :], start=True, stop=True), r=[WBD, qct[b]], w=[pq])
                        S.op("act", lambda e: e.activation(out=qlT[qt % 3][:, rc, hq * 4:(hq + 1) * 4, :],
                                                           in_=pq[:].rearrange("p (h q) -> p h q", h=4), func=AF.Copy), r=[pq], w=[qlT[qt % 3]])

            def stage2(qt):
                L, b = (qt + 1) * 128, qt % 2
                if qt < 2:
                    return
                I_ = Isc[b]
                S.op("dve", lambda e: e.tensor_reduce(out=st[:, 0:1], in_=I_[:, 0:L], axis=AX.X, op=ALU.min), r=[I_], w=[st])
                S.op("dve", lambda e: e.tensor_scalar(out=st[:, 1:2], in0=st[:, 0:1], scalar1=-1.0, scalar2=1.0,
                                                       op0=ALU.mult, op1=ALU.add), r=[st], w=[st])
                S.op("dve", lambda e: e.tensor_scalar(out=I_[:, 0:L], in0=I_[:, 0:L], scalar1=st[:, 1:2], scalar2=None,
                                                       op0=ALU.add), r=[I_, st], w=[I_])
                S.op("dve", lambda e: e.tensor_tensor(out=I_[:, L - 128:L], in0=I_[:, L - 128:L], in1=self.ltri[:], op=ALU.mult),
                     r=[I_, self.ltri], w=[I_])
                for r_ in range(32):
                    src_ = I_ if r_ == 0 else work
                    S.op("dve", lambda e: e.max(out=m8[:], in_=src_[:, 0:L]), r=[src_], w=[m8])
                    if r_ < 31:
                        S.op("dve", lambda e: e.scalar_tensor_tensor(out=work[:, 0:L], in0=src_[:, 0:L], scalar=m8[:, 7:8],
                                                                     in1=src_[:, 0:L], op0=ALU.is_lt, op1=ALU.mult),
                             r=[src_, m8], w=[work])
                S.op("dve", lambda e: e.tensor_scalar(out=negm[b][:, 0:L], in0=I_[:, 0:L], scalar1=m8[:, 7:8], scalar2=1.0,
                                                       op0=ALU.is_ge, op1=ALU.subtract), r=[I_, m8], w=[negm[b]])

            def stage3(qt):
                nk, b = qt + 1, qt % 2
                ts = slice(qt * 128, (qt + 1) * 128)
                masked = qt >= 2
                if masked:
                    for g in range((nk + 7) // 8):
                        n8 = min(8, nk - 8 * g)
                        pb = pIb[g % 2]
                        for t8 in range(n8):
                            kt = g * 8 + t8
                            S.op("pe", lambda e: e.transpose(out=pb[:, t8, :], in_=negm[b][:, kt * 128:(kt + 1) * 128],
                                                             identity=self.identb[:]), r=[negm[b], self.identb], w=[pI[g % 2]])
                        S.op("act", lambda e: e.activation(out=nmT[b][:, g * 8:g * 8 + n8, :], in_=pb[:, 0:n8, :], func=AF.Copy, scale=-NEG),
                             r=[pI[g % 2]], w=[nmT[b]])
                def emit_S(half, kt):
                    hs = slice(4 * half, 4 * half + 4)
                    m = kt - qt
                    off = 512 - 128 * m if m >= -1 else 768
                    p_, pt = pS[nS[0] % 2], PT[nS[0] % 2]
                    nS[0] += 1
                    for rc in range(2):
                        S.op("pe", lambda e: e.matmul(p_[:], cT[:, rc, kt * 128:(kt + 1) * 128], qlT[qt % 3][:, rc, hs, :],
                                                      start=(rc == 0), stop=False), r=[cT, qlT[qt % 3]], w=[p_])
                    S.op("pe", lambda e: e.matmul(p_[:], self.identb[:], toep[:, hs, off:off + 128], start=False, stop=not masked),
                         r=[self.identb, toep], w=[p_])
                    if masked:
                        for hh in range(4):
                            S.op("pe", lambda e: e.matmul(p_[:, hh, :], self.identb[:], nmT[b][:, kt, :], start=False, stop=(hh == 3)),
                                 r=[self.identb, nmT[b]], w=[p_])
                    return p_, pt

                def emit_PV(half, kt, pt):
                    for rc in range(2):
                        S.op("pe", lambda e: e.matmul(pOT[rc][:], c[:, kt, rc * 128:(rc + 1) * 128], pt[:].rearrange("p h q -> p (h q)"),
                                                      start=(kt == 0), stop=(kt == nk - 1)), r=[c, pt], w=[pOT[rc]])
                    S.op("pe", lambda e: e.matmul(pDen[:], self.onesb[:], pt[:].rearrange("p h q -> p (h q)"),
                                                  start=(kt == 0), stop=(kt == nk - 1)), r=[self.onesb, pt], w=[pDen])

                def finalize(half):
                    S.op("dve", lambda e: e.reciprocal(out=rdn[:], in_=pDen[:]), r=[pDen], w=[rdn])
                    for rc in range(2):
                        S.op("dve", lambda e: e.tensor_tensor(out=OTn[:, rc, :, :].rearrange("p h q -> p (h q)"), in0=pOT[rc][:], in1=rdn[:],
                                                               op=ALU.mult), r=[pOT[rc], rdn], w=[OTn])
                    for hh in range(4):
                        h = 4 * half + hh
                        for rc in range(2):
                            S.op("pe", lambda e: e.matmul(pAcc[:, hh * 64:(hh + 1) * 64], OTn[:, rc, hh, :], wuvb[:, rc, h * 64:(h + 1) * 64],
                                                          start=(rc == 0), stop=(rc == 1)), r=[OTn, wuvb], w=[pAcc])
                    S.op("act", lambda e: e.activation(out=oc[b][:, half * 256:(half + 1) * 256], in_=pAcc[:, 0:256], func=AF.Copy),
                         r=[pAcc], w=[oc[b]])

                steps = [(half, kt) for half in range(2) for kt in range(nk)]
                pending = emit_S(*steps[0])
                for i, (half, kt) in enumerate(steps):
                    p_, pt = pending
                    S.op("act", lambda e: e.activation(out=pt[:], in_=p_[:], func=AF.Exp), r=[p_], w=[pt])
                    if i + 1 < len(steps):
                        pending = emit_S(*steps[i + 1])
                    emit_PV(half, kt, pt)
                    if kt == nk - 1:
                        finalize(half)
                S.dma("pool", X["OC"][ts, :], oc[b][:], r=[oc[b]], w=["OC"])

            tiles = [qt for qt in range(NT) if not (dstop <= 2 and qt not in (0, 1, 2, 5))]
            n = len(tiles)
            for i in range(n + 2):
                if i < n:
                    stage1(tiles[i])
                if 0 <= i - 1 < n:
                    stage2(tiles[i - 1])
                if 0 <= i - 2 < n:
                    stage3(tiles[i - 2])

    def phase_c(self, l):
        nc, S, I, X = self.nc, self.S, self.I, self.X
        import os
        cstop = int(os.environ.get("CSTOP", "99"))
        QS = 128 ** -0.5
        UTc, UTs, BON, H0, H1 = (self.gdc[:, i, :] for i in range(5))
        with ExitStack() as es:
            sb = lambda n, s, d: es.enter_context(nc.sbuf_tensor("%s_L%d" % (n, l), s, d))
            ps = lambda n, s, d: es.enter_context(nc.psum_tensor("%s_L%d" % (n, l), s, d))
            cw = sb("c_cw", [128, 12, 4], F32)
            xp = [sb("c_xp%d" % i, [128, T + 3], F32) for i in range(2)]
            y = [sb("c_y%d" % i, [128, T], F32) for i in range(2)]
            sq = sb("c_sq", [128, T], F32)
            rn = sb("c_rn", [128, T], F32)
            pn = [ps("c_pn%d" % i, [128, 512], F32) for i in range(2)]
            cwr = sb("c_cwr", [4, 1536], F32)
            S.dma("sp", cwr[:], I["conv_w"][l], w=[cwr])
            for c_ in range(12):
                S.op("pe", lambda e: e.transpose(out=pn[0][:, c_ * 4:(c_ + 1) * 4], in_=cwr[:, c_ * 128:(c_ + 1) * 128],
                                                 identity=self.ident[0:4, 0:4]), r=[cwr, self.ident], w=[pn[0]])
            S.op("dve", lambda e: e.tensor_copy(out=cw[:].rearrange("p c j -> p (c j)"), in_=pn[0][:, 0:48]), r=[pn[0]], w=[cw])
            for i in range(2):
                S.op("pool", lambda e: e.memset(xp[i][:, 0:3], 0.0), w=[xp[i]])
            for c_ in range(12):
                b = c_ % 2
                S.dma("sp", xp[b][:, 3:], X["QKVB"][c_ * 128:(c_ + 1) * 128, :], r=["QKVB"], w=[xp[b]])
                S.op("dve", lambda e: e.tensor_scalar(out=y[b][:], in0=xp[b][:, 0:T], scalar1=cw[:, c_, 0:1], scalar2=None,
                                                       op0=ALU.mult), r=[xp[b], cw], w=[y[b]])
                for j in range(1, 4):
                    S.op("dve", lambda e: e.scalar_tensor_tensor(out=y[b][:], in0=xp[b][:, j:j + T], scalar=cw[:, c_, j:j + 1],
                                                                 in1=y[b][:], op0=ALU.mult, op1=ALU.add), r=[xp[b], cw, y[b]], w=[y[b]])
                S.op("act", lambda e: e.activation(out=y[b][:], in_=y[b][:], func=AF.Silu), r=[y[b]], w=[y[b]])
                if c_ < 8:
                    S.op("pool", lambda e: e.tensor_tensor(out=sq[:], in0=y[b][:], in1=y[b][:], op=ALU.mult), r=[y[b]], w=[sq])
                    for g in range(8):
                        p_ = pn[g % 2]
                        S.op("pe", lambda e: e.matmul(p_[:], self.onesf[:], sq[:, g * 512:(g + 1) * 512], start=True, stop=True),
                             r=[self.onesf, sq], w=[p_])
                        S.op("act", lambda e: e.activation(out=rn[:, g * 512:(g + 1) * 512], in_=p_[:], func=AF.Sqrt,
                                                           bias=self.epsln[:, 1:2]), r=[p_, self.epsln], w=[rn])
                    S.op("dve", lambda e: e.reciprocal(out=rn[:], in_=rn[:]), r=[rn], w=[rn])
                    S.op("dve", lambda e: e.tensor_tensor(out=y[b][:], in0=y[b][:], in1=rn[:], op=ALU.mult), r=[y[b], rn], w=[y[b]])
                S.dma("pool", X["QKVB"][c_ * 128:(c_ + 1) * 128, :], y[b][:], r=[y[b]], w=["QKVB"])
        S.barrier()
        if cstop <= 1:
            return
        with ExitStack() as es:
            sb = lambda n, s, d: es.enter_context(nc.sbuf_tensor("%s_L%d" % (n, l), s, d))
            ps = lambda n, s, d: es.enter_context(nc.psum_tensor("%s_L%d" % (n, l), s, d))
            ab = sb("c_ab", [128, NT, 8], F32)
            dtb = sb("c_dtb", [128, 4], F32)
            nea = sb("c_nea", [128, 4], F32)
            one1 = sb("c_one1", [128, 1], F32)
            LA = sb("c_LA", [128, NT, 4], F32)
            nbeta = sb("c_nbeta", [128, NT, 4], F32)
            beta = sb("c_beta", [128, NT, 4], F32)
            G = sb("c_G", [128, 128], F32)
            EGn = sb("c_EGn", [128, 128], F32)
            KD = sb("c_KD", [128, 128], F32)
            EGL = [sb("c_EGL%d" % j, [128, 128], F32) for j in range(2)]
            gn = sb("c_gn", [128, 128], F32)
            qkv = [sb("c_qkv%d" % i, [128, 12, 128], F32) for i in range(2)]
            zt = [sb("c_zt%d" % i, [128, 512], F32) for i in range(2)]
            vt = sb("c_vt", [128, 4, 128], F32)
            kd = sb("c_kd", [128, 4, 128], F32)
            O = sb("c_O", [128, 4, 128], F32)
            ob = [sb("c_ob%d" % i, [128, 512], BF16) for i in range(2)]
            ss = sb("c_ss", [128, 4], F32)
            junk = sb("c_junk", [128, 128], F32)
            Sh = [sb("c_S%d" % h, [128, 128], F32) for h in range(4)]
            Dg = [sb("c_Dg%d" % h, [128, 128], F32) for h in range(4)]
            t1 = [sb("c_t1%d" % h, [128, 128], F32) for h in range(4)]
            decT = [sb("c_dec%d" % h, [128, 128], F32) for h in range(4)]
            EGb = [sb("c_EGb%d" % h, [128, 128], F32) for h in range(4)]
            qeg = [sb("c_qeg%d" % h, [128, 128], F32) for h in range(4)]
            aT = [sb("c_aT%d" % h, [128, 128], F32) for h in range(4)]
            AT = [[sb("c_AT%d_%d" % (h, i), [128, 128], F32) for i in range(2)] for h in range(4)]
            A = [[sb("c_A%d_%d" % (h, i), [128, 128], F32) for i in range(2)] for h in range(4)]
            Xh = [sb("c_X%d" % h, [128, 128], F32) for h in range(4)]
            R = [sb("c_R%d" % h, [128, 128], F32) for h in range(4)]
            vn = [sb("c_vn%d" % h, [128, 128], F32) for h in range(4)]
            class Sub:
                def __init__(self, tile, i):
                    self.ap = tile[:, i, :]
                    self.name = tile.name

                def __getitem__(self, k):
                    return self.ap[k]

            ppb = [ps("c_pp%d" % i, [128, 4, 128], F32) for i in range(4)]
            pp = [Sub(ppb[i % 4], i // 4) for i in range(16)]
            pdb = [ps("c_pd%d" % i, [128, 128], F32) for i in range(2)]
            ptv = ps("c_ptv", [128, 4, 128], F32)
            ptk = ps("c_ptk", [128, 4, 128], F32)
            npp = [0]

            def P():
                npp[0] += 1
                return pp[npp[0] % 16]

            S.dma("sp", ab[:], X["AB"].rearrange("(tt p) c -> p tt c", p=128), r=["AB"], w=[ab])
            S.dma("sp", dtb[:], I["dt_bias"][l:l + 1, :].partition_broadcast(128), w=[dtb])
            S.dma("sp", nea[:], I["a_log"][l:l + 1, :].partition_broadcast(128), w=[nea])
            S.dma("sp", gn[:], I["gdn_norm"][l:l + 1, :].partition_broadcast(128), w=[gn])
            S.op("pool", lambda e: e.memset(one1[:], 1.0), w=[one1])
            for h in range(4):
                S.op("pool", lambda e: e.memset(Sh[h][:], 0.0), w=[Sh[h]])
                S.op("pool", lambda e: e.memset(vn[h][:], 0.0), w=[vn[h]])
                S.op("pool", lambda e: e.memset(R[h][:], 0.0), w=[R[h]])
            S.op("act", lambda e: e.activation(out=nea[:], in_=nea[:], func=AF.Exp), r=[nea], w=[nea])
            S.op("dve", lambda e: e.tensor_scalar(out=nea[:], in0=nea[:], scalar1=-1.0, scalar2=None, op0=ALU.mult), r=[nea], w=[nea])
            S.op("dve", lambda e: e.tensor_tensor(out=LA[:], in0=ab[:, :, 0:4], in1=dtb[:].unsqueeze(1).to_broadcast([128, NT, 4]),
                                                   op=ALU.add), r=[ab, dtb], w=[LA])
            S.op("act", lambda e: e.activation(out=LA[:], in_=LA[:], func=AF.Exp), r=[LA], w=[LA])
            S.op("act", lambda e: e.activation(out=LA[:], in_=LA[:], func=AF.Ln, bias=one1[:, 0:1]), r=[LA, one1], w=[LA])
            S.op("dve", lambda e: e.tensor_tensor(out=LA[:], in0=LA[:], in1=nea[:].unsqueeze(1).to_broadcast([128, NT, 4]),
                                                   op=ALU.mult), r=[LA, nea], w=[LA])
            S.op("act", lambda e: e.activation(out=beta[:], in_=ab[:, :, 4:8], func=AF.Sigmoid), r=[ab], w=[beta])
            S.op("dve", lambda e: e.tensor_scalar(out=nbeta[:], in0=beta[:], scalar1=-1.0, scalar2=None, op0=ALU.mult),
                 r=[beta], w=[nbeta])
            LA2 = LA[:].rearrange("p t h -> p (t h)")
            p1, p2, p3, p4 = P(), P(), P(), P()
            S.op("pe", lambda e: e.matmul(p1[:], UTc, LA2, start=True, stop=True), r=[self.gdc, LA], w=[p1])
            S.op("pe", lambda e: e.matmul(p2[:], BON, LA2, start=True, stop=True), r=[self.gdc, LA], w=[p2])
            S.op("pe", lambda e: e.matmul(p3[:], H0, LA2, start=True, stop=True), r=[self.gdc, LA], w=[p3])
            S.op("pe", lambda e: e.matmul(p4[:], H1, LA2, start=True, stop=True), r=[self.gdc, LA], w=[p4])
            S.op("dve", lambda e: e.tensor_copy(out=G[:], in_=p1[:]), r=[p1], w=[G])
            S.op("act", lambda e: e.activation(out=EGn[:], in_=p1[:], func=AF.Exp), r=[p1], w=[EGn])
            S.op("dve", lambda e: e.tensor_scalar(out=EGn[:], in0=EGn[:], scalar1=-1.0, scalar2=None, op0=ALU.mult), r=[EGn], w=[EGn])
            S.op("dve", lambda e: e.tensor_tensor(out=KD[:], in0=p2[:], in1=G[:], op=ALU.subtract), r=[p2, G], w=[KD])
            S.op("act", lambda e: e.activation(out=KD[:], in_=KD[:], func=AF.Exp), r=[KD], w=[KD])
            S.op("act", lambda e: e.activation(out=EGL[0][:], in_=p3[:], func=AF.Exp), r=[p3], w=[EGL[0]])
            S.op("act", lambda e: e.activation(out=EGL[1][:], in_=p4[:], func=AF.Exp), r=[p4], w=[EGL[1]])
            qsrc = X["QKVB"].rearrange("(c p) t -> p c t", p=128)
            for tt in range(NT):
                if tt >= int(os.environ.get("CTILES", "32")) or (cstop <= 2 and tt >= 2):
                    break
                b = tt % 2
                ts = slice(tt * 128, (tt + 1) * 128)
                S.dma("sp", qkv[b][:], qsrc[:, :, ts], r=["QKVB"], w=[qkv[b]])
                S.dma("sp", zt[b][:], X["ZB"][ts, :], r=["ZB"], w=[zt[b]])
                for h in range(4):
                    S.op("pe", lambda e: e.transpose(out=ptv[:, h, :], in_=qkv[b][:, 8 + h, :], identity=self.ident[:]),
                         r=[qkv[b], self.ident], w=[ptv])
                    S.op("pe", lambda e: e.transpose(out=ptk[:, h, :], in_=qkv[b][:, 4 + h, :], identity=self.ident[:]),
                         r=[qkv[b], self.ident], w=[ptk])
                S.op("act", lambda e: e.activation(out=vt[:], in_=ptv[:], func=AF.Copy), r=[ptv], w=[vt])
                for h in range(4):
                    col = tt * 4 + h
                    S.op("dve", lambda e: e.tensor_scalar(out=kd[:, h, :], in0=ptk[:, h, :], scalar1=KD[:, col:col + 1], scalar2=None,
                                                           op0=ALU.mult), r=[ptk, KD], w=[kd])
                for h in range(4):
                    col = tt * 4 + h
                    gcol = G[:, col:col + 1]
                    kTt = qkv[b][:, 4 + h, :]
                    qTt = qkv[b][:, h, :]
                    S.op("pool", lambda e: e.tensor_scalar(out=Dg[h][:], in0=self.ident[:], scalar1=gcol, scalar2=None, op0=ALU.mult),
                         r=[self.ident, G], w=[Dg[h]])
                    pg_, pkk, pqk = P(), P(), P()
                    S.op("pe", lambda e: e.matmul(pg_[:], self.onesf[:], Dg[h][:], start=True, stop=True), r=[self.onesf, Dg[h]], w=[pg_])
                    S.op("dve", lambda e: e.tensor_scalar(out=t1[h][:], in0=pg_[:], scalar1=gcol, scalar2=0.0, op0=ALU.subtract, op1=ALU.min),
                         r=[pg_, G], w=[t1[h]])
                    S.op("act", lambda e: e.activation(out=decT[h][:], in_=t1[h][:], func=AF.Exp), r=[t1[h]], w=[decT[h]])
                    S.op("act", lambda e: e.activation(out=EGb[h][:], in_=pg_[:], func=AF.Exp), r=[pg_], w=[EGb[h]])
                    S.op("dve", lambda e: e.scalar_tensor_tensor(out=qeg[h][:], in0=qTt, scalar=QS, in1=EGb[h][:], op0=ALU.mult, op1=ALU.mult),
                         r=[qkv[b], EGb[h]], w=[qeg[h]])
                    S.op("pe", lambda e: e.matmul(pkk[:], kTt, kTt, start=True, stop=True), r=[qkv[b]], w=[pkk])
                    S.op("pe", lambda e: e.matmul(pqk[:], kTt, qTt, start=True, stop=True), r=[qkv[b]], w=[pqk])
                    S.op("dve", lambda e: e.scalar_tensor_tensor(out=AT[h][0][:], in0=pkk[:], scalar=nbeta[:, tt, h:h + 1], in1=decT[h][:],
                                                                 op0=ALU.mult, op1=ALU.mult), r=[pkk, nbeta, decT[h]], w=[AT[h][0]])
                    S.op("pool", lambda e: e.tensor_tensor(out=AT[h][0][:], in0=AT[h][0][:], in1=UTs, op=ALU.mult),
                         r=[AT[h][0], self.gdc], w=[AT[h][0]])
                    S.op("dve", lambda e: e.scalar_tensor_tensor(out=aT[h][:], in0=pqk[:], scalar=QS, in1=decT[h][:],
                                                                 op0=ALU.mult, op1=ALU.mult), r=[pqk, decT[h]], w=[aT[h]])
                    S.op("pool", lambda e: e.tensor_tensor(out=aT[h][:], in0=aT[h][:], in1=UTc, op=ALU.mult), r=[aT[h], self.gdc], w=[aT[h]])
                    pt_ = P()
                    S.op("pe", lambda e: e.transpose(out=pt_[:], in_=AT[h][0][:], identity=self.ident[:]), r=[AT[h][0], self.ident], w=[pt_])
                    S.op("act", lambda e: e.activation(out=A[h][0][:], in_=pt_[:], func=AF.Copy), r=[pt_], w=[A[h][0]])
                    S.op("pool", lambda e: e.tensor_tensor(out=Xh[h][:], in0=AT[h][0][:], in1=self.ident[:], op=ALU.add),
                         r=[AT[h][0], self.ident], w=[Xh[h]])
                for k in range(5):
                    cu, nx = k % 2, (k + 1) % 2
                    for h in range(4):
                        pa = P()
                        S.op("pe", lambda e: e.matmul(pa[:], AT[h][cu][:], A[h][cu][:], start=True, stop=True), r=[AT[h][cu], A[h][cu]], w=[pa])
                        S.op("act", lambda e: e.activation(out=A[h][nx][:], in_=pa[:], func=AF.Copy), r=[pa], w=[A[h][nx]])
                        if k < 4:
                            pb_ = P()
                            S.op("pe", lambda e: e.matmul(pb_[:], A[h][cu][:], AT[h][cu][:], start=True, stop=True),
                                 r=[AT[h][cu], A[h][cu]], w=[pb_])
                            S.op("dve", lambda e: e.tensor_copy(out=AT[h][nx][:], in_=pb_[:]), r=[pb_], w=[AT[h][nx]])
                    for h in range(4):
                        px = P()
                        S.op("pe", lambda e: e.matmul(px[:], A[h][nx][:], Xh[h][:], start=True, stop=True), r=[A[h][nx], Xh[h]], w=[px])
                        S.op("dve", lambda e: e.tensor_tensor(out=Xh[h][:], in0=px[:], in1=Xh[h][:], op=ALU.add), r=[px, Xh[h]], w=[Xh[h]])
                for j in range(2):
                    rs = slice(64 * j, 64 * j + 64)
                    pk_ = [P() for _ in range(4)]
                    for h in range(4):
                        S.op("pe", lambda e: e.matmul(pk_[h][:], qkv[b][:, 4 + h, :], Sh[h][:], start=True, stop=True),
                             r=[qkv[b], Sh[h]], w=[pk_[h]])
                    for h in range(4):
                        col = tt * 4 + h
                        S.op("dve", lambda e: e.scalar_tensor_tensor(out=R[h][rs, :], in0=pk_[h][rs, :], scalar=EGn[rs, col:col + 1],
                                                                     in1=vt[rs, h, :], op0=ALU.mult, op1=ALU.add),
                             r=[pk_[h], EGn, vt], w=[R[h]])
                    py_ = [P() for _ in range(4)]
                    for h in range(4):
                        S.op("pe", lambda e: e.matmul(py_[h][:], Xh[h][:], R[h][:], start=True, stop=True), r=[Xh[h], R[h]], w=[py_[h]])
                    for h in range(4):
                        S.op("dve", lambda e: e.tensor_scalar(out=vn[h][rs, :], in0=py_[h][rs, :], scalar1=beta[rs, tt, h:h + 1], scalar2=None,
                                                               op0=ALU.mult), r=[py_[h], beta], w=[vn[h]])
                    po_ = [P() for _ in range(4)]
                    for h in range(4):
                        S.op("pe", lambda e: e.matmul(po_[h][:], qeg[h][:], Sh[h][:], start=True, stop=False), r=[qeg[h], Sh[h]], w=[po_[h]])
                        S.op("pe", lambda e: e.matmul(po_[h][:], aT[h][:], vn[h][:], start=False, stop=True), r=[aT[h], vn[h]], w=[po_[h]])
                    for h in range(4):
                        S.op("act", lambda e: e.activation(out=O[rs, h, :], in_=po_[h][rs, :], func=AF.Copy), r=[po_[h]], w=[O])
                    pd_ = [pdb[h % 2] for h in range(4)]
                    for h in range(4):
                        S.op("pe", lambda e: e.matmul(pd_[h][:], kd[rs, h, :], vn[h][rs, :], start=True, stop=True), r=[kd, vn[h]], w=[pd_[h]])
                        col = tt * 4 + h
                        S.op("dve", lambda e: e.scalar_tensor_tensor(out=Sh[h][:], in0=Sh[h][:], scalar=EGL[j][:, col:col + 1], in1=pd_[h][:],
                                                                     op0=ALU.mult, op1=ALU.add), r=[Sh[h], EGL[j], pd_[h]], w=[Sh[h]])
                for h in range(4):
                    S.op("act", lambda e: e.activation(out=junk[:], in_=O[:, h, :], func=AF.Square, accum_out=ss[:, h:h + 1]),
                         r=[O], w=[junk, ss])
                S.op("act", lambda e: e.activation(out=ss[:], in_=ss[:], func=AF.Sqrt, scale=1.0 / 128, bias=self.epsln[:, 1:2]),
                     r=[ss, self.epsln], w=[ss])
                S.op("dve", lambda e: e.reciprocal(out=ss[:], in_=ss[:]), r=[ss], w=[ss])
                S.op("dve", lambda e: e.tensor_tensor(out=O[:], in0=O[:], in1=ss[:].unsqueeze(2).to_broadcast([128, 4, 128]), op=ALU.mult),
                     r=[O, ss], w=[O])
                S.op("dve", lambda e: e.tensor_tensor(out=O[:], in0=O[:], in1=gn[:].unsqueeze(1).to_broadcast([128, 4, 128]), op=ALU.mult),
                     r=[O, gn], w=[O])
                S.op("act", lambda e: e.activation(out=zt[b][:], in_=zt[b][:], func=AF.Silu), r=[zt[b]], w=[zt[b]])
                S.op("dve", lambda e: e.tensor_tensor(out=ob[b][:], in0=O[:].rearrange("p h e -> p (h e)"), in1=zt[b][:], op=ALU.mult),
                     r=[O, zt[b]], w=[ob[b]])
                S.dma("pool", X["OB"][ts, :], ob[b][:], r=[ob[b]], w=["OB"])

_CACHE = {}


def _get_prog(nlayers=DEPTH, debug=(), stop_after=None):
    key = (nlayers, tuple(sorted(debug)), stop_after)
    if key not in _CACHE:
        p = Prog(nlayers, debug, stop_after)
        p.build()
        _CACHE[key] = p
    return _CACHE[key]


def make_in_maps(inputs, ncores=8, nlayers=DEPTH, layer0=0, xs=None):
    consts = host_constants(np.asarray(inputs["rel_bias"], np.float32))
    shared = {}
    for k, v in inputs.items():
        if k in ("x", "rel_bias"):
            continue
        a = np.ascontiguousarray(np.asarray(v, np.float32)[layer0:layer0 + nlayers])
        if k in ("w_uk", "w_uv"):
            a = a.reshape(nlayers, 256, 512)
        shared[k] = a
    shared.update(consts)
    x = np.asarray(inputs["x"], np.float32) if xs is None else xs
    maps = []
    for c in range(ncores):
        m = dict(shared)
        m["x"] = np.ascontiguousarray(x[c])
        maps.append(m)
    return maps


def kernel(**inputs):
    p = _get_prog(DEPTH)
    maps = make_in_maps(inputs, 8, DEPTH)
    res = run_bass_kernel_spmd(p.nc, maps, core_ids=list(range(8)))
    return np.stack([np.asarray(r["out"], np.float32) for r in res.results], axis=0)
```

```python
from contextlib import ExitStack
import math
import numpy as np
import ml_dtypes
import concourse.bass as bass
import concourse.mybir as mybir
from concourse.bass_utils import run_bass_kernel_spmd

F32 = mybir.dt.float32
BF16 = mybir.dt.bfloat16
AF = mybir.ActivationFunctionType
ALU = mybir.AluOpType
AX = mybir.AxisListType

T = 4096
D = 1024
NT = T // 128
DEPTH = 4
INW = 8016
DFF = 4096
ALPHA = (2 * DEPTH) ** 0.25
LN_EPS = 1e-5
RMS_EPS = 1e-6
NEG = -30000.0
IDX_WEIGHT_SCALE = 8 ** -0.5 * 64 ** -0.5

C_QA, C_KA, C_VA, C_QB, C_ZB, C_AB, C_QC, C_CKV, C_QI, C_KI, C_WI, C_GA = (
    0, 512, 1024, 1536, 3072, 3584, 3592, 4104, 4360, 4872, 4936, 4944)


class Sched:
    ENGS = ("pe", "act", "dve", "pool", "sp")
    NDS = 48

    def __init__(self, nc, es):
        self.nc = nc
        self.es = es
        self.epoch = 1
        self.eng = {"pe": nc.tensor, "act": nc.scalar, "dve": nc.vector, "pool": nc.gpsimd, "sp": nc.sync}
        self.sem = {e: es.enter_context(nc.semaphore("s_" + e)) for e in self.ENGS}
        self.dsem = [es.enter_context(nc.semaphore("d_%d" % i)) for i in range(self.NDS)]
        self.cnt = {e: 0 for e in self.ENGS}
        self.seen = {e: {} for e in self.ENGS}
        self.lastw = {}
        self.readers = {}
        self.ndma = 0
        self.ndq = {}
        half = self.NDS // 2
        self.dpool = {"sp": (0, half), "pool": (half, self.NDS - half)}
        self.dlast = {}
        self.ninst = 0
        self.scratch = es.enter_context(nc.sbuf_tensor("sched_scratch", [128, 1], F32))

    @staticmethod
    def _key(a):
        if isinstance(a, (str, tuple)):
            return a
        return a.name

    def _semof(self, sk):
        if isinstance(sk, tuple):
            return self.dsem[sk[1]]
        return self.sem[sk]

    def _deps(self, rk, wk):
        toks = []
        for k in rk:
            t = self.lastw.get(k)
            if t:
                toks.append(t)
        for k in wk:
            t = self.lastw.get(k)
            if t:
                toks.append(t)
            toks.extend(self.readers.get(k, {}).items())
        return toks

    def _wait(self, eng, toks):
        need = {}
        for (sk, v) in toks:
            if sk == "pe" and eng == "pe":
                continue
            if self.seen[eng].get(sk, 0) >= v:
                continue
            if need.get(sk, 0) < v:
                need[sk] = v
        e = self.eng[eng]
        for sk, v in need.items():
            self.seen[eng][sk] = v
            e.wait_ge(self._semof(sk), v)
            self.ninst += 1

    def _record(self, tok, rk, wk):
        for k in rk:
            d = self.readers.setdefault(k, {})
            if d.get(tok[0], 0) < tok[1]:
                d[tok[0]] = tok[1]
        for k in wk:
            self.lastw[k] = tok
            self.readers[k] = {}

    def op(self, eng, fn, r=(), w=()):
        rk = [self._key(a) for a in r]
        wk = [self._key(a) for a in w]
        self._wait(eng, self._deps(rk, wk))
        ins = fn(self.eng[eng])
        self.cnt[eng] += 1
        ins.then_inc(self.sem[eng], 1)
        self.ninst += 1
        tok = (eng, self.cnt[eng])
        self._record(tok, rk, wk)
        return tok

    def dma(self, qeng, out, in_, r=(), w=(), **kw):
        rk = [self._key(a) for a in r]
        wk = [self._key(a) for a in w]
        lo, n = self.dpool[qeng]
        k = self.ndq.get(qeng, 0)
        self.ndq[qeng] = k + 1
        idx = lo + k % n
        val = 16 * (k // n + 1)
        self.ndma += 1
        toks = self._deps(rk, wk)
        if val > 16:
            toks.append((("d", idx), val - 16))
        self._wait(qeng, toks)
        self.eng[qeng].dma_start(out=out, in_=in_, **kw).then_inc(self.dsem[idx], 16)
        self.ninst += 1
        tok = (("d", idx), val)
        self.dlast[idx] = val
        self._record(tok, rk, wk)
        return tok

    def barrier(self, new_epoch=False):
        toks = [(e, self.cnt[e]) for e in self.ENGS if self.cnt[e] > 0]
        toks += [(("d", i), v) for i, v in self.dlast.items()]
        for e in self.ENGS:
            self._wait(e, toks)
        self.lastw.clear()
        self.readers.clear()
        assert max(self.cnt.values()) < 30000, self.cnt
        if new_epoch:
            self.sem = {e: self.es.enter_context(self.nc.semaphore("s%d_%s" % (self.epoch, e))) for e in self.ENGS}
            self.epoch += 1
            self.cnt = {e: 0 for e in self.ENGS}
            for e in self.ENGS:
                for k in self.ENGS:
                    self.seen[e].pop(k, None)


def host_constants(rel_bias):
    def bucket(d):
        d = np.maximum(d, 0)
        large = 16 + (np.log(np.maximum(d, 16).astype(np.float32) / 16) / math.log(128 / 16) * 16).astype(np.int32)
        return np.where(d < 16, d, np.minimum(large, 31))
    j = np.arange(128)[:, None]
    c = np.arange(1280)[None, :]
    dist = c - j - 512
    b = bucket(dist)
    W = np.empty((128, 16, 1280), np.float32)
    for h in range(16):
        W[:, h, :] = np.where(dist >= 0, rel_bias[b, h], NEG)
    ident = np.eye(128, dtype=np.float32)
    ltri = np.tril(np.ones((128, 128), np.float32))
    s_ = np.arange(128)[:, None]
    c_ = np.arange(128)[None, :]
    same = (s_ // 64) == (c_ // 64)
    gd = np.stack([same & (s_ <= c_), same & (s_ < c_), same, np.broadcast_to(s_ < 64, (128, 128)),
                   np.broadcast_to(s_ >= 64, (128, 128))], axis=1).astype(np.float32)
    return {"c_toep": W.astype(ml_dtypes.bfloat16), "c_ident": ident, "c_ltri": ltri, "c_gdn": np.ascontiguousarray(gd)}


class Prog:
    def __init__(self, nlayers=DEPTH, debug=(), stop_after=None, inject=(), phases="ABCDEF"):
        self.nlayers = nlayers
        self.inject = set(inject)
        self.phases = phases
        self.debug = set(debug)
        self.stop_after = stop_after
        self.nc = bass.Bass("TRN2", target_bir_lowering=False)
        self.out_names = []

    def dram_in(self, name, shape, dt=F32):
        return self.nc.dram_tensor(name, list(shape), dt, kind="ExternalInput").ap()

    def dram_scr(self, name, shape, dt=F32):
        if name in self.inject:
            return self.nc.dram_tensor(name, list(shape), dt, kind="ExternalInput").ap()
        if name in self.debug:
            self.out_names.append(name)
            return self.nc.dram_tensor(name, list(shape), dt, kind="ExternalOutput").ap()
        return self.nc.dram_tensor(name, list(shape), dt).ap()

    def build(self):
        nc = self.nc
        L = self.nlayers
        I = {}
        I["x"] = self.dram_in("x", [T, D])
        I["w_in"] = self.dram_in("w_in", [L, D, INW])
        I["conv_w"] = self.dram_in("conv_w", [L, 4, 1536])
        I["a_log"] = self.dram_in("a_log", [L, 4])
        I["dt_bias"] = self.dram_in("dt_bias", [L, 4])
        I["gdn_norm"] = self.dram_in("gdn_norm", [L, 128])
        I["kv_norm"] = self.dram_in("kv_norm", [L, 256])
        I["idx_k_ln_g"] = self.dram_in("idx_k_ln_g", [L, 64])
        I["idx_k_ln_b"] = self.dram_in("idx_k_ln_b", [L, 64])
        I["w_uk"] = self.dram_in("w_uk", [L, 256, 512])
        I["w_uv"] = self.dram_in("w_uv", [L, 256, 512])
        I["w_branch_a"] = self.dram_in("w_branch_a", [L, 512, D])
        I["w_branch_b"] = self.dram_in("w_branch_b", [L, 512, D])
        I["w_branch_c"] = self.dram_in("w_branch_c", [L, 512, D])
        I["w_out"] = self.dram_in("w_out", [L, D, D])
        I["ln1_g"] = self.dram_in("ln1_g", [L, D])
        I["ln1_b"] = self.dram_in("ln1_b", [L, D])
        I["w_up"] = self.dram_in("w_up", [L, D, DFF])
        I["w_down"] = self.dram_in("w_down", [L, DFF, D])
        I["ln2_g"] = self.dram_in("ln2_g", [L, D])
        I["ln2_b"] = self.dram_in("ln2_b", [L, D])
        I["c_toep"] = self.dram_in("c_toep", [128, 16, 1280], BF16)
        I["c_ident"] = self.dram_in("c_ident", [128, 128])
        I["c_ltri"] = self.dram_in("c_ltri", [128, 128])
        I["c_gdn"] = self.dram_in("c_gdn", [128, 5, 128])
        self.I = I
        self.out = nc.dram_tensor("out", [T, D], F32, kind="ExternalOutput").ap()
        X = {}
        X["QA"] = self.dram_scr("QA", [512, T], BF16)
        X["KA"] = self.dram_scr("KA", [512, T], BF16)
        X["QKVB"] = self.dram_scr("QKVB", [1536, T], F32)
        X["QC"] = self.dram_scr("QC", [512, T], BF16)
        X["QI"] = self.dram_scr("QI", [512, T], BF16)
        X["VA"] = self.dram_scr("VA", [T, 520], BF16)
        X["ZB"] = self.dram_scr("ZB", [T, 512], F32)
        X["AB"] = self.dram_scr("AB", [T, 8], F32)
        X["CKV"] = self.dram_scr("CKV", [T, 256], F32)
        X["KIWI"] = self.dram_scr("KIWI", [T, 72], F32)
        X["GATES"] = self.dram_scr("GATES", [T, 3072], F32)
        X["OA"] = self.dram_scr("OA", [T, 512], BF16)
        X["OB"] = self.dram_scr("OB", [T, 512], BF16)
        X["OC"] = self.dram_scr("OC", [T, 512], BF16)
        X["X1"] = self.dram_scr("X1", [T, D], F32)
        X["XS"] = self.dram_scr("XS", [T, D], F32)
        self.X = X

        with ExitStack() as top:
            S = Sched(nc, top)
            self.S = S
            self.ident = top.enter_context(nc.sbuf_tensor("ident", [128, 128], F32))
            self.identb = top.enter_context(nc.sbuf_tensor("identb", [128, 128], BF16))
            self.i30k = top.enter_context(nc.sbuf_tensor("i30k", [128, 128], BF16))
            S.dma("sp", self.ident[:], I["c_ident"], w=[self.ident])
            S.op("dve", lambda e: e.tensor_copy(out=self.identb[:], in_=self.ident[:]), r=[self.ident], w=[self.identb])
            S.op("dve", lambda e: e.tensor_scalar(out=self.i30k[:], in0=self.ident[:], scalar1=-NEG, scalar2=None,
                                                   op0=ALU.mult), r=[self.ident], w=[self.i30k])
            self.ltri = top.enter_context(nc.sbuf_tensor("ltri", [128, 128], F32))
            S.dma("sp", self.ltri[:], I["c_ltri"], w=[self.ltri])
            self.gdc = top.enter_context(nc.sbuf_tensor("gdc", [128, 5, 128], F32))
            S.dma("sp", self.gdc[:], I["c_gdn"], w=[self.gdc])
            self.onesf = top.enter_context(nc.sbuf_tensor("onesf", [128, 128], F32))
            S.op("pool", lambda e: e.memset(self.onesf[:], 1.0), w=[self.onesf])
            self.onesb = top.enter_context(nc.sbuf_tensor("onesb", [128, 128], BF16))
            S.op("pool", lambda e: e.memset(self.onesb[:], 1.0), w=[self.onesb])
            self.epsln = top.enter_context(nc.sbuf_tensor("epsln", [128, 2], F32))
            S.op("pool", lambda e: e.memset(self.epsln[:, 0:1], LN_EPS), w=[self.epsln])
            S.op("pool", lambda e: e.memset(self.epsln[:, 1:2], RMS_EPS), w=[self.epsln])
            for l in range(self.nlayers):
                xin = I["x"] if l == 0 else X["XS"]
                xout = self.out if l == self.nlayers - 1 else X["XS"]
                for ph in "ABCDEF":
                    if ph not in self.phases:
                        continue
                    if ph == "A":
                        self.phase_a(l, xin)
                    elif ph == "B":
                        self.phase_b(l)
                    elif ph == "C":
                        self.phase_c(l)
                    elif ph == "D":
                        self.phase_d(l)
                    elif ph == "E":
                        self.phase_e(l, xin)
                    elif ph == "F":
                        self.phase_f(l, xout)
                    S.barrier(new_epoch=(ph in "CF"))
            S.barrier()
        return nc

    def phase_a(self, l, xin):
        nc, S, I, X = self.nc, self.S, self.I, self.X
        with ExitStack() as es:
            sb = lambda n, s, d: es.enter_context(nc.sbuf_tensor("%s_L%d" % (n, l), s, d))
            ps = lambda n, s, d: es.enter_context(nc.psum_tensor("%s_L%d" % (n, l), s, d))
            xT = sb("a_xT", [128, 8, T], BF16)
            xs = [sb("a_xs%d" % i, [128, D], F32) for i in range(2)]
            xb = [sb("a_xb%d" % i, [128, D], BF16) for i in range(2)]
            wst = [sb("a_wst%d" % i, [128, 8, 512], F32) for i in range(2)]
            wb = [sb("a_wb%d" % i, [128, 8, 512], BF16) for i in range(2)]
            stf = [sb("a_stf%d" % i, [128, T], F32) for i in range(2)]
            stb = [sb("a_stb%d" % i, [128, T], BF16) for i in range(2)]
            ptr = [ps("a_ptr%d" % i, [128, 8, 128], BF16) for i in range(2)]
            pacc = [ps("a_pacc%d" % i, [128, 512], F32) for i in range(4)]
            stv = [sb("a_stv%d" % i, [128, 8, 65], BF16) for i in range(2)]
            for i in range(2):
                S.op("pool", lambda e: e.memset(stv[i][:], 1.0), w=[stv[i]])

            for tt in range(NT):
                b = tt % 2
                S.dma("sp", xs[b][:], xin[tt * 128:(tt + 1) * 128, :], w=[xs[b]])
                S.op("pool", lambda e: e.tensor_copy(out=xb[b][:], in_=xs[b][:]), r=[xs[b]], w=[xb[b]])
                for kc in range(8):
                    S.op("pe", lambda e: e.transpose(out=ptr[b][:, kc, :], in_=xb[b][:, kc * 128:(kc + 1) * 128],
                                                     identity=self.identb[:]), r=[xb[b], self.identb], w=[ptr[b]])
                S.op("dve", lambda e: e.tensor_copy(out=xT[:, :, tt * 128:(tt + 1) * 128], in_=ptr[b][:]),
                     r=[ptr[b]], w=[xT])

            wsrc = I["w_in"][l].rearrange("(kc p) n -> p kc n", p=128)
            blocks = [
                ("FM", C_QA, 512, "QA", 0, 0.125), ("FM", C_KA, 512, "KA", 0, 1.0),
                ("FM", C_QB, 512, "QKVB", 0, 1.0), ("FM", C_QB + 512, 512, "QKVB", 512, 1.0),
                ("FM", C_QB + 1024, 512, "QKVB", 1024, 1.0),
                ("FM", C_QC, 512, "QC", 0, 0.125), ("FM", C_QI, 512, "QI", 0, 1.0),
                ("TM", C_VA, 512, "VA", 0, 1.0), ("TM", C_ZB, 512, "ZB", 0, 1.0), ("TM", C_AB, 8, "AB", 0, 1.0),
                ("TM", C_CKV, 256, "CKV", 0, 1.0), ("TM", C_KI, 72, "KIWI", 0, 1.0),
            ] + [("TM", C_GA + 512 * i, 512, "GATES", 512 * i, 1.0) for i in range(6)]
            ev = 0
            na = 0
            nst = 0
            def load_block(bi):
                _, c0_, ncol_, _, _, _ = blocks[bi]
                b_ = bi % 2
                S.dma("sp", wst[b_][:, :, 0:ncol_], wsrc[:, :, c0_:c0_ + ncol_], w=[wst[b_]])
                S.op("pool", lambda e: e.tensor_copy(out=wb[b_][:, :, 0:ncol_], in_=wst[b_][:, :, 0:ncol_]),
                     r=[wst[b_]], w=[wb[b_]])

            load_block(0)
            for bi, (kind, c0, ncol, dname, doff, scale) in enumerate(blocks):
                b = bi % 2
                dst = X[dname]
                isbf = dname in ("QA", "KA", "QC", "QI", "VA")
                if bi + 1 < len(blocks):
                    load_block(bi + 1)
                if kind == "FM":
                    for j in range(ncol // 128):
                        st = (stb if isbf else stf)[nst % 2]
                        nst += 1
                        for tg in range(8):
                            pa = pacc[na % 4]
                            na += 1
                            for kc in range(8):
                                S.op("pe", lambda e: e.matmul(pa[:], wb[b][:, kc, j * 128:(j + 1) * 128],
                                                              xT[:, kc, tg * 512:(tg + 1) * 512],
                                                              start=(kc == 0), stop=(kc == 7)),
                                     r=[wb[b], xT], w=[pa])
                            eng = "act" if ev % 2 == 0 else "dve"
                            ev += 1
                            if eng == "act":
                                S.op("act", lambda e: e.activation(out=st[:, tg * 512:(tg + 1) * 512], in_=pa[:],
                                                                   func=AF.Copy, scale=scale), r=[pa], w=[st])
                            else:
                                S.op("dve", lambda e: e.tensor_scalar(out=st[:, tg * 512:(tg + 1) * 512], in0=pa[:],
                                                                      scalar1=scale, scalar2=None, op0=ALU.mult),
                                     r=[pa], w=[st])
                        r0 = doff + j * 128
                        S.dma("pool", dst[r0:r0 + 128, :], st[:], r=[st], w=[dname])
                else:
                    for tt in range(NT):
                        pa = pacc[na % 4]
                        na += 1
                        for kc in range(8):
                            S.op("pe", lambda e: e.matmul(pa[:, 0:ncol], xT[:, kc, tt * 128:(tt + 1) * 128],
                                                          wb[b][:, kc, 0:ncol], start=(kc == 0), stop=(kc == 7)),
                                 r=[wb[b], xT], w=[pa])
                        if dname == "VA":
                            sv = stv[nst % 2]
                            nst += 1
                            S.op("act", lambda e: e.activation(out=sv[:, :, 0:64], in_=pa[:].rearrange("p (h d) -> p h d", h=8),
                                                               func=AF.Copy), r=[pa], w=[sv])
                            S.dma("pool", dst[tt * 128:(tt + 1) * 128, :], sv[:].rearrange("p h d -> p (h d)"), r=[sv], w=[dname])
                            continue
                        st = (stb if isbf else stf)[nst % 2]
                        nst += 1
                        eng = "act" if ev % 2 == 0 else "dve"
                        ev += 1
                        if eng == "act":
                            S.op("act", lambda e: e.activation(out=st[:, 0:ncol], in_=pa[:, 0:ncol], func=AF.Copy),
                                 r=[pa], w=[st])
                        else:
                            S.op("dve", lambda e: e.tensor_copy(out=st[:, 0:ncol], in_=pa[:, 0:ncol]), r=[pa], w=[st])
                        S.dma("pool", dst[tt * 128:(tt + 1) * 128, doff:doff + ncol], st[:, 0:ncol], r=[st], w=[dname])


    def layer_norm_tile(self, z, gt, bt, st1, junk, out):
        S = self.S
        S.op("dve", lambda e: e.tensor_reduce(out=st1[:, 0:1], in_=z[:], axis=AX.X, op=ALU.add), r=[z], w=[st1])
        S.op("dve", lambda e: e.tensor_scalar(out=st1[:, 1:2], in0=st1[:, 0:1], scalar1=-1.0 / D, scalar2=None,
                                               op0=ALU.mult), r=[st1], w=[st1])
        S.op("act", lambda e: e.activation(out=junk[:], in_=z[:], func=AF.Square, bias=st1[:, 1:2],
                                           accum_out=st1[:, 2:3]), r=[z, st1], w=[junk, st1])
        S.op("act", lambda e: e.activation(out=st1[:, 3:4], in_=st1[:, 2:3], func=AF.Sqrt, scale=1.0 / D,
                                           bias=self.epsln[:, 0:1]), r=[st1, self.epsln], w=[st1])
        S.op("dve", lambda e: e.reciprocal(out=st1[:, 4:5], in_=st1[:, 3:4]), r=[st1], w=[st1])
        S.op("dve", lambda e: e.tensor_scalar(out=z[:], in0=z[:], scalar1=st1[:, 1:2], scalar2=st1[:, 4:5],
                                               op0=ALU.add, op1=ALU.mult), r=[z, st1], w=[z])
        S.op("pool", lambda e: e.tensor_tensor(out=z[:], in0=z[:], in1=gt[:], op=ALU.mult), r=[z, gt], w=[z])
        S.op("pool", lambda e: e.tensor_tensor(out=out[:], in0=z[:], in1=bt[:], op=ALU.add), r=[z, bt], w=[out])

    def phase_e(self, l, xin):
        nc, S, I, X = self.nc, self.S, self.I, self.X
        with ExitStack() as es:
            sb = lambda n, s, d: es.enter_context(nc.sbuf_tensor("%s_L%d" % (n, l), s, d))
            ps = lambda n, s, d: es.enter_context(nc.psum_tensor("%s_L%d" % (n, l), s, d))
            wbr = sb("e_wbr", [128, 12, D], BF16)
            wout = sb("e_wout", [128, 8, D], BF16)
            wst = [sb("e_wst%d" % i, [128, 4, D], F32) for i in range(2)]
            gt = sb("e_g", [128, D], F32)
            bt = sb("e_b", [128, D], F32)
            obr = [sb("e_obr%d" % i, [128, 3, 512], BF16) for i in range(2)]
            obT = sb("e_obT", [128, 12, 128], BF16)
            sg = [sb("e_sg%d" % i, [128, 3072], F32) for i in range(2)]
            xs = [sb("e_xs%d" % i, [128, D], F32) for i in range(2)]
            y = sb("e_y", [128, D], F32)
            tmp = sb("e_tmp", [128, 512], F32)
            yb = sb("e_yb", [128, D], BF16)
            yT = sb("e_yT", [128, 8, 128], BF16)
            z = sb("e_z", [128, D], F32)
            junk = sb("e_junk", [128, D], F32)
            xo = [sb("e_xo%d" % i, [128, D], F32) for i in range(2)]
            st1 = sb("e_st1", [128, 8], F32)
            ptr = [ps("e_ptr%d" % i, [128, 8, 128], BF16) for i in range(2)]
            pacc = [ps("e_pacc%d" % i, [128, 512], F32) for i in range(4)]
            k = 0
            for bi, wn in enumerate(("w_branch_a", "w_branch_b", "w_branch_c")):
                S.dma("sp", wst[k % 2][:], I[wn][l].rearrange("(kc p) n -> p kc n", p=128), w=[wst[k % 2]])
                S.op("pool", lambda e: e.tensor_copy(out=wbr[:, bi * 4:(bi + 1) * 4, :], in_=wst[k % 2][:]),
                     r=[wst[k % 2]], w=[wbr])
                k += 1
            wo = I["w_out"][l].rearrange("(kc p) n -> p kc n", p=128)
            for hh in range(2):
                S.dma("sp", wst[k % 2][:], wo[:, hh * 4:(hh + 1) * 4, :], w=[wst[k % 2]])
                S.op("pool", lambda e: e.tensor_copy(out=wout[:, hh * 4:(hh + 1) * 4, :], in_=wst[k % 2][:]),
                     r=[wst[k % 2]], w=[wout])
                k += 1
            S.dma("sp", gt[:], I["ln1_g"][l:l + 1, :].partition_broadcast(128), w=[gt])
            S.dma("sp", bt[:], I["ln1_b"][l:l + 1, :].partition_broadcast(128), w=[bt])
            na = 0
            for tt in range(NT):
                b = tt % 2
                rows = slice(tt * 128, (tt + 1) * 128)
                for bi, nm in enumerate(("OA", "OB", "OC")):
                    S.dma("sp", obr[b][:, bi, :], X[nm][rows, :], r=[nm], w=[obr[b]])
                S.dma("sp", sg[b][:], X["GATES"][rows, :], r=["GATES"], w=[sg[b]])
                S.dma("sp", xs[b][:], xin[rows, :], r=["XS"], w=[xs[b]])
                S.op("act", lambda e: e.activation(out=sg[b][:], in_=sg[b][:], func=AF.Sigmoid), r=[sg[b]], w=[sg[b]])
                for c in range(12):
                    p = ptr[0] if c < 8 else ptr[1]
                    S.op("pe", lambda e: e.transpose(out=p[:, c % 8, :], in_=obr[b][:, c // 4, (c % 4) * 128:(c % 4 + 1) * 128],
                                                     identity=self.identb[:]), r=[obr[b], self.identb], w=[p])
                S.op("dve", lambda e: e.tensor_copy(out=obT[:, 0:8, :], in_=ptr[0][:]), r=[ptr[0]], w=[obT])
                S.op("dve", lambda e: e.tensor_copy(out=obT[:, 8:12, :], in_=ptr[1][:, 0:4, :]), r=[ptr[1]], w=[obT])
                for nh in range(2):
                    cs = slice(nh * 512, (nh + 1) * 512)
                    for bi in range(3):
                        pa = pacc[na % 4]
                        na += 1
                        for kc in range(4):
                            S.op("pe", lambda e: e.matmul(pa[:], obT[:, bi * 4 + kc, :], wbr[:, bi * 4 + kc, cs],
                                                          start=(kc == 0), stop=(kc == 3)), r=[obT, wbr], w=[pa])
                        gsl = sg[b][:, bi * 1024 + nh * 512: bi * 1024 + (nh + 1) * 512]
                        if bi == 0:
                            S.op("dve", lambda e: e.tensor_tensor(out=y[:, cs], in0=pa[:], in1=gsl, op=ALU.mult),
                                 r=[pa, sg[b]], w=[y])
                        else:
                            S.op("dve", lambda e: e.tensor_tensor(out=tmp[:], in0=pa[:], in1=gsl, op=ALU.mult),
                                 r=[pa, sg[b]], w=[tmp])
                            S.op("pool", lambda e: e.tensor_tensor(out=y[:, cs], in0=y[:, cs], in1=tmp[:], op=ALU.add),
                                 r=[y, tmp], w=[y])
                S.op("act", lambda e: e.activation(out=yb[:], in_=y[:], func=AF.Copy), r=[y], w=[yb])
                for kc in range(8):
                    S.op("pe", lambda e: e.transpose(out=ptr[0][:, kc, :], in_=yb[:, kc * 128:(kc + 1) * 128],
                                                     identity=self.identb[:]), r=[yb, self.identb], w=[ptr[0]])
                S.op("dve", lambda e: e.tensor_copy(out=yT[:], in_=ptr[0][:]), r=[ptr[0]], w=[yT])
                for nh in range(2):
                    cs = slice(nh * 512, (nh + 1) * 512)
                    pa = pacc[na % 4]
                    na += 1
                    for kc in range(8):
                        S.op("pe", lambda e: e.matmul(pa[:], yT[:, kc, :], wout[:, kc, cs], start=(kc == 0), stop=(kc == 7)),
                             r=[yT, wout], w=[pa])
                    S.op("dve", lambda e: e.scalar_tensor_tensor(out=z[:, cs], in0=xs[b][:, cs], scalar=ALPHA, in1=pa[:],
                                                                 op0=ALU.mult, op1=ALU.add), r=[xs[b], pa], w=[z])
                self.layer_norm_tile(z, gt, bt, st1, junk, xo[b])
                S.dma("pool", X["X1"][rows, :], xo[b][:], r=[xo[b]], w=["X1"])

    def phase_f(self, l, xout):
        nc, S, I, X = self.nc, self.S, self.I, self.X
        TG = 256
        with ExitStack() as es:
            sb = lambda n, s, d: es.enter_context(nc.sbuf_tensor("%s_L%d" % (n, l), s, d))
            ps = lambda n, s, d: es.enter_context(nc.psum_tensor("%s_L%d" % (n, l), s, d))
            wup = sb("f_wup", [128, 8, DFF], BF16)
            wdn = sb("f_wdn", [128, 32, D], BF16)
            wst = sb("f_wst", [128, 8, 512], F32)
            gt = sb("f_g", [128, D], F32)
            bt = sb("f_b", [128, D], F32)
            x1 = sb("f_x1", [128, 2, D], F32)
            x1b = sb("f_x1b", [128, D], BF16)
            x1T = sb("f_x1T", [128, 8, TG], BF16)
            hT = sb("f_hT", [128, 32, TG], BF16)
            rl = [sb("f_rl%d" % i, [128, TG], F32) for i in range(2)]
            z = sb("f_z", [128, D], F32)
            junk = sb("f_junk", [128, D], F32)
            xo = [sb("f_xo%d" % i, [128, D], F32) for i in range(2)]
            st1 = sb("f_st1", [128, 8], F32)
            ptr = [ps("f_ptr%d" % i, [128, 8, 128], BF16) for i in range(2)]
            pup = [ps("f_pup%d" % i, [128, TG], F32) for i in range(2)]
            pdn = [ps("f_pdn%d" % i, [128, 512], F32) for i in range(2)]
            wu = I["w_up"][l].rearrange("(kc p) n -> p kc n", p=128)
            for c in range(8):
                S.dma("sp", wst[:], wu[:, :, c * 512:(c + 1) * 512], w=[wst])
                S.op("pool", lambda e: e.tensor_copy(out=wup[:, :, c * 512:(c + 1) * 512], in_=wst[:]), r=[wst], w=[wup])
            wd = I["w_down"][l].rearrange("(fc p) n -> p fc n", p=128)

            def load_wdown():
                for c in range(8):
                    S.dma("sp", wst[:].rearrange("p a n -> p (a n)").rearrange("p (f n) -> p f n", f=4),
                          wd[:, c * 4:(c + 1) * 4, :], w=[wst])
                    S.op("pool", lambda e: e.tensor_copy(out=wdn[:, c * 4:(c + 1) * 4, :],
                                                         in_=wst[:].rearrange("p a n -> p (a n)").rearrange("p (f n) -> p f n", f=4)),
                         r=[wst], w=[wdn])
            S.dma("sp", gt[:], I["ln2_g"][l:l + 1, :].partition_broadcast(128), w=[gt])
            S.dma("sp", bt[:], I["ln2_b"][l:l + 1, :].partition_broadcast(128), w=[bt])
            nu = 0
            nd = 0
            no = 0
            for tg in range(T // TG):
                for u in range(2):
                    tt = tg * 2 + u
                    S.dma("sp", x1[:, u, :], X["X1"][tt * 128:(tt + 1) * 128, :], r=["X1"], w=[x1])
                for u in range(2):
                    S.op("act", lambda e: e.activation(out=x1b[:], in_=x1[:, u, :], func=AF.Copy), r=[x1], w=[x1b])
                    for kc in range(8):
                        S.op("pe", lambda e: e.transpose(out=ptr[u][:, kc, :], in_=x1b[:, kc * 128:(kc + 1) * 128],
                                                         identity=self.identb[:]), r=[x1b, self.identb], w=[ptr[u]])
                    S.op("dve", lambda e: e.tensor_copy(out=x1T[:, :, u * 128:(u + 1) * 128], in_=ptr[u][:]),
                         r=[ptr[u]], w=[x1T])
                for fc in range(32):
                    pu = pup[nu % 2]
                    r_ = rl[nu % 2]
                    nu += 1
                    for kc in range(8):
                        S.op("pe", lambda e: e.matmul(pu[:], wup[:, kc, fc * 128:(fc + 1) * 128], x1T[:, kc, :],
                                                      start=(kc == 0), stop=(kc == 7)), r=[wup, x1T], w=[pu])
                    S.op("act", lambda e: e.activation(out=r_[:], in_=pu[:], func=AF.Relu), r=[pu], w=[r_])
                    S.op("dve", lambda e: e.tensor_tensor(out=hT[:, fc, :], in0=r_[:], in1=r_[:], op=ALU.mult), r=[r_], w=[hT])
                if tg == 0:
                    load_wdown()
                for u in range(2):
                    tt = tg * 2 + u
                    for nh in range(2):
                        cs = slice(nh * 512, (nh + 1) * 512)
                        pd = pdn[nd % 2]
                        nd += 1
                        for fc in range(32):
                            S.op("pe", lambda e: e.matmul(pd[:], hT[:, fc, u * 128:(u + 1) * 128], wdn[:, fc, cs],
                                                          start=(fc == 0), stop=(fc == 31)), r=[hT, wdn], w=[pd])
                        S.op("dve", lambda e: e.scalar_tensor_tensor(out=z[:, cs], in0=x1[:, u, cs], scalar=ALPHA, in1=pd[:],
                                                                     op0=ALU.mult, op1=ALU.add), r=[x1, pd], w=[z])
                    o = xo[no % 2]
                    no += 1
                    self.layer_norm_tile(z, gt, bt, st1, junk, o)
                    oname = "OUT" if xout is self.out else "XS"
                    S.dma("pool", xout[tt * 128:(tt + 1) * 128, :], o[:], r=[o], w=[oname])


    def phase_b(self, l):
        nc, S, I, X = self.nc, self.S, self.I, self.X
        with ExitStack() as es:
            sb = lambda n, s, d: es.enter_context(nc.sbuf_tensor("%s_L%d" % (n, l), s, d))
            ps = lambda n, s, d: es.enter_context(nc.psum_tensor("%s_L%d" % (n, l), s, d))
            kT = sb("b_kT", [128, 4, T], BF16)
            qT = sb("b_qT", [128, 4, T], BF16)
            va = sb("b_va", [128, NT, 520], BF16)
            toep = sb("b_toep", [128, 8, 1280], BF16)
            nsT = sb("b_nsT", [128, T], BF16)
            ksum = sb("b_ksum", [128, 4, 16], F32)
            kmT = sb("b_kmT", [128, 4, 32], BF16)
            gm = sb("b_gm", [128, 8, 16], F32)
            m8 = sb("b_m8", [128, 8, 8], F32)
            ns = sb("b_ns", [128, 8, 16], BF16)
            PT = [sb("b_PT%d" % i, [128, 512], BF16) for i in range(4)]
            oa = [sb("b_oa%d" % i, [128, 4, 512], BF16) for i in range(2)]
            rden = sb("b_rden", [128, 4], F32)
            E = sb("b_E", [128, 128, 128], BF16)
            S.op("dve", lambda e: e.tensor_copy(out=E[:], in_=self.i30k[:].unsqueeze(2).to_broadcast([128, 128, 128])),
                 r=[self.i30k], w=[E])
            pS = [ps("b_pS%d" % i, [128, 512], F32) for i in range(2)]
            pO = [ps("b_pO%d" % i, [128, 65], F32) for i in range(4)]
            es_sel = ExitStack()
            pg = es_sel.enter_context(nc.psum_tensor("b_pg_L%d" % l, [128, 8, 16], F32))
            ptr = es_sel.enter_context(nc.psum_tensor("b_ptr_L%d" % l, [128, 128], BF16))
            for a in range(4):
                S.dma("sp", kT[:, a, :], X["KA"][a * 128:(a + 1) * 128, :], r=["KA"], w=[kT])
                S.dma("sp", qT[:, a, :], X["QA"][a * 128:(a + 1) * 128, :], r=["QA"], w=[qT])
            vsrc = X["VA"].rearrange("(tt p) c -> p tt c", p=128)
            for c in range(4):
                S.dma("sp", va[:, c * 8:(c + 1) * 8, :], vsrc[:, c * 8:(c + 1) * 8, :], r=["VA"], w=[va])
            S.dma("sp", toep[:], I["c_toep"][:, 0:8, :], w=[toep])
            S.op("pool", lambda e: e.memset(nsT[:], 0.0), w=[nsT])
            S.op("pool", lambda e: e.memset(gm[:], -1e30), w=[gm])
            S.op("pool", lambda e: e.memset(ns[:], 0.0), w=[ns])
            import os
            bstop = int(os.environ.get("BSTOP", "99"))
            if bstop <= 1:
                es_sel.close()
                return
            S.op("dve", lambda e: e.tensor_reduce(out=ksum[:], in_=kT[:].rearrange("p a (n s) -> p a n s", s=256),
                                                   axis=AX.X, op=ALU.add), r=[kT], w=[ksum])
            S.op("pool", lambda e: e.memset(kmT[:], 0.0), w=[kmT])
            S.op("dve", lambda e: e.tensor_scalar(out=kmT[0:64, :, 0:16], in0=ksum[0:64, :, :], scalar1=1.0 / 256, scalar2=None,
                                                   op0=ALU.mult), r=[ksum], w=[kmT])
            S.op("dve", lambda e: e.tensor_scalar(out=kmT[64:128, :, 16:32], in0=ksum[64:128, :, :], scalar1=1.0 / 256, scalar2=None,
                                                   op0=ALU.mult), r=[ksum], w=[kmT])
            if bstop <= 2:
                es_sel.close()
                return
            for tt in range(NT):
                cur = tt // 2
                if cur <= 3:
                    continue
                for a in range(4):
                    S.op("pe", lambda e: e.matmul(pg[:, 2 * a:2 * a + 2, :], qT[:, a, tt * 128:(tt + 1) * 128],
                                                  kmT[:, a, :].rearrange("p (h n) -> p h n", h=2), start=True, stop=True),
                         r=[qT, kmT], w=[pg])
                bsub = int(os.environ.get("BSUB", "99"))
                S.op("act", lambda e: e.activation(out=gm[:, :, 0:cur], in_=pg[:, :, 0:cur], func=AF.Copy), r=[pg], w=[gm])
                if bsub <= 1:
                    continue
                for h in range(8):
                    S.op("dve", lambda e: e.max(out=m8[:, h, :], in_=gm[:, h, :]), r=[gm], w=[m8])
                if bsub <= 2:
                    continue
                for h in range(8):
                    S.op("dve", lambda e: e.tensor_scalar(out=ns[:, h, 0:cur], in0=gm[:, h, 0:cur], scalar1=m8[:, h, 2:3],
                                                           scalar2=1.0, op0=ALU.is_ge, op1=ALU.subtract), r=[gm, m8], w=[ns])
                if bsub <= 3:
                    continue
                S.op("pe", lambda e: e.transpose(out=ptr[:], in_=ns[:].rearrange("p h n -> p (h n)"), identity=self.identb[:]),
                     r=[ns, self.identb], w=[ptr])
                S.op("act", lambda e: e.activation(out=nsT[:, tt * 128:(tt + 1) * 128], in_=ptr[:], func=AF.Copy),
                     r=[ptr], w=[nsT])
            S.barrier()
            es_sel.close()
            pS = pS + [ps("b_pS%d" % i, [128, 512], F32) for i in (2, 3)]
            if bstop <= 3:
                return
            nS = [0]
            for qg in range(8):
                if bstop <= 4 and qg >= 1:
                    break
                ob = oa[qg % 2]
                qs = slice(qg * 512, (qg + 1) * 512)
                nkt = 4 * (qg + 1)

                def emit_S(h, kt):
                    hb, a_ = 64 * (h % 2), h // 2
                    m = kt - 4 * qg
                    off = 512 - 128 * m if m >= -1 else 768
                    n = kt // 2
                    need_sel = (qg >= 2) and (n <= 2 * qg)
                    p, pt = pS[nS[0] % 4], PT[nS[0] % 4]
                    nS[0] += 1
                    S.op("pe", lambda e: e.matmul(p[:], kT[hb:hb + 64, a_, kt * 128:(kt + 1) * 128], qT[hb:hb + 64, a_, qs],
                                                  start=True, stop=False), r=[kT, qT], w=[p])
                    S.op("pe", lambda e: e.matmul(p[:], self.identb[:], toep[:, h, off:off + 512],
                                                  start=False, stop=not need_sel), r=[self.identb, toep], w=[p])
                    if need_sel:
                        rr = h * 16 + n
                        S.op("pe", lambda e: e.matmul(p[:], E[:, rr, :], nsT[:, qs], start=False, stop=True), r=[E, nsT], w=[p])
                    return p, pt

                def emit_PV(h, kt, pt):
                    for u in range(4):
                        last = 4 * qg + u
                        if kt <= last:
                            S.op("pe", lambda e: e.matmul(pO[u][:], pt[:, u * 128:(u + 1) * 128], va[:, kt, h * 65:(h + 1) * 65],
                                                          start=(kt == 0), stop=(kt == last)), r=[pt, va], w=[pO[u]])

                def finalize(h):
                    for u in range(4):
                        S.op("dve", lambda e: e.reciprocal(out=rden[:, u:u + 1], in_=pO[u][:, 64:65]), r=[pO[u]], w=[rden])
                        S.op("dve", lambda e: e.tensor_scalar(out=ob[:, u, h * 64:(h + 1) * 64], in0=pO[u][:, 0:64],
                                                               scalar1=rden[:, u:u + 1], scalar2=None, op0=ALU.mult),
                             r=[pO[u], rden], w=[ob])

                steps = [(h, kt) for h in range(8) for kt in range(nkt)]
                pending = emit_S(*steps[0])
                for i, (h, kt) in enumerate(steps):
                    p, pt = pending
                    S.op("act", lambda e: e.activation(out=pt[:], in_=p[:], func=AF.Exp), r=[p], w=[pt])
                    if i + 1 < len(steps):
                        pending = emit_S(*steps[i + 1])
                    emit_PV(h, kt, pt)
                    if kt == nkt - 1:
                        finalize(h)
                S.dma("pool", X["OA"][qg * 512:(qg + 1) * 512, :].rearrange("(u p) c -> p u c", p=128), ob[:],
                      r=[ob], w=["OA"])

    def phase_d(self, l):
        nc, S, I, X = self.nc, self.S, self.I, self.X
        import os
        dstop = int(os.environ.get("DSTOP", "99"))
        with ExitStack() as es:
            sb = lambda n, s, d: es.enter_context(nc.sbuf_tensor("%s_L%d" % (n, l), s, d))
            ps = lambda n, s, d: es.enter_context(nc.psum_tensor("%s_L%d" % (n, l), s, d))
            c = sb("d_c", [128, NT, 256], BF16)
            cT = sb("d_cT", [128, 2, T], BF16)
            kiT2 = sb("d_kiT2", [128, T], BF16)
            toep = sb("d_toep", [128, 8, 1280], BF16)
            absw = sb("d_absw", [128, NT, 8], F32)
            sgn = sb("d_sgn", [128, NT, 8], F32)
            wukT = sb("d_wukT", [128, 4, 256], BF16)
            WBD = sb("d_WBD", [128, 4, 2, 256], BF16)
            wuvb = sb("d_wuvb", [128, 2, 512], BF16)
            pI = [ps("d_pI%d" % i, [128, 512], F32) for i in range(2)]
            pAcc = ps("d_pAcc", [128, 512], F32)
            pS = [ps("d_pS%d" % i, [128, 4, 128], F32) for i in range(2)]
            pOT = [ps("d_pOT%d" % i, [128, 512], F32) for i in range(2)]
            pDen = ps("d_pDen", [128, 512], F32)
            pIb = [p[:].bitcast(BF16).rearrange("p (a n) -> p a n", a=8) for p in pI]

            S.dma("sp", toep[:], I["c_toep"][:, 8:16, :], w=[toep])
            with ExitStack() as es2:
                sb2 = lambda n, s, d: es2.enter_context(nc.sbuf_tensor("%s_L%d" % (n, l), s, d))
                ckv = sb2("d_ckv", [128, NT, 256], F32)
                kiwi = sb2("d_kiwi", [128, NT, 72], F32)
                kin = sb2("d_kin", [128, NT, 128], BF16)
                kif = sb2("d_kif", [128, NT, 64], F32)
                junk = sb2("d_junk", [128, 256], F32)
                wst = sb2("d_wst", [128, 2, 512], F32)
                wukb = sb2("d_wukb", [128, 2, 512], BF16)
                kvn = sb2("d_kvn", [128, 256], F32)
                lng = sb2("d_lng", [128, 64], F32)
                lnb = sb2("d_lnb", [128, 64], F32)
                ss = sb2("d_ss", [128, NT], F32)
                rs = sb2("d_rs", [128, NT], F32)
                mu = sb2("d_mu", [128, NT], F32)
                S.dma("sp", wst[:], I["w_uk"][l].rearrange("(rc p) n -> p rc n", p=128), w=[wst])
                S.op("pool", lambda e: e.tensor_copy(out=wukb[:], in_=wst[:]), r=[wst], w=[wukb])
                for rc in range(2):
                    for a in range(4):
                        S.op("pe", lambda e: e.transpose(out=pIb[0][:, rc * 4 + a, :], in_=wukb[:, rc, a * 128:(a + 1) * 128],
                                                         identity=self.identb[:]), r=[wukb, self.identb], w=[pI[0]])
                S.op("dve", lambda e: e.tensor_copy(out=wukT[:].rearrange("p a (rc r) -> p rc a r", rc=2),
                                                     in_=pIb[0].rearrange("p (rc a) r -> p rc a r", rc=2)), r=[pI[0]], w=[wukT])
                S.op("pool", lambda e: e.memset(WBD[:], 0.0), w=[WBD])
                S.op("dve", lambda e: e.tensor_copy(out=WBD[0:64, :, 0, :], in_=wukT[0:64, :, :]), r=[wukT], w=[WBD])
                S.op("dve", lambda e: e.tensor_copy(out=WBD[64:128, :, 1, :], in_=wukT[64:128, :, :]), r=[wukT], w=[WBD])
                S.dma("sp", wst[:], I["w_uv"][l].rearrange("(rc p) n -> p rc n", p=128), r=[], w=[wst])
                S.op("pool", lambda e: e.tensor_copy(out=wuvb[:], in_=wst[:]), r=[wst], w=[wuvb])
                S.dma("sp", kvn[:], I["kv_norm"][l:l + 1, :].partition_broadcast(128), w=[kvn])
                S.dma("sp", lng[:], I["idx_k_ln_g"][l:l + 1, :].partition_broadcast(128), w=[lng])
                S.dma("sp", lnb[:], I["idx_k_ln_b"][l:l + 1, :].partition_broadcast(128), w=[lnb])
                csrc = X["CKV"].rearrange("(tt p) c -> p tt c", p=128)
                for q4 in range(4):
                    S.dma("sp", ckv[:, q4 * 8:(q4 + 1) * 8, :], csrc[:, q4 * 8:(q4 + 1) * 8, :], r=["CKV"], w=[ckv])
                for tt in range(NT):
                    S.op("act", lambda e: e.activation(out=junk[:], in_=ckv[:, tt, :], func=AF.Square, accum_out=ss[:, tt:tt + 1]),
                         r=[ckv], w=[junk, ss])
                S.op("act", lambda e: e.activation(out=rs[:], in_=ss[:], func=AF.Sqrt, scale=1.0 / 256, bias=self.epsln[:, 1:2]),
                     r=[ss, self.epsln], w=[rs])
                S.op("dve", lambda e: e.reciprocal(out=rs[:], in_=rs[:]), r=[rs], w=[rs])
                S.op("dve", lambda e: e.tensor_tensor(out=ckv[:], in0=ckv[:], in1=rs[:].unsqueeze(2).to_broadcast([128, NT, 256]),
                                                       op=ALU.mult), r=[ckv, rs], w=[ckv])
                S.op("dve", lambda e: e.tensor_tensor(out=c[:], in0=ckv[:], in1=kvn[:].unsqueeze(1).to_broadcast([128, NT, 256]),
                                                       op=ALU.mult), r=[ckv, kvn], w=[c])
                for g in range(NT // 4):
                    pb = pIb[g % 2]
                    for t4 in range(4):
                        for rc in range(2):
                            S.op("pe", lambda e: e.transpose(out=pb[:, t4 * 2 + rc, :], in_=c[:, g * 4 + t4, rc * 128:(rc + 1) * 128],
                                                             identity=self.identb[:]), r=[c, self.identb], w=[pI[g % 2]])
                    S.op("act", lambda e: e.activation(out=cT[:, :, g * 512:(g + 1) * 512].rearrange("p rc (t q) -> p t rc q", t=4),
                                                       in_=pb.rearrange("p (t rc) q -> p t rc q", t=4), func=AF.Copy),
                         r=[pI[g % 2]], w=[cT])
                S.dma("sp", kiwi[:], X["KIWI"].rearrange("(tt p) c -> p tt c", p=128), r=["KIWI"], w=[kiwi])
                S.op("dve", lambda e: e.tensor_reduce(out=mu[:], in_=kiwi[:, :, 0:64], axis=AX.X, op=ALU.add), r=[kiwi], w=[mu])
                S.op("dve", lambda e: e.tensor_scalar(out=mu[:], in0=mu[:], scalar1=-1.0 / 64, scalar2=None, op0=ALU.mult),
                     r=[mu], w=[mu])
                S.op("dve", lambda e: e.tensor_tensor(out=kif[:], in0=kiwi[:, :, 0:64], in1=mu[:].unsqueeze(2).to_broadcast([128, NT, 64]),
                                                       op=ALU.add), r=[kiwi, mu], w=[kif])
                S.op("dve", lambda e: e.tensor_tensor(out=ckv[:, :, 0:64], in0=kif[:], in1=kif[:], op=ALU.mult), r=[kif], w=[ckv])
                S.op("dve", lambda e: e.tensor_reduce(out=ss[:], in_=ckv[:, :, 0:64], axis=AX.X, op=ALU.add), r=[ckv], w=[ss])
                S.op("act", lambda e: e.activation(out=rs[:], in_=ss[:], func=AF.Sqrt, scale=1.0 / 64, bias=self.epsln[:, 0:1]),
                     r=[ss, self.epsln], w=[rs])
                S.op("dve", lambda e: e.reciprocal(out=rs[:], in_=rs[:]), r=[rs], w=[rs])
                S.op("dve", lambda e: e.tensor_tensor(out=kif[:], in0=kif[:], in1=rs[:].unsqueeze(2).to_broadcast([128, NT, 64]),
                                                       op=ALU.mult), r=[kif, rs], w=[kif])
                S.op("dve", lambda e: e.tensor_tensor(out=kif[:], in0=kif[:], in1=lng[:].unsqueeze(1).to_broadcast([128, NT, 64]),
                                                       op=ALU.mult), r=[kif, lng], w=[kif])
                for hf in range(2):
                    S.op("dve", lambda e: e.tensor_tensor(out=kin[:, :, hf * 64:(hf + 1) * 64], in0=kif[:],
                                                           in1=lnb[:].unsqueeze(1).to_broadcast([128, NT, 64]), op=ALU.add),
                         r=[kif, lnb], w=[kin])
                for g in range(NT // 8):
                    pb = pIb[g % 2]
                    for t8 in range(8):
                        S.op("pe", lambda e: e.transpose(out=pb[:, t8, :], in_=kin[:, g * 8 + t8, :], identity=self.identb[:]),
                             r=[kin, self.identb], w=[pI[g % 2]])
                    S.op("act", lambda e: e.activation(out=kiT2[:, g * 1024:(g + 1) * 1024].rearrange("p (t q) -> p t q", t=8),
                                                       in_=pb, func=AF.Copy), r=[pI[g % 2]], w=[kiT2])
                S.op("act", lambda e: e.activation(out=absw[:], in_=kiwi[:, :, 64:72], func=AF.Abs, scale=IDX_WEIGHT_SCALE),
                     r=[kiwi], w=[absw])
                S.op("dve", lambda e: e.tensor_scalar(out=sgn[:], in0=kiwi[:, :, 64:72], scalar1=0.0, scalar2=2.0,
                                                       op0=ALU.is_ge, op1=ALU.mult), r=[kiwi], w=[sgn])
                S.op("dve", lambda e: e.tensor_scalar(out=sgn[:], in0=sgn[:], scalar1=-1.0, scalar2=None, op0=ALU.add),
                     r=[sgn], w=[sgn])
                S.barrier()
            Isc = [sb("d_Isc%d" % i, [128, T], F32) for i in range(2)]
            work = sb("d_work", [128, T], F32)
            negm = [sb("d_negm%d" % i, [128, T], BF16) for i in range(2)]
            nmT = [sb("d_nmT%d" % i, [128, NT, 128], BF16) for i in range(2)]
            qit = [sb("d_qit%d" % i, [128, 4, 128], BF16) for i in range(2)]
            qct = [sb("d_qct%d" % i, [128, 4, 128], BF16) for i in range(2)]
            qlT = [sb("d_qlT%d" % i, [128, 2, 8, 128], BF16) for i in range(3)]
            Dsg = sb("d_Dsg", [128, 8, 128], BF16)
            Ph = [sb("d_Ph%d" % i, [128, 512], BF16) for i in range(2)]
            PT = [sb("d_PT%d" % i, [128, 4, 128], BF16) for i in range(2)]
            rdn = sb("d_rdn", [128, 512], F32)
            OTn = sb("d_OTn", [128, 2, 4, 128], BF16)
            oc = [sb("d_oc%d" % i, [128, 512], BF16) for i in range(2)]
            m8 = sb("d_m8", [128, 8], F32)
            st = sb("d_st", [128, 4], F32)
            if dstop <= 1:
                return
            qisrc = X["QI"].rearrange("(a p) t -> p a t", p=128)
            qcsrc = X["QC"].rearrange("(a p) t -> p a t", p=128)
            nS = [0]

            def stage1(qt):
                L, b = (qt + 1) * 128, qt % 2
                ts = slice(qt * 128, (qt + 1) * 128)
                S.dma("sp", qit[b][:], qisrc[:, :, ts], r=["QI"], w=[qit[b]])
                S.dma("sp", qct[b][:], qcsrc[:, :, ts], r=["QC"], w=[qct[b]])
                if qt >= 2:
                    S.op("pool", lambda e: e.tensor_tensor(out=Dsg[:], in0=self.identb[:].unsqueeze(1).to_broadcast([128, 8, 128]),
                                                           in1=sgn[:, qt, :].unsqueeze(2).to_broadcast([128, 8, 128]), op=ALU.mult),
                         r=[self.identb, sgn], w=[Dsg])
                    for kg in range((L + 511) // 512):
                        w_ = min(512, L - 512 * kg)
                        for h in range(8):
                            hb, a_ = 64 * (h % 2), h // 2
                            pi, ph = pI[h % 2], Ph[h % 2]
                            S.op("pe", lambda e: e.matmul(pi[:, 0:w_], qit[b][hb:hb + 64, a_, :], kiT2[hb:hb + 64, kg * 512:kg * 512 + w_],
                                                          start=True, stop=True), r=[qit[b], kiT2], w=[pi])
                            S.op("act", lambda e: e.activation(out=ph[:, 0:w_], in_=pi[:, 0:w_], func=AF.Relu,
                                                               scale=absw[:, qt, h:h + 1]), r=[pi, absw], w=[ph])
                            S.op("pe", lambda e: e.matmul(pAcc[:, 0:w_], Dsg[:, h, :], ph[:, 0:w_], start=(h == 0), stop=(h == 7)),
                                 r=[Dsg, ph], w=[pAcc])
                        S.op("act", lambda e: e.activation(out=Isc[b][:, kg * 512:kg * 512 + w_], in_=pAcc[:, 0:w_], func=AF.Copy),
                             r=[pAcc], w=[Isc[b]])
                for rc in range(2):
                    for hq in range(2):
                        pq = pI[(rc * 2 + hq) % 2]
                        for hh in range(4):
                            h = hq * 4 + hh
                            S.op("pe", lambda e: e.matmul(pq[:, hh * 128:(hh + 1) * 128], WBD[:, h // 2, h % 2, rc * 128:(rc + 1) * 128],
                                                          qct[b][:, h // 2, :], start=True, stop=True), r=[WBD, qct[b]], w=[pq])
                        S.op("act", lambda e: e.activation(out=qlT[qt % 3][:, rc, hq * 4:(hq + 1) * 4, :],
                                                           in_=pq[:].rearrange("p (h q) -> p h q", h=4), func=AF.Copy), r=[pq], w=[qlT[qt % 3]])

            def stage2(qt):
                L, b = (qt + 1) * 128, qt % 2
                if qt < 2:
                    return
                I_ = Isc[b]
                S.op("dve", lambda e: e.tensor_reduce(out=st[:, 0:1], in_=I_[:, 0:L], axis=AX.X, op=ALU.min), r=[I_], w=[st])
                S.op("dve", lambda e: e.tensor_scalar(out=st[:, 1:2], in0=st[:, 0:1], scalar1=-1.0, scalar2=1.0,
                                                       op0=ALU.mult, op1=ALU.add), r=[st], w=[st])
                S.op("dve", lambda e: e.tensor_scalar(out=I_[:, 0:L], in0=I_[:, 0:L], scalar1=st[:, 1:2], scalar2=None,
                                                       op0=ALU.add), r=[I_, st], w=[I_])
                S.op("dve", lambda e: e.tensor_tensor(out=I_[:, L - 128:L], in0=I_[:, L - 128:L], in1=self.ltri[:], op=ALU.mult),
                     r=[I_, self.ltri], w=[I_])
                for r_ in range(32):
                    src_ = I_ if r_ == 0 else work
                    S.op("dve", lambda e: e.max(out=m8[:], in_=src_[:, 0:L]), r=[src_], w=[m8])
                    if r_ < 31:
                        S.op("dve", lambda e: e.scalar_tensor_tensor(out=work[:, 0:L], in0=src_[:, 0:L], scalar=m8[:, 7:8],
                                                                     in1=src_[:, 0:L], op0=ALU.is_lt, op1=ALU.mult),
                             r=[src_, m8], w=[work])
                S.op("dve", lambda e: e.tensor_scalar(out=negm[b][:, 0:L], in0=I_[:, 0:L], scalar1=m8[:, 7:8], scalar2=1.0,
                                                       op0=ALU.is_ge, op1=ALU.subtract), r=[I_, m8], w=[negm[b]])

            def stage3(qt):
                nk, b = qt + 1, qt % 2
                ts = slice(qt * 128, (qt + 1) * 128)
                masked = qt >= 2
                if masked:
                    for g in range((nk + 7) // 8):
                        n8 = min(8, nk - 8 * g)
                        pb = pIb[g % 2]
                        for t8 in range(n8):
                            kt = g * 8 + t8
                            S.op("pe", lambda e: e.transpose(out=pb[:, t8, :], in_=negm[b][:, kt * 128:(kt + 1) * 128],
                                                             identity=self.identb[:]), r=[negm[b], self.identb], w=[pI[g % 2]])
                        S.op("act", lambda e: e.activation(out=nmT[b][:, g * 8:g * 8 + n8, :], in_=pb[:, 0:n8, :], func=AF.Copy, scale=-NEG),
                             r=[pI[g % 2]], w=[nmT[b]])
                def emit_S(half, kt):
                    hs = slice(4 * half, 4 * half + 4)
                    m = kt - qt
                    off = 512 - 128 * m if m >= -1 else 768
                    p_, pt = pS[nS[0] % 2], PT[nS[0] % 2]
                    nS[0] += 1
                    for rc in range(2):
                        S.op("pe", lambda e: e.matmul(p_[:], cT[:, rc, kt * 128:(kt + 1) * 128], qlT[qt % 3][:, rc, hs, :],
                                                      start=(rc == 0), stop=False), r=[cT, qlT[qt % 3]], w=[p_])
                    S.op("pe", lambda e: e.matmul(p_[:], self.identb[:], toep[:, hs, off:off + 128], start=False, stop=not masked),
                         r=[self.identb, toep], w=[p_])
                    if masked:
                        for hh in range(4):
                            S.op("pe", lambda e: e.matmul(p_[:, hh, :], self.identb[:], nmT[b][:, kt, :], start=False, stop=(hh == 3)),
                                 r=[self.identb, nmT[b]], w=[p_])
                    return p_, pt

                def emit_PV(half, kt, pt):
                    for rc in range(2):
                        S.op("pe", lambda e: e.matmul(pOT[rc][:], c[:, kt, rc * 128:(rc + 1) * 128], pt[:].rearrange("p h q -> p (h q)"),
                                                      start=(kt == 0), stop=(kt == nk - 1)), r=[c, pt], w=[pOT[rc]])
                    S.op("pe", lambda e: e.matmul(pDen[:], self.onesb[:], pt[:].rearrange("p h q -> p (h q)"),
                                                  start=(kt == 0), stop=(kt == nk - 1)), r=[self.onesb, pt], w=[pDen])

                def finalize(half):
                    S.op("dve", lambda e: e.reciprocal(out=rdn[:], in_=pDen[:]), r=[pDen], w=[rdn])
                    for rc in range(2):
                        S.op("dve", lambda e: e.tensor_tensor(out=OTn[:, rc, :, :].rearrange("p h q -> p (h q)"), in0=pOT[rc][:], in1=rdn[:],
                                                               op=ALU.mult), r=[pOT[rc], rdn], w=[OTn])
                    for hh in range(4):
                        h = 4 * half + hh
                        for rc in range(2):
                            S.op("pe", lambda e: e.matmul(pAcc[:, hh * 64:(hh + 1) * 64], OTn[:, rc, hh, :], wuvb[:, rc, h * 64:(h + 1) * 64],
                                                          start=(rc == 0), stop=(rc == 1)), r=[OTn, wuvb], w=[pAcc])
                    S.op("act", lambda e: e.activation(out=oc[b][:, half * 256:(half + 1) * 256], in_=pAcc[:, 0:256], func=AF.Copy),
                         r=[pAcc], w=[oc[b]])

                steps = [(half, kt) for half in range(2) for kt in range(nk)]
                pending = emit_S(*steps[0])
                for i, (half, kt) in enumerate(steps):
                    p_, pt = pending
                    S.op("act", lambda e: e.activation(out=pt[:], in_=p_[:], func=AF.Exp), r=[p_], w=[pt])
                    if i + 1 < len(steps):
                        pending = emit_S(*steps[i + 1])
                    emit_PV(half, kt, pt)
                    if kt == nk - 1:
                        finalize(half)
                S.dma("pool", X["OC"][ts, :], oc[b][:], r=[oc[b]], w=["OC"])

            tiles = [qt for qt in range(NT) if not (dstop <= 2 and qt not in (0, 1, 2, 5))]
            n = len(tiles)
            for i in range(n + 2):
                if i < n:
                    stage1(tiles[i])
                if 0 <= i - 1 < n:
                    stage2(tiles[i - 1])
                if 0 <= i - 2 < n:
                    stage3(tiles[i - 2])

    def phase_c(self, l):
        nc, S, I, X = self.nc, self.S, self.I, self.X
        import os
        cstop = int(os.environ.get("CSTOP", "99"))
        QS = 128 ** -0.5
        UTc, UTs, BON, H0, H1 = (self.gdc[:, i, :] for i in range(5))
        with ExitStack() as es:
            sb = lambda n, s, d: es.enter_context(nc.sbuf_tensor("%s_L%d" % (n, l), s, d))
            ps = lambda n, s, d: es.enter_context(nc.psum_tensor("%s_L%d" % (n, l), s, d))
            cw = sb("c_cw", [128, 12, 4], F32)
            xp = [sb("c_xp%d" % i, [128, T + 3], F32) for i in range(2)]
            y = [sb("c_y%d" % i, [128, T], F32) for i in range(2)]
            sq = sb("c_sq", [128, T], F32)
            rn = sb("c_rn", [128, T], F32)
            pn = [ps("c_pn%d" % i, [128, 512], F32) for i in range(2)]
            cwr = sb("c_cwr", [4, 1536], F32)
            S.dma("sp", cwr[:], I["conv_w"][l], w=[cwr])
            for c_ in range(12):
                S.op("pe", lambda e: e.transpose(out=pn[0][:, c_ * 4:(c_ + 1) * 4], in_=cwr[:, c_ * 128:(c_ + 1) * 128],
                                                 identity=self.ident[0:4, 0:4]), r=[cwr, self.ident], w=[pn[0]])
            S.op("dve", lambda e: e.tensor_copy(out=cw[:].rearrange("p c j -> p (c j)"), in_=pn[0][:, 0:48]), r=[pn[0]], w=[cw])
            for i in range(2):
                S.op("pool", lambda e: e.memset(xp[i][:, 0:3], 0.0), w=[xp[i]])
            for c_ in range(12):
                b = c_ % 2
                S.dma("sp", xp[b][:, 3:], X["QKVB"][c_ * 128:(c_ + 1) * 128, :], r=["QKVB"], w=[xp[b]])
                S.op("dve", lambda e: e.tensor_scalar(out=y[b][:], in0=xp[b][:, 0:T], scalar1=cw[:, c_, 0:1], scalar2=None,
                                                       op0=ALU.mult), r=[xp[b], cw], w=[y[b]])
                for j in range(1, 4):
                    S.op("dve", lambda e: e.scalar_tensor_tensor(out=y[b][:], in0=xp[b][:, j:j + T], scalar=cw[:, c_, j:j + 1],
                                                                 in1=y[b][:], op0=ALU.mult, op1=ALU.add), r=[xp[b], cw, y[b]], w=[y[b]])
                S.op("act", lambda e: e.activation(out=y[b][:], in_=y[b][:], func=AF.Silu), r=[y[b]], w=[y[b]])
                if c_ < 8:
                    S.op("pool", lambda e: e.tensor_tensor(out=sq[:], in0=y[b][:], in1=y[b][:], op=ALU.mult), r=[y[b]], w=[sq])
                    for g in range(8):
                        p_ = pn[g % 2]
                        S.op("pe", lambda e: e.matmul(p_[:], self.onesf[:], sq[:, g * 512:(g + 1) * 512], start=True, stop=True),
                             r=[self.onesf, sq], w=[p_])
                        S.op("act", lambda e: e.activation(out=rn[:, g * 512:(g + 1) * 512], in_=p_[:], func=AF.Sqrt,
                                                           bias=self.epsln[:, 1:2]), r=[p_, self.epsln], w=[rn])
                    S.op("dve", lambda e: e.reciprocal(out=rn[:], in_=rn[:]), r=[rn], w=[rn])
                    S.op("dve", lambda e: e.tensor_tensor(out=y[b][:], in0=y[b][:], in1=rn[:], op=ALU.mult), r=[y[b], rn], w=[y[b]])
                S.dma("pool", X["QKVB"][c_ * 128:(c_ + 1) * 128, :], y[b][:], r=[y[b]], w=["QKVB"])
        S.barrier()
        if cstop <= 1:
            return
        with ExitStack() as es:
            sb = lambda n, s, d: es.enter_context(nc.sbuf_tensor("%s_L%d" % (n, l), s, d))
            ps = lambda n, s, d: es.enter_context(nc.psum_tensor("%s_L%d" % (n, l), s, d))
            ab = sb("c_ab", [128, NT, 8], F32)
            dtb = sb("c_dtb", [128, 4], F32)
            nea = sb("c_nea", [128, 4], F32)
            one1 = sb("c_one1", [128, 1], F32)
            LA = sb("c_LA", [128, NT, 4], F32)
            nbeta = sb("c_nbeta", [128, NT, 4], F32)
            beta = sb("c_beta", [128, NT, 4], F32)
            G = sb("c_G", [128, 128], F32)
            EGn = sb("c_EGn", [128, 128], F32)
            KD = sb("c_KD", [128, 128], F32)
            EGL = [sb("c_EGL%d" % j, [128, 128], F32) for j in range(2)]
            gn = sb("c_gn", [128, 128], F32)
            qkv = [sb("c_qkv%d" % i, [128, 12, 128], F32) for i in range(2)]
            zt = [sb("c_zt%d" % i, [128, 512], F32) for i in range(2)]
            vt = sb("c_vt", [128, 4, 128], F32)
            kd = sb("c_kd", [128, 4, 128], F32)
            O = sb("c_O", [128, 4, 128], F32)
            ob = [sb("c_ob%d" % i, [128, 512], BF16) for i in range(2)]
            ss = sb("c_ss", [128, 4], F32)
            junk = sb("c_junk", [128, 128], F32)
            Sh = [sb("c_S%d" % h, [128, 128], F32) for h in range(4)]
            Dg = [sb("c_Dg%d" % h, [128, 128], F32) for h in range(4)]
            t1 = [sb("c_t1%d" % h, [128, 128], F32) for h in range(4)]
            decT = [sb("c_dec%d" % h, [128, 128], F32) for h in range(4)]
            EGb = [sb("c_EGb%d" % h, [128, 128], F32) for h in range(4)]
            qeg = [sb("c_qeg%d" % h, [128, 128], F32) for h in range(4)]
            aT = [sb("c_aT%d" % h, [128, 128], F32) for h in range(4)]
            AT = [[sb("c_AT%d_%d" % (h, i), [128, 128], F32) for i in range(2)] for h in range(4)]
            A = [[sb("c_A%d_%d" % (h, i), [128, 128], F32) for i in range(2)] for h in range(4)]
            Xh = [sb("c_X%d" % h, [128, 128], F32) for h in range(4)]
            R = [sb("c_R%d" % h, [128, 128], F32) for h in range(4)]
            vn = [sb("c_vn%d" % h, [128, 128], F32) for h in range(4)]
            class Sub:
                def __init__(self, tile, i):
                    self.ap = tile[:, i, :]
                    self.name = tile.name

                def __getitem__(self, k):
                    return self.ap[k]

            ppb = [ps("c_pp%d" % i, [128, 4, 128], F32) for i in range(4)]
            pp = [Sub(ppb[i % 4], i // 4) for i in range(16)]
            pdb = [ps("c_pd%d" % i, [128, 128], F32) for i in range(2)]
            ptv = ps("c_ptv", [128, 4, 128], F32)
            ptk = ps("c_ptk", [128, 4, 128], F32)
            npp = [0]

            def P():
                npp[0] += 1
                return pp[npp[0] % 16]

            S.dma("sp", ab[:], X["AB"].rearrange("(tt p) c -> p tt c", p=128), r=["AB"], w=[ab])
            S.dma("sp", dtb[:], I["dt_bias"][l:l + 1, :].partition_broadcast(128), w=[dtb])
            S.dma("sp", nea[:], I["a_log"][l:l + 1, :].partition_broadcast(128), w=[nea])
            S.dma("sp", gn[:], I["gdn_norm"][l:l + 1, :].partition_broadcast(128), w=[gn])
            S.op("pool", lambda e: e.memset(one1[:], 1.0), w=[one1])
            for h in range(4):
                S.op("pool", lambda e: e.memset(Sh[h][:], 0.0), w=[Sh[h]])
                S.op("pool", lambda e: e.memset(vn[h][:], 0.0), w=[vn[h]])
                S.op("pool", lambda e: e.memset(R[h][:], 0.0), w=[R[h]])
            S.op("act", lambda e: e.activation(out=nea[:], in_=nea[:], func=AF.Exp), r=[nea], w=[nea])
            S.op("dve", lambda e: e.tensor_scalar(out=nea[:], in0=nea[:], scalar1=-1.0, scalar2=None, op0=ALU.mult), r=[nea], w=[nea])
            S.op("dve", lambda e: e.tensor_tensor(out=LA[:], in0=ab[:, :, 0:4], in1=dtb[:].unsqueeze(1).to_broadcast([128, NT, 4]),
                                                   op=ALU.add), r=[ab, dtb], w=[LA])
            S.op("act", lambda e: e.activation(out=LA[:], in_=LA[:], func=AF.Exp), r=[LA], w=[LA])
            S.op("act", lambda e: e.activation(out=LA[:], in_=LA[:], func=AF.Ln, bias=one1[:, 0:1]), r=[LA, one1], w=[LA])
            S.op("dve", lambda e: e.tensor_tensor(out=LA[:], in0=LA[:], in1=nea[:].unsqueeze(1).to_broadcast([128, NT, 4]),
                                                   op=ALU.mult), r=[LA, nea], w=[LA])
            S.op("act", lambda e: e.activation(out=beta[:], in_=ab[:, :, 4:8], func=AF.Sigmoid), r=[ab], w=[beta])
            S.op("dve", lambda e: e.tensor_scalar(out=nbeta[:], in0=beta[:], scalar1=-1.0, scalar2=None, op0=ALU.mult),
                 r=[beta], w=[nbeta])
            LA2 = LA[:].rearrange("p t h -> p (t h)")
            p1, p2, p3, p4 = P(), P(), P(), P()
            S.op("pe", lambda e: e.matmul(p1[:], UTc, LA2, start=True, stop=True), r=[self.gdc, LA], w=[p1])
            S.op("pe", lambda e: e.matmul(p2[:], BON, LA2, start=True, stop=True), r=[self.gdc, LA], w=[p2])
            S.op("pe", lambda e: e.matmul(p3[:], H0, LA2, start=True, stop=True), r=[self.gdc, LA], w=[p3])
            S.op("pe", lambda e: e.matmul(p4[:], H1, LA2, start=True, stop=True), r=[self.gdc, LA], w=[p4])
            S.op("dve", lambda e: e.tensor_copy(out=G[:], in_=p1[:]), r=[p1], w=[G])
            S.op("act", lambda e: e.activation(out=EGn[:], in_=p1[:], func=AF.Exp), r=[p1], w=[EGn])
            S.op("dve", lambda e: e.tensor_scalar(out=EGn[:], in0=EGn[:], scalar1=-1.0, scalar2=None, op0=ALU.mult), r=[EGn], w=[EGn])
            S.op("dve", lambda e: e.tensor_tensor(out=KD[:], in0=p2[:], in1=G[:], op=ALU.subtract), r=[p2, G], w=[KD])
            S.op("act", lambda e: e.activation(out=KD[:], in_=KD[:], func=AF.Exp), r=[KD], w=[KD])
            S.op("act", lambda e: e.activation(out=EGL[0][:], in_=p3[:], func=AF.Exp), r=[p3], w=[EGL[0]])
            S.op("act", lambda e: e.activation(out=EGL[1][:], in_=p4[:], func=AF.Exp), r=[p4], w=[EGL[1]])
            qsrc = X["QKVB"].rearrange("(c p) t -> p c t", p=128)
            for tt in range(NT):
                if tt >= int(os.environ.get("CTILES", "32")) or (cstop <= 2 and tt >= 2):
                    break
                b = tt % 2
                ts = slice(tt * 128, (tt + 1) * 128)
                S.dma("sp", qkv[b][:], qsrc[:, :, ts], r=["QKVB"], w=[qkv[b]])
                S.dma("sp", zt[b][:], X["ZB"][ts, :], r=["ZB"], w=[zt[b]])
                for h in range(4):
                    S.op("pe", lambda e: e.transpose(out=ptv[:, h, :], in_=qkv[b][:, 8 + h, :], identity=self.ident[:]),
                         r=[qkv[b], self.ident], w=[ptv])
                    S.op("pe", lambda e: e.transpose(out=ptk[:, h, :], in_=qkv[b][:, 4 + h, :], identity=self.ident[:]),
                         r=[qkv[b], self.ident], w=[ptk])
                S.op("act", lambda e: e.activation(out=vt[:], in_=ptv[:], func=AF.Copy), r=[ptv], w=[vt])
                for h in range(4):
                    col = tt * 4 + h
                    S.op("dve", lambda e: e.tensor_scalar(out=kd[:, h, :], in0=ptk[:, h, :], scalar1=KD[:, col:col + 1], scalar2=None,
                                                           op0=ALU.mult), r=[ptk, KD], w=[kd])
                H4 = range(4)
                gcol = [G[:, tt * 4 + h:tt * 4 + h + 1] for h in H4]
                kTt = [qkv[b][:, 4 + h, :] for h in H4]
                qTt = [qkv[b][:, h, :] for h in H4]
                for h in H4:
                    S.op("pool", lambda e: e.tensor_scalar(out=Dg[h][:], in0=self.ident[:], scalar1=gcol[h], scalar2=None, op0=ALU.mult),
                         r=[self.ident, G], w=[Dg[h]])
                pg_ = [P() for _ in H4]
                for h in H4:
                    S.op("pe", lambda e: e.matmul(pg_[h][:], self.onesf[:], Dg[h][:], start=True, stop=True), r=[self.onesf, Dg[h]], w=[pg_[h]])
                for h in H4:
                    S.op("dve", lambda e: e.tensor_scalar(out=t1[h][:], in0=pg_[h][:], scalar1=gcol[h], scalar2=0.0, op0=ALU.subtract, op1=ALU.min),
                         r=[pg_[h], G], w=[t1[h]])
                for h in H4:
                    S.op("act", lambda e: e.activation(out=decT[h][:], in_=t1[h][:], func=AF.Exp), r=[t1[h]], w=[decT[h]])
                    S.op("act", lambda e: e.activation(out=EGb[h][:], in_=pg_[h][:], func=AF.Exp), r=[pg_[h]], w=[EGb[h]])
                for h in H4:
                    S.op("dve", lambda e: e.scalar_tensor_tensor(out=qeg[h][:], in0=qTt[h], scalar=QS, in1=EGb[h][:], op0=ALU.mult, op1=ALU.mult),
                         r=[qkv[b], EGb[h]], w=[qeg[h]])
                pkk = [P() for _ in H4]
                pqk = [P() for _ in H4]
                for h in H4:
                    S.op("pe", lambda e: e.matmul(pkk[h][:], kTt[h], kTt[h], start=True, stop=True), r=[qkv[b]], w=[pkk[h]])
                    S.op("pe", lambda e: e.matmul(pqk[h][:], kTt[h], qTt[h], start=True, stop=True), r=[qkv[b]], w=[pqk[h]])
                for h in H4:
                    S.op("dve", lambda e: e.scalar_tensor_tensor(out=AT[h][0][:], in0=pkk[h][:], scalar=nbeta[:, tt, h:h + 1], in1=decT[h][:],
                                                                 op0=ALU.mult, op1=ALU.mult), r=[pkk[h], nbeta, decT[h]], w=[AT[h][0]])
                    S.op("dve", lambda e: e.scalar_tensor_tensor(out=aT[h][:], in0=pqk[h][:], scalar=QS, in1=decT[h][:],
                                                                 op0=ALU.mult, op1=ALU.mult), r=[pqk[h], decT[h]], w=[aT[h]])
                for h in H4:
                    S.op("pool", lambda e: e.tensor_tensor(out=AT[h][0][:], in0=AT[h][0][:], in1=UTs, op=ALU.mult),
                         r=[AT[h][0], self.gdc], w=[AT[h][0]])
                    S.op("pool", lambda e: e.tensor_tensor(out=aT[h][:], in0=aT[h][:], in1=UTc, op=ALU.mult), r=[aT[h], self.gdc], w=[aT[h]])
                pt_ = [P() for _ in H4]
                for h in H4:
                    S.op("pe", lambda e: e.transpose(out=pt_[h][:], in_=AT[h][0][:], identity=self.ident[:]), r=[AT[h][0], self.ident], w=[pt_[h]])
                for h in H4:
                    S.op("act", lambda e: e.activation(out=A[h][0][:], in_=pt_[h][:], func=AF.Copy), r=[pt_[h]], w=[A[h][0]])
                    S.op("pool", lambda e: e.tensor_tensor(out=Xh[h][:], in0=AT[h][0][:], in1=self.ident[:], op=ALU.add),
                         r=[AT[h][0], self.ident], w=[Xh[h]])
                for k in range(5):
                    cu, nx = k % 2, (k + 1) % 2
                    for h in range(4):
                        pa = P()
                        S.op("pe", lambda e: e.matmul(pa[:], AT[h][cu][:], A[h][cu][:], start=True, stop=True), r=[AT[h][cu], A[h][cu]], w=[pa])
                        S.op("act", lambda e: e.activation(out=A[h][nx][:], in_=pa[:], func=AF.Copy), r=[pa], w=[A[h][nx]])
                        if k < 4:
                            pb_ = P()
                            S.op("pe", lambda e: e.matmul(pb_[:], A[h][cu][:], AT[h][cu][:], start=True, stop=True),
                                 r=[AT[h][cu], A[h][cu]], w=[pb_])
                            S.op("dve", lambda e: e.tensor_copy(out=AT[h][nx][:], in_=pb_[:]), r=[pb_], w=[AT[h][nx]])
                    for h in range(4):
                        px = P()
                        S.op("pe", lambda e: e.matmul(px[:], A[h][nx][:], Xh[h][:], start=True, stop=True), r=[A[h][nx], Xh[h]], w=[px])
                        S.op("dve", lambda e: e.tensor_tensor(out=Xh[h][:], in0=px[:], in1=Xh[h][:], op=ALU.add), r=[px, Xh[h]], w=[Xh[h]])
                for j in range(2):
                    rs = slice(64 * j, 64 * j + 64)
                    pk_ = [P() for _ in range(4)]
                    for h in range(4):
                        S.op("pe", lambda e: e.matmul(pk_[h][:], qkv[b][:, 4 + h, :], Sh[h][:], start=True, stop=True),
                             r=[qkv[b], Sh[h]], w=[pk_[h]])
                    for h in range(4):
                        col = tt * 4 + h
                        S.op("dve", lambda e: e.scalar_tensor_tensor(out=R[h][rs, :], in0=pk_[h][rs, :], scalar=EGn[rs, col:col + 1],
                                                                     in1=vt[rs, h, :], op0=ALU.mult, op1=ALU.add),
                             r=[pk_[h], EGn, vt], w=[R[h]])
                    py_ = [P() for _ in range(4)]
                    for h in range(4):
                        S.op("pe", lambda e: e.matmul(py_[h][:], Xh[h][:], R[h][:], start=True, stop=True), r=[Xh[h], R[h]], w=[py_[h]])
                    for h in range(4):
                        S.op("dve", lambda e: e.tensor_scalar(out=vn[h][rs, :], in0=py_[h][rs, :], scalar1=beta[rs, tt, h:h + 1], scalar2=None,
                                                               op0=ALU.mult), r=[py_[h], beta], w=[vn[h]])
                    po_ = [P() for _ in range(4)]
                    for h in range(4):
                        S.op("pe", lambda e: e.matmul(po_[h][:], qeg[h][:], Sh[h][:], start=True, stop=False), r=[qeg[h], Sh[h]], w=[po_[h]])
                        S.op("pe", lambda e: e.matmul(po_[h][:], aT[h][:], vn[h][:], start=False, stop=True), r=[aT[h], vn[h]], w=[po_[h]])
                    for h in range(4):
                        S.op("act", lambda e: e.activation(out=O[rs, h, :], in_=po_[h][rs, :], func=AF.Copy), r=[po_[h]], w=[O])
                    pd_ = [pdb[h % 2] for h in range(4)]
                    for h in range(4):
                        S.op("pe", lambda e: e.matmul(pd_[h][:], kd[rs, h, :], vn[h][rs, :], start=True, stop=True), r=[kd, vn[h]], w=[pd_[h]])
                        col = tt * 4 + h
                        S.op("dve", lambda e: e.scalar_tensor_tensor(out=Sh[h][:], in0=Sh[h][:], scalar=EGL[j][:, col:col + 1], in1=pd_[h][:],
                                                                     op0=ALU.mult, op1=ALU.add), r=[Sh[h], EGL[j], pd_[h]], w=[Sh[h]])
                for h in range(4):
                    S.op("act", lambda e: e.activation(out=junk[:], in_=O[:, h, :], func=AF.Square, accum_out=ss[:, h:h + 1]),
                         r=[O], w=[junk, ss])
                S.op("act", lambda e: e.activation(out=ss[:], in_=ss[:], func=AF.Sqrt, scale=1.0 / 128, bias=self.epsln[:, 1:2]),
                     r=[ss, self.epsln], w=[ss])
                S.op("dve", lambda e: e.reciprocal(out=ss[:], in_=ss[:]), r=[ss], w=[ss])
                S.op("dve", lambda e: e.tensor_tensor(out=O[:], in0=O[:], in1=ss[:].unsqueeze(2).to_broadcast([128, 4, 128]), op=ALU.mult),
                     r=[O, ss], w=[O])
                S.op("dve", lambda e: e.tensor_tensor(out=O[:], in0=O[:], in1=gn[:].unsqueeze(1).to_broadcast([128, 4, 128]), op=ALU.mult),
                     r=[O, gn], w=[O])
                S.op("act", lambda e: e.activation(out=zt[b][:], in_=zt[b][:], func=AF.Silu), r=[zt[b]], w=[zt[b]])
                S.op("dve", lambda e: e.tensor_tensor(out=ob[b][:], in0=O[:].rearrange("p h e -> p (h e)"), in1=zt[b][:], op=ALU.mult),
                     r=[O, zt[b]], w=[ob[b]])
                S.dma("pool", X["OB"][ts, :], ob[b][:], r=[ob[b]], w=["OB"])

_CACHE = {}


def _get_prog(nlayers=DEPTH, debug=(), stop_after=None):
    key = (nlayers, tuple(sorted(debug)), stop_after)
    if key not in _CACHE:
        p = Prog(nlayers, debug, stop_after)
        p.build()
        _CACHE[key] = p
    return _CACHE[key]


def make_in_maps(inputs, ncores=8, nlayers=DEPTH, layer0=0, xs=None):
    consts = host_constants(np.asarray(inputs["rel_bias"], np.float32))
    shared = {}
    for k, v in inputs.items():
        if k in ("x", "rel_bias"):
            continue
        a = np.ascontiguousarray(np.asarray(v, np.float32)[layer0:layer0 + nlayers])
        if k in ("w_uk", "w_uv"):
            a = a.reshape(nlayers, 256, 512)
        shared[k] = a
    shared.update(consts)
    x = np.asarray(inputs["x"], np.float32) if xs is None else xs
    maps = []
    for c in range(ncores):
        m = dict(shared)
        m["x"] = np.ascontiguousarray(x[c])
        maps.append(m)
    return maps


def kernel(**inputs):
    p = _get_prog(DEPTH)
    maps = make_in_maps(inputs, 8, DEPTH)
    res = run_bass_kernel_spmd(p.nc, maps, core_ids=list(range(8)))
    return np.stack([np.asarray(r["out"], np.float32) for r in res.results], axis=0)
```

```python
from contextlib import ExitStack
import math
import numpy as np
import ml_dtypes
import concourse.bass as bass
import concourse.mybir as mybir
from concourse.bass_utils import run_bass_kernel_spmd

F32 = mybir.dt.float32
BF16 = mybir.dt.bfloat16
AF = mybir.ActivationFunctionType
ALU = mybir.AluOpType
AX = mybir.AxisListType

T = 4096
D = 1024
NT = T // 128
DEPTH = 4
INW = 8016
DFF = 4096
ALPHA = (2 * DEPTH) ** 0.25
LN_EPS = 1e-5
RMS_EPS = 1e-6
NEG = -30000.0
IDX_WEIGHT_SCALE = 8 ** -0.5 * 64 ** -0.5

C_QA, C_KA, C_VA, C_QB, C_ZB, C_AB, C_QC, C_CKV, C_QI, C_KI, C_WI, C_GA = (
    0, 512, 1024, 1536, 3072, 3584, 3592, 4104, 4360, 4872, 4936, 4944)


class Sched:
    ENGS = ("pe", "act", "dve", "pool", "sp")
    NDS = 48

    def __init__(self, nc, es):
        self.nc = nc
        self.es = es
        self.epoch = 1
        self.eng = {"pe": nc.tensor, "act": nc.scalar, "dve": nc.vector, "pool": nc.gpsimd, "sp": nc.sync}
        self.sem = {e: es.enter_context(nc.semaphore("s_" + e)) for e in self.ENGS}
        self.dsem = [es.enter_context(nc.semaphore("d_%d" % i)) for i in range(self.NDS)]
        self.cnt = {e: 0 for e in self.ENGS}
        self.seen = {e: {} for e in self.ENGS}
        self.lastw = {}
        self.readers = {}
        self.ndma = 0
        self.ndq = {}
        half = self.NDS // 2
        self.dpool = {"sp": (0, half), "pool": (half, self.NDS - half)}
        self.dlast = {}
        self.ninst = 0
        self.scratch = es.enter_context(nc.sbuf_tensor("sched_scratch", [128, 1], F32))

    @staticmethod
    def _key(a):
        if isinstance(a, (str, tuple)):
            return a
        return a.name

    def _semof(self, sk):
        if isinstance(sk, tuple):
            return self.dsem[sk[1]]
        return self.sem[sk]

    def _deps(self, rk, wk):
        toks = []
        for k in rk:
            t = self.lastw.get(k)
            if t:
                toks.append(t)
        for k in wk:
            t = self.lastw.get(k)
            if t:
                toks.append(t)
            toks.extend(self.readers.get(k, {}).items())
        return toks

    def _wait(self, eng, toks):
        need = {}
        for (sk, v) in toks:
            if sk == "pe" and eng == "pe":
                continue
            if self.seen[eng].get(sk, 0) >= v:
                continue
            if need.get(sk, 0) < v:
                need[sk] = v
        e = self.eng[eng]
        for sk, v in need.items():
            self.seen[eng][sk] = v
            e.wait_ge(self._semof(sk), v)
            self.ninst += 1

    def _record(self, tok, rk, wk):
        for k in rk:
            d = self.readers.setdefault(k, {})
            if d.get(tok[0], 0) < tok[1]:
                d[tok[0]] = tok[1]
        for k in wk:
            self.lastw[k] = tok
            self.readers[k] = {}

    def op(self, eng, fn, r=(), w=()):
        rk = [self._key(a) for a in r]
        wk = [self._key(a) for a in w]
        self._wait(eng, self._deps(rk, wk))
        ins = fn(self.eng[eng])
        self.cnt[eng] += 1
        ins.then_inc(self.sem[eng], 1)
        self.ninst += 1
        tok = (eng, self.cnt[eng])
        self._record(tok, rk, wk)
        return tok

    def dma(self, qeng, out, in_, r=(), w=(), **kw):
        rk = [self._key(a) for a in r]
        wk = [self._key(a) for a in w]
        lo, n = self.dpool[qeng]
        k = self.ndq.get(qeng, 0)
        self.ndq[qeng] = k + 1
        idx = lo + k % n
        val = 16 * (k // n + 1)
        self.ndma += 1
        toks = self._deps(rk, wk)
        if val > 16:
            toks.append((("d", idx), val - 16))
        self._wait(qeng, toks)
        self.eng[qeng].dma_start(out=out, in_=in_, **kw).then_inc(self.dsem[idx], 16)
        self.ninst += 1
        tok = (("d", idx), val)
        self.dlast[idx] = val
        self._record(tok, rk, wk)
        return tok

    def barrier(self, new_epoch=False):
        toks = [(e, self.cnt[e]) for e in self.ENGS if self.cnt[e] > 0]
        toks += [(("d", i), v) for i, v in self.dlast.items()]
        for e in self.ENGS:
            self._wait(e, toks)
        self.lastw.clear()
        self.readers.clear()
        assert max(self.cnt.values()) < 30000, self.cnt
        if new_epoch:
            self.sem = {e: self.es.enter_context(self.nc.semaphore("s%d_%s" % (self.epoch, e))) for e in self.ENGS}
            self.epoch += 1
            self.cnt = {e: 0 for e in self.ENGS}
            for e in self.ENGS:
                for k in self.ENGS:
                    self.seen[e].pop(k, None)


def host_constants(rel_bias):
    def bucket(d):
        d = np.maximum(d, 0)
        large = 16 + (np.log(np.maximum(d, 16).astype(np.float32) / 16) / math.log(128 / 16) * 16).astype(np.int32)
        return np.where(d < 16, d, np.minimum(large, 31))
    j = np.arange(128)[:, None]
    c = np.arange(1280)[None, :]
    dist = c - j - 512
    b = bucket(dist)
    W = np.empty((128, 16, 1280), np.float32)
    for h in range(16):
        W[:, h, :] = np.where(dist >= 0, rel_bias[b, h], NEG)
    ident = np.eye(128, dtype=np.float32)
    ltri = np.tril(np.ones((128, 128), np.float32))
    s_ = np.arange(128)[:, None]
    c_ = np.arange(128)[None, :]
    same = (s_ // 64) == (c_ // 64)
    gd = np.stack([same & (s_ <= c_), same & (s_ < c_), same, np.broadcast_to(s_ < 64, (128, 128)),
                   np.broadcast_to(s_ >= 64, (128, 128))], axis=1).astype(np.float32)
    return {"c_toep": W.astype(ml_dtypes.bfloat16), "c_ident": ident, "c_ltri": ltri, "c_gdn": np.ascontiguousarray(gd)}


class Prog:
    def __init__(self, nlayers=DEPTH, debug=(), stop_after=None, inject=(), phases="ABCDEF"):
        self.nlayers = nlayers
        self.inject = set(inject)
        self.phases = phases
        self.debug = set(debug)
        self.stop_after = stop_after
        self.nc = bass.Bass("TRN2", target_bir_lowering=False)
        self.out_names = []

    def dram_in(self, name, shape, dt=F32):
        return self.nc.dram_tensor(name, list(shape), dt, kind="ExternalInput").ap()

    def dram_scr(self, name, shape, dt=F32):
        if name in self.inject:
            return self.nc.dram_tensor(name, list(shape), dt, kind="ExternalInput").ap()
        if name in self.debug:
            self.out_names.append(name)
            return self.nc.dram_tensor(name, list(shape), dt, kind="ExternalOutput").ap()
        return self.nc.dram_tensor(name, list(shape), dt).ap()

    def build(self):
        nc = self.nc
        L = self.nlayers
        I = {}
        I["x"] = self.dram_in("x", [T, D])
        I["w_in"] = self.dram_in("w_in", [L, D, INW])
        I["conv_w"] = self.dram_in("conv_w", [L, 4, 1536])
        I["a_log"] = self.dram_in("a_log", [L, 4])
        I["dt_bias"] = self.dram_in("dt_bias", [L, 4])
        I["gdn_norm"] = self.dram_in("gdn_norm", [L, 128])
        I["kv_norm"] = self.dram_in("kv_norm", [L, 256])
        I["idx_k_ln_g"] = self.dram_in("idx_k_ln_g", [L, 64])
        I["idx_k_ln_b"] = self.dram_in("idx_k_ln_b", [L, 64])
        I["w_uk"] = self.dram_in("w_uk", [L, 256, 512])
        I["w_uv"] = self.dram_in("w_uv", [L, 256, 512])
        I["w_branch_a"] = self.dram_in("w_branch_a", [L, 512, D])
        I["w_branch_b"] = self.dram_in("w_branch_b", [L, 512, D])
        I["w_branch_c"] = self.dram_in("w_branch_c", [L, 512, D])
        I["w_out"] = self.dram_in("w_out", [L, D, D])
        I["ln1_g"] = self.dram_in("ln1_g", [L, D])
        I["ln1_b"] = self.dram_in("ln1_b", [L, D])
        I["w_up"] = self.dram_in("w_up", [L, D, DFF])
        I["w_down"] = self.dram_in("w_down", [L, DFF, D])
        I["ln2_g"] = self.dram_in("ln2_g", [L, D])
        I["ln2_b"] = self.dram_in("ln2_b", [L, D])
        I["c_toep"] = self.dram_in("c_toep", [128, 16, 1280], BF16)
        I["c_ident"] = self.dram_in("c_ident", [128, 128])
        I["c_ltri"] = self.dram_in("c_ltri", [128, 128])
        I["c_gdn"] = self.dram_in("c_gdn", [128, 5, 128])
        self.I = I
        self.out = nc.dram_tensor("out", [T, D], F32, kind="ExternalOutput").ap()
        X = {}
        X["QA"] = self.dram_scr("QA", [512, T], BF16)
        X["KA"] = self.dram_scr("KA", [512, T], BF16)
        X["QKVB"] = self.dram_scr("QKVB", [1536, T], F32)
        X["QC"] = self.dram_scr("QC", [512, T], BF16)
        X["QI"] = self.dram_scr("QI", [512, T], BF16)
        X["VA"] = self.dram_scr("VA", [T, 520], BF16)
        X["ZB"] = self.dram_scr("ZB", [T, 512], F32)
        X["AB"] = self.dram_scr("AB", [T, 8], F32)
        X["CKV"] = self.dram_scr("CKV", [T, 256], F32)
        X["KIWI"] = self.dram_scr("KIWI", [T, 72], F32)
        X["GATES"] = self.dram_scr("GATES", [T, 3072], F32)
        X["OA"] = self.dram_scr("OA", [T, 512], BF16)
        X["OB"] = self.dram_scr("OB", [T, 512], BF16)
        X["OC"] = self.dram_scr("OC", [T, 512], BF16)
        X["X1"] = self.dram_scr("X1", [T, D], F32)
        X["XS"] = self.dram_scr("XS", [T, D], F32)
        self.X = X

        with ExitStack() as top:
            S = Sched(nc, top)
            self.S = S
            self.ident = top.enter_context(nc.sbuf_tensor("ident", [128, 128], F32))
            self.identb = top.enter_context(nc.sbuf_tensor("identb", [128, 128], BF16))
            self.i30k = top.enter_context(nc.sbuf_tensor("i30k", [128, 128], BF16))
            S.dma("sp", self.ident[:], I["c_ident"], w=[self.ident])
            S.op("dve", lambda e: e.tensor_copy(out=self.identb[:], in_=self.ident[:]), r=[self.ident], w=[self.identb])
            S.op("dve", lambda e: e.tensor_scalar(out=self.i30k[:], in0=self.ident[:], scalar1=-NEG, scalar2=None,
                                                   op0=ALU.mult), r=[self.ident], w=[self.i30k])
            self.ltri = top.enter_context(nc.sbuf_tensor("ltri", [128, 128], F32))
            S.dma("sp", self.ltri[:], I["c_ltri"], w=[self.ltri])
            self.gdc = top.enter_context(nc.sbuf_tensor("gdc", [128, 5, 128], F32))
            S.dma("sp", self.gdc[:], I["c_gdn"], w=[self.gdc])
            self.onesf = top.enter_context(nc.sbuf_tensor("onesf", [128, 128], F32))
            S.op("pool", lambda e: e.memset(self.onesf[:], 1.0), w=[self.onesf])
            self.onesb = top.enter_context(nc.sbuf_tensor("onesb", [128, 128], BF16))
            S.op("pool", lambda e: e.memset(self.onesb[:], 1.0), w=[self.onesb])
            self.epsln = top.enter_context(nc.sbuf_tensor("epsln", [128, 2], F32))
            S.op("pool", lambda e: e.memset(self.epsln[:, 0:1], LN_EPS), w=[self.epsln])
            S.op("pool", lambda e: e.memset(self.epsln[:, 1:2], RMS_EPS), w=[self.epsln])
            for l in range(self.nlayers):
                xin = I["x"] if l == 0 else X["XS"]
                xout = self.out if l == self.nlayers - 1 else X["XS"]
                for ph in "ABCDEF":
                    if ph not in self.phases:
                        continue
                    if ph == "A":
                        self.phase_a(l, xin)
                    elif ph == "B":
                        self.phase_b(l)
                    elif ph == "C":
                        self.phase_c(l)
                    elif ph == "D":
                        self.phase_d(l)
                    elif ph == "E":
                        self.phase_e(l, xin)
                    elif ph == "F":
                        self.phase_f(l, xout)
                    S.barrier(new_epoch=(ph in "CF"))
            S.barrier()
        return nc

    def phase_a(self, l, xin):
        nc, S, I, X = self.nc, self.S, self.I, self.X
        with ExitStack() as es:
            sb = lambda n, s, d: es.enter_context(nc.sbuf_tensor("%s_L%d" % (n, l), s, d))
            ps = lambda n, s, d: es.enter_context(nc.psum_tensor("%s_L%d" % (n, l), s, d))
            xT = sb("a_xT", [128, 8, T], BF16)
            xs = [sb("a_xs%d" % i, [128, D], F32) for i in range(2)]
            xb = [sb("a_xb%d" % i, [128, D], BF16) for i in range(2)]
            wst = [sb("a_wst%d" % i, [128, 8, 512], F32) for i in range(2)]
            wb = [sb("a_wb%d" % i, [128, 8, 512], BF16) for i in range(2)]
            stf = [sb("a_stf%d" % i, [128, T], F32) for i in range(2)]
            stb = [sb("a_stb%d" % i, [128, T], BF16) for i in range(2)]
            ptr = [ps("a_ptr%d" % i, [128, 8, 128], BF16) for i in range(2)]
            pacc = [ps("a_pacc%d" % i, [128, 512], F32) for i in range(4)]
            stv = [sb("a_stv%d" % i, [128, 8, 65], BF16) for i in range(2)]
            for i in range(2):
                S.op("pool", lambda e: e.memset(stv[i][:], 1.0), w=[stv[i]])

            for tt in range(NT):
                b = tt % 2
                S.dma("sp", xs[b][:], xin[tt * 128:(tt + 1) * 128, :], w=[xs[b]])
                S.op("pool", lambda e: e.tensor_copy(out=xb[b][:], in_=xs[b][:]), r=[xs[b]], w=[xb[b]])
                for kc in range(8):
                    S.op("pe", lambda e: e.transpose(out=ptr[b][:, kc, :], in_=xb[b][:, kc * 128:(kc + 1) * 128],
                                                     identity=self.identb[:]), r=[xb[b], self.identb], w=[ptr[b]])
                S.op("dve", lambda e: e.tensor_copy(out=xT[:, :, tt * 128:(tt + 1) * 128], in_=ptr[b][:]),
                     r=[ptr[b]], w=[xT])

            wsrc = I["w_in"][l].rearrange("(kc p) n -> p kc n", p=128)
            blocks = [
                ("FM", C_QA, 512, "QA", 0, 0.125), ("FM", C_KA, 512, "KA", 0, 1.0),
                ("FM", C_QB, 512, "QKVB", 0, 1.0), ("FM", C_QB + 512, 512, "QKVB", 512, 1.0),
                ("FM", C_QB + 1024, 512, "QKVB", 1024, 1.0),
                ("FM", C_QC, 512, "QC", 0, 0.125), ("FM", C_QI, 512, "QI", 0, 1.0),
                ("TM", C_VA, 512, "VA", 0, 1.0), ("TM", C_ZB, 512, "ZB", 0, 1.0), ("TM", C_AB, 8, "AB", 0, 1.0),
                ("TM", C_CKV, 256, "CKV", 0, 1.0), ("TM", C_KI, 72, "KIWI", 0, 1.0),
            ] + [("TM", C_GA + 512 * i, 512, "GATES", 512 * i, 1.0) for i in range(6)]
            ev = 0
            na = 0
            nst = 0
            def load_block(bi):
                _, c0_, ncol_, _, _, _ = blocks[bi]
                b_ = bi % 2
                S.dma("sp", wst[b_][:, :, 0:ncol_], wsrc[:, :, c0_:c0_ + ncol_], w=[wst[b_]])
                S.op("pool", lambda e: e.tensor_copy(out=wb[b_][:, :, 0:ncol_], in_=wst[b_][:, :, 0:ncol_]),
                     r=[wst[b_]], w=[wb[b_]])

            load_block(0)
            for bi, (kind, c0, ncol, dname, doff, scale) in enumerate(blocks):
                b = bi % 2
                dst = X[dname]
                isbf = dname in ("QA", "KA", "QC", "QI", "VA")
                if bi + 1 < len(blocks):
                    load_block(bi + 1)
                if kind == "FM":
                    for j in range(ncol // 128):
                        st = (stb if isbf else stf)[nst % 2]
                        nst += 1
                        for tg in range(8):
                            pa = pacc[na % 4]
                            na += 1
                            for kc in range(8):
                                S.op("pe", lambda e: e.matmul(pa[:], wb[b][:, kc, j * 128:(j + 1) * 128],
                                                              xT[:, kc, tg * 512:(tg + 1) * 512],
                                                              start=(kc == 0), stop=(kc == 7)),
                                     r=[wb[b], xT], w=[pa])
                            eng = "act" if ev % 2 == 0 else "dve"
                            ev += 1
                            if eng == "act":
                                S.op("act", lambda e: e.activation(out=st[:, tg * 512:(tg + 1) * 512], in_=pa[:],
                                                                   func=AF.Copy, scale=scale), r=[pa], w=[st])
                            else:
                                S.op("dve", lambda e: e.tensor_scalar(out=st[:, tg * 512:(tg + 1) * 512], in0=pa[:],
                                                                      scalar1=scale, scalar2=None, op0=ALU.mult),
                                     r=[pa], w=[st])
                        r0 = doff + j * 128
                        S.dma("pool", dst[r0:r0 + 128, :], st[:], r=[st], w=[dname])
                else:
                    for tt in range(NT):
                        pa = pacc[na % 4]
                        na += 1
                        for kc in range(8):
                            S.op("pe", lambda e: e.matmul(pa[:, 0:ncol], xT[:, kc, tt * 128:(tt + 1) * 128],
                                                          wb[b][:, kc, 0:ncol], start=(kc == 0), stop=(kc == 7)),
                                 r=[wb[b], xT], w=[pa])
                        if dname == "VA":
                            sv = stv[nst % 2]
                            nst += 1
                            S.op("act", lambda e: e.activation(out=sv[:, :, 0:64], in_=pa[:].rearrange("p (h d) -> p h d", h=8),
                                                               func=AF.Copy), r=[pa], w=[sv])
                            S.dma("pool", dst[tt * 128:(tt + 1) * 128, :], sv[:].rearrange("p h d -> p (h d)"), r=[sv], w=[dname])
                            continue
                        st = (stb if isbf else stf)[nst % 2]
                        nst += 1
                        eng = "act" if ev % 2 == 0 else "dve"
                        ev += 1
                        if eng == "act":
                            S.op("act", lambda e: e.activation(out=st[:, 0:ncol], in_=pa[:, 0:ncol], func=AF.Copy),
                                 r=[pa], w=[st])
                        else:
                            S.op("dve", lambda e: e.tensor_copy(out=st[:, 0:ncol], in_=pa[:, 0:ncol]), r=[pa], w=[st])
                        S.dma("pool", dst[tt * 128:(tt + 1) * 128, doff:doff + ncol], st[:, 0:ncol], r=[st], w=[dname])


    def layer_norm_tile(self, z, gt, bt, st1, junk, out):
        S = self.S
        S.op("dve", lambda e: e.tensor_reduce(out=st1[:, 0:1], in_=z[:], axis=AX.X, op=ALU.add), r=[z], w=[st1])
        S.op("dve", lambda e: e.tensor_scalar(out=st1[:, 1:2], in0=st1[:, 0:1], scalar1=-1.0 / D, scalar2=None,
                                               op0=ALU.mult), r=[st1], w=[st1])
        S.op("act", lambda e: e.activation(out=junk[:], in_=z[:], func=AF.Square, bias=st1[:, 1:2],
                                           accum_out=st1[:, 2:3]), r=[z, st1], w=[junk, st1])
        S.op("act", lambda e: e.activation(out=st1[:, 3:4], in_=st1[:, 2:3], func=AF.Sqrt, scale=1.0 / D,
                                           bias=self.epsln[:, 0:1]), r=[st1, self.epsln], w=[st1])
        S.op("dve", lambda e: e.reciprocal(out=st1[:, 4:5], in_=st1[:, 3:4]), r=[st1], w=[st1])
        S.op("dve", lambda e: e.tensor_scalar(out=z[:], in0=z[:], scalar1=st1[:, 1:2], scalar2=st1[:, 4:5],
                                               op0=ALU.add, op1=ALU.mult), r=[z, st1], w=[z])
        S.op("pool", lambda e: e.tensor_tensor(out=z[:], in0=z[:], in1=gt[:], op=ALU.mult), r=[z, gt], w=[z])
        S.op("pool", lambda e: e.tensor_tensor(out=out[:], in0=z[:], in1=bt[:], op=ALU.add), r=[z, bt], w=[out])

    def phase_e(self, l, xin):
        nc, S, I, X = self.nc, self.S, self.I, self.X
        with ExitStack() as es:
            sb = lambda n, s, d: es.enter_context(nc.sbuf_tensor("%s_L%d" % (n, l), s, d))
            ps = lambda n, s, d: es.enter_context(nc.psum_tensor("%s_L%d" % (n, l), s, d))
            wbr = sb("e_wbr", [128, 12, D], BF16)
            wout = sb("e_wout", [128, 8, D], BF16)
            wst = [sb("e_wst%d" % i, [128, 4, D], F32) for i in range(2)]
            gt = sb("e_g", [128, D], F32)
            bt = sb("e_b", [128, D], F32)
            obr = [sb("e_obr%d" % i, [128, 3, 512], BF16) for i in range(2)]
            obT2 = [sb("e_obT%d" % i, [128, 12, 128], BF16) for i in range(2)]
            sg = [sb("e_sg%d" % i, [128, 3072], F32) for i in range(2)]
            xs = [sb("e_xs%d" % i, [128, D], F32) for i in range(2)]
            y2 = [sb("e_y%d" % i, [128, D], F32) for i in range(2)]
            tmp2 = [sb("e_tmp%d" % i, [128, 512], F32) for i in range(2)]
            yb2 = [sb("e_yb%d" % i, [128, D], BF16) for i in range(2)]
            yT2 = [sb("e_yT%d" % i, [128, 8, 128], BF16) for i in range(2)]
            z2 = [sb("e_z%d" % i, [128, D], F32) for i in range(2)]
            junk2 = [sb("e_junk%d" % i, [128, D], F32) for i in range(2)]
            xo = [sb("e_xo%d" % i, [128, D], F32) for i in range(2)]
            st12 = [sb("e_st1%d" % i, [128, 8], F32) for i in range(2)]
            ptr = [ps("e_ptr%d" % i, [128, 8, 128], BF16) for i in range(2)]
            pacc = [ps("e_pacc%d" % i, [128, 512], F32) for i in range(4)]
            k = 0
            for bi, wn in enumerate(("w_branch_a", "w_branch_b", "w_branch_c")):
                S.dma("sp", wst[k % 2][:], I[wn][l].rearrange("(kc p) n -> p kc n", p=128), w=[wst[k % 2]])
                S.op("pool", lambda e: e.tensor_copy(out=wbr[:, bi * 4:(bi + 1) * 4, :], in_=wst[k % 2][:]),
                     r=[wst[k % 2]], w=[wbr])
                k += 1
            wo = I["w_out"][l].rearrange("(kc p) n -> p kc n", p=128)
            for hh in range(2):
                S.dma("sp", wst[k % 2][:], wo[:, hh * 4:(hh + 1) * 4, :], w=[wst[k % 2]])
                S.op("pool", lambda e: e.tensor_copy(out=wout[:, hh * 4:(hh + 1) * 4, :], in_=wst[k % 2][:]),
                     r=[wst[k % 2]], w=[wout])
                k += 1
            S.dma("sp", gt[:], I["ln1_g"][l:l + 1, :].partition_broadcast(128), w=[gt])
            S.dma("sp", bt[:], I["ln1_b"][l:l + 1, :].partition_broadcast(128), w=[bt])
            na = 0
            na_ = [na]

            def bind(tt):
                b = tt % 2
                return (b, slice(tt * 128, (tt + 1) * 128), obT2[b], y2[b], tmp2[b], yb2[b], yT2[b], z2[b], junk2[b], st12[b])

            def stage_x(tt):
                b, rows, obT, y, tmp, yb, yT, z, junk, st1 = bind(tt)
                na = na_[0]
                for bi, nm in enumerate(("OA", "OB", "OC")):
                    S.dma("sp", obr[b][:, bi, :], X[nm][rows, :], r=[nm], w=[obr[b]])
                S.dma("sp", sg[b][:], X["GATES"][rows, :], r=["GATES"], w=[sg[b]])
                S.dma("sp", xs[b][:], xin[rows, :], r=["XS"], w=[xs[b]])
                S.op("act", lambda e: e.activation(out=sg[b][:], in_=sg[b][:], func=AF.Sigmoid), r=[sg[b]], w=[sg[b]])
                for c in range(12):
                    p = ptr[0] if c < 8 else ptr[1]
                    S.op("pe", lambda e: e.transpose(out=p[:, c % 8, :], in_=obr[b][:, c // 4, (c % 4) * 128:(c % 4 + 1) * 128],
                                                     identity=self.identb[:]), r=[obr[b], self.identb], w=[p])
                S.op("dve", lambda e: e.tensor_copy(out=obT[:, 0:8, :], in_=ptr[0][:]), r=[ptr[0]], w=[obT])
                S.op("dve", lambda e: e.tensor_copy(out=obT[:, 8:12, :], in_=ptr[1][:, 0:4, :]), r=[ptr[1]], w=[obT])
                for nh in range(2):
                    cs = slice(nh * 512, (nh + 1) * 512)
                    for bi in range(3):
                        pa = pacc[na % 4]
                        na += 1
                        for kc in range(4):
                            S.op("pe", lambda e: e.matmul(pa[:], obT[:, bi * 4 + kc, :], wbr[:, bi * 4 + kc, cs],
                                                          start=(kc == 0), stop=(kc == 3)), r=[obT, wbr], w=[pa])
                        gsl = sg[b][:, bi * 1024 + nh * 512: bi * 1024 + (nh + 1) * 512]
                        if bi == 0:
                            S.op("dve", lambda e: e.tensor_tensor(out=y[:, cs], in0=pa[:], in1=gsl, op=ALU.mult),
                                 r=[pa, sg[b]], w=[y])
                        else:
                            S.op("dve", lambda e: e.tensor_tensor(out=tmp[:], in0=pa[:], in1=gsl, op=ALU.mult),
                                 r=[pa, sg[b]], w=[tmp])
                            S.op("pool", lambda e: e.tensor_tensor(out=y[:, cs], in0=y[:, cs], in1=tmp[:], op=ALU.add),
                                 r=[y, tmp], w=[y])
                na_[0] = na

            def stage_y(tt):
                b, rows, obT, y, tmp, yb, yT, z, junk, st1 = bind(tt)
                na = na_[0]
                S.op("act", lambda e: e.activation(out=yb[:], in_=y[:], func=AF.Copy), r=[y], w=[yb])
                for kc in range(8):
                    S.op("pe", lambda e: e.transpose(out=ptr[0][:, kc, :], in_=yb[:, kc * 128:(kc + 1) * 128],
                                                     identity=self.identb[:]), r=[yb, self.identb], w=[ptr[0]])
                S.op("dve", lambda e: e.tensor_copy(out=yT[:], in_=ptr[0][:]), r=[ptr[0]], w=[yT])
                for nh in range(2):
                    cs = slice(nh * 512, (nh + 1) * 512)
                    pa = pacc[na % 4]
                    na += 1
                    for kc in range(8):
                        S.op("pe", lambda e: e.matmul(pa[:], yT[:, kc, :], wout[:, kc, cs], start=(kc == 0), stop=(kc == 7)),
                             r=[yT, wout], w=[pa])
                    S.op("dve", lambda e: e.scalar_tensor_tensor(out=z[:, cs], in0=xs[b][:, cs], scalar=ALPHA, in1=pa[:],
                                                                 op0=ALU.mult, op1=ALU.add), r=[xs[b], pa], w=[z])
                self.layer_norm_tile(z, gt, bt, st1, junk, xo[b])
                S.dma("pool", X["X1"][rows, :], xo[b][:], r=[xo[b]], w=["X1"])
                na_[0] = na

            stage_x(0)
            for tt in range(NT):
                if tt + 1 < NT:
                    stage_x(tt + 1)
                stage_y(tt)

    def phase_f(self, l, xout):
        nc, S, I, X = self.nc, self.S, self.I, self.X
        TG = 256
        with ExitStack() as es:
            sb = lambda n, s, d: es.enter_context(nc.sbuf_tensor("%s_L%d" % (n, l), s, d))
            ps = lambda n, s, d: es.enter_context(nc.psum_tensor("%s_L%d" % (n, l), s, d))
            wup = sb("f_wup", [128, 8, DFF], BF16)
            wdn = sb("f_wdn", [128, 32, D], BF16)
            wst = sb("f_wst", [128, 8, 512], F32)
            gt = sb("f_g", [128, D], F32)
            bt = sb("f_b", [128, D], F32)
            x1 = sb("f_x1", [128, 2, D], F32)
            x1b = sb("f_x1b", [128, D], BF16)
            x1T = sb("f_x1T", [128, 8, TG], BF16)
            hT = sb("f_hT", [128, 32, TG], BF16)
            rl = [sb("f_rl%d" % i, [128, TG], F32) for i in range(2)]
            z = sb("f_z", [128, D], F32)
            junk = sb("f_junk", [128, D], F32)
            xo = [sb("f_xo%d" % i, [128, D], F32) for i in range(2)]
            st1 = sb("f_st1", [128, 8], F32)
            ptr = [ps("f_ptr%d" % i, [128, 8, 128], BF16) for i in range(2)]
            pup = [ps("f_pup%d" % i, [128, TG], F32) for i in range(2)]
            pdn = [ps("f_pdn%d" % i, [128, 512], F32) for i in range(2)]
            wu = I["w_up"][l].rearrange("(kc p) n -> p kc n", p=128)
            for c in range(8):
                S.dma("sp", wst[:], wu[:, :, c * 512:(c + 1) * 512], w=[wst])
                S.op("pool", lambda e: e.tensor_copy(out=wup[:, :, c * 512:(c + 1) * 512], in_=wst[:]), r=[wst], w=[wup])
            wd = I["w_down"][l].rearrange("(fc p) n -> p fc n", p=128)

            def load_wdown():
                for c in range(8):
                    S.dma("sp", wst[:].rearrange("p a n -> p (a n)").rearrange("p (f n) -> p f n", f=4),
                          wd[:, c * 4:(c + 1) * 4, :], w=[wst])
                    S.op("pool", lambda e: e.tensor_copy(out=wdn[:, c * 4:(c + 1) * 4, :],
                                                         in_=wst[:].rearrange("p a n -> p (a n)").rearrange("p (f n) -> p f n", f=4)),
                         r=[wst], w=[wdn])
            S.dma("sp", gt[:], I["ln2_g"][l:l + 1, :].partition_broadcast(128), w=[gt])
            S.dma("sp", bt[:], I["ln2_b"][l:l + 1, :].partition_broadcast(128), w=[bt])
            nu = 0
            nd = 0
            no = 0
            for tg in range(T // TG):
                for u in range(2):
                    tt = tg * 2 + u
                    S.dma("sp", x1[:, u, :], X["X1"][tt * 128:(tt + 1) * 128, :], r=["X1"], w=[x1])
                for u in range(2):
                    S.op("act", lambda e: e.activation(out=x1b[:], in_=x1[:, u, :], func=AF.Copy), r=[x1], w=[x1b])
                    for kc in range(8):
                        S.op("pe", lambda e: e.transpose(out=ptr[u][:, kc, :], in_=x1b[:, kc * 128:(kc + 1) * 128],
                                                         identity=self.identb[:]), r=[x1b, self.identb], w=[ptr[u]])
                    S.op("dve", lambda e: e.tensor_copy(out=x1T[:, :, u * 128:(u + 1) * 128], in_=ptr[u][:]),
                         r=[ptr[u]], w=[x1T])
                for fc in range(32):
                    pu = pup[nu % 2]
                    r_ = rl[nu % 2]
                    nu += 1
                    for kc in range(8):
                        S.op("pe", lambda e: e.matmul(pu[:], wup[:, kc, fc * 128:(fc + 1) * 128], x1T[:, kc, :],
                                                      start=(kc == 0), stop=(kc == 7)), r=[wup, x1T], w=[pu])
                    S.op("act", lambda e: e.activation(out=r_[:], in_=pu[:], func=AF.Relu), r=[pu], w=[r_])
                    S.op("dve", lambda e: e.tensor_tensor(out=hT[:, fc, :], in0=r_[:], in1=r_[:], op=ALU.mult), r=[r_], w=[hT])
                if tg == 0:
                    load_wdown()
                for u in range(2):
                    tt = tg * 2 + u
                    for nh in range(2):
                        cs = slice(nh * 512, (nh + 1) * 512)
                        pd = pdn[nd % 2]
                        nd += 1
                        for fc in range(32):
                            S.op("pe", lambda e: e.matmul(pd[:], hT[:, fc, u * 128:(u + 1) * 128], wdn[:, fc, cs],
                                                          start=(fc == 0), stop=(fc == 31)), r=[hT, wdn], w=[pd])
                        S.op("dve", lambda e: e.scalar_tensor_tensor(out=z[:, cs], in0=x1[:, u, cs], scalar=ALPHA, in1=pd[:],
                                                                     op0=ALU.mult, op1=ALU.add), r=[x1, pd], w=[z])
                    o = xo[no % 2]
                    no += 1
                    self.layer_norm_tile(z, gt, bt, st1, junk, o)
                    oname = "OUT" if xout is self.out else "XS"
                    S.dma("pool", xout[tt * 128:(tt + 1) * 128, :], o[:], r=[o], w=[oname])


    def phase_b(self, l):
        nc, S, I, X = self.nc, self.S, self.I, self.X
        with ExitStack() as es:
            sb = lambda n, s, d: es.enter_context(nc.sbuf_tensor("%s_L%d" % (n, l), s, d))
            ps = lambda n, s, d: es.enter_context(nc.psum_tensor("%s_L%d" % (n, l), s, d))
            kT = sb("b_kT", [128, 4, T], BF16)
            qT = sb("b_qT", [128, 4, T], BF16)
            va = sb("b_va", [128, NT, 520], BF16)
            toep = sb("b_toep", [128, 8, 1280], BF16)
            nsT = sb("b_nsT", [128, T], BF16)
            ksum = sb("b_ksum", [128, 4, 16], F32)
            kmT = sb("b_kmT", [128, 4, 32], BF16)
            gm = sb("b_gm", [128, 8, 16], F32)
            m8 = sb("b_m8", [128, 8, 8], F32)
            ns = sb("b_ns", [128, 8, 16], BF16)
            PT = [sb("b_PT%d" % i, [128, 512], BF16) for i in range(4)]
            oa = [sb("b_oa%d" % i, [128, 4, 512], BF16) for i in range(2)]
            rden = sb("b_rden", [128, 4], F32)
            E = sb("b_E", [128, 128, 128], BF16)
            S.op("dve", lambda e: e.tensor_copy(out=E[:], in_=self.i30k[:].unsqueeze(2).to_broadcast([128, 128, 128])),
                 r=[self.i30k], w=[E])
            pS = [ps("b_pS%d" % i, [128, 512], F32) for i in range(2)]
            pO = [ps("b_pO%d" % i, [128, 65], F32) for i in range(4)]
            es_sel = ExitStack()
            pg = es_sel.enter_context(nc.psum_tensor("b_pg_L%d" % l, [128, 8, 16], F32))
            ptr = es_sel.enter_context(nc.psum_tensor("b_ptr_L%d" % l, [128, 128], BF16))
            for a in range(4):
                S.dma("sp", kT[:, a, :], X["KA"][a * 128:(a + 1) * 128, :], r=["KA"], w=[kT])
                S.dma("sp", qT[:, a, :], X["QA"][a * 128:(a + 1) * 128, :], r=["QA"], w=[qT])
            vsrc = X["VA"].rearrange("(tt p) c -> p tt c", p=128)
            for c in range(4):
                S.dma("sp", va[:, c * 8:(c + 1) * 8, :], vsrc[:, c * 8:(c + 1) * 8, :], r=["VA"], w=[va])
            S.dma("sp", toep[:], I["c_toep"][:, 0:8, :], w=[toep])
            S.op("pool", lambda e: e.memset(nsT[:], 0.0), w=[nsT])
            S.op("pool", lambda e: e.memset(gm[:], -1e30), w=[gm])
            S.op("pool", lambda e: e.memset(ns[:], 0.0), w=[ns])
            import os
            bstop = int(os.environ.get("BSTOP", "99"))
            if bstop <= 1:
                es_sel.close()
                return
            S.op("dve", lambda e: e.tensor_reduce(out=ksum[:], in_=kT[:].rearrange("p a (n s) -> p a n s", s=256),
                                                   axis=AX.X, op=ALU.add), r=[kT], w=[ksum])
            S.op("pool", lambda e: e.memset(kmT[:], 0.0), w=[kmT])
            S.op("dve", lambda e: e.tensor_scalar(out=kmT[0:64, :, 0:16], in0=ksum[0:64, :, :], scalar1=1.0 / 256, scalar2=None,
                                                   op0=ALU.mult), r=[ksum], w=[kmT])
            S.op("dve", lambda e: e.tensor_scalar(out=kmT[64:128, :, 16:32], in0=ksum[64:128, :, :], scalar1=1.0 / 256, scalar2=None,
                                                   op0=ALU.mult), r=[ksum], w=[kmT])
            if bstop <= 2:
                es_sel.close()
                return
            for tt in range(NT):
                cur = tt // 2
                if cur <= 3:
                    continue
                for a in range(4):
                    S.op("pe", lambda e: e.matmul(pg[:, 2 * a:2 * a + 2, :], qT[:, a, tt * 128:(tt + 1) * 128],
                                                  kmT[:, a, :].rearrange("p (h n) -> p h n", h=2), start=True, stop=True),
                         r=[qT, kmT], w=[pg])
                bsub = int(os.environ.get("BSUB", "99"))
                S.op("act", lambda e: e.activation(out=gm[:, :, 0:cur], in_=pg[:, :, 0:cur], func=AF.Copy), r=[pg], w=[gm])
                if bsub <= 1:
                    continue
                for h in range(8):
                    S.op("dve", lambda e: e.max(out=m8[:, h, :], in_=gm[:, h, :]), r=[gm], w=[m8])
                if bsub <= 2:
                    continue
                for h in range(8):
                    S.op("dve", lambda e: e.tensor_scalar(out=ns[:, h, 0:cur], in0=gm[:, h, 0:cur], scalar1=m8[:, h, 2:3],
                                                           scalar2=1.0, op0=ALU.is_ge, op1=ALU.subtract), r=[gm, m8], w=[ns])
                if bsub <= 3:
                    continue
                S.op("pe", lambda e: e.transpose(out=ptr[:], in_=ns[:].rearrange("p h n -> p (h n)"), identity=self.identb[:]),
                     r=[ns, self.identb], w=[ptr])
                S.op("act", lambda e: e.activation(out=nsT[:, tt * 128:(tt + 1) * 128], in_=ptr[:], func=AF.Copy),
                     r=[ptr], w=[nsT])
            S.barrier()
            es_sel.close()
            pS = pS + [ps("b_pS%d" % i, [128, 512], F32) for i in (2, 3)]
            if bstop <= 3:
                return
            nS = [0]
            for qg in range(8):
                if bstop <= 4 and qg >= 1:
                    break
                ob = oa[qg % 2]
                qs = slice(qg * 512, (qg + 1) * 512)
                nkt = 4 * (qg + 1)

                def emit_S(h, kt):
                    hb, a_ = 64 * (h % 2), h // 2
                    m = kt - 4 * qg
                    off = 512 - 128 * m if m >= -1 else 768
                    n = kt // 2
                    need_sel = (qg >= 2) and (n <= 2 * qg)
                    p, pt = pS[nS[0] % 4], PT[nS[0] % 4]
                    nS[0] += 1
                    S.op("pe", lambda e: e.matmul(p[:], kT[hb:hb + 64, a_, kt * 128:(kt + 1) * 128], qT[hb:hb + 64, a_, qs],
                                                  start=True, stop=False), r=[kT, qT], w=[p])
                    S.op("pe", lambda e: e.matmul(p[:], self.identb[:], toep[:, h, off:off + 512],
                                                  start=False, stop=not need_sel), r=[self.identb, toep], w=[p])
                    if need_sel:
                        rr = h * 16 + n
                        S.op("pe", lambda e: e.matmul(p[:], E[:, rr, :], nsT[:, qs], start=False, stop=True), r=[E, nsT], w=[p])
                    return p, pt

                def emit_PV(h, kt, pt):
                    for u in range(4):
                        last = 4 * qg + u
                        if kt <= last:
                            S.op("pe", lambda e: e.matmul(pO[u][:], pt[:, u * 128:(u + 1) * 128], va[:, kt, h * 65:(h + 1) * 65],
                                                          start=(kt == 0), stop=(kt == last)), r=[pt, va], w=[pO[u]])

                def finalize(h):
                    for u in range(4):
                        S.op("dve", lambda e: e.reciprocal(out=rden[:, u:u + 1], in_=pO[u][:, 64:65]), r=[pO[u]], w=[rden])
                        S.op("dve", lambda e: e.tensor_scalar(out=ob[:, u, h * 64:(h + 1) * 64], in0=pO[u][:, 0:64],
                                                               scalar1=rden[:, u:u + 1], scalar2=None, op0=ALU.mult),
                             r=[pO[u], rden], w=[ob])

                steps = [(h, kt) for h in range(8) for kt in range(nkt)]
                pending = emit_S(*steps[0])
                for i, (h, kt) in enumerate(steps):
                    p, pt = pending
                    S.op("act", lambda e: e.activation(out=pt[:], in_=p[:], func=AF.Exp), r=[p], w=[pt])
                    if i + 1 < len(steps):
                        pending = emit_S(*steps[i + 1])
                    emit_PV(h, kt, pt)
                    if kt == nkt - 1:
                        finalize(h)
                S.dma("pool", X["OA"][qg * 512:(qg + 1) * 512, :].rearrange("(u p) c -> p u c", p=128), ob[:],
                      r=[ob], w=["OA"])

    def phase_d(self, l):
        nc, S, I, X = self.nc, self.S, self.I, self.X
        import os
        dstop = int(os.environ.get("DSTOP", "99"))
        with ExitStack() as es:
            sb = lambda n, s, d: es.enter_context(nc.sbuf_tensor("%s_L%d" % (n, l), s, d))
            ps = lambda n, s, d: es.enter_context(nc.psum_tensor("%s_L%d" % (n, l), s, d))
            c = sb("d_c", [128, NT, 256], BF16)
            cT = sb("d_cT", [128, 2, T], BF16)
            kiT2 = sb("d_kiT2", [128, T], BF16)
            toep = sb("d_toep", [128, 8, 1280], BF16)
            absw = sb("d_absw", [128, NT, 8], F32)
            sgn = sb("d_sgn", [128, NT, 8], F32)
            wukT = sb("d_wukT", [128, 4, 256], BF16)
            WBD = sb("d_WBD", [128, 4, 2, 256], BF16)
            wuvb = sb("d_wuvb", [128, 2, 512], BF16)
            pI = [ps("d_pI%d" % i, [128, 512], F32) for i in range(2)]
            pAcc = ps("d_pAcc", [128, 512], F32)
            pS = [ps("d_pS%d" % i, [128, 4, 128], F32) for i in range(2)]
            pOT = [ps("d_pOT%d" % i, [128, 512], F32) for i in range(2)]
            pDen = ps("d_pDen", [128, 512], F32)
            pIb = [p[:].bitcast(BF16).rearrange("p (a n) -> p a n", a=8) for p in pI]

            S.dma("sp", toep[:], I["c_toep"][:, 8:16, :], w=[toep])
            with ExitStack() as es2:
                sb2 = lambda n, s, d: es2.enter_context(nc.sbuf_tensor("%s_L%d" % (n, l), s, d))
                ckv = sb2("d_ckv", [128, NT, 256], F32)
                kiwi = sb2("d_kiwi", [128, NT, 72], F32)
                kin = sb2("d_kin", [128, NT, 128], BF16)
                kif = sb2("d_kif", [128, NT, 64], F32)
                junk = sb2("d_junk", [128, 256], F32)
                wst = sb2("d_wst", [128, 2, 512], F32)
                wukb = sb2("d_wukb", [128, 2, 512], BF16)
                kvn = sb2("d_kvn", [128, 256], F32)
                lng = sb2("d_lng", [128, 64], F32)
                lnb = sb2("d_lnb", [128, 64], F32)
                ss = sb2("d_ss", [128, NT], F32)
                rs = sb2("d_rs", [128, NT], F32)
                mu = sb2("d_mu", [128, NT], F32)
                S.dma("sp", wst[:], I["w_uk"][l].rearrange("(rc p) n -> p rc n", p=128), w=[wst])
                S.op("pool", lambda e: e.tensor_copy(out=wukb[:], in_=wst[:]), r=[wst], w=[wukb])
                for rc in range(2):
                    for a in range(4):
                        S.op("pe", lambda e: e.transpose(out=pIb[0][:, rc * 4 + a, :], in_=wukb[:, rc, a * 128:(a + 1) * 128],
                                                         identity=self.identb[:]), r=[wukb, self.identb], w=[pI[0]])
                S.op("dve", lambda e: e.tensor_copy(out=wukT[:].rearrange("p a (rc r) -> p rc a r", rc=2),
                                                     in_=pIb[0].rearrange("p (rc a) r -> p rc a r", rc=2)), r=[pI[0]], w=[wukT])
                S.op("pool", lambda e: e.memset(WBD[:], 0.0), w=[WBD])
                S.op("dve", lambda e: e.tensor_copy(out=WBD[0:64, :, 0, :], in_=wukT[0:64, :, :]), r=[wukT], w=[WBD])
                S.op("dve", lambda e: e.tensor_copy(out=WBD[64:128, :, 1, :], in_=wukT[64:128, :, :]), r=[wukT], w=[WBD])
                S.dma("sp", wst[:], I["w_uv"][l].rearrange("(rc p) n -> p rc n", p=128), r=[], w=[wst])
                S.op("pool", lambda e: e.tensor_copy(out=wuvb[:], in_=wst[:]), r=[wst], w=[wuvb])
                S.dma("sp", kvn[:], I["kv_norm"][l:l + 1, :].partition_broadcast(128), w=[kvn])
                S.dma("sp", lng[:], I["idx_k_ln_g"][l:l + 1, :].partition_broadcast(128), w=[lng])
                S.dma("sp", lnb[:], I["idx_k_ln_b"][l:l + 1, :].partition_broadcast(128), w=[lnb])
                csrc = X["CKV"].rearrange("(tt p) c -> p tt c", p=128)
                for q4 in range(4):
                    S.dma("sp", ckv[:, q4 * 8:(q4 + 1) * 8, :], csrc[:, q4 * 8:(q4 + 1) * 8, :], r=["CKV"], w=[ckv])
                for tt in range(NT):
                    S.op("act", lambda e: e.activation(out=junk[:], in_=ckv[:, tt, :], func=AF.Square, accum_out=ss[:, tt:tt + 1]),
                         r=[ckv], w=[junk, ss])
                S.op("act", lambda e: e.activation(out=rs[:], in_=ss[:], func=AF.Sqrt, scale=1.0 / 256, bias=self.epsln[:, 1:2]),
                     r=[ss, self.epsln], w=[rs])
                S.op("dve", lambda e: e.reciprocal(out=rs[:], in_=rs[:]), r=[rs], w=[rs])
                S.op("dve", lambda e: e.tensor_tensor(out=ckv[:], in0=ckv[:], in1=rs[:].unsqueeze(2).to_broadcast([128, NT, 256]),
                                                       op=ALU.mult), r=[ckv, rs], w=[ckv])
                S.op("dve", lambda e: e.tensor_tensor(out=c[:], in0=ckv[:], in1=kvn[:].unsqueeze(1).to_broadcast([128, NT, 256]),
                                                       op=ALU.mult), r=[ckv, kvn], w=[c])
                for g in range(NT // 4):
                    pb = pIb[g % 2]
                    for t4 in range(4):
                        for rc in range(2):
                            S.op("pe", lambda e: e.transpose(out=pb[:, t4 * 2 + rc, :], in_=c[:, g * 4 + t4, rc * 128:(rc + 1) * 128],
                                                             identity=self.identb[:]), r=[c, self.identb], w=[pI[g % 2]])
                    S.op("act", lambda e: e.activation(out=cT[:, :, g * 512:(g + 1) * 512].rearrange("p rc (t q) -> p t rc q", t=4),
                                                       in_=pb.rearrange("p (t rc) q -> p t rc q", t=4), func=AF.Copy),
                         r=[pI[g % 2]], w=[cT])
                S.dma("sp", kiwi[:], X["KIWI"].rearrange("(tt p) c -> p tt c", p=128), r=["KIWI"], w=[kiwi])
                S.op("dve", lambda e: e.tensor_reduce(out=mu[:], in_=kiwi[:, :, 0:64], axis=AX.X, op=ALU.add), r=[kiwi], w=[mu])
                S.op("dve", lambda e: e.tensor_scalar(out=mu[:], in0=mu[:], scalar1=-1.0 / 64, scalar2=None, op0=ALU.mult),
                     r=[mu], w=[mu])
                S.op("dve", lambda e: e.tensor_tensor(out=kif[:], in0=kiwi[:, :, 0:64], in1=mu[:].unsqueeze(2).to_broadcast([128, NT, 64]),
                                                       op=ALU.add), r=[kiwi, mu], w=[kif])
                S.op("dve", lambda e: e.tensor_tensor(out=ckv[:, :, 0:64], in0=kif[:], in1=kif[:], op=ALU.mult), r=[kif], w=[ckv])
                S.op("dve", lambda e: e.tensor_reduce(out=ss[:], in_=ckv[:, :, 0:64], axis=AX.X, op=ALU.add), r=[ckv], w=[ss])
                S.op("act", lambda e: e.activation(out=rs[:], in_=ss[:], func=AF.Sqrt, scale=1.0 / 64, bias=self.epsln[:, 0:1]),
                     r=[ss, self.epsln], w=[rs])
                S.op("dve", lambda e: e.reciprocal(out=rs[:], in_=rs[:]), r=[rs], w=[rs])
                S.op("dve", lambda e: e.tensor_tensor(out=kif[:], in0=kif[:], in1=rs[:].unsqueeze(2).to_broadcast([128, NT, 64]),
                                                       op=ALU.mult), r=[kif, rs], w=[kif])
                S.op("dve", lambda e: e.tensor_tensor(out=kif[:], in0=kif[:], in1=lng[:].unsqueeze(1).to_broadcast([128, NT, 64]),
                                                       op=ALU.mult), r=[kif, lng], w=[kif])
                for hf in range(2):
                    S.op("dve", lambda e: e.tensor_tensor(out=kin[:, :, hf * 64:(hf + 1) * 64], in0=kif[:],
                                                           in1=lnb[:].unsqueeze(1).to_broadcast([128, NT, 64]), op=ALU.add),
                         r=[kif, lnb], w=[kin])
                for g in range(NT // 8):
                    pb = pIb[g % 2]
                    for t8 in range(8):
                        S.op("pe", lambda e: e.transpose(out=pb[:, t8, :], in_=kin[:, g * 8 + t8, :], identity=self.identb[:]),
                             r=[kin, self.identb], w=[pI[g % 2]])
                    S.op("act", lambda e: e.activation(out=kiT2[:, g * 1024:(g + 1) * 1024].rearrange("p (t q) -> p t q", t=8),
                                                       in_=pb, func=AF.Copy), r=[pI[g % 2]], w=[kiT2])
                S.op("act", lambda e: e.activation(out=absw[:], in_=kiwi[:, :, 64:72], func=AF.Abs, scale=IDX_WEIGHT_SCALE),
                     r=[kiwi], w=[absw])
                S.op("dve", lambda e: e.tensor_scalar(out=sgn[:], in0=kiwi[:, :, 64:72], scalar1=0.0, scalar2=2.0,
                                                       op0=ALU.is_ge, op1=ALU.mult), r=[kiwi], w=[sgn])
                S.op("dve", lambda e: e.tensor_scalar(out=sgn[:], in0=sgn[:], scalar1=-1.0, scalar2=None, op0=ALU.add),
                     r=[sgn], w=[sgn])
                S.barrier()
            Isc = [sb("d_Isc%d" % i, [128, T], F32) for i in range(2)]
            work = sb("d_work", [128, T], F32)
            negm = [sb("d_negm%d" % i, [128, T], BF16) for i in range(2)]
            nmT = [sb("d_nmT%d" % i, [128, NT, 128], BF16) for i in range(2)]
            qit = [sb("d_qit%d" % i, [128, 4, 128], BF16) for i in range(2)]
            qct = [sb("d_qct%d" % i, [128, 4, 128], BF16) for i in range(2)]
            qlT = [sb("d_qlT%d" % i, [128, 2, 8, 128], BF16) for i in range(3)]
            Dsg = sb("d_Dsg", [128, 8, 128], BF16)
            Ph = [sb("d_Ph%d" % i, [128, 512], BF16) for i in range(2)]
            PT = [sb("d_PT%d" % i, [128, 4, 128], BF16) for i in range(2)]
            rdn = sb("d_rdn", [128, 512], F32)
            OTn = sb("d_OTn", [128, 2, 4, 128], BF16)
            oc = [sb("d_oc%d" % i, [128, 512], BF16) for i in range(2)]
            m8 = sb("d_m8", [128, 8], F32)
            st = sb("d_st", [128, 4], F32)
            if dstop <= 1:
                return
            qisrc = X["QI"].rearrange("(a p) t -> p a t", p=128)
            qcsrc = X["QC"].rearrange("(a p) t -> p a t", p=128)
            nS = [0]

            def stage1(qt):
                L, b = (qt + 1) * 128, qt % 2
                ts = slice(qt * 128, (qt + 1) * 128)
                S.dma("sp", qit[b][:], qisrc[:, :, ts], r=["QI"], w=[qit[b]])
                S.dma("sp", qct[b][:], qcsrc[:, :, ts], r=["QC"], w=[qct[b]])
                if qt >= 2:
                    S.op("pool", lambda e: e.tensor_tensor(out=Dsg[:], in0=self.identb[:].unsqueeze(1).to_broadcast([128, 8, 128]),
                                                           in1=sgn[:, qt, :].unsqueeze(2).to_broadcast([128, 8, 128]), op=ALU.mult),
                         r=[self.identb, sgn], w=[Dsg])
                    for kg in range((L + 511) // 512):
                        w_ = min(512, L - 512 * kg)
                        for h in range(8):
                            hb, a_ = 64 * (h % 2), h // 2
                            pi, ph = pI[h % 2], Ph[h % 2]
                            S.op("pe", lambda e: e.matmul(pi[:, 0:w_], qit[b][hb:hb + 64, a_, :], kiT2[hb:hb + 64, kg * 512:kg * 512 + w_],
                                                          start=True, stop=True), r=[qit[b], kiT2], w=[pi])
                            S.op("act", lambda e: e.activation(out=ph[:, 0:w_], in_=pi[:, 0:w_], func=AF.Relu,
                                                               scale=absw[:, qt, h:h + 1]), r=[pi, absw], w=[ph])
                            S.op("pe", lambda e: e.matmul(pAcc[:, 0:w_], Dsg[:, h, :], ph[:, 0:w_], start=(h == 0), stop=(h == 7)),
                                 r=[Dsg, ph], w=[pAcc])
                        S.op("act", lambda e: e.activation(out=Isc[b][:, kg * 512:kg * 512 + w_], in_=pAcc[:, 0:w_], func=AF.Copy),
                             r=[pAcc], w=[Isc[b]])
                for rc in range(2):
                    for hq in range(2):
                        pq = pI[(rc * 2 + hq) % 2]
                        for hh in range(4):
                            h = hq * 4 + hh
                            S.op("pe", lambda e: e.matmul(pq[:, hh * 128:(hh + 1) * 128], WBD[:, h // 2, h % 2, rc * 128:(rc + 1) * 128],
                                                          qct[b][:, h // 2, :], start=True, stop=True), r=[WBD, qct[b]], w=[pq])
                        S.op("act", lambda e: e.activation(out=qlT[qt % 3][:, rc, hq * 4:(hq + 1) * 4, :],
                                                           in_=pq[:].rearrange("p (h q) -> p h q", h=4), func=AF.Copy), r=[pq], w=[qlT[qt % 3]])

            def stage2(qt):
                L, b = (qt + 1) * 128, qt % 2
                if qt < 2:
                    return
                I_ = Isc[b]
                S.op("dve", lambda e: e.tensor_reduce(out=st[:, 0:1], in_=I_[:, 0:L], axis=AX.X, op=ALU.min), r=[I_], w=[st])
                S.op("dve", lambda e: e.tensor_scalar(out=st[:, 1:2], in0=st[:, 0:1], scalar1=-1.0, scalar2=1.0,
                                                       op0=ALU.mult, op1=ALU.add), r=[st], w=[st])
                S.op("dve", lambda e: e.tensor_scalar(out=I_[:, 0:L], in0=I_[:, 0:L], scalar1=st[:, 1:2], scalar2=None,
                                                       op0=ALU.add), r=[I_, st], w=[I_])
                S.op("dve", lambda e: e.tensor_tensor(out=I_[:, L - 128:L], in0=I_[:, L - 128:L], in1=self.ltri[:], op=ALU.mult),
                     r=[I_, self.ltri], w=[I_])
                for r_ in range(32):
                    src_ = I_ if r_ == 0 else work
                    S.op("dve", lambda e: e.max(out=m8[:], in_=src_[:, 0:L]), r=[src_], w=[m8])
                    if r_ < 31:
                        S.op("dve", lambda e: e.scalar_tensor_tensor(out=work[:, 0:L], in0=src_[:, 0:L], scalar=m8[:, 7:8],
                                                                     in1=src_[:, 0:L], op0=ALU.is_lt, op1=ALU.mult),
                             r=[src_, m8], w=[work])
                S.op("dve", lambda e: e.tensor_scalar(out=negm[b][:, 0:L], in0=I_[:, 0:L], scalar1=m8[:, 7:8], scalar2=1.0,
                                                       op0=ALU.is_ge, op1=ALU.subtract), r=[I_, m8], w=[negm[b]])

            def stage3(qt):
                nk, b = qt + 1, qt % 2
                ts = slice(qt * 128, (qt + 1) * 128)
                masked = qt >= 2
                if masked:
                    for g in range((nk + 7) // 8):
                        n8 = min(8, nk - 8 * g)
                        pb = pIb[g % 2]
                        for t8 in range(n8):
                            kt = g * 8 + t8
                            S.op("pe", lambda e: e.transpose(out=pb[:, t8, :], in_=negm[b][:, kt * 128:(kt + 1) * 128],
                                                             identity=self.identb[:]), r=[negm[b], self.identb], w=[pI[g % 2]])
                        S.op("act", lambda e: e.activation(out=nmT[b][:, g * 8:g * 8 + n8, :], in_=pb[:, 0:n8, :], func=AF.Copy, scale=-NEG),
                             r=[pI[g % 2]], w=[nmT[b]])
                def emit_S(half, kt):
                    hs = slice(4 * half, 4 * half + 4)
                    m = kt - qt
                    off = 512 - 128 * m if m >= -1 else 768
                    p_, pt = pS[nS[0] % 2], PT[nS[0] % 2]
                    nS[0] += 1
                    for rc in range(2):
                        S.op("pe", lambda e: e.matmul(p_[:], cT[:, rc, kt * 128:(kt + 1) * 128], qlT[qt % 3][:, rc, hs, :],
                                                      start=(rc == 0), stop=False), r=[cT, qlT[qt % 3]], w=[p_])
                    S.op("pe", lambda e: e.matmul(p_[:], self.identb[:], toep[:, hs, off:off + 128], start=False, stop=not masked),
                         r=[self.identb, toep], w=[p_])
                    if masked:
                        for hh in range(4):
                            S.op("pe", lambda e: e.matmul(p_[:, hh, :], self.identb[:], nmT[b][:, kt, :], start=False, stop=(hh == 3)),
                                 r=[self.identb, nmT[b]], w=[p_])
                    return p_, pt

                def emit_PV(half, kt, pt):
                    for rc in range(2):
                        S.op("pe", lambda e: e.matmul(pOT[rc][:], c[:, kt, rc * 128:(rc + 1) * 128], pt[:].rearrange("p h q -> p (h q)"),
                                                      start=(kt == 0), stop=(kt == nk - 1)), r=[c, pt], w=[pOT[rc]])
                    S.op("pe", lambda e: e.matmul(pDen[:], self.onesb[:], pt[:].rearrange("p h q -> p (h q)"),
                                                  start=(kt == 0), stop=(kt == nk - 1)), r=[self.onesb, pt], w=[pDen])

                def finalize(half):
                    S.op("dve", lambda e: e.reciprocal(out=rdn[:], in_=pDen[:]), r=[pDen], w=[rdn])
                    for rc in range(2):
                        S.op("dve", lambda e: e.tensor_tensor(out=OTn[:, rc, :, :].rearrange("p h q -> p (h q)"), in0=pOT[rc][:], in1=rdn[:],
                                                               op=ALU.mult), r=[pOT[rc], rdn], w=[OTn])
                    for hh in range(4):
                        h = 4 * half + hh
                        for rc in range(2):
                            S.op("pe", lambda e: e.matmul(pAcc[:, hh * 64:(hh + 1) * 64], OTn[:, rc, hh, :], wuvb[:, rc, h * 64:(h + 1) * 64],
                                                          start=(rc == 0), stop=(rc == 1)), r=[OTn, wuvb], w=[pAcc])
                    S.op("act", lambda e: e.activation(out=oc[b][:, half * 256:(half + 1) * 256], in_=pAcc[:, 0:256], func=AF.Copy),
                         r=[pAcc], w=[oc[b]])

                steps = [(half, kt) for half in range(2) for kt in range(nk)]
                pending = emit_S(*steps[0])
                for i, (half, kt) in enumerate(steps):
                    p_, pt = pending
                    S.op("act", lambda e: e.activation(out=pt[:], in_=p_[:], func=AF.Exp), r=[p_], w=[pt])
                    if i + 1 < len(steps):
                        pending = emit_S(*steps[i + 1])
                    emit_PV(half, kt, pt)
                    if kt == nk - 1:
                        finalize(half)
                S.dma("pool", X["OC"][ts, :], oc[b][:], r=[oc[b]], w=["OC"])

            tiles = [qt for qt in range(NT) if not (dstop <= 2 and qt not in (0, 1, 2, 5))]
            n = len(tiles)
            for i in range(n + 2):
                if i < n:
                    stage1(tiles[i])
                if 0 <= i - 1 < n:
                    stage2(tiles[i - 1])
                if 0 <= i - 2 < n:
                    stage3(tiles[i - 2])

    def phase_c(self, l):
        nc, S, I, X = self.nc, self.S, self.I, self.X
        import os
        cstop = int(os.environ.get("CSTOP", "99"))
        QS = 128 ** -0.5
        UTc, UTs, BON, H0, H1 = (self.gdc[:, i, :] for i in range(5))
        with ExitStack() as es:
            sb = lambda n, s, d: es.enter_context(nc.sbuf_tensor("%s_L%d" % (n, l), s, d))
            ps = lambda n, s, d: es.enter_context(nc.psum_tensor("%s_L%d" % (n, l), s, d))
            cw = sb("c_cw", [128, 12, 4], F32)
            xp = [sb("c_xp%d" % i, [128, T + 3], F32) for i in range(2)]
            y = [sb("c_y%d" % i, [128, T], F32) for i in range(2)]
            sq = sb("c_sq", [128, T], F32)
            rn = sb("c_rn", [128, T], F32)
            pn = [ps("c_pn%d" % i, [128, 512], F32) for i in range(2)]
            cwr = sb("c_cwr", [4, 1536], F32)
            S.dma("sp", cwr[:], I["conv_w"][l], w=[cwr])
            for c_ in range(12):
                S.op("pe", lambda e: e.transpose(out=pn[0][:, c_ * 4:(c_ + 1) * 4], in_=cwr[:, c_ * 128:(c_ + 1) * 128],
                                                 identity=self.ident[0:4, 0:4]), r=[cwr, self.ident], w=[pn[0]])
            S.op("dve", lambda e: e.tensor_copy(out=cw[:].rearrange("p c j -> p (c j)"), in_=pn[0][:, 0:48]), r=[pn[0]], w=[cw])
            for i in range(2):
                S.op("pool", lambda e: e.memset(xp[i][:, 0:3], 0.0), w=[xp[i]])
            for c_ in range(12):
                b = c_ % 2
                S.dma("sp", xp[b][:, 3:], X["QKVB"][c_ * 128:(c_ + 1) * 128, :], r=["QKVB"], w=[xp[b]])
                S.op("dve", lambda e: e.tensor_scalar(out=y[b][:], in0=xp[b][:, 0:T], scalar1=cw[:, c_, 0:1], scalar2=None,
                                                       op0=ALU.mult), r=[xp[b], cw], w=[y[b]])
                for j in range(1, 4):
                    S.op("dve", lambda e: e.scalar_tensor_tensor(out=y[b][:], in0=xp[b][:, j:j + T], scalar=cw[:, c_, j:j + 1],
                                                                 in1=y[b][:], op0=ALU.mult, op1=ALU.add), r=[xp[b], cw, y[b]], w=[y[b]])
                S.op("act", lambda e: e.activation(out=y[b][:], in_=y[b][:], func=AF.Silu), r=[y[b]], w=[y[b]])
                if c_ < 8:
                    S.op("pool", lambda e: e.tensor_tensor(out=sq[:], in0=y[b][:], in1=y[b][:], op=ALU.mult), r=[y[b]], w=[sq])
                    for g in range(8):
                        p_ = pn[g % 2]
                        S.op("pe", lambda e: e.matmul(p_[:], self.onesf[:], sq[:, g * 512:(g + 1) * 512], start=True, stop=True),
                             r=[self.onesf, sq], w=[p_])
                        S.op("act", lambda e: e.activation(out=rn[:, g * 512:(g + 1) * 512], in_=p_[:], func=AF.Sqrt,
                                                           bias=self.epsln[:, 1:2]), r=[p_, self.epsln], w=[rn])
                    S.op("dve", lambda e: e.reciprocal(out=rn[:], in_=rn[:]), r=[rn], w=[rn])
                    S.op("dve", lambda e: e.tensor_tensor(out=y[b][:], in0=y[b][:], in1=rn[:], op=ALU.mult), r=[y[b], rn], w=[y[b]])
                S.dma("pool", X["QKVB"][c_ * 128:(c_ + 1) * 128, :], y[b][:], r=[y[b]], w=["QKVB"])
        S.barrier()
        if cstop <= 1:
            return
        with ExitStack() as es:
            sb = lambda n, s, d: es.enter_context(nc.sbuf_tensor("%s_L%d" % (n, l), s, d))
            ps = lambda n, s, d: es.enter_context(nc.psum_tensor("%s_L%d" % (n, l), s, d))
            ab = sb("c_ab", [128, NT, 8], F32)
            dtb = sb("c_dtb", [128, 4], F32)
            nea = sb("c_nea", [128, 4], F32)
            one1 = sb("c_one1", [128, 1], F32)
            LA = sb("c_LA", [128, NT, 4], F32)
            nbeta = sb("c_nbeta", [128, NT, 4], F32)
            beta = sb("c_beta", [128, NT, 4], F32)
            G = sb("c_G", [128, 128], F32)
            EGn = sb("c_EGn", [128, 128], F32)
            KD = sb("c_KD", [128, 128], F32)
            EGL = [sb("c_EGL%d" % j, [128, 128], F32) for j in range(2)]
            gn = sb("c_gn", [128, 128], F32)
            qkv = [sb("c_qkv%d" % i, [128, 12, 128], F32) for i in range(2)]
            zt = [sb("c_zt%d" % i, [128, 512], F32) for i in range(2)]
            vt = sb("c_vt", [128, 4, 128], F32)
            kd = sb("c_kd", [128, 4, 128], F32)
            O = sb("c_O", [128, 4, 128], F32)
            ob = [sb("c_ob%d" % i, [128, 512], BF16) for i in range(2)]
            ss = sb("c_ss", [128, 4], F32)
            junk = sb("c_junk", [128, 128], F32)
            Sh = [sb("c_S%d" % h, [128, 128], F32) for h in range(4)]
            Dg = [sb("c_Dg%d" % h, [128, 128], F32) for h in range(4)]
            t1 = [sb("c_t1%d" % h, [128, 128], F32) for h in range(4)]
            decT = [sb("c_dec%d" % h, [128, 128], F32) for h in range(4)]
            EGb = [sb("c_EGb%d" % h, [128, 128], F32) for h in range(4)]
            qeg = [sb("c_qeg%d" % h, [128, 128], F32) for h in range(4)]
            aT = [sb("c_aT%d" % h, [128, 128], F32) for h in range(4)]
            AT = [[sb("c_AT%d_%d" % (h, i), [128, 128], F32) for i in range(2)] for h in range(4)]
            A = [[sb("c_A%d_%d" % (h, i), [128, 128], F32) for i in range(2)] for h in range(4)]
            Xh = [sb("c_X%d" % h, [128, 128], F32) for h in range(4)]
            R = [sb("c_R%d" % h, [128, 128], F32) for h in range(4)]
            vn = [sb("c_vn%d" % h, [128, 128], F32) for h in range(4)]
            class Sub:
                def __init__(self, tile, i):
                    self.ap = tile[:, i, :]
                    self.name = tile.name

                def __getitem__(self, k):
                    return self.ap[k]

            ppb = [ps("c_pp%d" % i, [128, 4, 128], F32) for i in range(4)]
            pp = [Sub(ppb[i % 4], i // 4) for i in range(16)]
            pdb = [ps("c_pd%d" % i, [128, 128], F32) for i in range(2)]
            ptv = ps("c_ptv", [128, 4, 128], F32)
            ptk = ps("c_ptk", [128, 4, 128], F32)
            npp = [0]

            def P():
                npp[0] += 1
                return pp[npp[0] % 16]

            S.dma("sp", ab[:], X["AB"].rearrange("(tt p) c -> p tt c", p=128), r=["AB"], w=[ab])
            S.dma("sp", dtb[:], I["dt_bias"][l:l + 1, :].partition_broadcast(128), w=[dtb])
            S.dma("sp", nea[:], I["a_log"][l:l + 1, :].partition_broadcast(128), w=[nea])
            S.dma("sp", gn[:], I["gdn_norm"][l:l + 1, :].partition_broadcast(128), w=[gn])
            S.op("pool", lambda e: e.memset(one1[:], 1.0), w=[one1])
            for h in range(4):
                S.op("pool", lambda e: e.memset(Sh[h][:], 0.0), w=[Sh[h]])
                S.op("pool", lambda e: e.memset(vn[h][:], 0.0), w=[vn[h]])
                S.op("pool", lambda e: e.memset(R[h][:], 0.0), w=[R[h]])
            S.op("act", lambda e: e.activation(out=nea[:], in_=nea[:], func=AF.Exp), r=[nea], w=[nea])
            S.op("dve", lambda e: e.tensor_scalar(out=nea[:], in0=nea[:], scalar1=-1.0, scalar2=None, op0=ALU.mult), r=[nea], w=[nea])
            S.op("dve", lambda e: e.tensor_tensor(out=LA[:], in0=ab[:, :, 0:4], in1=dtb[:].unsqueeze(1).to_broadcast([128, NT, 4]),
                                                   op=ALU.add), r=[ab, dtb], w=[LA])
            S.op("act", lambda e: e.activation(out=LA[:], in_=LA[:], func=AF.Exp), r=[LA], w=[LA])
            S.op("act", lambda e: e.activation(out=LA[:], in_=LA[:], func=AF.Ln, bias=one1[:, 0:1]), r=[LA, one1], w=[LA])
            S.op("dve", lambda e: e.tensor_tensor(out=LA[:], in0=LA[:], in1=nea[:].unsqueeze(1).to_broadcast([128, NT, 4]),
                                                   op=ALU.mult), r=[LA, nea], w=[LA])
            S.op("act", lambda e: e.activation(out=beta[:], in_=ab[:, :, 4:8], func=AF.Sigmoid), r=[ab], w=[beta])
            S.op("dve", lambda e: e.tensor_scalar(out=nbeta[:], in0=beta[:], scalar1=-1.0, scalar2=None, op0=ALU.mult),
                 r=[beta], w=[nbeta])
            LA2 = LA[:].rearrange("p t h -> p (t h)")
            p1, p2, p3, p4 = P(), P(), P(), P()
            S.op("pe", lambda e: e.matmul(p1[:], UTc, LA2, start=True, stop=True), r=[self.gdc, LA], w=[p1])
            S.op("pe", lambda e: e.matmul(p2[:], BON, LA2, start=True, stop=True), r=[self.gdc, LA], w=[p2])
            S.op("pe", lambda e: e.matmul(p3[:], H0, LA2, start=True, stop=True), r=[self.gdc, LA], w=[p3])
            S.op("pe", lambda e: e.matmul(p4[:], H1, LA2, start=True, stop=True), r=[self.gdc, LA], w=[p4])
            S.op("dve", lambda e: e.tensor_copy(out=G[:], in_=p1[:]), r=[p1], w=[G])
            S.op("act", lambda e: e.activation(out=EGn[:], in_=p1[:], func=AF.Exp), r=[p1], w=[EGn])
            S.op("dve", lambda e: e.tensor_scalar(out=EGn[:], in0=EGn[:], scalar1=-1.0, scalar2=None, op0=ALU.mult), r=[EGn], w=[EGn])
            S.op("dve", lambda e: e.tensor_tensor(out=KD[:], in0=p2[:], in1=G[:], op=ALU.subtract), r=[p2, G], w=[KD])
            S.op("act", lambda e: e.activation(out=KD[:], in_=KD[:], func=AF.Exp), r=[KD], w=[KD])
            S.op("act", lambda e: e.activation(out=EGL[0][:], in_=p3[:], func=AF.Exp), r=[p3], w=[EGL[0]])
            S.op("act", lambda e: e.activation(out=EGL[1][:], in_=p4[:], func=AF.Exp), r=[p4], w=[EGL[1]])
            qsrc = X["QKVB"].rearrange("(c p) t -> p c t", p=128)
            for tt in range(NT):
                if tt >= int(os.environ.get("CTILES", "32")) or (cstop <= 2 and tt >= 2):
                    break
                b = tt % 2
                ts = slice(tt * 128, (tt + 1) * 128)
                S.dma("sp", qkv[b][:], qsrc[:, :, ts], r=["QKVB"], w=[qkv[b]])
                S.dma("sp", zt[b][:], X["ZB"][ts, :], r=["ZB"], w=[zt[b]])
                for h in range(4):
                    S.op("pe", lambda e: e.transpose(out=ptv[:, h, :], in_=qkv[b][:, 8 + h, :], identity=self.ident[:]),
                         r=[qkv[b], self.ident], w=[ptv])
                    S.op("pe", lambda e: e.transpose(out=ptk[:, h, :], in_=qkv[b][:, 4 + h, :], identity=self.ident[:]),
                         r=[qkv[b], self.ident], w=[ptk])
                S.op("act", lambda e: e.activation(out=vt[:], in_=ptv[:], func=AF.Copy), r=[ptv], w=[vt])
                for h in range(4):
                    col = tt * 4 + h
                    S.op("dve", lambda e: e.tensor_scalar(out=kd[:, h, :], in0=ptk[:, h, :], scalar1=KD[:, col:col + 1], scalar2=None,
                                                           op0=ALU.mult), r=[ptk, KD], w=[kd])
                H4 = range(4)
                gcol = [G[:, tt * 4 + h:tt * 4 + h + 1] for h in H4]
                kTt = [qkv[b][:, 4 + h, :] for h in H4]
                qTt = [qkv[b][:, h, :] for h in H4]
                for h in H4:
                    S.op("pool", lambda e: e.tensor_scalar(out=Dg[h][:], in0=self.ident[:], scalar1=gcol[h], scalar2=None, op0=ALU.mult),
                         r=[self.ident, G], w=[Dg[h]])
                pg_ = [P() for _ in H4]
                for h in H4:
                    S.op("pe", lambda e: e.matmul(pg_[h][:], self.onesf[:], Dg[h][:], start=True, stop=True), r=[self.onesf, Dg[h]], w=[pg_[h]])
                for h in H4:
                    S.op("dve", lambda e: e.tensor_scalar(out=t1[h][:], in0=pg_[h][:], scalar1=gcol[h], scalar2=0.0, op0=ALU.subtract, op1=ALU.min),
                         r=[pg_[h], G], w=[t1[h]])
                for h in H4:
                    S.op("act", lambda e: e.activation(out=decT[h][:], in_=t1[h][:], func=AF.Exp), r=[t1[h]], w=[decT[h]])
                    S.op("act", lambda e: e.activation(out=EGb[h][:], in_=pg_[h][:], func=AF.Exp), r=[pg_[h]], w=[EGb[h]])
                for h in H4:
                    S.op("dve", lambda e: e.scalar_tensor_tensor(out=qeg[h][:], in0=qTt[h], scalar=QS, in1=EGb[h][:], op0=ALU.mult, op1=ALU.mult),
                         r=[qkv[b], EGb[h]], w=[qeg[h]])
                pkk = [P() for _ in H4]
                pqk = [P() for _ in H4]
                for h in H4:
                    S.op("pe", lambda e: e.matmul(pkk[h][:], kTt[h], kTt[h], start=True, stop=True), r=[qkv[b]], w=[pkk[h]])
                    S.op("pe", lambda e: e.matmul(pqk[h][:], kTt[h], qTt[h], start=True, stop=True), r=[qkv[b]], w=[pqk[h]])
                for h in H4:
                    S.op("dve", lambda e: e.scalar_tensor_tensor(out=AT[h][0][:], in0=pkk[h][:], scalar=nbeta[:, tt, h:h + 1], in1=decT[h][:],
                                                                 op0=ALU.mult, op1=ALU.mult), r=[pkk[h], nbeta, decT[h]], w=[AT[h][0]])
                    S.op("dve", lambda e: e.scalar_tensor_tensor(out=aT[h][:], in0=pqk[h][:], scalar=QS, in1=decT[h][:],
                                                                 op0=ALU.mult, op1=ALU.mult), r=[pqk[h], decT[h]], w=[aT[h]])
                for h in H4:
                    S.op("pool", lambda e: e.tensor_tensor(out=AT[h][0][:], in0=AT[h][0][:], in1=UTs, op=ALU.mult),
                         r=[AT[h][0], self.gdc], w=[AT[h][0]])
                    S.op("pool", lambda e: e.tensor_tensor(out=aT[h][:], in0=aT[h][:], in1=UTc, op=ALU.mult), r=[aT[h], self.gdc], w=[aT[h]])
                pt_ = [P() for _ in H4]
                for h in H4:
                    S.op("pe", lambda e: e.transpose(out=pt_[h][:], in_=AT[h][0][:], identity=self.ident[:]), r=[AT[h][0], self.ident], w=[pt_[h]])
                for h in H4:
                    S.op("act", lambda e: e.activation(out=A[h][0][:], in_=pt_[h][:], func=AF.Copy), r=[pt_[h]], w=[A[h][0]])
                    S.op("pool", lambda e: e.tensor_tensor(out=Xh[h][:], in0=AT[h][0][:], in1=self.ident[:], op=ALU.add),
                         r=[AT[h][0], self.ident], w=[Xh[h]])
                for k in range(5):
                    cu, nx = k % 2, (k + 1) % 2
                    for h in range(4):
                        pa = P()
                        S.op("pe", lambda e: e.matmul(pa[:], AT[h][cu][:], A[h][cu][:], start=True, stop=True), r=[AT[h][cu], A[h][cu]], w=[pa])
                        S.op("act", lambda e: e.activation(out=A[h][nx][:], in_=pa[:], func=AF.Copy), r=[pa], w=[A[h][nx]])
                        if k < 4:
                            pb_ = P()
                            S.op("pe", lambda e: e.matmul(pb_[:], A[h][cu][:], AT[h][cu][:], start=True, stop=True),
                                 r=[AT[h][cu], A[h][cu]], w=[pb_])
                            S.op("dve", lambda e: e.tensor_copy(out=AT[h][nx][:], in_=pb_[:]), r=[pb_], w=[AT[h][nx]])
                    for h in range(4):
                        px = P()
                        S.op("pe", lambda e: e.matmul(px[:], A[h][nx][:], Xh[h][:], start=True, stop=True), r=[A[h][nx], Xh[h]], w=[px])
                        S.op("dve", lambda e: e.tensor_tensor(out=Xh[h][:], in0=px[:], in1=Xh[h][:], op=ALU.add), r=[px, Xh[h]], w=[Xh[h]])
                for j in range(2):
                    rs = slice(64 * j, 64 * j + 64)
                    pk_ = [P() for _ in range(4)]
                    for h in range(4):
                        S.op("pe", lambda e: e.matmul(pk_[h][:], qkv[b][:, 4 + h, :], Sh[h][:], start=True, stop=True),
                             r=[qkv[b], Sh[h]], w=[pk_[h]])
                    for h in range(4):
                        col = tt * 4 + h
                        S.op("dve", lambda e: e.scalar_tensor_tensor(out=R[h][rs, :], in0=pk_[h][rs, :], scalar=EGn[rs, col:col + 1],
                                                                     in1=vt[rs, h, :], op0=ALU.mult, op1=ALU.add),
                             r=[pk_[h], EGn, vt], w=[R[h]])
                    py_ = [P() for _ in range(4)]
                    for h in range(4):
                        S.op("pe", lambda e: e.matmul(py_[h][:], Xh[h][:], R[h][:], start=True, stop=True), r=[Xh[h], R[h]], w=[py_[h]])
                    for h in range(4):
                        S.op("dve", lambda e: e.tensor_scalar(out=vn[h][rs, :], in0=py_[h][rs, :], scalar1=beta[rs, tt, h:h + 1], scalar2=None,
                                                               op0=ALU.mult), r=[py_[h], beta], w=[vn[h]])
                    po_ = [P() for _ in range(4)]
                    for h in range(4):
                        S.op("pe", lambda e: e.matmul(po_[h][:], qeg[h][:], Sh[h][:], start=True, stop=False), r=[qeg[h], Sh[h]], w=[po_[h]])
                        S.op("pe", lambda e: e.matmul(po_[h][:], aT[h][:], vn[h][:], start=False, stop=True), r=[aT[h], vn[h]], w=[po_[h]])
                    for h in range(4):
                        S.op("act", lambda e: e.activation(out=O[rs, h, :], in_=po_[h][rs, :], func=AF.Copy), r=[po_[h]], w=[O])
                    pd_ = [pdb[h % 2] for h in range(4)]
                    for h in range(4):
                        S.op("pe", lambda e: e.matmul(pd_[h][:], kd[rs, h, :], vn[h][rs, :], start=True, stop=True), r=[kd, vn[h]], w=[pd_[h]])
                        col = tt * 4 + h
                        S.op("dve", lambda e: e.scalar_tensor_tensor(out=Sh[h][:], in0=Sh[h][:], scalar=EGL[j][:, col:col + 1], in1=pd_[h][:],
                                                                     op0=ALU.mult, op1=ALU.add), r=[Sh[h], EGL[j], pd_[h]], w=[Sh[h]])
                for h in range(4):
                    S.op("act", lambda e: e.activation(out=junk[:], in_=O[:, h, :], func=AF.Square, accum_out=ss[:, h:h + 1]),
                         r=[O], w=[junk, ss])
                S.op("act", lambda e: e.activation(out=ss[:], in_=ss[:], func=AF.Sqrt, scale=1.0 / 128, bias=self.epsln[:, 1:2]),
                     r=[ss, self.epsln], w=[ss])
                S.op("dve", lambda e: e.reciprocal(out=ss[:], in_=ss[:]), r=[ss], w=[ss])
                S.op("dve", lambda e: e.tensor_tensor(out=O[:], in0=O[:], in1=ss[:].unsqueeze(2).to_broadcast([128, 4, 128]), op=ALU.mult),
                     r=[O, ss], w=[O])
                S.op("dve", lambda e: e.tensor_tensor(out=O[:], in0=O[:], in1=gn[:].unsqueeze(1).to_broadcast([128, 4, 128]), op=ALU.mult),
                     r=[O, gn], w=[O])
                S.op("act", lambda e: e.activation(out=zt[b][:], in_=zt[b][:], func=AF.Silu), r=[zt[b]], w=[zt[b]])
                S.op("dve", lambda e: e.tensor_tensor(out=ob[b][:], in0=O[:].rearrange("p h e -> p (h e)"), in1=zt[b][:], op=ALU.mult),
                     r=[O, zt[b]], w=[ob[b]])
                S.dma("pool", X["OB"][ts, :], ob[b][:], r=[ob[b]], w=["OB"])

_CACHE = {}


def _get_prog(nlayers=DEPTH, debug=(), stop_after=None):
    key = (nlayers, tuple(sorted(debug)), stop_after)
    if key not in _CACHE:
        p = Prog(nlayers, debug, stop_after)
        p.build()
        _CACHE[key] = p
    return _CACHE[key]


def make_in_maps(inputs, ncores=8, nlayers=DEPTH, layer0=0, xs=None):
    consts = host_constants(np.asarray(inputs["rel_bias"], np.float32))
    shared = {}
    for k, v in inputs.items():
        if k in ("x", "rel_bias"):
            continue
        a = np.ascontiguousarray(np.asarray(v, np.float32)[layer0:layer0 + nlayers])
        if k in ("w_uk", "w_uv"):
            a = a.reshape(nlayers, 256, 512)
        shared[k] = a
    shared.update(consts)
    x = np.asarray(inputs["x"], np.float32) if xs is None else xs
    maps = []
    for c in range(ncores):
        m = dict(shared)
        m["x"] = np.ascontiguousarray(x[c])
        maps.append(m)
    return maps


def kernel(**inputs):
    p = _get_prog(DEPTH)
    maps = make_in_maps(inputs, 8, DEPTH)
    res = run_bass_kernel_spmd(p.nc, maps, core_ids=list(range(8)))
    return np.stack([np.asarray(r["out"], np.float32) for r in res.results], axis=0)
```

```python
from contextlib import ExitStack
import math
import numpy as np
import ml_dtypes
import concourse.bass as bass
import concourse.mybir as mybir
from concourse.bass_utils import run_bass_kernel_spmd

F32 = mybir.dt.float32
BF16 = mybir.dt.bfloat16
AF = mybir.ActivationFunctionType
ALU = mybir.AluOpType
AX = mybir.AxisListType

T = 4096
D = 1024
NT = T // 128
DEPTH = 4
INW = 8016
DFF = 4096
ALPHA = (2 * DEPTH) ** 0.25
LN_EPS = 1e-5
RMS_EPS = 1e-6
NEG = -30000.0
IDX_WEIGHT_SCALE = 8 ** -0.5 * 64 ** -0.5

C_QA, C_KA, C_VA, C_QB, C_ZB, C_AB, C_QC, C_CKV, C_QI, C_KI, C_WI, C_GA = (
    0, 512, 1024, 1536, 3072, 3584, 3592, 4104, 4360, 4872, 4936, 4944)


class Sched:
    ENGS = ("pe", "act", "dve", "pool", "sp")
    NDS = 48

    def __init__(self, nc, es):
        self.nc = nc
        self.es = es
        self.epoch = 1
        self.eng = {"pe": nc.tensor, "act": nc.scalar, "dve": nc.vector, "pool": nc.gpsimd, "sp": nc.sync}
        self.sem = {e: es.enter_context(nc.semaphore("s_" + e)) for e in self.ENGS}
        self.dsem = [es.enter_context(nc.semaphore("d_%d" % i)) for i in range(self.NDS)]
        self.cnt = {e: 0 for e in self.ENGS}
        self.seen = {e: {} for e in self.ENGS}
        self.lastw = {}
        self.readers = {}
        self.ndma = 0
        self.ndq = {}
        nsp = (3 * self.NDS) // 4
        self.dpool = {"sp": (0, nsp), "pool": (nsp, self.NDS - nsp)}
        self.dlast = {}
        self.ninst = 0
        self.scratch = es.enter_context(nc.sbuf_tensor("sched_scratch", [128, 1], F32))

    @staticmethod
    def _key(a):
        if isinstance(a, (str, tuple)):
            return a
        return a.name

    def _semof(self, sk):
        if isinstance(sk, tuple):
            return self.dsem[sk[1]]
        return self.sem[sk]

    def _deps(self, rk, wk):
        toks = []
        for k in rk:
            t = self.lastw.get(k)
            if t:
                toks.append(t)
        for k in wk:
            t = self.lastw.get(k)
            if t:
                toks.append(t)
            toks.extend(self.readers.get(k, {}).items())
        return toks

    def _wait(self, eng, toks):
        need = {}
        for (sk, v) in toks:
            if sk == "pe" and eng == "pe":
                continue
            if self.seen[eng].get(sk, 0) >= v:
                continue
            if need.get(sk, 0) < v:
                need[sk] = v
        e = self.eng[eng]
        for sk, v in need.items():
            self.seen[eng][sk] = v
            e.wait_ge(self._semof(sk), v)
            self.ninst += 1

    def _record(self, tok, rk, wk):
        for k in rk:
            d = self.readers.setdefault(k, {})
            if d.get(tok[0], 0) < tok[1]:
                d[tok[0]] = tok[1]
        for k in wk:
            self.lastw[k] = tok
            self.readers[k] = {}

    def op(self, eng, fn, r=(), w=()):
        rk = [self._key(a) for a in r]
        wk = [self._key(a) for a in w]
        self._wait(eng, self._deps(rk, wk))
        ins = fn(self.eng[eng])
        self.cnt[eng] += 1
        ins.then_inc(self.sem[eng], 1)
        self.ninst += 1
        tok = (eng, self.cnt[eng])
        self._record(tok, rk, wk)
        return tok

    def dma(self, qeng, out, in_, r=(), w=(), **kw):
        rk = [self._key(a) for a in r]
        wk = [self._key(a) for a in w]
        lo, n = self.dpool[qeng]
        k = self.ndq.get(qeng, 0)
        self.ndq[qeng] = k + 1
        idx = lo + k % n
        val = 16 * (k // n + 1)
        self.ndma += 1
        toks = self._deps(rk, wk)
        if val > 16:
            toks.append((("d", idx), val - 16))
        self._wait(qeng, toks)
        self.eng[qeng].dma_start(out=out, in_=in_, **kw).then_inc(self.dsem[idx], 16)
        self.ninst += 1
        tok = (("d", idx), val)
        self.dlast[idx] = val
        self._record(tok, rk, wk)
        return tok

    def barrier(self, new_epoch=False):
        toks = [(e, self.cnt[e]) for e in self.ENGS if self.cnt[e] > 0]
        toks += [(("d", i), v) for i, v in self.dlast.items()]
        for e in self.ENGS:
            self._wait(e, toks)
        self.lastw.clear()
        self.readers.clear()
        assert max(self.cnt.values()) < 30000, self.cnt
        if new_epoch:
            self.sem = {e: self.es.enter_context(self.nc.semaphore("s%d_%s" % (self.epoch, e))) for e in self.ENGS}
            self.epoch += 1
            self.cnt = {e: 0 for e in self.ENGS}
            for e in self.ENGS:
                for k in self.ENGS:
                    self.seen[e].pop(k, None)


def host_constants(rel_bias):
    def bucket(d):
        d = np.maximum(d, 0)
        large = 16 + (np.log(np.maximum(d, 16).astype(np.float32) / 16) / math.log(128 / 16) * 16).astype(np.int32)
        return np.where(d < 16, d, np.minimum(large, 31))
    j = np.arange(128)[:, None]
    c = np.arange(1280)[None, :]
    dist = c - j - 512
    b = bucket(dist)
    W = np.empty((128, 16, 1280), np.float32)
    for h in range(16):
        W[:, h, :] = np.where(dist >= 0, rel_bias[b, h], NEG)
    ident = np.eye(128, dtype=np.float32)
    ltri = np.tril(np.ones((128, 128), np.float32))
    s_ = np.arange(128)[:, None]
    c_ = np.arange(128)[None, :]
    same = (s_ // 64) == (c_ // 64)
    gd = np.stack([same & (s_ <= c_), same & (s_ < c_), same, np.broadcast_to(s_ < 64, (128, 128)),
                   np.broadcast_to(s_ >= 64, (128, 128))], axis=1).astype(np.float32)
    return {"c_toep": W.astype(ml_dtypes.bfloat16), "c_ident": ident, "c_ltri": ltri, "c_gdn": np.ascontiguousarray(gd)}


class Prog:
    def __init__(self, nlayers=DEPTH, debug=(), stop_after=None, inject=(), phases="ABCDEF"):
        self.nlayers = nlayers
        self.inject = set(inject)
        self.phases = phases
        self.debug = set(debug)
        self.stop_after = stop_after
        self.nc = bass.Bass("TRN2", target_bir_lowering=False)
        self.out_names = []

    def dram_in(self, name, shape, dt=F32):
        return self.nc.dram_tensor(name, list(shape), dt, kind="ExternalInput").ap()

    def dram_scr(self, name, shape, dt=F32):
        if name in self.inject:
            return self.nc.dram_tensor(name, list(shape), dt, kind="ExternalInput").ap()
        if name in self.debug:
            self.out_names.append(name)
            return self.nc.dram_tensor(name, list(shape), dt, kind="ExternalOutput").ap()
        return self.nc.dram_tensor(name, list(shape), dt).ap()

    def build(self):
        nc = self.nc
        L = self.nlayers
        I = {}
        I["x"] = self.dram_in("x", [T, D])
        I["w_in"] = self.dram_in("w_in", [L, D, INW])
        I["conv_w"] = self.dram_in("conv_w", [L, 4, 1536])
        I["a_log"] = self.dram_in("a_log", [L, 4])
        I["dt_bias"] = self.dram_in("dt_bias", [L, 4])
        I["gdn_norm"] = self.dram_in("gdn_norm", [L, 128])
        I["kv_norm"] = self.dram_in("kv_norm", [L, 256])
        I["idx_k_ln_g"] = self.dram_in("idx_k_ln_g", [L, 64])
        I["idx_k_ln_b"] = self.dram_in("idx_k_ln_b", [L, 64])
        I["w_uk"] = self.dram_in("w_uk", [L, 256, 512])
        I["w_uv"] = self.dram_in("w_uv", [L, 256, 512])
        I["w_branch_a"] = self.dram_in("w_branch_a", [L, 512, D])
        I["w_branch_b"] = self.dram_in("w_branch_b", [L, 512, D])
        I["w_branch_c"] = self.dram_in("w_branch_c", [L, 512, D])
        I["w_out"] = self.dram_in("w_out", [L, D, D])
        I["ln1_g"] = self.dram_in("ln1_g", [L, D])
        I["ln1_b"] = self.dram_in("ln1_b", [L, D])
        I["w_up"] = self.dram_in("w_up", [L, D, DFF])
        I["w_down"] = self.dram_in("w_down", [L, DFF, D])
        I["ln2_g"] = self.dram_in("ln2_g", [L, D])
        I["ln2_b"] = self.dram_in("ln2_b", [L, D])
        I["c_toep"] = self.dram_in("c_toep", [128, 16, 1280], BF16)
        I["c_ident"] = self.dram_in("c_ident", [128, 128])
        I["c_ltri"] = self.dram_in("c_ltri", [128, 128])
        I["c_gdn"] = self.dram_in("c_gdn", [128, 5, 128])
        self.I = I
        self.out = nc.dram_tensor("out", [T, D], F32, kind="ExternalOutput").ap()
        X = {}
        X["QA"] = self.dram_scr("QA", [512, T], BF16)
        X["KA"] = self.dram_scr("KA", [512, T], BF16)
        X["QKVB"] = self.dram_scr("QKVB", [1536, T], F32)
        X["QC"] = self.dram_scr("QC", [512, T], BF16)
        X["QI"] = self.dram_scr("QI", [512, T], BF16)
        X["VA"] = self.dram_scr("VA", [T, 520], BF16)
        X["ZB"] = self.dram_scr("ZB", [T, 512], F32)
        X["AB"] = self.dram_scr("AB", [T, 8], F32)
        X["CKV"] = self.dram_scr("CKV", [T, 256], F32)
        X["KIWI"] = self.dram_scr("KIWI", [T, 72], F32)
        X["GATES"] = self.dram_scr("GATES", [T, 3072], F32)
        X["OA"] = self.dram_scr("OA", [T, 512], BF16)
        X["OB"] = self.dram_scr("OB", [T, 512], BF16)
        X["OC"] = self.dram_scr("OC", [T, 512], BF16)
        X["X1"] = self.dram_scr("X1", [T, D], F32)
        X["XS"] = self.dram_scr("XS", [T, D], F32)
        self.X = X

        with ExitStack() as top:
            S = Sched(nc, top)
            self.S = S
            self.ident = top.enter_context(nc.sbuf_tensor("ident", [128, 128], F32))
            self.identb = top.enter_context(nc.sbuf_tensor("identb", [128, 128], BF16))
            self.i30k = top.enter_context(nc.sbuf_tensor("i30k", [128, 128], BF16))
            S.dma("sp", self.ident[:], I["c_ident"], w=[self.ident])
            S.op("dve", lambda e: e.tensor_copy(out=self.identb[:], in_=self.ident[:]), r=[self.ident], w=[self.identb])
            S.op("dve", lambda e: e.tensor_scalar(out=self.i30k[:], in0=self.ident[:], scalar1=-NEG, scalar2=None,
                                                   op0=ALU.mult), r=[self.ident], w=[self.i30k])
            self.ltri = top.enter_context(nc.sbuf_tensor("ltri", [128, 128], F32))
            S.dma("sp", self.ltri[:], I["c_ltri"], w=[self.ltri])
            self.gdc = top.enter_context(nc.sbuf_tensor("gdc", [128, 5, 128], F32))
            S.dma("sp", self.gdc[:], I["c_gdn"], w=[self.gdc])
            self.onesf = top.enter_context(nc.sbuf_tensor("onesf", [128, 128], F32))
            S.op("pool", lambda e: e.memset(self.onesf[:], 1.0), w=[self.onesf])
            self.onesb = top.enter_context(nc.sbuf_tensor("onesb", [128, 128], BF16))
            S.op("pool", lambda e: e.memset(self.onesb[:], 1.0), w=[self.onesb])
            self.epsln = top.enter_context(nc.sbuf_tensor("epsln", [128, 2], F32))
            S.op("pool", lambda e: e.memset(self.epsln[:, 0:1], LN_EPS), w=[self.epsln])
            S.op("pool", lambda e: e.memset(self.epsln[:, 1:2], RMS_EPS), w=[self.epsln])
            for l in range(self.nlayers):
                xin = I["x"] if l == 0 else X["XS"]
                xout = self.out if l == self.nlayers - 1 else X["XS"]
                for ph in "ABCDEF":
                    if ph not in self.phases:
                        continue
                    if ph == "A":
                        self.phase_a(l, xin)
                    elif ph == "B":
                        self.phase_b(l)
                    elif ph == "C":
                        self.phase_c(l)
                    elif ph == "D":
                        self.phase_d(l)
                    elif ph == "E":
                        self.phase_e(l, xin)
                    elif ph == "F":
                        self.phase_f(l, xout)
                    S.barrier(new_epoch=(ph in "CF"))
            S.barrier()
        return nc

    def phase_a(self, l, xin):
        nc, S, I, X = self.nc, self.S, self.I, self.X
        with ExitStack() as es:
            sb = lambda n, s, d: es.enter_context(nc.sbuf_tensor("%s_L%d" % (n, l), s, d))
            ps = lambda n, s, d: es.enter_context(nc.psum_tensor("%s_L%d" % (n, l), s, d))
            xT = sb("a_xT", [128, 8, T], BF16)
            xs = [sb("a_xs%d" % i, [128, D], F32) for i in range(2)]
            xb = [sb("a_xb%d" % i, [128, D], BF16) for i in range(2)]
            wst = [sb("a_wst%d" % i, [128, 8, 512], F32) for i in range(2)]
            wb = [sb("a_wb%d" % i, [128, 8, 512], BF16) for i in range(2)]
            stf = [sb("a_stf%d" % i, [128, T], F32) for i in range(2)]
            stb = [sb("a_stb%d" % i, [128, T], BF16) for i in range(2)]
            ptr = [ps("a_ptr%d" % i, [128, 8, 128], BF16) for i in range(2)]
            pacc = [ps("a_pacc%d" % i, [128, 512], F32) for i in range(4)]
            stv = [sb("a_stv%d" % i, [128, 8, 65], BF16) for i in range(2)]
            for i in range(2):
                S.op("pool", lambda e: e.memset(stv[i][:], 1.0), w=[stv[i]])

            for tt in range(NT):
                b = tt % 2
                S.dma("sp", xs[b][:], xin[tt * 128:(tt + 1) * 128, :], w=[xs[b]])
                S.op("pool", lambda e: e.tensor_copy(out=xb[b][:], in_=xs[b][:]), r=[xs[b]], w=[xb[b]])
                for kc in range(8):
                    S.op("pe", lambda e: e.transpose(out=ptr[b][:, kc, :], in_=xb[b][:, kc * 128:(kc + 1) * 128],
                                                     identity=self.identb[:]), r=[xb[b], self.identb], w=[ptr[b]])
                S.op("dve", lambda e: e.tensor_copy(out=xT[:, :, tt * 128:(tt + 1) * 128], in_=ptr[b][:]),
                     r=[ptr[b]], w=[xT])

            wsrc = I["w_in"][l].rearrange("(kc p) n -> p kc n", p=128)
            blocks = [
                ("FM", C_QA, 512, "QA", 0, 0.125), ("FM", C_KA, 512, "KA", 0, 1.0),
                ("FM", C_QB, 512, "QKVB", 0, 1.0), ("FM", C_QB + 512, 512, "QKVB", 512, 1.0),
                ("FM", C_QB + 1024, 512, "QKVB", 1024, 1.0),
                ("FM", C_QC, 512, "QC", 0, 0.125), ("FM", C_QI, 512, "QI", 0, 1.0),
                ("TM", C_VA, 512, "VA", 0, 1.0), ("TM", C_ZB, 512, "ZB", 0, 1.0), ("TM", C_AB, 8, "AB", 0, 1.0),
                ("TM", C_CKV, 256, "CKV", 0, 1.0), ("TM", C_KI, 72, "KIWI", 0, 1.0),
            ] + [("TM", C_GA + 512 * i, 512, "GATES", 512 * i, 1.0) for i in range(6)]
            ev = 0
            na = 0
            nst = 0
            def load_block(bi):
                _, c0_, ncol_, _, _, _ = blocks[bi]
                b_ = bi % 2
                S.dma("sp", wst[b_][:, :, 0:ncol_], wsrc[:, :, c0_:c0_ + ncol_], w=[wst[b_]])
                S.op("pool", lambda e: e.tensor_copy(out=wb[b_][:, :, 0:ncol_], in_=wst[b_][:, :, 0:ncol_]),
                     r=[wst[b_]], w=[wb[b_]])

            load_block(0)
            for bi, (kind, c0, ncol, dname, doff, scale) in enumerate(blocks):
                b = bi % 2
                dst = X[dname]
                isbf = dname in ("QA", "KA", "QC", "QI", "VA")
                if bi + 1 < len(blocks):
                    load_block(bi + 1)
                if kind == "FM":
                    for j in range(ncol // 128):
                        st = (stb if isbf else stf)[nst % 2]
                        nst += 1
                        for tg in range(8):
                            pa = pacc[na % 4]
                            na += 1
                            for kc in range(8):
                                S.op("pe", lambda e: e.matmul(pa[:], wb[b][:, kc, j * 128:(j + 1) * 128],
                                                              xT[:, kc, tg * 512:(tg + 1) * 512],
                                                              start=(kc == 0), stop=(kc == 7)),
                                     r=[wb[b], xT], w=[pa])
                            eng = "act" if ev % 2 == 0 else "dve"
                            ev += 1
                            if eng == "act":
                                S.op("act", lambda e: e.activation(out=st[:, tg * 512:(tg + 1) * 512], in_=pa[:],
                                                                   func=AF.Copy, scale=scale), r=[pa], w=[st])
                            else:
                                S.op("dve", lambda e: e.tensor_scalar(out=st[:, tg * 512:(tg + 1) * 512], in0=pa[:],
                                                                      scalar1=scale, scalar2=None, op0=ALU.mult),
                                     r=[pa], w=[st])
                        r0 = doff + j * 128
                        S.dma("pool", dst[r0:r0 + 128, :], st[:], r=[st], w=[dname])
                else:
                    for tt in range(NT):
                        pa = pacc[na % 4]
                        na += 1
                        for kc in range(8):
                            S.op("pe", lambda e: e.matmul(pa[:, 0:ncol], xT[:, kc, tt * 128:(tt + 1) * 128],
                                                          wb[b][:, kc, 0:ncol], start=(kc == 0), stop=(kc == 7)),
                                 r=[wb[b], xT], w=[pa])
                        if dname == "VA":
                            sv = stv[nst % 2]
                            nst += 1
                            S.op("act", lambda e: e.activation(out=sv[:, :, 0:64], in_=pa[:].rearrange("p (h d) -> p h d", h=8),
                                                               func=AF.Copy), r=[pa], w=[sv])
                            S.dma("sp", dst[tt * 128:(tt + 1) * 128, :], sv[:].rearrange("p h d -> p (h d)"),
                                  r=[sv], w=[dname])
                            continue
                        st = (stb if isbf else stf)[nst % 2]
                        nst += 1
                        eng = "act" if ev % 2 == 0 else "dve"
                        ev += 1
                        if eng == "act":
                            S.op("act", lambda e: e.activation(out=st[:, 0:ncol], in_=pa[:, 0:ncol], func=AF.Copy),
                                 r=[pa], w=[st])
                        else:
                            S.op("dve", lambda e: e.tensor_copy(out=st[:, 0:ncol], in_=pa[:, 0:ncol]), r=[pa], w=[st])
                        S.dma("sp", dst[tt * 128:(tt + 1) * 128, doff:doff + ncol], st[:, 0:ncol],
                              r=[st], w=[dname])


    def layer_norm_tile(self, z, gt, bt, st1, junk, out):
        S = self.S
        S.op("dve", lambda e: e.tensor_reduce(out=st1[:, 0:1], in_=z[:], axis=AX.X, op=ALU.add), r=[z], w=[st1])
        S.op("dve", lambda e: e.tensor_scalar(out=st1[:, 1:2], in0=st1[:, 0:1], scalar1=-1.0 / D, scalar2=None,
                                               op0=ALU.mult), r=[st1], w=[st1])
        S.op("act", lambda e: e.activation(out=junk[:], in_=z[:], func=AF.Square, bias=st1[:, 1:2],
                                           accum_out=st1[:, 2:3]), r=[z, st1], w=[junk, st1])
        S.op("act", lambda e: e.activation(out=st1[:, 3:4], in_=st1[:, 2:3], func=AF.Sqrt, scale=1.0 / D,
                                           bias=self.epsln[:, 0:1]), r=[st1, self.epsln], w=[st1])
        S.op("dve", lambda e: e.reciprocal(out=st1[:, 4:5], in_=st1[:, 3:4]), r=[st1], w=[st1])
        S.op("dve", lambda e: e.tensor_scalar(out=z[:], in0=z[:], scalar1=st1[:, 1:2], scalar2=st1[:, 4:5],
                                               op0=ALU.add, op1=ALU.mult), r=[z, st1], w=[z])
        S.op("pool", lambda e: e.tensor_tensor(out=z[:], in0=z[:], in1=gt[:], op=ALU.mult), r=[z, gt], w=[z])
        S.op("pool", lambda e: e.tensor_tensor(out=out[:], in0=z[:], in1=bt[:], op=ALU.add), r=[z, bt], w=[out])

    def phase_e(self, l, xin):
        nc, S, I, X = self.nc, self.S, self.I, self.X
        with ExitStack() as es:
            sb = lambda n, s, d: es.enter_context(nc.sbuf_tensor("%s_L%d" % (n, l), s, d))
            ps = lambda n, s, d: es.enter_context(nc.psum_tensor("%s_L%d" % (n, l), s, d))
            wbr = sb("e_wbr", [128, 12, D], BF16)
            wout = sb("e_wout", [128, 8, D], BF16)
            wst = [sb("e_wst%d" % i, [128, 4, D], F32) for i in range(2)]
            gt = sb("e_g", [128, D], F32)
            bt = sb("e_b", [128, D], F32)
            obr = [sb("e_obr%d" % i, [128, 3, 512], BF16) for i in range(2)]
            obT2 = [sb("e_obT%d" % i, [128, 12, 128], BF16) for i in range(2)]
            sg = [sb("e_sg%d" % i, [128, 3072], F32) for i in range(2)]
            xs = [sb("e_xs%d" % i, [128, D], F32) for i in range(2)]
            y2 = [sb("e_y%d" % i, [128, D], F32) for i in range(2)]
            tmp2 = [sb("e_tmp%d" % i, [128, 512], F32) for i in range(2)]
            yb2 = [sb("e_yb%d" % i, [128, D], BF16) for i in range(2)]
            yT2 = [sb("e_yT%d" % i, [128, 8, 128], BF16) for i in range(2)]
            z2 = [sb("e_z%d" % i, [128, D], F32) for i in range(2)]
            junk2 = [sb("e_junk%d" % i, [128, D], F32) for i in range(2)]
            xo = [sb("e_xo%d" % i, [128, D], F32) for i in range(2)]
            st12 = [sb("e_st1%d" % i, [128, 8], F32) for i in range(2)]
            ptr = [ps("e_ptr%d" % i, [128, 8, 128], BF16) for i in range(2)]
            pacc = [ps("e_pacc%d" % i, [128, 512], F32) for i in range(4)]
            k = 0
            for bi, wn in enumerate(("w_branch_a", "w_branch_b", "w_branch_c")):
                S.dma("sp", wst[k % 2][:], I[wn][l].rearrange("(kc p) n -> p kc n", p=128), w=[wst[k % 2]])
                S.op("pool", lambda e: e.tensor_copy(out=wbr[:, bi * 4:(bi + 1) * 4, :], in_=wst[k % 2][:]),
                     r=[wst[k % 2]], w=[wbr])
                k += 1
            wo = I["w_out"][l].rearrange("(kc p) n -> p kc n", p=128)
            for hh in range(2):
                S.dma("sp", wst[k % 2][:], wo[:, hh * 4:(hh + 1) * 4, :], w=[wst[k % 2]])
                S.op("pool", lambda e: e.tensor_copy(out=wout[:, hh * 4:(hh + 1) * 4, :], in_=wst[k % 2][:]),
                     r=[wst[k % 2]], w=[wout])
                k += 1
            S.dma("sp", gt[:], I["ln1_g"][l:l + 1, :].partition_broadcast(128), w=[gt])
            S.dma("sp", bt[:], I["ln1_b"][l:l + 1, :].partition_broadcast(128), w=[bt])
            na = 0
            na_ = [na]

            def bind(tt):
                b = tt % 2
                return (b, slice(tt * 128, (tt + 1) * 128), obT2[b], y2[b], tmp2[b], yb2[b], yT2[b], z2[b], junk2[b], st12[b])

            def stage_x(tt):
                b, rows, obT, y, tmp, yb, yT, z, junk, st1 = bind(tt)
                na = na_[0]
                for bi, nm in enumerate(("OA", "OB", "OC")):
                    S.dma("sp", obr[b][:, bi, :], X[nm][rows, :], r=[nm], w=[obr[b]])
                S.dma("sp", sg[b][:], X["GATES"][rows, :], r=["GATES"], w=[sg[b]])
                S.dma("sp", xs[b][:], xin[rows, :], r=["XS"], w=[xs[b]])
                S.op("act", lambda e: e.activation(out=sg[b][:], in_=sg[b][:], func=AF.Sigmoid), r=[sg[b]], w=[sg[b]])
                for c in range(12):
                    p = ptr[0] if c < 8 else ptr[1]
                    S.op("pe", lambda e: e.transpose(out=p[:, c % 8, :], in_=obr[b][:, c // 4, (c % 4) * 128:(c % 4 + 1) * 128],
                                                     identity=self.identb[:]), r=[obr[b], self.identb], w=[p])
                S.op("dve", lambda e: e.tensor_copy(out=obT[:, 0:8, :], in_=ptr[0][:]), r=[ptr[0]], w=[obT])
                S.op("dve", lambda e: e.tensor_copy(out=obT[:, 8:12, :], in_=ptr[1][:, 0:4, :]), r=[ptr[1]], w=[obT])
                for nh in range(2):
                    cs = slice(nh * 512, (nh + 1) * 512)
                    for bi in range(3):
                        pa = pacc[na % 4]
                        na += 1
                        for kc in range(4):
                            S.op("pe", lambda e: e.matmul(pa[:], obT[:, bi * 4 + kc, :], wbr[:, bi * 4 + kc, cs],
                                                          start=(kc == 0), stop=(kc == 3)), r=[obT, wbr], w=[pa])
                        gsl = sg[b][:, bi * 1024 + nh * 512: bi * 1024 + (nh + 1) * 512]
                        if bi == 0:
                            S.op("dve", lambda e: e.tensor_tensor(out=y[:, cs], in0=pa[:], in1=gsl, op=ALU.mult),
                                 r=[pa, sg[b]], w=[y])
                        else:
                            S.op("dve", lambda e: e.tensor_tensor(out=tmp[:], in0=pa[:], in1=gsl, op=ALU.mult),
                                 r=[pa, sg[b]], w=[tmp])
                            S.op("pool", lambda e: e.tensor_tensor(out=y[:, cs], in0=y[:, cs], in1=tmp[:], op=ALU.add),
                                 r=[y, tmp], w=[y])
                na_[0] = na

            def stage_y(tt):
                b, rows, obT, y, tmp, yb, yT, z, junk, st1 = bind(tt)
                na = na_[0]
                S.op("act", lambda e: e.activation(out=yb[:], in_=y[:], func=AF.Copy), r=[y], w=[yb])
                for kc in range(8):
                    S.op("pe", lambda e: e.transpose(out=ptr[0][:, kc, :], in_=yb[:, kc * 128:(kc + 1) * 128],
                                                     identity=self.identb[:]), r=[yb, self.identb], w=[ptr[0]])
                S.op("dve", lambda e: e.tensor_copy(out=yT[:], in_=ptr[0][:]), r=[ptr[0]], w=[yT])
                for nh in range(2):
                    cs = slice(nh * 512, (nh + 1) * 512)
                    pa = pacc[na % 4]
                    na += 1
                    for kc in range(8):
                        S.op("pe", lambda e: e.matmul(pa[:], yT[:, kc, :], wout[:, kc, cs], start=(kc == 0), stop=(kc == 7)),
                             r=[yT, wout], w=[pa])
                    S.op("dve", lambda e: e.scalar_tensor_tensor(out=z[:, cs], in0=xs[b][:, cs], scalar=ALPHA, in1=pa[:],
                                                                 op0=ALU.mult, op1=ALU.add), r=[xs[b], pa], w=[z])
                self.layer_norm_tile(z, gt, bt, st1, junk, xo[b])
                S.dma("pool", X["X1"][rows, :], xo[b][:], r=[xo[b]], w=["X1"])
                na_[0] = na

            stage_x(0)
            for tt in range(NT):
                if tt + 1 < NT:
                    stage_x(tt + 1)
                stage_y(tt)

    def phase_f(self, l, xout):
        nc, S, I, X = self.nc, self.S, self.I, self.X
        TG = 256
        with ExitStack() as es:
            sb = lambda n, s, d: es.enter_context(nc.sbuf_tensor("%s_L%d" % (n, l), s, d))
            ps = lambda n, s, d: es.enter_context(nc.psum_tensor("%s_L%d" % (n, l), s, d))
            wup = sb("f_wup", [128, 8, DFF], BF16)
            wdn = sb("f_wdn", [128, 32, D], BF16)
            wst = sb("f_wst", [128, 8, 512], F32)
            gt = sb("f_g", [128, D], F32)
            bt = sb("f_b", [128, D], F32)
            x1 = sb("f_x1", [128, 2, D], F32)
            x1b = sb("f_x1b", [128, D], BF16)
            x1T = sb("f_x1T", [128, 8, TG], BF16)
            hT = sb("f_hT", [128, 32, TG], BF16)
            rl = [sb("f_rl%d" % i, [128, TG], F32) for i in range(2)]
            z = sb("f_z", [128, D], F32)
            junk = sb("f_junk", [128, D], F32)
            xo = [sb("f_xo%d" % i, [128, D], F32) for i in range(2)]
            st1 = sb("f_st1", [128, 8], F32)
            ptr = [ps("f_ptr%d" % i, [128, 8, 128], BF16) for i in range(2)]
            pup = [ps("f_pup%d" % i, [128, TG], F32) for i in range(2)]
            pdn = [ps("f_pdn%d" % i, [128, 512], F32) for i in range(2)]
            wu = I["w_up"][l].rearrange("(kc p) n -> p kc n", p=128)
            for c in range(8):
                S.dma("sp", wst[:], wu[:, :, c * 512:(c + 1) * 512], w=[wst])
                S.op("pool", lambda e: e.tensor_copy(out=wup[:, :, c * 512:(c + 1) * 512], in_=wst[:]), r=[wst], w=[wup])
            wd = I["w_down"][l].rearrange("(fc p) n -> p fc n", p=128)

            def load_wdown():
                for c in range(8):
                    S.dma("sp", wst[:].rearrange("p a n -> p (a n)").rearrange("p (f n) -> p f n", f=4),
                          wd[:, c * 4:(c + 1) * 4, :], w=[wst])
                    S.op("pool", lambda e: e.tensor_copy(out=wdn[:, c * 4:(c + 1) * 4, :],
                                                         in_=wst[:].rearrange("p a n -> p (a n)").rearrange("p (f n) -> p f n", f=4)),
                         r=[wst], w=[wdn])
            S.dma("sp", gt[:], I["ln2_g"][l:l + 1, :].partition_broadcast(128), w=[gt])
            S.dma("sp", bt[:], I["ln2_b"][l:l + 1, :].partition_broadcast(128), w=[bt])
            nu = 0
            nd = 0
            no = 0
            for tg in range(T // TG):
                for u in range(2):
                    tt = tg * 2 + u
                    S.dma("sp", x1[:, u, :], X["X1"][tt * 128:(tt + 1) * 128, :], r=["X1"], w=[x1])
                for u in range(2):
                    S.op("act", lambda e: e.activation(out=x1b[:], in_=x1[:, u, :], func=AF.Copy), r=[x1], w=[x1b])
                    for kc in range(8):
                        S.op("pe", lambda e: e.transpose(out=ptr[u][:, kc, :], in_=x1b[:, kc * 128:(kc + 1) * 128],
                                                         identity=self.identb[:]), r=[x1b, self.identb], w=[ptr[u]])
                    S.op("dve", lambda e: e.tensor_copy(out=x1T[:, :, u * 128:(u + 1) * 128], in_=ptr[u][:]),
                         r=[ptr[u]], w=[x1T])
                for fc in range(32):
                    pu = pup[nu % 2]
                    r_ = rl[nu % 2]
                    nu += 1
                    for kc in range(8):
                        S.op("pe", lambda e: e.matmul(pu[:], wup[:, kc, fc * 128:(fc + 1) * 128], x1T[:, kc, :],
                                                      start=(kc == 0), stop=(kc == 7)), r=[wup, x1T], w=[pu])
                    S.op("act", lambda e: e.activation(out=r_[:], in_=pu[:], func=AF.Relu), r=[pu], w=[r_])
                    S.op("dve", lambda e: e.tensor_tensor(out=hT[:, fc, :], in0=r_[:], in1=r_[:], op=ALU.mult), r=[r_], w=[hT])
                if tg == 0:
                    load_wdown()
                for u in range(2):
                    tt = tg * 2 + u
                    for nh in range(2):
                        cs = slice(nh * 512, (nh + 1) * 512)
                        pd = pdn[nd % 2]
                        nd += 1
                        for fc in range(32):
                            S.op("pe", lambda e: e.matmul(pd[:], hT[:, fc, u * 128:(u + 1) * 128], wdn[:, fc, cs],
                                                          start=(fc == 0), stop=(fc == 31)), r=[hT, wdn], w=[pd])
                        S.op("dve", lambda e: e.scalar_tensor_tensor(out=z[:, cs], in0=x1[:, u, cs], scalar=ALPHA, in1=pd[:],
                                                                     op0=ALU.mult, op1=ALU.add), r=[x1, pd], w=[z])
                    o = xo[no % 2]
                    no += 1
                    self.layer_norm_tile(z, gt, bt, st1, junk, o)
                    oname = "OUT" if xout is self.out else "XS"
                    S.dma("pool", xout[tt * 128:(tt + 1) * 128, :], o[:], r=[o], w=[oname])


    def phase_b(self, l):
        nc, S, I, X = self.nc, self.S, self.I, self.X
        with ExitStack() as es:
            sb = lambda n, s, d: es.enter_context(nc.sbuf_tensor("%s_L%d" % (n, l), s, d))
            ps = lambda n, s, d: es.enter_context(nc.psum_tensor("%s_L%d" % (n, l), s, d))
            kT = sb("b_kT", [128, 4, T], BF16)
            qT = sb("b_qT", [128, 4, T], BF16)
            va = sb("b_va", [128, NT, 520], BF16)
            toep = sb("b_toep", [128, 8, 1280], BF16)
            nsT = sb("b_nsT", [128, T], BF16)
            ksum = sb("b_ksum", [128, 4, 16], F32)
            kmT = sb("b_kmT", [128, 4, 32], BF16)
            gm = sb("b_gm", [128, 8, 16], F32)
            m8 = sb("b_m8", [128, 8, 8], F32)
            ns = sb("b_ns", [128, 8, 16], BF16)
            PT = [sb("b_PT%d" % i, [128, 512], BF16) for i in range(4)]
            oa = [sb("b_oa%d" % i, [128, 4, 512], BF16) for i in range(2)]
            rden = sb("b_rden", [128, 4], F32)
            E = sb("b_E", [128, 128, 128], BF16)
            S.op("dve", lambda e: e.tensor_copy(out=E[:], in_=self.i30k[:].unsqueeze(2).to_broadcast([128, 128, 128])),
                 r=[self.i30k], w=[E])
            pS = [ps("b_pS%d" % i, [128, 512], F32) for i in range(2)]
            pO = [ps("b_pO%d" % i, [128, 65], F32) for i in range(4)]
            es_sel = ExitStack()
            pg = es_sel.enter_context(nc.psum_tensor("b_pg_L%d" % l, [128, 8, 16], F32))
            ptr = es_sel.enter_context(nc.psum_tensor("b_ptr_L%d" % l, [128, 128], BF16))
            for a in range(4):
                S.dma("sp", kT[:, a, :], X["KA"][a * 128:(a + 1) * 128, :], r=["KA"], w=[kT])
                S.dma("sp", qT[:, a, :], X["QA"][a * 128:(a + 1) * 128, :], r=["QA"], w=[qT])
            vsrc = X["VA"].rearrange("(tt p) c -> p tt c", p=128)
            for c in range(4):
                S.dma("sp", va[:, c * 8:(c + 1) * 8, :], vsrc[:, c * 8:(c + 1) * 8, :], r=["VA"], w=[va])
            S.dma("sp", toep[:], I["c_toep"][:, 0:8, :], w=[toep])
            S.op("pool", lambda e: e.memset(nsT[:], 0.0), w=[nsT])
            S.op("pool", lambda e: e.memset(gm[:], -1e30), w=[gm])
            S.op("pool", lambda e: e.memset(ns[:], 0.0), w=[ns])
            import os
            bstop = int(os.environ.get("BSTOP", "99"))
            if bstop <= 1:
                es_sel.close()
                return
            S.op("dve", lambda e: e.tensor_reduce(out=ksum[:], in_=kT[:].rearrange("p a (n s) -> p a n s", s=256),
                                                   axis=AX.X, op=ALU.add), r=[kT], w=[ksum])
            S.op("pool", lambda e: e.memset(kmT[:], 0.0), w=[kmT])
            S.op("dve", lambda e: e.tensor_scalar(out=kmT[0:64, :, 0:16], in0=ksum[0:64, :, :], scalar1=1.0 / 256, scalar2=None,
                                                   op0=ALU.mult), r=[ksum], w=[kmT])
            S.op("dve", lambda e: e.tensor_scalar(out=kmT[64:128, :, 16:32], in0=ksum[64:128, :, :], scalar1=1.0 / 256, scalar2=None,
                                                   op0=ALU.mult), r=[ksum], w=[kmT])
            if bstop <= 2:
                es_sel.close()
                return
            for tt in range(NT):
                cur = tt // 2
                if cur <= 3:
                    continue
                for a in range(4):
                    S.op("pe", lambda e: e.matmul(pg[:, 2 * a:2 * a + 2, :], qT[:, a, tt * 128:(tt + 1) * 128],
                                                  kmT[:, a, :].rearrange("p (h n) -> p h n", h=2), start=True, stop=True),
                         r=[qT, kmT], w=[pg])
                bsub = int(os.environ.get("BSUB", "99"))
                S.op("act", lambda e: e.activation(out=gm[:, :, 0:cur], in_=pg[:, :, 0:cur], func=AF.Copy), r=[pg], w=[gm])
                if bsub <= 1:
                    continue
                for h in range(8):
                    S.op("dve", lambda e: e.max(out=m8[:, h, :], in_=gm[:, h, :]), r=[gm], w=[m8])
                if bsub <= 2:
                    continue
                for h in range(8):
                    S.op("dve", lambda e: e.tensor_scalar(out=ns[:, h, 0:cur], in0=gm[:, h, 0:cur], scalar1=m8[:, h, 2:3],
                                                           scalar2=1.0, op0=ALU.is_ge, op1=ALU.subtract), r=[gm, m8], w=[ns])
                if bsub <= 3:
                    continue
                S.op("pe", lambda e: e.transpose(out=ptr[:], in_=ns[:].rearrange("p h n -> p (h n)"), identity=self.identb[:]),
                     r=[ns, self.identb], w=[ptr])
                S.op("act", lambda e: e.activation(out=nsT[:, tt * 128:(tt + 1) * 128], in_=ptr[:], func=AF.Copy),
                     r=[ptr], w=[nsT])
            S.barrier()
            es_sel.close()
            pS = pS + [ps("b_pS%d" % i, [128, 512], F32) for i in (2, 3)]
            if bstop <= 3:
                return
            nS = [0]
            for qg in range(8):
                if bstop <= 4 and qg >= 1:
                    break
                ob = oa[qg % 2]
                qs = slice(qg * 512, (qg + 1) * 512)
                nkt = 4 * (qg + 1)

                def emit_S(h, kt):
                    hb, a_ = 64 * (h % 2), h // 2
                    m = kt - 4 * qg
                    off = 512 - 128 * m if m >= -1 else 768
                    n = kt // 2
                    need_sel = (qg >= 2) and (n <= 2 * qg)
                    p, pt = pS[nS[0] % 4], PT[nS[0] % 4]
                    nS[0] += 1
                    S.op("pe", lambda e: e.matmul(p[:], kT[hb:hb + 64, a_, kt * 128:(kt + 1) * 128], qT[hb:hb + 64, a_, qs],
                                                  start=True, stop=False), r=[kT, qT], w=[p])
                    S.op("pe", lambda e: e.matmul(p[:], self.identb[:], toep[:, h, off:off + 512],
                                                  start=False, stop=not need_sel), r=[self.identb, toep], w=[p])
                    if need_sel:
                        rr = h * 16 + n
                        S.op("pe", lambda e: e.matmul(p[:], E[:, rr, :], nsT[:, qs], start=False, stop=True), r=[E, nsT], w=[p])
                    return p, pt

                def emit_PV(h, kt, pt):
                    for u in range(4):
                        last = 4 * qg + u
                        if kt <= last:
                            S.op("pe", lambda e: e.matmul(pO[u][:], pt[:, u * 128:(u + 1) * 128], va[:, kt, h * 65:(h + 1) * 65],
                                                          start=(kt == 0), stop=(kt == last)), r=[pt, va], w=[pO[u]])

                def finalize(h):
                    for u in range(4):
                        S.op("dve", lambda e: e.reciprocal(out=rden[:, u:u + 1], in_=pO[u][:, 64:65]), r=[pO[u]], w=[rden])
                        S.op("dve", lambda e: e.tensor_scalar(out=ob[:, u, h * 64:(h + 1) * 64], in0=pO[u][:, 0:64],
                                                               scalar1=rden[:, u:u + 1], scalar2=None, op0=ALU.mult),
                             r=[pO[u], rden], w=[ob])

                steps = [(h, kt) for h in range(8) for kt in range(nkt)]
                pending = emit_S(*steps[0])
                for i, (h, kt) in enumerate(steps):
                    p, pt = pending
                    S.op("act", lambda e: e.activation(out=pt[:], in_=p[:], func=AF.Exp), r=[p], w=[pt])
                    if i + 1 < len(steps):
                        pending = emit_S(*steps[i + 1])
                    emit_PV(h, kt, pt)
                    if kt == nkt - 1:
                        finalize(h)
                S.dma("pool", X["OA"][qg * 512:(qg + 1) * 512, :].rearrange("(u p) c -> p u c", p=128), ob[:],
                      r=[ob], w=["OA"])

    def phase_d(self, l):
        nc, S, I, X = self.nc, self.S, self.I, self.X
        import os
        dstop = int(os.environ.get("DSTOP", "99"))
        with ExitStack() as es:
            sb = lambda n, s, d: es.enter_context(nc.sbuf_tensor("%s_L%d" % (n, l), s, d))
            ps = lambda n, s, d: es.enter_context(nc.psum_tensor("%s_L%d" % (n, l), s, d))
            c = sb("d_c", [128, NT, 256], BF16)
            cT = sb("d_cT", [128, 2, T], BF16)
            kiT2 = sb("d_kiT2", [128, T], BF16)
            toep = sb("d_toep", [128, 8, 1280], BF16)
            absw = sb("d_absw", [128, NT, 8], F32)
            sgn = sb("d_sgn", [128, NT, 8], F32)
            wukT = sb("d_wukT", [128, 4, 256], BF16)
            WBD = sb("d_WBD", [128, 4, 2, 256], BF16)
            wuvb = sb("d_wuvb", [128, 2, 512], BF16)
            pI = [ps("d_pI%d" % i, [128, 512], F32) for i in range(2)]
            pAcc = ps("d_pAcc", [128, 512], F32)
            pS = [ps("d_pS%d" % i, [128, 4, 128], F32) for i in range(2)]
            pOT = [ps("d_pOT%d" % i, [128, 512], F32) for i in range(2)]
            pDen = ps("d_pDen", [128, 512], F32)
            pIb = [p[:].bitcast(BF16).rearrange("p (a n) -> p a n", a=8) for p in pI]

            S.dma("sp", toep[:], I["c_toep"][:, 8:16, :], w=[toep])
            with ExitStack() as es2:
                sb2 = lambda n, s, d: es2.enter_context(nc.sbuf_tensor("%s_L%d" % (n, l), s, d))
                ckv = sb2("d_ckv", [128, NT, 256], F32)
                kiwi = sb2("d_kiwi", [128, NT, 72], F32)
                kin = sb2("d_kin", [128, NT, 128], BF16)
                kif = sb2("d_kif", [128, NT, 64], F32)
                junk = sb2("d_junk", [128, 256], F32)
                wst = sb2("d_wst", [128, 2, 512], F32)
                wukb = sb2("d_wukb", [128, 2, 512], BF16)
                kvn = sb2("d_kvn", [128, 256], F32)
                lng = sb2("d_lng", [128, 64], F32)
                lnb = sb2("d_lnb", [128, 64], F32)
                ss = sb2("d_ss", [128, NT], F32)
                rs = sb2("d_rs", [128, NT], F32)
                mu = sb2("d_mu", [128, NT], F32)
                S.dma("sp", wst[:], I["w_uk"][l].rearrange("(rc p) n -> p rc n", p=128), w=[wst])
                S.op("pool", lambda e: e.tensor_copy(out=wukb[:], in_=wst[:]), r=[wst], w=[wukb])
                for rc in range(2):
                    for a in range(4):
                        S.op("pe", lambda e: e.transpose(out=pIb[0][:, rc * 4 + a, :], in_=wukb[:, rc, a * 128:(a + 1) * 128],
                                                         identity=self.identb[:]), r=[wukb, self.identb], w=[pI[0]])
                S.op("dve", lambda e: e.tensor_copy(out=wukT[:].rearrange("p a (rc r) -> p rc a r", rc=2),
                                                     in_=pIb[0].rearrange("p (rc a) r -> p rc a r", rc=2)), r=[pI[0]], w=[wukT])
                S.op("pool", lambda e: e.memset(WBD[:], 0.0), w=[WBD])
                S.op("dve", lambda e: e.tensor_copy(out=WBD[0:64, :, 0, :], in_=wukT[0:64, :, :]), r=[wukT], w=[WBD])
                S.op("dve", lambda e: e.tensor_copy(out=WBD[64:128, :, 1, :], in_=wukT[64:128, :, :]), r=[wukT], w=[WBD])
                S.dma("sp", wst[:], I["w_uv"][l].rearrange("(rc p) n -> p rc n", p=128), r=[], w=[wst])
                S.op("pool", lambda e: e.tensor_copy(out=wuvb[:], in_=wst[:]), r=[wst], w=[wuvb])
                S.dma("sp", kvn[:], I["kv_norm"][l:l + 1, :].partition_broadcast(128), w=[kvn])
                S.dma("sp", lng[:], I["idx_k_ln_g"][l:l + 1, :].partition_broadcast(128), w=[lng])
                S.dma("sp", lnb[:], I["idx_k_ln_b"][l:l + 1, :].partition_broadcast(128), w=[lnb])
                csrc = X["CKV"].rearrange("(tt p) c -> p tt c", p=128)
                for q4 in range(4):
                    S.dma("sp", ckv[:, q4 * 8:(q4 + 1) * 8, :], csrc[:, q4 * 8:(q4 + 1) * 8, :], r=["CKV"], w=[ckv])
                for tt in range(NT):
                    S.op("act", lambda e: e.activation(out=junk[:], in_=ckv[:, tt, :], func=AF.Square, accum_out=ss[:, tt:tt + 1]),
                         r=[ckv], w=[junk, ss])
                S.op("act", lambda e: e.activation(out=rs[:], in_=ss[:], func=AF.Sqrt, scale=1.0 / 256, bias=self.epsln[:, 1:2]),
                     r=[ss, self.epsln], w=[rs])
                S.op("dve", lambda e: e.reciprocal(out=rs[:], in_=rs[:]), r=[rs], w=[rs])
                S.op("dve", lambda e: e.tensor_tensor(out=ckv[:], in0=ckv[:], in1=rs[:].unsqueeze(2).to_broadcast([128, NT, 256]),
                                                       op=ALU.mult), r=[ckv, rs], w=[ckv])
                S.op("dve", lambda e: e.tensor_tensor(out=c[:], in0=ckv[:], in1=kvn[:].unsqueeze(1).to_broadcast([128, NT, 256]),
                                                       op=ALU.mult), r=[ckv, kvn], w=[c])
                for g in range(NT // 4):
                    pb = pIb[g % 2]
                    for t4 in range(4):
                        for rc in range(2):
                            S.op("pe", lambda e: e.transpose(out=pb[:, t4 * 2 + rc, :], in_=c[:, g * 4 + t4, rc * 128:(rc + 1) * 128],
                                                             identity=self.identb[:]), r=[c, self.identb], w=[pI[g % 2]])
                    S.op("act", lambda e: e.activation(out=cT[:, :, g * 512:(g + 1) * 512].rearrange("p rc (t q) -> p t rc q", t=4),
                                                       in_=pb.rearrange("p (t rc) q -> p t rc q", t=4), func=AF.Copy),
                         r=[pI[g % 2]], w=[cT])
                S.dma("sp", kiwi[:], X["KIWI"].rearrange("(tt p) c -> p tt c", p=128), r=["KIWI"], w=[kiwi])
                S.op("dve", lambda e: e.tensor_reduce(out=mu[:], in_=kiwi[:, :, 0:64], axis=AX.X, op=ALU.add), r=[kiwi], w=[mu])
                S.op("dve", lambda e: e.tensor_scalar(out=mu[:], in0=mu[:], scalar1=-1.0 / 64, scalar2=None, op0=ALU.mult),
                     r=[mu], w=[mu])
                S.op("dve", lambda e: e.tensor_tensor(out=kif[:], in0=kiwi[:, :, 0:64], in1=mu[:].unsqueeze(2).to_broadcast([128, NT, 64]),
                                                       op=ALU.add), r=[kiwi, mu], w=[kif])
                S.op("dve", lambda e: e.tensor_tensor(out=ckv[:, :, 0:64], in0=kif[:], in1=kif[:], op=ALU.mult), r=[kif], w=[ckv])
                S.op("dve", lambda e: e.tensor_reduce(out=ss[:], in_=ckv[:, :, 0:64], axis=AX.X, op=ALU.add), r=[ckv], w=[ss])
                S.op("act", lambda e: e.activation(out=rs[:], in_=ss[:], func=AF.Sqrt, scale=1.0 / 64, bias=self.epsln[:, 0:1]),
                     r=[ss, self.epsln], w=[rs])
                S.op("dve", lambda e: e.reciprocal(out=rs[:], in_=rs[:]), r=[rs], w=[rs])
                S.op("dve", lambda e: e.tensor_tensor(out=kif[:], in0=kif[:], in1=rs[:].unsqueeze(2).to_broadcast([128, NT, 64]),
                                                       op=ALU.mult), r=[kif, rs], w=[kif])
                S.op("dve", lambda e: e.tensor_tensor(out=kif[:], in0=kif[:], in1=lng[:].unsqueeze(1).to_broadcast([128, NT, 64]),
                                                       op=ALU.mult), r=[kif, lng], w=[kif])
                for hf in range(2):
                    S.op("dve", lambda e: e.tensor_tensor(out=kin[:, :, hf * 64:(hf + 1) * 64], in0=kif[:],
                                                           in1=lnb[:].unsqueeze(1).to_broadcast([128, NT, 64]), op=ALU.add),
                         r=[kif, lnb], w=[kin])
                for g in range(NT // 8):
                    pb = pIb[g % 2]
                    for t8 in range(8):
                        S.op("pe", lambda e: e.transpose(out=pb[:, t8, :], in_=kin[:, g * 8 + t8, :], identity=self.identb[:]),
                             r=[kin, self.identb], w=[pI[g % 2]])
                    S.op("act", lambda e: e.activation(out=kiT2[:, g * 1024:(g + 1) * 1024].rearrange("p (t q) -> p t q", t=8),
                                                       in_=pb, func=AF.Copy), r=[pI[g % 2]], w=[kiT2])
                S.op("act", lambda e: e.activation(out=absw[:], in_=kiwi[:, :, 64:72], func=AF.Abs, scale=IDX_WEIGHT_SCALE),
                     r=[kiwi], w=[absw])
                S.op("dve", lambda e: e.tensor_scalar(out=sgn[:], in0=kiwi[:, :, 64:72], scalar1=0.0, scalar2=2.0,
                                                       op0=ALU.is_ge, op1=ALU.mult), r=[kiwi], w=[sgn])
                S.op("dve", lambda e: e.tensor_scalar(out=sgn[:], in0=sgn[:], scalar1=-1.0, scalar2=None, op0=ALU.add),
                     r=[sgn], w=[sgn])
                S.barrier()
            Isc = [sb("d_Isc%d" % i, [128, T], F32) for i in range(2)]
            work = sb("d_work", [128, T], F32)
            negm = [sb("d_negm%d" % i, [128, T], BF16) for i in range(2)]
            nmT = [sb("d_nmT%d" % i, [128, NT, 128], BF16) for i in range(2)]
            qit = [sb("d_qit%d" % i, [128, 4, 128], BF16) for i in range(2)]
            qct = [sb("d_qct%d" % i, [128, 4, 128], BF16) for i in range(2)]
            qlT = [sb("d_qlT%d" % i, [128, 2, 8, 128], BF16) for i in range(3)]
            Dsg = sb("d_Dsg", [128, 8, 128], BF16)
            Ph = [sb("d_Ph%d" % i, [128, 512], BF16) for i in range(2)]
            PT = [sb("d_PT%d" % i, [128, 4, 128], BF16) for i in range(2)]
            rdn = sb("d_rdn", [128, 512], F32)
            OTn = sb("d_OTn", [128, 2, 4, 128], BF16)
            oc = [sb("d_oc%d" % i, [128, 512], BF16) for i in range(2)]
            m8 = sb("d_m8", [128, 8], F32)
            st = sb("d_st", [128, 4], F32)
            if dstop <= 1:
                return
            qisrc = X["QI"].rearrange("(a p) t -> p a t", p=128)
            qcsrc = X["QC"].rearrange("(a p) t -> p a t", p=128)
            nS = [0]

            def stage1(qt):
                L, b = (qt + 1) * 128, qt % 2
                ts = slice(qt * 128, (qt + 1) * 128)
                S.dma("sp", qit[b][:], qisrc[:, :, ts], r=["QI"], w=[qit[b]])
                S.dma("sp", qct[b][:], qcsrc[:, :, ts], r=["QC"], w=[qct[b]])
                if qt >= 2:
                    S.op("pool", lambda e: e.tensor_tensor(out=Dsg[:], in0=self.identb[:].unsqueeze(1).to_broadcast([128, 8, 128]),
                                                           in1=sgn[:, qt, :].unsqueeze(2).to_broadcast([128, 8, 128]), op=ALU.mult),
                         r=[self.identb, sgn], w=[Dsg])
                    for kg in range((L + 511) // 512):
                        w_ = min(512, L - 512 * kg)
                        for h in range(8):
                            hb, a_ = 64 * (h % 2), h // 2
                            pi, ph = pI[h % 2], Ph[h % 2]
                            S.op("pe", lambda e: e.matmul(pi[:, 0:w_], qit[b][hb:hb + 64, a_, :], kiT2[hb:hb + 64, kg * 512:kg * 512 + w_],
                                                          start=True, stop=True), r=[qit[b], kiT2], w=[pi])
                            S.op("act", lambda e: e.activation(out=ph[:, 0:w_], in_=pi[:, 0:w_], func=AF.Relu,
                                                               scale=absw[:, qt, h:h + 1]), r=[pi, absw], w=[ph])
                            S.op("pe", lambda e: e.matmul(pAcc[:, 0:w_], Dsg[:, h, :], ph[:, 0:w_], start=(h == 0), stop=(h == 7)),
                                 r=[Dsg, ph], w=[pAcc])
                        S.op("act", lambda e: e.activation(out=Isc[b][:, kg * 512:kg * 512 + w_], in_=pAcc[:, 0:w_], func=AF.Copy),
                             r=[pAcc], w=[Isc[b]])
                for rc in range(2):
                    for hq in range(2):
                        pq = pI[(rc * 2 + hq) % 2]
                        for hh in range(4):
                            h = hq * 4 + hh
                            S.op("pe", lambda e: e.matmul(pq[:, hh * 128:(hh + 1) * 128], WBD[:, h // 2, h % 2, rc * 128:(rc + 1) * 128],
                                                          qct[b][:, h // 2, :], start=True, stop=True), r=[WBD, qct[b]], w=[pq])
                        S.op("act", lambda e: e.activation(out=qlT[qt % 3][:, rc, hq * 4:(hq + 1) * 4, :],
                                                           in_=pq[:].rearrange("p (h q) -> p h q", h=4), func=AF.Copy), r=[pq], w=[qlT[qt % 3]])

            def stage2(qt):
                L, b = (qt + 1) * 128, qt % 2
                if qt < 2:
                    return
                I_ = Isc[b]
                S.op("dve", lambda e: e.tensor_reduce(out=st[:, 0:1], in_=I_[:, 0:L], axis=AX.X, op=ALU.min), r=[I_], w=[st])
                S.op("dve", lambda e: e.tensor_scalar(out=st[:, 1:2], in0=st[:, 0:1], scalar1=-1.0, scalar2=1.0,
                                                       op0=ALU.mult, op1=ALU.add), r=[st], w=[st])
                S.op("dve", lambda e: e.tensor_scalar(out=I_[:, 0:L], in0=I_[:, 0:L], scalar1=st[:, 1:2], scalar2=None,
                                                       op0=ALU.add), r=[I_, st], w=[I_])
                S.op("dve", lambda e: e.tensor_tensor(out=I_[:, L - 128:L], in0=I_[:, L - 128:L], in1=self.ltri[:], op=ALU.mult),
                     r=[I_, self.ltri], w=[I_])
                for r_ in range(32):
                    src_ = I_ if r_ == 0 else work
                    S.op("dve", lambda e: e.max(out=m8[:], in_=src_[:, 0:L]), r=[src_], w=[m8])
                    if r_ < 31:
                        S.op("dve", lambda e: e.scalar_tensor_tensor(out=work[:, 0:L], in0=src_[:, 0:L], scalar=m8[:, 7:8],
                                                                     in1=src_[:, 0:L], op0=ALU.is_lt, op1=ALU.mult),
                             r=[src_, m8], w=[work])
                S.op("dve", lambda e: e.tensor_scalar(out=negm[b][:, 0:L], in0=I_[:, 0:L], scalar1=m8[:, 7:8], scalar2=1.0,
                                                       op0=ALU.is_ge, op1=ALU.subtract), r=[I_, m8], w=[negm[b]])

            def stage3(qt):
                nk, b = qt + 1, qt % 2
                ts = slice(qt * 128, (qt + 1) * 128)
                masked = qt >= 2
                if masked:
                    for g in range((nk + 7) // 8):
                        n8 = min(8, nk - 8 * g)
                        pb = pIb[g % 2]
                        for t8 in range(n8):
                            kt = g * 8 + t8
                            S.op("pe", lambda e: e.transpose(out=pb[:, t8, :], in_=negm[b][:, kt * 128:(kt + 1) * 128],
                                                             identity=self.identb[:]), r=[negm[b], self.identb], w=[pI[g % 2]])
                        S.op("act", lambda e: e.activation(out=nmT[b][:, g * 8:g * 8 + n8, :], in_=pb[:, 0:n8, :], func=AF.Copy, scale=-NEG),
                             r=[pI[g % 2]], w=[nmT[b]])
                def emit_S(half, kt):
                    hs = slice(4 * half, 4 * half + 4)
                    m = kt - qt
                    off = 512 - 128 * m if m >= -1 else 768
                    p_, pt = pS[nS[0] % 2], PT[nS[0] % 2]
                    nS[0] += 1
                    for rc in range(2):
                        S.op("pe", lambda e: e.matmul(p_[:], cT[:, rc, kt * 128:(kt + 1) * 128], qlT[qt % 3][:, rc, hs, :],
                                                      start=(rc == 0), stop=False), r=[cT, qlT[qt % 3]], w=[p_])
                    S.op("pe", lambda e: e.matmul(p_[:], self.identb[:], toep[:, hs, off:off + 128], start=False, stop=not masked),
                         r=[self.identb, toep], w=[p_])
                    if masked:
                        for hh in range(4):
                            S.op("pe", lambda e: e.matmul(p_[:, hh, :], self.identb[:], nmT[b][:, kt, :], start=False, stop=(hh == 3)),
                                 r=[self.identb, nmT[b]], w=[p_])
                    return p_, pt

                def emit_PV(half, kt, pt):
                    for rc in range(2):
                        S.op("pe", lambda e: e.matmul(pOT[rc][:], c[:, kt, rc * 128:(rc + 1) * 128], pt[:].rearrange("p h q -> p (h q)"),
                                                      start=(kt == 0), stop=(kt == nk - 1)), r=[c, pt], w=[pOT[rc]])
                    S.op("pe", lambda e: e.matmul(pDen[:], self.onesb[:], pt[:].rearrange("p h q -> p (h q)"),
                                                  start=(kt == 0), stop=(kt == nk - 1)), r=[self.onesb, pt], w=[pDen])

                def finalize(half):
                    S.op("dve", lambda e: e.reciprocal(out=rdn[:], in_=pDen[:]), r=[pDen], w=[rdn])
                    for rc in range(2):
                        S.op("dve", lambda e: e.tensor_tensor(out=OTn[:, rc, :, :].rearrange("p h q -> p (h q)"), in0=pOT[rc][:], in1=rdn[:],
                                                               op=ALU.mult), r=[pOT[rc], rdn], w=[OTn])
                    for hh in range(4):
                        h = 4 * half + hh
                        for rc in range(2):
                            S.op("pe", lambda e: e.matmul(pAcc[:, hh * 64:(hh + 1) * 64], OTn[:, rc, hh, :], wuvb[:, rc, h * 64:(h + 1) * 64],
                                                          start=(rc == 0), stop=(rc == 1)), r=[OTn, wuvb], w=[pAcc])
                    S.op("act", lambda e: e.activation(out=oc[b][:, half * 256:(half + 1) * 256], in_=pAcc[:, 0:256], func=AF.Copy),
                         r=[pAcc], w=[oc[b]])

                steps = [(half, kt) for half in range(2) for kt in range(nk)]
                pending = emit_S(*steps[0])
                for i, (half, kt) in enumerate(steps):
                    p_, pt = pending
                    S.op("act", lambda e: e.activation(out=pt[:], in_=p_[:], func=AF.Exp), r=[p_], w=[pt])
                    if i + 1 < len(steps):
                        pending = emit_S(*steps[i + 1])
                    emit_PV(half, kt, pt)
                    if kt == nk - 1:
                        finalize(half)
                S.dma("pool", X["OC"][ts, :], oc[b][:], r=[oc[b]], w=["OC"])

            tiles = [qt for qt in range(NT) if not (dstop <= 2 and qt not in (0, 1, 2, 5))]
            n = len(tiles)
            for i in range(n + 2):
                if i < n:
                    stage1(tiles[i])
                if 0 <= i - 1 < n:
                    stage2(tiles[i - 1])
                if 0 <= i - 2 < n:
                    stage3(tiles[i - 2])

    def phase_c(self, l):
        nc, S, I, X = self.nc, self.S, self.I, self.X
        import os
        cstop = int(os.environ.get("CSTOP", "99"))
        QS = 128 ** -0.5
        UTc, UTs, BON, H0, H1 = (self.gdc[:, i, :] for i in range(5))
        with ExitStack() as es:
            sb = lambda n, s, d: es.enter_context(nc.sbuf_tensor("%s_L%d" % (n, l), s, d))
            ps = lambda n, s, d: es.enter_context(nc.psum_tensor("%s_L%d" % (n, l), s, d))
            cw = sb("c_cw", [128, 12, 4], F32)
            xp = [sb("c_xp%d" % i, [128, T + 3], F32) for i in range(2)]
            y = [sb("c_y%d" % i, [128, T], F32) for i in range(2)]
            sq = sb("c_sq", [128, T], F32)
            rn = sb("c_rn", [128, T], F32)
            pn = [ps("c_pn%d" % i, [128, 512], F32) for i in range(2)]
            cwr = sb("c_cwr", [4, 1536], F32)
            S.dma("sp", cwr[:], I["conv_w"][l], w=[cwr])
            for c_ in range(12):
                S.op("pe", lambda e: e.transpose(out=pn[0][:, c_ * 4:(c_ + 1) * 4], in_=cwr[:, c_ * 128:(c_ + 1) * 128],
                                                 identity=self.ident[0:4, 0:4]), r=[cwr, self.ident], w=[pn[0]])
            S.op("dve", lambda e: e.tensor_copy(out=cw[:].rearrange("p c j -> p (c j)"), in_=pn[0][:, 0:48]), r=[pn[0]], w=[cw])
            for i in range(2):
                S.op("pool", lambda e: e.memset(xp[i][:, 0:3], 0.0), w=[xp[i]])
            for c_ in range(12):
                b = c_ % 2
                S.dma("sp", xp[b][:, 3:], X["QKVB"][c_ * 128:(c_ + 1) * 128, :], r=["QKVB"], w=[xp[b]])
                S.op("dve", lambda e: e.tensor_scalar(out=y[b][:], in0=xp[b][:, 0:T], scalar1=cw[:, c_, 0:1], scalar2=None,
                                                       op0=ALU.mult), r=[xp[b], cw], w=[y[b]])
                for j in range(1, 4):
                    S.op("dve", lambda e: e.scalar_tensor_tensor(out=y[b][:], in0=xp[b][:, j:j + T], scalar=cw[:, c_, j:j + 1],
                                                                 in1=y[b][:], op0=ALU.mult, op1=ALU.add), r=[xp[b], cw, y[b]], w=[y[b]])
                S.op("act", lambda e: e.activation(out=y[b][:], in_=y[b][:], func=AF.Silu), r=[y[b]], w=[y[b]])
                if c_ < 8:
                    S.op("pool", lambda e: e.tensor_tensor(out=sq[:], in0=y[b][:], in1=y[b][:], op=ALU.mult), r=[y[b]], w=[sq])
                    for g in range(8):
                        p_ = pn[g % 2]
                        S.op("pe", lambda e: e.matmul(p_[:], self.onesf[:], sq[:, g * 512:(g + 1) * 512], start=True, stop=True),
                             r=[self.onesf, sq], w=[p_])
                        S.op("act", lambda e: e.activation(out=rn[:, g * 512:(g + 1) * 512], in_=p_[:], func=AF.Sqrt,
                                                           bias=self.epsln[:, 1:2]), r=[p_, self.epsln], w=[rn])
                    S.op("dve", lambda e: e.reciprocal(out=rn[:], in_=rn[:]), r=[rn], w=[rn])
                    S.op("dve", lambda e: e.tensor_tensor(out=y[b][:], in0=y[b][:], in1=rn[:], op=ALU.mult), r=[y[b], rn], w=[y[b]])
                S.dma("pool", X["QKVB"][c_ * 128:(c_ + 1) * 128, :], y[b][:], r=[y[b]], w=["QKVB"])
        S.barrier()
        if cstop <= 1:
            return
        with ExitStack() as es:
            sb = lambda n, s, d: es.enter_context(nc.sbuf_tensor("%s_L%d" % (n, l), s, d))
            ps = lambda n, s, d: es.enter_context(nc.psum_tensor("%s_L%d" % (n, l), s, d))
            ab = sb("c_ab", [128, NT, 8], F32)
            dtb = sb("c_dtb", [128, 4], F32)
            nea = sb("c_nea", [128, 4], F32)
            one1 = sb("c_one1", [128, 1], F32)
            LA = sb("c_LA", [128, NT, 4], F32)
            nbeta = sb("c_nbeta", [128, NT, 4], F32)
            beta = sb("c_beta", [128, NT, 4], F32)
            G = sb("c_G", [128, 128], F32)
            EGn = sb("c_EGn", [128, 128], F32)
            KD = sb("c_KD", [128, 128], F32)
            EGL = [sb("c_EGL%d" % j, [128, 128], F32) for j in range(2)]
            gn = sb("c_gn", [128, 128], F32)
            qkv = [sb("c_qkv%d" % i, [128, 12, 128], F32) for i in range(2)]
            zt = [sb("c_zt%d" % i, [128, 512], F32) for i in range(2)]
            vt = sb("c_vt", [128, 4, 128], F32)
            kd = sb("c_kd", [128, 4, 128], F32)
            O = sb("c_O", [128, 4, 128], F32)
            ob = [sb("c_ob%d" % i, [128, 512], BF16) for i in range(2)]
            ss = sb("c_ss", [128, 4], F32)
            junk = sb("c_junk", [128, 128], F32)
            Sh = [sb("c_S%d" % h, [128, 128], F32) for h in range(4)]
            Dg = [sb("c_Dg%d" % h, [128, 128], F32) for h in range(4)]
            t1 = [sb("c_t1%d" % h, [128, 128], F32) for h in range(4)]
            decT = [sb("c_dec%d" % h, [128, 128], F32) for h in range(4)]
            EGb = [sb("c_EGb%d" % h, [128, 128], F32) for h in range(4)]
            qeg = [sb("c_qeg%d" % h, [128, 128], F32) for h in range(4)]
            aT = [sb("c_aT%d" % h, [128, 128], F32) for h in range(4)]
            AT = [[sb("c_AT%d_%d" % (h, i), [128, 128], F32) for i in range(2)] for h in range(4)]
            A = [[sb("c_A%d_%d" % (h, i), [128, 128], F32) for i in range(2)] for h in range(4)]
            Xh = [sb("c_X%d" % h, [128, 128], F32) for h in range(4)]
            R = [sb("c_R%d" % h, [128, 128], F32) for h in range(4)]
            vn = [sb("c_vn%d" % h, [128, 128], F32) for h in range(4)]
            class Sub:
                def __init__(self, tile, i):
                    self.ap = tile[:, i, :]
                    self.name = tile.name

                def __getitem__(self, k):
                    return self.ap[k]

            ppb = [ps("c_pp%d" % i, [128, 4, 128], F32) for i in range(4)]
            pp = [Sub(ppb[i % 4], i // 4) for i in range(16)]
            pdb = [ps("c_pd%d" % i, [128, 128], F32) for i in range(2)]
            ptv = ps("c_ptv", [128, 4, 128], F32)
            ptk = ps("c_ptk", [128, 4, 128], F32)
            npp = [0]

            def P():
                npp[0] += 1
                return pp[npp[0] % 16]

            S.dma("sp", ab[:], X["AB"].rearrange("(tt p) c -> p tt c", p=128), r=["AB"], w=[ab])
            S.dma("sp", dtb[:], I["dt_bias"][l:l + 1, :].partition_broadcast(128), w=[dtb])
            S.dma("sp", nea[:], I["a_log"][l:l + 1, :].partition_broadcast(128), w=[nea])
            S.dma("sp", gn[:], I["gdn_norm"][l:l + 1, :].partition_broadcast(128), w=[gn])
            S.op("pool", lambda e: e.memset(one1[:], 1.0), w=[one1])
            for h in range(4):
                S.op("pool", lambda e: e.memset(Sh[h][:], 0.0), w=[Sh[h]])
                S.op("pool", lambda e: e.memset(vn[h][:], 0.0), w=[vn[h]])
                S.op("pool", lambda e: e.memset(R[h][:], 0.0), w=[R[h]])
            S.op("act", lambda e: e.activation(out=nea[:], in_=nea[:], func=AF.Exp), r=[nea], w=[nea])
            S.op("dve", lambda e: e.tensor_scalar(out=nea[:], in0=nea[:], scalar1=-1.0, scalar2=None, op0=ALU.mult), r=[nea], w=[nea])
            S.op("dve", lambda e: e.tensor_tensor(out=LA[:], in0=ab[:, :, 0:4], in1=dtb[:].unsqueeze(1).to_broadcast([128, NT, 4]),
                                                   op=ALU.add), r=[ab, dtb], w=[LA])
            S.op("act", lambda e: e.activation(out=LA[:], in_=LA[:], func=AF.Exp), r=[LA], w=[LA])
            S.op("act", lambda e: e.activation(out=LA[:], in_=LA[:], func=AF.Ln, bias=one1[:, 0:1]), r=[LA, one1], w=[LA])
            S.op("dve", lambda e: e.tensor_tensor(out=LA[:], in0=LA[:], in1=nea[:].unsqueeze(1).to_broadcast([128, NT, 4]),
                                                   op=ALU.mult), r=[LA, nea], w=[LA])
            S.op("act", lambda e: e.activation(out=beta[:], in_=ab[:, :, 4:8], func=AF.Sigmoid), r=[ab], w=[beta])
            S.op("dve", lambda e: e.tensor_scalar(out=nbeta[:], in0=beta[:], scalar1=-1.0, scalar2=None, op0=ALU.mult),
                 r=[beta], w=[nbeta])
            LA2 = LA[:].rearrange("p t h -> p (t h)")
            p1, p2, p3, p4 = P(), P(), P(), P()
            S.op("pe", lambda e: e.matmul(p1[:], UTc, LA2, start=True, stop=True), r=[self.gdc, LA], w=[p1])
            S.op("pe", lambda e: e.matmul(p2[:], BON, LA2, start=True, stop=True), r=[self.gdc, LA], w=[p2])
            S.op("pe", lambda e: e.matmul(p3[:], H0, LA2, start=True, stop=True), r=[self.gdc, LA], w=[p3])
            S.op("pe", lambda e: e.matmul(p4[:], H1, LA2, start=True, stop=True), r=[self.gdc, LA], w=[p4])
            S.op("dve", lambda e: e.tensor_copy(out=G[:], in_=p1[:]), r=[p1], w=[G])
            S.op("act", lambda e: e.activation(out=EGn[:], in_=p1[:], func=AF.Exp), r=[p1], w=[EGn])
            S.op("dve", lambda e: e.tensor_scalar(out=EGn[:], in0=EGn[:], scalar1=-1.0, scalar2=None, op0=ALU.mult), r=[EGn], w=[EGn])
            S.op("dve", lambda e: e.tensor_tensor(out=KD[:], in0=p2[:], in1=G[:], op=ALU.subtract), r=[p2, G], w=[KD])
            S.op("act", lambda e: e.activation(out=KD[:], in_=KD[:], func=AF.Exp), r=[KD], w=[KD])
            S.op("act", lambda e: e.activation(out=EGL[0][:], in_=p3[:], func=AF.Exp), r=[p3], w=[EGL[0]])
            S.op("act", lambda e: e.activation(out=EGL[1][:], in_=p4[:], func=AF.Exp), r=[p4], w=[EGL[1]])
            qsrc = X["QKVB"].rearrange("(c p) t -> p c t", p=128)
            for tt in range(NT):
                if tt >= int(os.environ.get("CTILES", "32")) or (cstop <= 2 and tt >= 2):
                    break
                b = tt % 2
                ts = slice(tt * 128, (tt + 1) * 128)
                S.dma("sp", qkv[b][:], qsrc[:, :, ts], r=["QKVB"], w=[qkv[b]])
                S.dma("sp", zt[b][:], X["ZB"][ts, :], r=["ZB"], w=[zt[b]])
                for h in range(4):
                    S.op("pe", lambda e: e.transpose(out=ptv[:, h, :], in_=qkv[b][:, 8 + h, :], identity=self.ident[:]),
                         r=[qkv[b], self.ident], w=[ptv])
                    S.op("pe", lambda e: e.transpose(out=ptk[:, h, :], in_=qkv[b][:, 4 + h, :], identity=self.ident[:]),
                         r=[qkv[b], self.ident], w=[ptk])
                S.op("act", lambda e: e.activation(out=vt[:], in_=ptv[:], func=AF.Copy), r=[ptv], w=[vt])
                for h in range(4):
                    col = tt * 4 + h
                    S.op("dve", lambda e: e.tensor_scalar(out=kd[:, h, :], in0=ptk[:, h, :], scalar1=KD[:, col:col + 1], scalar2=None,
                                                           op0=ALU.mult), r=[ptk, KD], w=[kd])
                H4 = range(4)
                gcol = [G[:, tt * 4 + h:tt * 4 + h + 1] for h in H4]
                kTt = [qkv[b][:, 4 + h, :] for h in H4]
                qTt = [qkv[b][:, h, :] for h in H4]
                for h in H4:
                    S.op("pool", lambda e: e.tensor_scalar(out=Dg[h][:], in0=self.ident[:], scalar1=gcol[h], scalar2=None, op0=ALU.mult),
                         r=[self.ident, G], w=[Dg[h]])
                pg_ = [P() for _ in H4]
                for h in H4:
                    S.op("pe", lambda e: e.matmul(pg_[h][:], self.onesf[:], Dg[h][:], start=True, stop=True), r=[self.onesf, Dg[h]], w=[pg_[h]])
                for h in H4:
                    S.op("dve", lambda e: e.tensor_scalar(out=t1[h][:], in0=pg_[h][:], scalar1=gcol[h], scalar2=0.0, op0=ALU.subtract, op1=ALU.min),
                         r=[pg_[h], G], w=[t1[h]])
                for h in H4:
                    S.op("act", lambda e: e.activation(out=decT[h][:], in_=t1[h][:], func=AF.Exp), r=[t1[h]], w=[decT[h]])
                    S.op("act", lambda e: e.activation(out=EGb[h][:], in_=pg_[h][:], func=AF.Exp), r=[pg_[h]], w=[EGb[h]])
                for h in H4:
                    S.op("dve", lambda e: e.scalar_tensor_tensor(out=qeg[h][:], in0=qTt[h], scalar=QS, in1=EGb[h][:], op0=ALU.mult, op1=ALU.mult),
                         r=[qkv[b], EGb[h]], w=[qeg[h]])
                pkk = [P() for _ in H4]
                pqk = [P() for _ in H4]
                for h in H4:
                    S.op("pe", lambda e: e.matmul(pkk[h][:], kTt[h], kTt[h], start=True, stop=True), r=[qkv[b]], w=[pkk[h]])
                    S.op("pe", lambda e: e.matmul(pqk[h][:], kTt[h], qTt[h], start=True, stop=True), r=[qkv[b]], w=[pqk[h]])
                for h in H4:
                    S.op("dve", lambda e: e.scalar_tensor_tensor(out=AT[h][0][:], in0=pkk[h][:], scalar=nbeta[:, tt, h:h + 1], in1=decT[h][:],
                                                                 op0=ALU.mult, op1=ALU.mult), r=[pkk[h], nbeta, decT[h]], w=[AT[h][0]])
                    S.op("dve", lambda e: e.scalar_tensor_tensor(out=aT[h][:], in0=pqk[h][:], scalar=QS, in1=decT[h][:],
                                                                 op0=ALU.mult, op1=ALU.mult), r=[pqk[h], decT[h]], w=[aT[h]])
                for h in H4:
                    S.op("pool", lambda e: e.tensor_tensor(out=AT[h][0][:], in0=AT[h][0][:], in1=UTs, op=ALU.mult),
                         r=[AT[h][0], self.gdc], w=[AT[h][0]])
                    S.op("pool", lambda e: e.tensor_tensor(out=aT[h][:], in0=aT[h][:], in1=UTc, op=ALU.mult), r=[aT[h], self.gdc], w=[aT[h]])
                pt_ = [P() for _ in H4]
                for h in H4:
                    S.op("pe", lambda e: e.transpose(out=pt_[h][:], in_=AT[h][0][:], identity=self.ident[:]), r=[AT[h][0], self.ident], w=[pt_[h]])
                for h in H4:
                    S.op("act", lambda e: e.activation(out=A[h][0][:], in_=pt_[h][:], func=AF.Copy), r=[pt_[h]], w=[A[h][0]])
                    S.op("pool", lambda e: e.tensor_tensor(out=Xh[h][:], in0=AT[h][0][:], in1=self.ident[:], op=ALU.add),
                         r=[AT[h][0], self.ident], w=[Xh[h]])
                for k in range(5):
                    cu, nx = k % 2, (k + 1) % 2
                    for h in range(4):
                        pa = P()
                        S.op("pe", lambda e: e.matmul(pa[:], AT[h][cu][:], A[h][cu][:], start=True, stop=True), r=[AT[h][cu], A[h][cu]], w=[pa])
                        S.op("act", lambda e: e.activation(out=A[h][nx][:], in_=pa[:], func=AF.Copy), r=[pa], w=[A[h][nx]])
                        if k < 4:
                            pb_ = P()
                            S.op("pe", lambda e: e.matmul(pb_[:], A[h][cu][:], AT[h][cu][:], start=True, stop=True),
                                 r=[AT[h][cu], A[h][cu]], w=[pb_])
                            S.op("dve", lambda e: e.tensor_copy(out=AT[h][nx][:], in_=pb_[:]), r=[pb_], w=[AT[h][nx]])
                    for h in range(4):
                        px = P()
                        S.op("pe", lambda e: e.matmul(px[:], A[h][nx][:], Xh[h][:], start=True, stop=True), r=[A[h][nx], Xh[h]], w=[px])
                        S.op("dve", lambda e: e.tensor_tensor(out=Xh[h][:], in0=px[:], in1=Xh[h][:], op=ALU.add), r=[px, Xh[h]], w=[Xh[h]])
                for j in range(2):
                    rs = slice(64 * j, 64 * j + 64)
                    pk_ = [P() for _ in range(4)]
                    for h in range(4):
                        S.op("pe", lambda e: e.matmul(pk_[h][:], qkv[b][:, 4 + h, :], Sh[h][:], start=True, stop=True),
                             r=[qkv[b], Sh[h]], w=[pk_[h]])
                    for h in range(4):
                        col = tt * 4 + h
                        S.op("dve", lambda e: e.scalar_tensor_tensor(out=R[h][rs, :], in0=pk_[h][rs, :], scalar=EGn[rs, col:col + 1],
                                                                     in1=vt[rs, h, :], op0=ALU.mult, op1=ALU.add),
                             r=[pk_[h], EGn, vt], w=[R[h]])
                    py_ = [P() for _ in range(4)]
                    for h in range(4):
                        S.op("pe", lambda e: e.matmul(py_[h][:], Xh[h][:], R[h][:], start=True, stop=True), r=[Xh[h], R[h]], w=[py_[h]])
                    for h in range(4):
                        S.op("dve", lambda e: e.tensor_scalar(out=vn[h][rs, :], in0=py_[h][rs, :], scalar1=beta[rs, tt, h:h + 1], scalar2=None,
                                                               op0=ALU.mult), r=[py_[h], beta], w=[vn[h]])
                    po_ = [P() for _ in range(4)]
                    for h in range(4):
                        S.op("pe", lambda e: e.matmul(po_[h][:], qeg[h][:], Sh[h][:], start=True, stop=False), r=[qeg[h], Sh[h]], w=[po_[h]])
                        S.op("pe", lambda e: e.matmul(po_[h][:], aT[h][:], vn[h][:], start=False, stop=True), r=[aT[h], vn[h]], w=[po_[h]])
                    for h in range(4):
                        S.op("act", lambda e: e.activation(out=O[rs, h, :], in_=po_[h][rs, :], func=AF.Copy), r=[po_[h]], w=[O])
                    pd_ = [pdb[h % 2] for h in range(4)]
                    for h in range(4):
                        S.op("pe", lambda e: e.matmul(pd_[h][:], kd[rs, h, :], vn[h][rs, :], start=True, stop=True), r=[kd, vn[h]], w=[pd_[h]])
                        col = tt * 4 + h
                        S.op("dve", lambda e: e.scalar_tensor_tensor(out=Sh[h][:], in0=Sh[h][:], scalar=EGL[j][:, col:col + 1], in1=pd_[h][:],
                                                                     op0=ALU.mult, op1=ALU.add), r=[Sh[h], EGL[j], pd_[h]], w=[Sh[h]])
                for h in range(4):
                    S.op("act", lambda e: e.activation(out=junk[:], in_=O[:, h, :], func=AF.Square, accum_out=ss[:, h:h + 1]),
                         r=[O], w=[junk, ss])
                S.op("act", lambda e: e.activation(out=ss[:], in_=ss[:], func=AF.Sqrt, scale=1.0 / 128, bias=self.epsln[:, 1:2]),
                     r=[ss, self.epsln], w=[ss])
                S.op("dve", lambda e: e.reciprocal(out=ss[:], in_=ss[:]), r=[ss], w=[ss])
                S.op("dve", lambda e: e.tensor_tensor(out=O[:], in0=O[:], in1=ss[:].unsqueeze(2).to_broadcast([128, 4, 128]), op=ALU.mult),
                     r=[O, ss], w=[O])
                S.op("dve", lambda e: e.tensor_tensor(out=O[:], in0=O[:], in1=gn[:].unsqueeze(1).to_broadcast([128, 4, 128]), op=ALU.mult),
                     r=[O, gn], w=[O])
                S.op("act", lambda e: e.activation(out=zt[b][:], in_=zt[b][:], func=AF.Silu), r=[zt[b]], w=[zt[b]])
                S.op("dve", lambda e: e.tensor_tensor(out=ob[b][:], in0=O[:].rearrange("p h e -> p (h e)"), in1=zt[b][:], op=ALU.mult),
                     r=[O, zt[b]], w=[ob[b]])
                S.dma("pool", X["OB"][ts, :], ob[b][:], r=[ob[b]], w=["OB"])

_CACHE = {}


def _get_prog(nlayers=DEPTH, debug=(), stop_after=None):
    key = (nlayers, tuple(sorted(debug)), stop_after)
    if key not in _CACHE:
        p = Prog(nlayers, debug, stop_after)
        p.build()
        _CACHE[key] = p
    return _CACHE[key]


def make_in_maps(inputs, ncores=8, nlayers=DEPTH, layer0=0, xs=None):
    consts = host_constants(np.asarray(inputs["rel_bias"], np.float32))
    shared = {}
    for k, v in inputs.items():
        if k in ("x", "rel_bias"):
            continue
        a = np.ascontiguousarray(np.asarray(v, np.float32)[layer0:layer0 + nlayers])
        if k in ("w_uk", "w_uv"):
            a = a.reshape(nlayers, 256, 512)
        shared[k] = a
    shared.update(consts)
    x = np.asarray(inputs["x"], np.float32) if xs is None else xs
    maps = []
    for c in range(ncores):
        m = dict(shared)
        m["x"] = np.ascontiguousarray(x[c])
        maps.append(m)
    return maps


def kernel(**inputs):
    p = _get_prog(DEPTH)
    maps = make_in_maps(inputs, 8, DEPTH)
    res = run_bass_kernel_spmd(p.nc, maps, core_ids=list(range(8)))
    return np.stack([np.asarray(r["out"], np.float32) for r in res.results], axis=0)
```

```python
from contextlib import ExitStack
import math
import numpy as np
import ml_dtypes
import concourse.bass as bass
import concourse.mybir as mybir
from concourse.bass_utils import run_bass_kernel_spmd

F32 = mybir.dt.float32
BF16 = mybir.dt.bfloat16
AF = mybir.ActivationFunctionType
ALU = mybir.AluOpType
AX = mybir.AxisListType

T = 4096
D = 1024
NT = T // 128
DEPTH = 4
INW = 8016
DFF = 4096
ALPHA = (2 * DEPTH) ** 0.25
LN_EPS = 1e-5
RMS_EPS = 1e-6
NEG = -30000.0
IDX_WEIGHT_SCALE = 8 ** -0.5 * 64 ** -0.5

C_QA, C_KA, C_VA, C_QB, C_ZB, C_AB, C_QC, C_CKV, C_QI, C_KI, C_WI, C_GA = (
    0, 512, 1024, 1536, 3072, 3584, 3592, 4104, 4360, 4872, 4936, 4944)


class Sched:
    ENGS = ("pe", "act", "dve", "pool", "sp")
    NDS = 48

    def __init__(self, nc, es):
        self.nc = nc
        self.es = es
        self.epoch = 1
        self.eng = {"pe": nc.tensor, "act": nc.scalar, "dve": nc.vector, "pool": nc.gpsimd, "sp": nc.sync}
        self.sem = {e: es.enter_context(nc.semaphore("s_" + e)) for e in self.ENGS}
        self.dsem = [es.enter_context(nc.semaphore("d_%d" % i)) for i in range(self.NDS)]
        self.cnt = {e: 0 for e in self.ENGS}
        self.seen = {e: {} for e in self.ENGS}
        self.lastw = {}
        self.readers = {}
        self.ndma = 0
        self.ndq = {}
        nsp = (3 * self.NDS) // 4
        self.dpool = {"sp": (0, nsp), "pool": (nsp, self.NDS - nsp)}
        self.dlast = {}
        self.ninst = 0
        self.scratch = es.enter_context(nc.sbuf_tensor("sched_scratch", [128, 1], F32))

    @staticmethod
    def _key(a):
        if isinstance(a, (str, tuple)):
            return a
        return a.name

    def _semof(self, sk):
        if isinstance(sk, tuple):
            return self.dsem[sk[1]]
        return self.sem[sk]

    def _deps(self, rk, wk):
        toks = []
        for k in rk:
            t = self.lastw.get(k)
            if t:
                toks.append(t)
        for k in wk:
            t = self.lastw.get(k)
            if t:
                toks.append(t)
            toks.extend(self.readers.get(k, {}).items())
        return toks

    def _wait(self, eng, toks):
        need = {}
        for (sk, v) in toks:
            if sk == "pe" and eng == "pe":
                continue
            if self.seen[eng].get(sk, 0) >= v:
                continue
            if need.get(sk, 0) < v:
                need[sk] = v
        e = self.eng[eng]
        for sk, v in need.items():
            self.seen[eng][sk] = v
            e.wait_ge(self._semof(sk), v)
            self.ninst += 1

    def _record(self, tok, rk, wk):
        for k in rk:
            d = self.readers.setdefault(k, {})
            if d.get(tok[0], 0) < tok[1]:
                d[tok[0]] = tok[1]
        for k in wk:
            self.lastw[k] = tok
            self.readers[k] = {}

    def op(self, eng, fn, r=(), w=()):
        rk = [self._key(a) for a in r]
        wk = [self._key(a) for a in w]
        self._wait(eng, self._deps(rk, wk))
        ins = fn(self.eng[eng])
        self.cnt[eng] += 1
        ins.then_inc(self.sem[eng], 1)
        self.ninst += 1
        tok = (eng, self.cnt[eng])
        self._record(tok, rk, wk)
        return tok

    def dma(self, qeng, out, in_, r=(), w=(), **kw):
        rk = [self._key(a) for a in r]
        wk = [self._key(a) for a in w]
        lo, n = self.dpool[qeng]
        k = self.ndq.get(qeng, 0)
        self.ndq[qeng] = k + 1
        idx = lo + k % n
        val = 16 * (k // n + 1)
        self.ndma += 1
        toks = self._deps(rk, wk)
        if val > 16:
            toks.append((("d", idx), val - 16))
        self._wait(qeng, toks)
        self.eng[qeng].dma_start(out=out, in_=in_, **kw).then_inc(self.dsem[idx], 16)
        self.ninst += 1
        tok = (("d", idx), val)
        self.dlast[idx] = val
        self._record(tok, rk, wk)
        return tok

    def barrier(self, new_epoch=False):
        toks = [(e, self.cnt[e]) for e in self.ENGS if self.cnt[e] > 0]
        toks += [(("d", i), v) for i, v in self.dlast.items()]
        for e in self.ENGS:
            self._wait(e, toks)
        self.lastw.clear()
        self.readers.clear()
        assert max(self.cnt.values()) < 30000, self.cnt
        if new_epoch:
            self.sem = {e: self.es.enter_context(self.nc.semaphore("s%d_%s" % (self.epoch, e))) for e in self.ENGS}
            self.epoch += 1
            self.cnt = {e: 0 for e in self.ENGS}
            for e in self.ENGS:
                for k in self.ENGS:
                    self.seen[e].pop(k, None)


def host_constants(rel_bias):
    def bucket(d):
        d = np.maximum(d, 0)
        large = 16 + (np.log(np.maximum(d, 16).astype(np.float32) / 16) / math.log(128 / 16) * 16).astype(np.int32)
        return np.where(d < 16, d, np.minimum(large, 31))
    j = np.arange(128)[:, None]
    c = np.arange(1280)[None, :]
    dist = c - j - 512
    b = bucket(dist)
    W = np.empty((128, 16, 1280), np.float32)
    for h in range(16):
        W[:, h, :] = np.where(dist >= 0, rel_bias[b, h], NEG)
    ident = np.eye(128, dtype=np.float32)
    ltri = np.tril(np.ones((128, 128), np.float32))
    s_ = np.arange(128)[:, None]
    c_ = np.arange(128)[None, :]
    same = (s_ // 64) == (c_ // 64)
    gd = np.stack([same & (s_ <= c_), same & (s_ < c_), same, np.broadcast_to(s_ < 64, (128, 128)),
                   np.broadcast_to(s_ >= 64, (128, 128))], axis=1).astype(np.float32)
    return {"c_toep": W.astype(ml_dtypes.bfloat16), "c_ident": ident, "c_ltri": ltri, "c_gdn": np.ascontiguousarray(gd)}


class Prog:
    def __init__(self, nlayers=DEPTH, debug=(), stop_after=None, inject=(), phases="ABCDEF"):
        self.nlayers = nlayers
        self.inject = set(inject)
        self.phases = phases
        self.debug = set(debug)
        self.stop_after = stop_after
        self.nc = bass.Bass("TRN2", target_bir_lowering=False)
        self.out_names = []

    def dram_in(self, name, shape, dt=F32):
        return self.nc.dram_tensor(name, list(shape), dt, kind="ExternalInput").ap()

    def dram_scr(self, name, shape, dt=F32):
        if name in self.inject:
            return self.nc.dram_tensor(name, list(shape), dt, kind="ExternalInput").ap()
        if name in self.debug:
            self.out_names.append(name)
            return self.nc.dram_tensor(name, list(shape), dt, kind="ExternalOutput").ap()
        return self.nc.dram_tensor(name, list(shape), dt).ap()

    def build(self):
        nc = self.nc
        L = self.nlayers
        I = {}
        I["x"] = self.dram_in("x", [T, D])
        I["w_in"] = self.dram_in("w_in", [L, D, INW])
        I["conv_w"] = self.dram_in("conv_w", [L, 4, 1536])
        I["a_log"] = self.dram_in("a_log", [L, 4])
        I["dt_bias"] = self.dram_in("dt_bias", [L, 4])
        I["gdn_norm"] = self.dram_in("gdn_norm", [L, 128])
        I["kv_norm"] = self.dram_in("kv_norm", [L, 256])
        I["idx_k_ln_g"] = self.dram_in("idx_k_ln_g", [L, 64])
        I["idx_k_ln_b"] = self.dram_in("idx_k_ln_b", [L, 64])
        I["w_uk"] = self.dram_in("w_uk", [L, 256, 512])
        I["w_uv"] = self.dram_in("w_uv", [L, 256, 512])
        I["w_branch_a"] = self.dram_in("w_branch_a", [L, 512, D])
        I["w_branch_b"] = self.dram_in("w_branch_b", [L, 512, D])
        I["w_branch_c"] = self.dram_in("w_branch_c", [L, 512, D])
        I["w_out"] = self.dram_in("w_out", [L, D, D])
        I["ln1_g"] = self.dram_in("ln1_g", [L, D])
        I["ln1_b"] = self.dram_in("ln1_b", [L, D])
        I["w_up"] = self.dram_in("w_up", [L, D, DFF])
        I["w_down"] = self.dram_in("w_down", [L, DFF, D])
        I["ln2_g"] = self.dram_in("ln2_g", [L, D])
        I["ln2_b"] = self.dram_in("ln2_b", [L, D])
        I["c_toep"] = self.dram_in("c_toep", [128, 16, 1280], BF16)
        I["c_ident"] = self.dram_in("c_ident", [128, 128])
        I["c_ltri"] = self.dram_in("c_ltri", [128, 128])
        I["c_gdn"] = self.dram_in("c_gdn", [128, 5, 128])
        self.I = I
        self.out = nc.dram_tensor("out", [T, D], F32, kind="ExternalOutput").ap()
        X = {}
        X["QA"] = self.dram_scr("QA", [512, T], BF16)
        X["KA"] = self.dram_scr("KA", [512, T], BF16)
        X["QKVB"] = self.dram_scr("QKVB", [1536, T], F32)
        X["QC"] = self.dram_scr("QC", [512, T], BF16)
        X["QI"] = self.dram_scr("QI", [512, T], BF16)
        X["VA"] = self.dram_scr("VA", [T, 520], BF16)
        X["ZB"] = self.dram_scr("ZB", [T, 512], F32)
        X["AB"] = self.dram_scr("AB", [T, 8], F32)
        X["CKV"] = self.dram_scr("CKV", [T, 256], F32)
        X["KIWI"] = self.dram_scr("KIWI", [T, 72], F32)
        X["GATES"] = self.dram_scr("GATES", [T, 3072], F32)
        X["OA"] = self.dram_scr("OA", [T, 512], BF16)
        X["OB"] = self.dram_scr("OB", [T, 512], BF16)
        X["OC"] = self.dram_scr("OC", [T, 512], BF16)
        X["X1"] = self.dram_scr("X1", [T, D], F32)
        X["XS"] = self.dram_scr("XS", [T, D], F32)
        self.X = X

        with ExitStack() as top:
            S = Sched(nc, top)
            self.S = S
            self.ident = top.enter_context(nc.sbuf_tensor("ident", [128, 128], F32))
            self.identb = top.enter_context(nc.sbuf_tensor("identb", [128, 128], BF16))
            self.i30k = top.enter_context(nc.sbuf_tensor("i30k", [128, 128], BF16))
            S.dma("sp", self.ident[:], I["c_ident"], w=[self.ident])
            S.op("dve", lambda e: e.tensor_copy(out=self.identb[:], in_=self.ident[:]), r=[self.ident], w=[self.identb])
            S.op("dve", lambda e: e.tensor_scalar(out=self.i30k[:], in0=self.ident[:], scalar1=-NEG, scalar2=None,
                                                   op0=ALU.mult), r=[self.ident], w=[self.i30k])
            self.ltri = top.enter_context(nc.sbuf_tensor("ltri", [128, 128], F32))
            S.dma("sp", self.ltri[:], I["c_ltri"], w=[self.ltri])
            self.gdc = top.enter_context(nc.sbuf_tensor("gdc", [128, 5, 128], F32))
            S.dma("sp", self.gdc[:], I["c_gdn"], w=[self.gdc])
            self.onesf = top.enter_context(nc.sbuf_tensor("onesf", [128, 128], F32))
            S.op("pool", lambda e: e.memset(self.onesf[:], 1.0), w=[self.onesf])
            self.onesb = top.enter_context(nc.sbuf_tensor("onesb", [128, 128], BF16))
            S.op("pool", lambda e: e.memset(self.onesb[:], 1.0), w=[self.onesb])
            self.epsln = top.enter_context(nc.sbuf_tensor("epsln", [128, 2], F32))
            S.op("pool", lambda e: e.memset(self.epsln[:, 0:1], LN_EPS), w=[self.epsln])
            S.op("pool", lambda e: e.memset(self.epsln[:, 1:2], RMS_EPS), w=[self.epsln])
            for l in range(self.nlayers):
                xin = I["x"] if l == 0 else X["XS"]
                xout = self.out if l == self.nlayers - 1 else X["XS"]
                for ph in "ABCDEF":
                    if ph not in self.phases:
                        continue
                    if ph == "A":
                        self.phase_a(l, xin)
                    elif ph == "B":
                        self.phase_b(l)
                    elif ph == "C":
                        self.phase_c(l)
                    elif ph == "D":
                        self.phase_d(l)
                    elif ph == "E":
                        self.phase_e(l, xin)
                    elif ph == "F":
                        self.phase_f(l, xout)
                    S.barrier(new_epoch=(ph in "CF"))
            S.barrier()
        return nc

    def phase_a(self, l, xin):
        nc, S, I, X = self.nc, self.S, self.I, self.X
        with ExitStack() as es:
            sb = lambda n, s, d: es.enter_context(nc.sbuf_tensor("%s_L%d" % (n, l), s, d))
            ps = lambda n, s, d: es.enter_context(nc.psum_tensor("%s_L%d" % (n, l), s, d))
            xT = sb("a_xT", [128, 8, T], BF16)
            xs = [sb("a_xs%d" % i, [128, D], F32) for i in range(2)]
            xb = [sb("a_xb%d" % i, [128, D], BF16) for i in range(2)]
            wst = [sb("a_wst%d" % i, [128, 8, 512], F32) for i in range(2)]
            wb = [sb("a_wb%d" % i, [128, 8, 512], BF16) for i in range(2)]
            stf = [sb("a_stf%d" % i, [128, T], F32) for i in range(2)]
            stb = [sb("a_stb%d" % i, [128, T], BF16) for i in range(2)]
            ptr = [ps("a_ptr%d" % i, [128, 8, 128], BF16) for i in range(2)]
            pacc = [ps("a_pacc%d" % i, [128, 512], F32) for i in range(4)]
            stv = [sb("a_stv%d" % i, [128, 8, 65], BF16) for i in range(2)]
            for i in range(2):
                S.op("pool", lambda e: e.memset(stv[i][:], 1.0), w=[stv[i]])

            for tt in range(NT):
                b = tt % 2
                S.dma("sp", xs[b][:], xin[tt * 128:(tt + 1) * 128, :], w=[xs[b]])
                S.op("pool", lambda e: e.tensor_copy(out=xb[b][:], in_=xs[b][:]), r=[xs[b]], w=[xb[b]])
                for kc in range(8):
                    S.op("pe", lambda e: e.transpose(out=ptr[b][:, kc, :], in_=xb[b][:, kc * 128:(kc + 1) * 128],
                                                     identity=self.identb[:]), r=[xb[b], self.identb], w=[ptr[b]])
                S.op("dve", lambda e: e.tensor_copy(out=xT[:, :, tt * 128:(tt + 1) * 128], in_=ptr[b][:]),
                     r=[ptr[b]], w=[xT])

            wsrc = I["w_in"][l].rearrange("(kc p) n -> p kc n", p=128)
            blocks = [
                ("FM", C_QA, 512, "QA", 0, 0.125), ("FM", C_KA, 512, "KA", 0, 1.0),
                ("FM", C_QB, 512, "QKVB", 0, 1.0), ("FM", C_QB + 512, 512, "QKVB", 512, 1.0),
                ("FM", C_QB + 1024, 512, "QKVB", 1024, 1.0),
                ("FM", C_QC, 512, "QC", 0, 0.125), ("FM", C_QI, 512, "QI", 0, 1.0),
                ("TM", C_VA, 512, "VA", 0, 1.0), ("TM", C_ZB, 512, "ZB", 0, 1.0), ("TM", C_AB, 8, "AB", 0, 1.0),
                ("TM", C_CKV, 256, "CKV", 0, 1.0), ("TM", C_KI, 72, "KIWI", 0, 1.0),
            ] + [("TM", C_GA + 512 * i, 512, "GATES", 512 * i, 1.0) for i in range(6)]
            ev = 0
            na = 0
            nst = 0
            def load_block(bi):
                _, c0_, ncol_, _, _, _ = blocks[bi]
                b_ = bi % 2
                S.dma("sp", wst[b_][:, :, 0:ncol_], wsrc[:, :, c0_:c0_ + ncol_], w=[wst[b_]])
                S.op("pool", lambda e: e.tensor_copy(out=wb[b_][:, :, 0:ncol_], in_=wst[b_][:, :, 0:ncol_]),
                     r=[wst[b_]], w=[wb[b_]])

            load_block(0)
            for bi, (kind, c0, ncol, dname, doff, scale) in enumerate(blocks):
                b = bi % 2
                dst = X[dname]
                isbf = dname in ("QA", "KA", "QC", "QI", "VA")
                if bi + 1 < len(blocks):
                    load_block(bi + 1)
                if kind == "FM":
                    for j in range(ncol // 128):
                        st = (stb if isbf else stf)[nst % 2]
                        nst += 1
                        for tg in range(8):
                            pa = pacc[na % 4]
                            na += 1
                            for kc in range(8):
                                S.op("pe", lambda e: e.matmul(pa[:], wb[b][:, kc, j * 128:(j + 1) * 128],
                                                              xT[:, kc, tg * 512:(tg + 1) * 512],
                                                              start=(kc == 0), stop=(kc == 7)),
                                     r=[wb[b], xT], w=[pa])
                            eng = "act" if ev % 2 == 0 else "dve"
                            ev += 1
                            if eng == "act":
                                S.op("act", lambda e: e.activation(out=st[:, tg * 512:(tg + 1) * 512], in_=pa[:],
                                                                   func=AF.Copy, scale=scale), r=[pa], w=[st])
                            else:
                                S.op("dve", lambda e: e.tensor_scalar(out=st[:, tg * 512:(tg + 1) * 512], in0=pa[:],
                                                                      scalar1=scale, scalar2=None, op0=ALU.mult),
                                     r=[pa], w=[st])
                        r0 = doff + j * 128
                        S.dma("pool", dst[r0:r0 + 128, :], st[:], r=[st], w=[dname])
                else:
                    for tt in range(NT):
                        pa = pacc[na % 4]
                        na += 1
                        for kc in range(8):
                            S.op("pe", lambda e: e.matmul(pa[:, 0:ncol], xT[:, kc, tt * 128:(tt + 1) * 128],
                                                          wb[b][:, kc, 0:ncol], start=(kc == 0), stop=(kc == 7)),
                                 r=[wb[b], xT], w=[pa])
                        if dname == "VA":
                            sv = stv[nst % 2]
                            nst += 1
                            S.op("act", lambda e: e.activation(out=sv[:, :, 0:64], in_=pa[:].rearrange("p (h d) -> p h d", h=8),
                                                               func=AF.Copy), r=[pa], w=[sv])
                            S.dma("sp", dst[tt * 128:(tt + 1) * 128, :], sv[:].rearrange("p h d -> p (h d)"),
                                  r=[sv], w=[dname])
                            continue
                        st = (stb if isbf else stf)[nst % 2]
                        nst += 1
                        eng = "act" if ev % 2 == 0 else "dve"
                        ev += 1
                        if eng == "act":
                            S.op("act", lambda e: e.activation(out=st[:, 0:ncol], in_=pa[:, 0:ncol], func=AF.Copy),
                                 r=[pa], w=[st])
                        else:
                            S.op("dve", lambda e: e.tensor_copy(out=st[:, 0:ncol], in_=pa[:, 0:ncol]), r=[pa], w=[st])
                        S.dma("sp", dst[tt * 128:(tt + 1) * 128, doff:doff + ncol], st[:, 0:ncol],
                              r=[st], w=[dname])


    def layer_norm_tile(self, z, gt, bt, st1, junk, out):
        S = self.S
        S.op("dve", lambda e: e.tensor_reduce(out=st1[:, 0:1], in_=z[:], axis=AX.X, op=ALU.add), r=[z], w=[st1])
        S.op("dve", lambda e: e.tensor_scalar(out=st1[:, 1:2], in0=st1[:, 0:1], scalar1=-1.0 / D, scalar2=None,
                                               op0=ALU.mult), r=[st1], w=[st1])
        S.op("act", lambda e: e.activation(out=junk[:], in_=z[:], func=AF.Square, bias=st1[:, 1:2],
                                           accum_out=st1[:, 2:3]), r=[z, st1], w=[junk, st1])
        S.op("act", lambda e: e.activation(out=st1[:, 3:4], in_=st1[:, 2:3], func=AF.Sqrt, scale=1.0 / D,
                                           bias=self.epsln[:, 0:1]), r=[st1, self.epsln], w=[st1])
        S.op("dve", lambda e: e.reciprocal(out=st1[:, 4:5], in_=st1[:, 3:4]), r=[st1], w=[st1])
        S.op("dve", lambda e: e.tensor_scalar(out=z[:], in0=z[:], scalar1=st1[:, 1:2], scalar2=st1[:, 4:5],
                                               op0=ALU.add, op1=ALU.mult), r=[z, st1], w=[z])
        S.op("pool", lambda e: e.tensor_tensor(out=z[:], in0=z[:], in1=gt[:], op=ALU.mult), r=[z, gt], w=[z])
        S.op("pool", lambda e: e.tensor_tensor(out=out[:], in0=z[:], in1=bt[:], op=ALU.add), r=[z, bt], w=[out])

    def phase_e(self, l, xin):
        nc, S, I, X = self.nc, self.S, self.I, self.X
        with ExitStack() as es:
            sb = lambda n, s, d: es.enter_context(nc.sbuf_tensor("%s_L%d" % (n, l), s, d))
            ps = lambda n, s, d: es.enter_context(nc.psum_tensor("%s_L%d" % (n, l), s, d))
            wbr = sb("e_wbr", [128, 12, D], BF16)
            wout = sb("e_wout", [128, 8, D], BF16)
            wst = [sb("e_wst%d" % i, [128, 4, D], F32) for i in range(2)]
            gt = sb("e_g", [128, D], F32)
            bt = sb("e_b", [128, D], F32)
            obr = [sb("e_obr%d" % i, [128, 3, 512], BF16) for i in range(2)]
            obT2 = [sb("e_obT%d" % i, [128, 12, 128], BF16) for i in range(2)]
            sg = [sb("e_sg%d" % i, [128, 3072], F32) for i in range(2)]
            xs = [sb("e_xs%d" % i, [128, D], F32) for i in range(2)]
            y2 = [sb("e_y%d" % i, [128, D], F32) for i in range(2)]
            tmp2 = [sb("e_tmp%d" % i, [128, 512], F32) for i in range(2)]
            yb2 = [sb("e_yb%d" % i, [128, D], BF16) for i in range(2)]
            yT2 = [sb("e_yT%d" % i, [128, 8, 128], BF16) for i in range(2)]
            z2 = [sb("e_z%d" % i, [128, D], F32) for i in range(2)]
            junk2 = [sb("e_junk%d" % i, [128, D], F32) for i in range(2)]
            xo = [sb("e_xo%d" % i, [128, D], F32) for i in range(2)]
            st12 = [sb("e_st1%d" % i, [128, 8], F32) for i in range(2)]
            ptr = [ps("e_ptr%d" % i, [128, 8, 128], BF16) for i in range(2)]
            pacc = [ps("e_pacc%d" % i, [128, 512], F32) for i in range(4)]
            k = 0
            for bi, wn in enumerate(("w_branch_a", "w_branch_b", "w_branch_c")):
                S.dma("sp", wst[k % 2][:], I[wn][l].rearrange("(kc p) n -> p kc n", p=128), w=[wst[k % 2]])
                S.op("pool", lambda e: e.tensor_copy(out=wbr[:, bi * 4:(bi + 1) * 4, :], in_=wst[k % 2][:]),
                     r=[wst[k % 2]], w=[wbr])
                k += 1
            wo = I["w_out"][l].rearrange("(kc p) n -> p kc n", p=128)
            for hh in range(2):
                S.dma("sp", wst[k % 2][:], wo[:, hh * 4:(hh + 1) * 4, :], w=[wst[k % 2]])
                S.op("pool", lambda e: e.tensor_copy(out=wout[:, hh * 4:(hh + 1) * 4, :], in_=wst[k % 2][:]),
                     r=[wst[k % 2]], w=[wout])
                k += 1
            S.dma("sp", gt[:], I["ln1_g"][l:l + 1, :].partition_broadcast(128), w=[gt])
            S.dma("sp", bt[:], I["ln1_b"][l:l + 1, :].partition_broadcast(128), w=[bt])
            na = 0
            na_ = [na]

            def bind(tt):
                b = tt % 2
                return (b, slice(tt * 128, (tt + 1) * 128), obT2[b], y2[b], tmp2[b], yb2[b], yT2[b], z2[b], junk2[b], st12[b])

            def stage_x(tt):
                b, rows, obT, y, tmp, yb, yT, z, junk, st1 = bind(tt)
                na = na_[0]
                for bi, nm in enumerate(("OA", "OB", "OC")):
                    S.dma("sp", obr[b][:, bi, :], X[nm][rows, :], r=[nm], w=[obr[b]])
                S.dma("sp", sg[b][:], X["GATES"][rows, :], r=["GATES"], w=[sg[b]])
                S.dma("sp", xs[b][:], xin[rows, :], r=["XS"], w=[xs[b]])
                S.op("act", lambda e: e.activation(out=sg[b][:], in_=sg[b][:], func=AF.Sigmoid), r=[sg[b]], w=[sg[b]])
                for c in range(12):
                    p = ptr[0] if c < 8 else ptr[1]
                    S.op("pe", lambda e: e.transpose(out=p[:, c % 8, :], in_=obr[b][:, c // 4, (c % 4) * 128:(c % 4 + 1) * 128],
                                                     identity=self.identb[:]), r=[obr[b], self.identb], w=[p])
                S.op("dve", lambda e: e.tensor_copy(out=obT[:, 0:8, :], in_=ptr[0][:]), r=[ptr[0]], w=[obT])
                S.op("dve", lambda e: e.tensor_copy(out=obT[:, 8:12, :], in_=ptr[1][:, 0:4, :]), r=[ptr[1]], w=[obT])
                for nh in range(2):
                    cs = slice(nh * 512, (nh + 1) * 512)
                    for bi in range(3):
                        pa = pacc[na % 4]
                        na += 1
                        for kc in range(4):
                            S.op("pe", lambda e: e.matmul(pa[:], obT[:, bi * 4 + kc, :], wbr[:, bi * 4 + kc, cs],
                                                          start=(kc == 0), stop=(kc == 3)), r=[obT, wbr], w=[pa])
                        gsl = sg[b][:, bi * 1024 + nh * 512: bi * 1024 + (nh + 1) * 512]
                        if bi == 0:
                            S.op("dve", lambda e: e.tensor_tensor(out=y[:, cs], in0=pa[:], in1=gsl, op=ALU.mult),
                                 r=[pa, sg[b]], w=[y])
                        else:
                            S.op("dve", lambda e: e.tensor_tensor(out=tmp[:], in0=pa[:], in1=gsl, op=ALU.mult),
                                 r=[pa, sg[b]], w=[tmp])
                            S.op("pool", lambda e: e.tensor_tensor(out=y[:, cs], in0=y[:, cs], in1=tmp[:], op=ALU.add),
                                 r=[y, tmp], w=[y])
                na_[0] = na

            def stage_y(tt):
                b, rows, obT, y, tmp, yb, yT, z, junk, st1 = bind(tt)
                na = na_[0]
                S.op("act", lambda e: e.activation(out=yb[:], in_=y[:], func=AF.Copy), r=[y], w=[yb])
                for kc in range(8):
                    S.op("pe", lambda e: e.transpose(out=ptr[0][:, kc, :], in_=yb[:, kc * 128:(kc + 1) * 128],
                                                     identity=self.identb[:]), r=[yb, self.identb], w=[ptr[0]])
                S.op("dve", lambda e: e.tensor_copy(out=yT[:], in_=ptr[0][:]), r=[ptr[0]], w=[yT])
                for nh in range(2):
                    cs = slice(nh * 512, (nh + 1) * 512)
                    pa = pacc[na % 4]
                    na += 1
                    for kc in range(8):
                        S.op("pe", lambda e: e.matmul(pa[:], yT[:, kc, :], wout[:, kc, cs], start=(kc == 0), stop=(kc == 7)),
                             r=[yT, wout], w=[pa])
                    S.op("dve", lambda e: e.scalar_tensor_tensor(out=z[:, cs], in0=xs[b][:, cs], scalar=ALPHA, in1=pa[:],
                                                                 op0=ALU.mult, op1=ALU.add), r=[xs[b], pa], w=[z])
                self.layer_norm_tile(z, gt, bt, st1, junk, xo[b])
                S.dma("pool", X["X1"][rows, :], xo[b][:], r=[xo[b]], w=["X1"])
                na_[0] = na

            stage_x(0)
            for tt in range(NT):
                if tt + 1 < NT:
                    stage_x(tt + 1)
                stage_y(tt)

    def phase_f(self, l, xout):
        nc, S, I, X = self.nc, self.S, self.I, self.X
        TG = 256
        with ExitStack() as es:
            sb = lambda n, s, d: es.enter_context(nc.sbuf_tensor("%s_L%d" % (n, l), s, d))
            ps = lambda n, s, d: es.enter_context(nc.psum_tensor("%s_L%d" % (n, l), s, d))
            wup = sb("f_wup", [128, 8, DFF], BF16)
            wdn = sb("f_wdn", [128, 32, D], BF16)
            wst = sb("f_wst", [128, 8, 512], F32)
            gt = sb("f_g", [128, D], F32)
            bt = sb("f_b", [128, D], F32)
            x1 = sb("f_x1", [128, 2, D], F32)
            x1b = sb("f_x1b", [128, D], BF16)
            x1T = sb("f_x1T", [128, 8, TG], BF16)
            hT = sb("f_hT", [128, 32, TG], BF16)
            rl = [sb("f_rl%d" % i, [128, TG], F32) for i in range(2)]
            z = sb("f_z", [128, D], F32)
            junk = sb("f_junk", [128, D], F32)
            xo = [sb("f_xo%d" % i, [128, D], F32) for i in range(2)]
            st1 = sb("f_st1", [128, 8], F32)
            ptr = [ps("f_ptr%d" % i, [128, 8, 128], BF16) for i in range(2)]
            pup = [ps("f_pup%d" % i, [128, TG], F32) for i in range(2)]
            pdn = [ps("f_pdn%d" % i, [128, 512], F32) for i in range(2)]
            wu = I["w_up"][l].rearrange("(kc p) n -> p kc n", p=128)
            for c in range(8):
                S.dma("sp", wst[:], wu[:, :, c * 512:(c + 1) * 512], w=[wst])
                S.op("pool", lambda e: e.tensor_copy(out=wup[:, :, c * 512:(c + 1) * 512], in_=wst[:]), r=[wst], w=[wup])
            wd = I["w_down"][l].rearrange("(fc p) n -> p fc n", p=128)

            def load_wdown():
                for c in range(8):
                    S.dma("sp", wst[:].rearrange("p a n -> p (a n)").rearrange("p (f n) -> p f n", f=4),
                          wd[:, c * 4:(c + 1) * 4, :], w=[wst])
                    S.op("pool", lambda e: e.tensor_copy(out=wdn[:, c * 4:(c + 1) * 4, :],
                                                         in_=wst[:].rearrange("p a n -> p (a n)").rearrange("p (f n) -> p f n", f=4)),
                         r=[wst], w=[wdn])
            S.dma("sp", gt[:], I["ln2_g"][l:l + 1, :].partition_broadcast(128), w=[gt])
            S.dma("sp", bt[:], I["ln2_b"][l:l + 1, :].partition_broadcast(128), w=[bt])
            nu = 0
            nd = 0
            no = 0
            for tg in range(T // TG):
                for u in range(2):
                    tt = tg * 2 + u
                    S.dma("sp", x1[:, u, :], X["X1"][tt * 128:(tt + 1) * 128, :], r=["X1"], w=[x1])
                for u in range(2):
                    S.op("act", lambda e: e.activation(out=x1b[:], in_=x1[:, u, :], func=AF.Copy), r=[x1], w=[x1b])
                    for kc in range(8):
                        S.op("pe", lambda e: e.transpose(out=ptr[u][:, kc, :], in_=x1b[:, kc * 128:(kc + 1) * 128],
                                                         identity=self.identb[:]), r=[x1b, self.identb], w=[ptr[u]])
                    S.op("dve", lambda e: e.tensor_copy(out=x1T[:, :, u * 128:(u + 1) * 128], in_=ptr[u][:]),
                         r=[ptr[u]], w=[x1T])
                for fc in range(32):
                    pu = pup[nu % 2]
                    r_ = rl[nu % 2]
                    nu += 1
                    for kc in range(8):
                        S.op("pe", lambda e: e.matmul(pu[:], wup[:, kc, fc * 128:(fc + 1) * 128], x1T[:, kc, :],
                                                      start=(kc == 0), stop=(kc == 7)), r=[wup, x1T], w=[pu])
                    S.op("act", lambda e: e.activation(out=r_[:], in_=pu[:], func=AF.Relu), r=[pu], w=[r_])
                    S.op("dve", lambda e: e.tensor_tensor(out=hT[:, fc, :], in0=r_[:], in1=r_[:], op=ALU.mult), r=[r_], w=[hT])
                if tg == 0:
                    load_wdown()
                for u in range(2):
                    tt = tg * 2 + u
                    for nh in range(2):
                        cs = slice(nh * 512, (nh + 1) * 512)
                        pd = pdn[nd % 2]
                        nd += 1
                        for fc in range(32):
                            S.op("pe", lambda e: e.matmul(pd[:], hT[:, fc, u * 128:(u + 1) * 128], wdn[:, fc, cs],
                                                          start=(fc == 0), stop=(fc == 31)), r=[hT, wdn], w=[pd])
                        S.op("dve", lambda e: e.scalar_tensor_tensor(out=z[:, cs], in0=x1[:, u, cs], scalar=ALPHA, in1=pd[:],
                                                                     op0=ALU.mult, op1=ALU.add), r=[x1, pd], w=[z])
                    o = xo[no % 2]
                    no += 1
                    self.layer_norm_tile(z, gt, bt, st1, junk, o)
                    oname = "OUT" if xout is self.out else "XS"
                    S.dma("pool", xout[tt * 128:(tt + 1) * 128, :], o[:], r=[o], w=[oname])


    def phase_b(self, l):
        nc, S, I, X = self.nc, self.S, self.I, self.X
        with ExitStack() as es:
            sb = lambda n, s, d: es.enter_context(nc.sbuf_tensor("%s_L%d" % (n, l), s, d))
            ps = lambda n, s, d: es.enter_context(nc.psum_tensor("%s_L%d" % (n, l), s, d))
            kT = sb("b_kT", [128, 4, T], BF16)
            qT = sb("b_qT", [128, 4, T], BF16)
            va = sb("b_va", [128, NT, 520], BF16)
            toep = sb("b_toep", [128, 8, 1280], BF16)
            nsT = sb("b_nsT", [128, T], BF16)
            ksum = sb("b_ksum", [128, 4, 16], F32)
            kmT = sb("b_kmT", [128, 4, 32], BF16)
            gm = sb("b_gm", [128, 8, 16], F32)
            m8 = sb("b_m8", [128, 8, 8], F32)
            ns = sb("b_ns", [128, 8, 16], BF16)
            PT = [sb("b_PT%d" % i, [128, 512], BF16) for i in range(4)]
            oa = [sb("b_oa%d" % i, [128, 4, 512], BF16) for i in range(2)]
            rden = sb("b_rden", [128, 4], F32)
            E = sb("b_E", [128, 128, 128], BF16)
            S.op("dve", lambda e: e.tensor_copy(out=E[:], in_=self.i30k[:].unsqueeze(2).to_broadcast([128, 128, 128])),
                 r=[self.i30k], w=[E])
            pS = [ps("b_pS%d" % i, [128, 512], F32) for i in range(2)]
            pO = [ps("b_pO%d" % i, [128, 65], F32) for i in range(4)]
            es_sel = ExitStack()
            pg = es_sel.enter_context(nc.psum_tensor("b_pg_L%d" % l, [128, 8, 16], F32))
            ptr = es_sel.enter_context(nc.psum_tensor("b_ptr_L%d" % l, [128, 128], BF16))
            for a in range(4):
                S.dma("sp", kT[:, a, :], X["KA"][a * 128:(a + 1) * 128, :], r=["KA"], w=[kT])
                S.dma("sp", qT[:, a, :], X["QA"][a * 128:(a + 1) * 128, :], r=["QA"], w=[qT])
            vsrc = X["VA"].rearrange("(tt p) c -> p tt c", p=128)
            for c in range(4):
                S.dma("sp", va[:, c * 8:(c + 1) * 8, :], vsrc[:, c * 8:(c + 1) * 8, :], r=["VA"], w=[va])
            S.dma("sp", toep[:], I["c_toep"][:, 0:8, :], w=[toep])
            S.op("pool", lambda e: e.memset(nsT[:], 0.0), w=[nsT])
            S.op("pool", lambda e: e.memset(gm[:], -1e30), w=[gm])
            S.op("pool", lambda e: e.memset(ns[:], 0.0), w=[ns])
            import os
            bstop = int(os.environ.get("BSTOP", "99"))
            if bstop <= 1:
                es_sel.close()
                return
            S.op("dve", lambda e: e.tensor_reduce(out=ksum[:], in_=kT[:].rearrange("p a (n s) -> p a n s", s=256),
                                                   axis=AX.X, op=ALU.add), r=[kT], w=[ksum])
            S.op("pool", lambda e: e.memset(kmT[:], 0.0), w=[kmT])
            S.op("dve", lambda e: e.tensor_scalar(out=kmT[0:64, :, 0:16], in0=ksum[0:64, :, :], scalar1=1.0 / 256, scalar2=None,
                                                   op0=ALU.mult), r=[ksum], w=[kmT])
            S.op("dve", lambda e: e.tensor_scalar(out=kmT[64:128, :, 16:32], in0=ksum[64:128, :, :], scalar1=1.0 / 256, scalar2=None,
                                                   op0=ALU.mult), r=[ksum], w=[kmT])
            if bstop <= 2:
                es_sel.close()
                return
            for tt in range(NT):
                cur = tt // 2
                if cur <= 3:
                    continue
                for a in range(4):
                    S.op("pe", lambda e: e.matmul(pg[:, 2 * a:2 * a + 2, :], qT[:, a, tt * 128:(tt + 1) * 128],
                                                  kmT[:, a, :].rearrange("p (h n) -> p h n", h=2), start=True, stop=True),
                         r=[qT, kmT], w=[pg])
                bsub = int(os.environ.get("BSUB", "99"))
                S.op("act", lambda e: e.activation(out=gm[:, :, 0:cur], in_=pg[:, :, 0:cur], func=AF.Copy), r=[pg], w=[gm])
                if bsub <= 1:
                    continue
                for h in range(8):
                    S.op("dve", lambda e: e.max(out=m8[:, h, :], in_=gm[:, h, :]), r=[gm], w=[m8])
                if bsub <= 2:
                    continue
                for h in range(8):
                    S.op("dve", lambda e: e.tensor_scalar(out=ns[:, h, 0:cur], in0=gm[:, h, 0:cur], scalar1=m8[:, h, 2:3],
                                                           scalar2=1.0, op0=ALU.is_ge, op1=ALU.subtract), r=[gm, m8], w=[ns])
                if bsub <= 3:
                    continue
                S.op("pe", lambda e: e.transpose(out=ptr[:], in_=ns[:].rearrange("p h n -> p (h n)"), identity=self.identb[:]),
                     r=[ns, self.identb], w=[ptr])
                S.op("act", lambda e: e.activation(out=nsT[:, tt * 128:(tt + 1) * 128], in_=ptr[:], func=AF.Copy),
                     r=[ptr], w=[nsT])
            S.barrier()
            es_sel.close()
            pS = pS + [ps("b_pS%d" % i, [128, 512], F32) for i in (2, 3)]
            if bstop <= 3:
                return
            nS = [0]
            for qg in range(8):
                if bstop <= 4 and qg >= 1:
                    break
                ob = oa[qg % 2]
                qs = slice(qg * 512, (qg + 1) * 512)
                nkt = 4 * (qg + 1)

                def emit_S(h, kt):
                    hb, a_ = 64 * (h % 2), h // 2
                    m = kt - 4 * qg
                    off = 512 - 128 * m if m >= -1 else 768
                    n = kt // 2
                    need_sel = (qg >= 2) and (n <= 2 * qg)
                    p, pt = pS[nS[0] % 4], PT[nS[0] % 4]
                    nS[0] += 1
                    S.op("pe", lambda e: e.matmul(p[:], kT[hb:hb + 64, a_, kt * 128:(kt + 1) * 128], qT[hb:hb + 64, a_, qs],
                                                  start=True, stop=False), r=[kT, qT], w=[p])
                    S.op("pe", lambda e: e.matmul(p[:], self.identb[:], toep[:, h, off:off + 512],
                                                  start=False, stop=not need_sel), r=[self.identb, toep], w=[p])
                    if need_sel:
                        rr = h * 16 + n
                        S.op("pe", lambda e: e.matmul(p[:], E[:, rr, :], nsT[:, qs], start=False, stop=True), r=[E, nsT], w=[p])
                    return p, pt

                def emit_PV(h, kt, pt):
                    for u in range(4):
                        last = 4 * qg + u
                        if kt <= last:
                            S.op("pe", lambda e: e.matmul(pO[u][:], pt[:, u * 128:(u + 1) * 128], va[:, kt, h * 65:(h + 1) * 65],
                                                          start=(kt == 0), stop=(kt == last)), r=[pt, va], w=[pO[u]])

                def finalize(h):
                    for u in range(4):
                        S.op("dve", lambda e: e.reciprocal(out=rden[:, u:u + 1], in_=pO[u][:, 64:65]), r=[pO[u]], w=[rden])
                        S.op("dve", lambda e: e.tensor_scalar(out=ob[:, u, h * 64:(h + 1) * 64], in0=pO[u][:, 0:64],
                                                               scalar1=rden[:, u:u + 1], scalar2=None, op0=ALU.mult),
                             r=[pO[u], rden], w=[ob])

                steps = [(h, kt) for h in range(8) for kt in range(nkt)]
                pending = emit_S(*steps[0])
                for i, (h, kt) in enumerate(steps):
                    p, pt = pending
                    S.op("act", lambda e: e.activation(out=pt[:], in_=p[:], func=AF.Exp), r=[p], w=[pt])
                    if i + 1 < len(steps):
                        pending = emit_S(*steps[i + 1])
                    emit_PV(h, kt, pt)
                    if kt == nkt - 1:
                        finalize(h)
                S.dma("pool", X["OA"][qg * 512:(qg + 1) * 512, :].rearrange("(u p) c -> p u c", p=128), ob[:],
                      r=[ob], w=["OA"])

    def phase_d(self, l):
        nc, S, I, X = self.nc, self.S, self.I, self.X
        import os
        dstop = int(os.environ.get("DSTOP", "99"))
        with ExitStack() as es:
            sb = lambda n, s, d: es.enter_context(nc.sbuf_tensor("%s_L%d" % (n, l), s, d))
            ps = lambda n, s, d: es.enter_context(nc.psum_tensor("%s_L%d" % (n, l), s, d))
            c = sb("d_c", [128, NT, 256], BF16)
            cT = sb("d_cT", [128, 2, T], BF16)
            kiT2 = sb("d_kiT2", [128, T], BF16)
            toep = sb("d_toep", [128, 8, 1280], BF16)
            absw = sb("d_absw", [128, NT, 8], F32)
            sgn = sb("d_sgn", [128, NT, 8], F32)
            wukT = sb("d_wukT", [128, 4, 256], BF16)
            WBD = sb("d_WBD", [128, 4, 2, 256], BF16)
            wuvb = sb("d_wuvb", [128, 2, 512], BF16)
            pI = [ps("d_pI%d" % i, [128, 512], F32) for i in range(2)]
            pAcc = ps("d_pAcc", [128, 512], F32)
            pS = [ps("d_pS%d" % i, [128, 4, 128], F32) for i in range(2)]
            pOT = [ps("d_pOT%d" % i, [128, 512], F32) for i in range(2)]
            pDen = ps("d_pDen", [128, 512], F32)
            pIb = [p[:].bitcast(BF16).rearrange("p (a n) -> p a n", a=8) for p in pI]

            S.dma("sp", toep[:], I["c_toep"][:, 8:16, :], w=[toep])
            with ExitStack() as es2:
                sb2 = lambda n, s, d: es2.enter_context(nc.sbuf_tensor("%s_L%d" % (n, l), s, d))
                ckv = sb2("d_ckv", [128, NT, 256], F32)
                kiwi = sb2("d_kiwi", [128, NT, 72], F32)
                kin = sb2("d_kin", [128, NT, 128], BF16)
                kif = sb2("d_kif", [128, NT, 64], F32)
                junk = sb2("d_junk", [128, 256], F32)
                wst = sb2("d_wst", [128, 2, 512], F32)
                wukb = sb2("d_wukb", [128, 2, 512], BF16)
                kvn = sb2("d_kvn", [128, 256], F32)
                lng = sb2("d_lng", [128, 64], F32)
                lnb = sb2("d_lnb", [128, 64], F32)
                ss = sb2("d_ss", [128, NT], F32)
                rs = sb2("d_rs", [128, NT], F32)
                mu = sb2("d_mu", [128, NT], F32)
                S.dma("sp", wst[:], I["w_uk"][l].rearrange("(rc p) n -> p rc n", p=128), w=[wst])
                S.op("pool", lambda e: e.tensor_copy(out=wukb[:], in_=wst[:]), r=[wst], w=[wukb])
                for rc in range(2):
                    for a in range(4):
                        S.op("pe", lambda e: e.transpose(out=pIb[0][:, rc * 4 + a, :], in_=wukb[:, rc, a * 128:(a + 1) * 128],
                                                         identity=self.identb[:]), r=[wukb, self.identb], w=[pI[0]])
                S.op("dve", lambda e: e.tensor_copy(out=wukT[:].rearrange("p a (rc r) -> p rc a r", rc=2),
                                                     in_=pIb[0].rearrange("p (rc a) r -> p rc a r", rc=2)), r=[pI[0]], w=[wukT])
                S.op("pool", lambda e: e.memset(WBD[:], 0.0), w=[WBD])
                S.op("dve", lambda e: e.tensor_copy(out=WBD[0:64, :, 0, :], in_=wukT[0:64, :, :]), r=[wukT], w=[WBD])
                S.op("dve", lambda e: e.tensor_copy(out=WBD[64:128, :, 1, :], in_=wukT[64:128, :, :]), r=[wukT], w=[WBD])
                S.dma("sp", wst[:], I["w_uv"][l].rearrange("(rc p) n -> p rc n", p=128), r=[], w=[wst])
                S.op("pool", lambda e: e.tensor_copy(out=wuvb[:], in_=wst[:]), r=[wst], w=[wuvb])
                S.dma("sp", kvn[:], I["kv_norm"][l:l + 1, :].partition_broadcast(128), w=[kvn])
                S.dma("sp", lng[:], I["idx_k_ln_g"][l:l + 1, :].partition_broadcast(128), w=[lng])
                S.dma("sp", lnb[:], I["idx_k_ln_b"][l:l + 1, :].partition_broadcast(128), w=[lnb])
                csrc = X["CKV"].rearrange("(tt p) c -> p tt c", p=128)
                for q4 in range(4):
                    S.dma("sp", ckv[:, q4 * 8:(q4 + 1) * 8, :], csrc[:, q4 * 8:(q4 + 1) * 8, :], r=["CKV"], w=[ckv])
                for tt in range(NT):
                    S.op("act", lambda e: e.activation(out=junk[:], in_=ckv[:, tt, :], func=AF.Square, accum_out=ss[:, tt:tt + 1]),
                         r=[ckv], w=[junk, ss])
                S.op("act", lambda e: e.activation(out=rs[:], in_=ss[:], func=AF.Sqrt, scale=1.0 / 256, bias=self.epsln[:, 1:2]),
                     r=[ss, self.epsln], w=[rs])
                S.op("dve", lambda e: e.reciprocal(out=rs[:], in_=rs[:]), r=[rs], w=[rs])
                S.op("dve", lambda e: e.tensor_tensor(out=ckv[:], in0=ckv[:], in1=rs[:].unsqueeze(2).to_broadcast([128, NT, 256]),
                                                       op=ALU.mult), r=[ckv, rs], w=[ckv])
                S.op("dve", lambda e: e.tensor_tensor(out=c[:], in0=ckv[:], in1=kvn[:].unsqueeze(1).to_broadcast([128, NT, 256]),
                                                       op=ALU.mult), r=[ckv, kvn], w=[c])
                for g in range(NT // 4):
                    pb = pIb[g % 2]
                    for t4 in range(4):
                        for rc in range(2):
                            S.op("pe", lambda e: e.transpose(out=pb[:, t4 * 2 + rc, :], in_=c[:, g * 4 + t4, rc * 128:(rc + 1) * 128],
                                                             identity=self.identb[:]), r=[c, self.identb], w=[pI[g % 2]])
                    S.op("act", lambda e: e.activation(out=cT[:, :, g * 512:(g + 1) * 512].rearrange("p rc (t q) -> p t rc q", t=4),
                                                       in_=pb.rearrange("p (t rc) q -> p t rc q", t=4), func=AF.Copy),
                         r=[pI[g % 2]], w=[cT])
                S.dma("sp", kiwi[:], X["KIWI"].rearrange("(tt p) c -> p tt c", p=128), r=["KIWI"], w=[kiwi])
                S.op("dve", lambda e: e.tensor_reduce(out=mu[:], in_=kiwi[:, :, 0:64], axis=AX.X, op=ALU.add), r=[kiwi], w=[mu])
                S.op("dve", lambda e: e.tensor_scalar(out=mu[:], in0=mu[:], scalar1=-1.0 / 64, scalar2=None, op0=ALU.mult),
                     r=[mu], w=[mu])
                S.op("dve", lambda e: e.tensor_tensor(out=kif[:], in0=kiwi[:, :, 0:64], in1=mu[:].unsqueeze(2).to_broadcast([128, NT, 64]),
                                                       op=ALU.add), r=[kiwi, mu], w=[kif])
                S.op("dve", lambda e: e.tensor_tensor(out=ckv[:, :, 0:64], in0=kif[:], in1=kif[:], op=ALU.mult), r=[kif], w=[ckv])
                S.op("dve", lambda e: e.tensor_reduce(out=ss[:], in_=ckv[:, :, 0:64], axis=AX.X, op=ALU.add), r=[ckv], w=[ss])
                S.op("act", lambda e: e.activation(out=rs[:], in_=ss[:], func=AF.Sqrt, scale=1.0 / 64, bias=self.epsln[:, 0:1]),
                     r=[ss, self.epsln], w=[rs])
                S.op("dve", lambda e: e.reciprocal(out=rs[:], in_=rs[:]), r=[rs], w=[rs])
                S.op("dve", lambda e: e.tensor_tensor(out=kif[:], in0=kif[:], in1=rs[:].unsqueeze(2).to_broadcast([128, NT, 64]),
                                                       op=ALU.mult), r=[kif, rs], w=[kif])
                S.op("dve", lambda e: e.tensor_tensor(out=kif[:], in0=kif[:], in1=lng[:].unsqueeze(1).to_broadcast([128, NT, 64]),
                                                       op=ALU.mult), r=[kif, lng], w=[kif])
                for hf in range(2):
                    S.op("dve", lambda e: e.tensor_tensor(out=kin[:, :, hf * 64:(hf + 1) * 64], in0=kif[:],
                                                           in1=lnb[:].unsqueeze(1).to_broadcast([128, NT, 64]), op=ALU.add),
                         r=[kif, lnb], w=[kin])
                for g in range(NT // 8):
                    pb = pIb[g % 2]
                    for t8 in range(8):
                        S.op("pe", lambda e: e.transpose(out=pb[:, t8, :], in_=kin[:, g * 8 + t8, :], identity=self.identb[:]),
                             r=[kin, self.identb], w=[pI[g % 2]])
                    S.op("act", lambda e: e.activation(out=kiT2[:, g * 1024:(g + 1) * 1024].rearrange("p (t q) -> p t q", t=8),
                                                       in_=pb, func=AF.Copy), r=[pI[g % 2]], w=[kiT2])
                S.op("act", lambda e: e.activation(out=absw[:], in_=kiwi[:, :, 64:72], func=AF.Abs, scale=IDX_WEIGHT_SCALE),
                     r=[kiwi], w=[absw])
                S.op("dve", lambda e: e.tensor_scalar(out=sgn[:], in0=kiwi[:, :, 64:72], scalar1=0.0, scalar2=2.0,
                                                       op0=ALU.is_ge, op1=ALU.mult), r=[kiwi], w=[sgn])
                S.op("dve", lambda e: e.tensor_scalar(out=sgn[:], in0=sgn[:], scalar1=-1.0, scalar2=None, op0=ALU.add),
                     r=[sgn], w=[sgn])
                S.barrier()
            Isc = [sb("d_Isc%d" % i, [128, T], F32) for i in range(2)]
            work = sb("d_work", [128, T], F32)
            negm = [sb("d_negm%d" % i, [128, T], BF16) for i in range(2)]
            nmT = [sb("d_nmT%d" % i, [128, NT, 128], BF16) for i in range(2)]
            qit = [sb("d_qit%d" % i, [128, 4, 128], BF16) for i in range(2)]
            qct = [sb("d_qct%d" % i, [128, 4, 128], BF16) for i in range(2)]
            qlT = [sb("d_qlT%d" % i, [128, 2, 8, 128], BF16) for i in range(3)]
            Dsg = sb("d_Dsg", [128, 8, 128], BF16)
            Ph = [sb("d_Ph%d" % i, [128, 512], BF16) for i in range(2)]
            PT = [sb("d_PT%d" % i, [128, 4, 128], BF16) for i in range(2)]
            rdn = sb("d_rdn", [128, 512], F32)
            OTn = sb("d_OTn", [128, 2, 4, 128], BF16)
            oc = [sb("d_oc%d" % i, [128, 512], BF16) for i in range(2)]
            m8 = sb("d_m8", [128, 8], F32)
            st = sb("d_st", [128, 4], F32)
            if dstop <= 1:
                return
            qisrc = X["QI"].rearrange("(a p) t -> p a t", p=128)
            qcsrc = X["QC"].rearrange("(a p) t -> p a t", p=128)
            nS = [0]

            def stage1(qt):
                L, b = (qt + 1) * 128, qt % 2
                ts = slice(qt * 128, (qt + 1) * 128)
                S.dma("sp", qit[b][:], qisrc[:, :, ts], r=["QI"], w=[qit[b]])
                S.dma("sp", qct[b][:], qcsrc[:, :, ts], r=["QC"], w=[qct[b]])
                if qt >= 2:
                    S.op("pool", lambda e: e.tensor_tensor(out=Dsg[:], in0=self.identb[:].unsqueeze(1).to_broadcast([128, 8, 128]),
                                                           in1=sgn[:, qt, :].unsqueeze(2).to_broadcast([128, 8, 128]), op=ALU.mult),
                         r=[self.identb, sgn], w=[Dsg])
                    for kg in range((L + 511) // 512):
                        w_ = min(512, L - 512 * kg)
                        for h in range(8):
                            hb, a_ = 64 * (h % 2), h // 2
                            pi, ph = pI[h % 2], Ph[h % 2]
                            S.op("pe", lambda e: e.matmul(pi[:, 0:w_], qit[b][hb:hb + 64, a_, :], kiT2[hb:hb + 64, kg * 512:kg * 512 + w_],
                                                          start=True, stop=True), r=[qit[b], kiT2], w=[pi])
                            S.op("act", lambda e: e.activation(out=ph[:, 0:w_], in_=pi[:, 0:w_], func=AF.Relu,
                                                               scale=absw[:, qt, h:h + 1]), r=[pi, absw], w=[ph])
                            S.op("pe", lambda e: e.matmul(pAcc[:, 0:w_], Dsg[:, h, :], ph[:, 0:w_], start=(h == 0), stop=(h == 7)),
                                 r=[Dsg, ph], w=[pAcc])
                        S.op("act", lambda e: e.activation(out=Isc[b][:, kg * 512:kg * 512 + w_], in_=pAcc[:, 0:w_], func=AF.Copy),
                             r=[pAcc], w=[Isc[b]])
                for rc in range(2):
                    for hq in range(2):
                        pq = pI[(rc * 2 + hq) % 2]
                        for hh in range(4):
                            h = hq * 4 + hh
                            S.op("pe", lambda e: e.matmul(pq[:, hh * 128:(hh + 1) * 128], WBD[:, h // 2, h % 2, rc * 128:(rc + 1) * 128],
                                                          qct[b][:, h // 2, :], start=True, stop=True), r=[WBD, qct[b]], w=[pq])
                        S.op("act", lambda e: e.activation(out=qlT[qt % 3][:, rc, hq * 4:(hq + 1) * 4, :],
                                                           in_=pq[:].rearrange("p (h q) -> p h q", h=4), func=AF.Copy), r=[pq], w=[qlT[qt % 3]])

            def stage2(qt):
                L, b = (qt + 1) * 128, qt % 2
                if qt < 2:
                    return
                I_ = Isc[b]
                S.op("dve", lambda e: e.tensor_reduce(out=st[:, 0:1], in_=I_[:, 0:L], axis=AX.X, op=ALU.min), r=[I_], w=[st])
                S.op("dve", lambda e: e.tensor_scalar(out=st[:, 1:2], in0=st[:, 0:1], scalar1=-1.0, scalar2=1.0,
                                                       op0=ALU.mult, op1=ALU.add), r=[st], w=[st])
                S.op("dve", lambda e: e.tensor_scalar(out=I_[:, 0:L], in0=I_[:, 0:L], scalar1=st[:, 1:2], scalar2=None,
                                                       op0=ALU.add), r=[I_, st], w=[I_])
                S.op("dve", lambda e: e.tensor_tensor(out=I_[:, L - 128:L], in0=I_[:, L - 128:L], in1=self.ltri[:], op=ALU.mult),
                     r=[I_, self.ltri], w=[I_])
                for r_ in range(32):
                    src_ = I_ if r_ == 0 else work
                    S.op("dve", lambda e: e.max(out=m8[:], in_=src_[:, 0:L]), r=[src_], w=[m8])
                    if r_ < 31:
                        S.op("dve", lambda e: e.scalar_tensor_tensor(out=work[:, 0:L], in0=src_[:, 0:L], scalar=m8[:, 7:8],
                                                                     in1=src_[:, 0:L], op0=ALU.is_lt, op1=ALU.mult),
                             r=[src_, m8], w=[work])
                S.op("dve", lambda e: e.tensor_scalar(out=negm[b][:, 0:L], in0=I_[:, 0:L], scalar1=m8[:, 7:8], scalar2=1.0,
                                                       op0=ALU.is_ge, op1=ALU.subtract), r=[I_, m8], w=[negm[b]])

            def stage3(qt):
                nk, b = qt + 1, qt % 2
                ts = slice(qt * 128, (qt + 1) * 128)
                masked = qt >= 2
                if masked:
                    for g in range((nk + 7) // 8):
                        n8 = min(8, nk - 8 * g)
                        pb = pIb[g % 2]
                        for t8 in range(n8):
                            kt = g * 8 + t8
                            S.op("pe", lambda e: e.transpose(out=pb[:, t8, :], in_=negm[b][:, kt * 128:(kt + 1) * 128],
                                                             identity=self.identb[:]), r=[negm[b], self.identb], w=[pI[g % 2]])
                        S.op("act", lambda e: e.activation(out=nmT[b][:, g * 8:g * 8 + n8, :], in_=pb[:, 0:n8, :], func=AF.Copy, scale=-NEG),
                             r=[pI[g % 2]], w=[nmT[b]])
                def emit_S(half, kt):
                    hs = slice(4 * half, 4 * half + 4)
                    m = kt - qt
                    off = 512 - 128 * m if m >= -1 else 768
                    p_, pt = pS[nS[0] % 2], PT[nS[0] % 2]
                    nS[0] += 1
                    for rc in range(2):
                        S.op("pe", lambda e: e.matmul(p_[:], cT[:, rc, kt * 128:(kt + 1) * 128], qlT[qt % 3][:, rc, hs, :],
                                                      start=(rc == 0), stop=False), r=[cT, qlT[qt % 3]], w=[p_])
                    S.op("pe", lambda e: e.matmul(p_[:], self.identb[:], toep[:, hs, off:off + 128], start=False, stop=not masked),
                         r=[self.identb, toep], w=[p_])
                    if masked:
                        for hh in range(4):
                            S.op("pe", lambda e: e.matmul(p_[:, hh, :], self.identb[:], nmT[b][:, kt, :], start=False, stop=(hh == 3)),
                                 r=[self.identb, nmT[b]], w=[p_])
                    return p_, pt

                def emit_PV(half, kt, pt):
                    for rc in range(2):
                        S.op("pe", lambda e: e.matmul(pOT[rc][:], c[:, kt, rc * 128:(rc + 1) * 128], pt[:].rearrange("p h q -> p (h q)"),
                                                      start=(kt == 0), stop=(kt == nk - 1)), r=[c, pt], w=[pOT[rc]])
                    S.op("pe", lambda e: e.matmul(pDen[:], self.onesb[:], pt[:].rearrange("p h q -> p (h q)"),
                                                  start=(kt == 0), stop=(kt == nk - 1)), r=[self.onesb, pt], w=[pDen])

                def finalize(half):
                    S.op("dve", lambda e: e.reciprocal(out=rdn[:], in_=pDen[:]), r=[pDen], w=[rdn])
                    for rc in range(2):
                        S.op("dve", lambda e: e.tensor_tensor(out=OTn[:, rc, :, :].rearrange("p h q -> p (h q)"), in0=pOT[rc][:], in1=rdn[:],
                                                               op=ALU.mult), r=[pOT[rc], rdn], w=[OTn])
                    for hh in range(4):
                        h = 4 * half + hh
                        for rc in range(2):
                            S.op("pe", lambda e: e.matmul(pAcc[:, hh * 64:(hh + 1) * 64], OTn[:, rc, hh, :], wuvb[:, rc, h * 64:(h + 1) * 64],
                                                          start=(rc == 0), stop=(rc == 1)), r=[OTn, wuvb], w=[pAcc])
                    S.op("act", lambda e: e.activation(out=oc[b][:, half * 256:(half + 1) * 256], in_=pAcc[:, 0:256], func=AF.Copy),
                         r=[pAcc], w=[oc[b]])

                steps = [(half, kt) for half in range(2) for kt in range(nk)]
                pending = emit_S(*steps[0])
                for i, (half, kt) in enumerate(steps):
                    p_, pt = pending
                    S.op("act", lambda e: e.activation(out=pt[:], in_=p_[:], func=AF.Exp), r=[p_], w=[pt])
                    if i + 1 < len(steps):
                        pending = emit_S(*steps[i + 1])
                    emit_PV(half, kt, pt)
                    if kt == nk - 1:
                        finalize(half)
                S.dma("pool", X["OC"][ts, :], oc[b][:], r=[oc[b]], w=["OC"])

            tiles = [qt for qt in range(NT) if not (dstop <= 2 and qt not in (0, 1, 2, 5))]
            n = len(tiles)
            for i in range(n + 2):
                if i < n:
                    stage1(tiles[i])
                if 0 <= i - 1 < n:
                    stage2(tiles[i - 1])
                if 0 <= i - 2 < n:
                    stage3(tiles[i - 2])

    def phase_c(self, l):
        nc, S, I, X = self.nc, self.S, self.I, self.X
        import os
        cstop = int(os.environ.get("CSTOP", "99"))
        QS = 128 ** -0.5
        UTc, UTs, BON, H0, H1 = (self.gdc[:, i, :] for i in range(5))
        with ExitStack() as es:
            sb = lambda n, s, d: es.enter_context(nc.sbuf_tensor("%s_L%d" % (n, l), s, d))
            ps = lambda n, s, d: es.enter_context(nc.psum_tensor("%s_L%d" % (n, l), s, d))
            cw = sb("c_cw", [128, 12, 4], F32)
            xp = [sb("c_xp%d" % i, [128, T + 3], F32) for i in range(2)]
            y = [sb("c_y%d" % i, [128, T], F32) for i in range(2)]
            sq = sb("c_sq", [128, T], F32)
            rn = sb("c_rn", [128, T], F32)
            pn = [ps("c_pn%d" % i, [128, 512], F32) for i in range(2)]
            cwr = sb("c_cwr", [4, 1536], F32)
            S.dma("sp", cwr[:], I["conv_w"][l], w=[cwr])
            for c_ in range(12):
                S.op("pe", lambda e: e.transpose(out=pn[0][:, c_ * 4:(c_ + 1) * 4], in_=cwr[:, c_ * 128:(c_ + 1) * 128],
                                                 identity=self.ident[0:4, 0:4]), r=[cwr, self.ident], w=[pn[0]])
            S.op("dve", lambda e: e.tensor_copy(out=cw[:].rearrange("p c j -> p (c j)"), in_=pn[0][:, 0:48]), r=[pn[0]], w=[cw])
            for i in range(2):
                S.op("pool", lambda e: e.memset(xp[i][:, 0:3], 0.0), w=[xp[i]])
            def load_chunk(c_):
                S.dma("sp", xp[c_ % 2][:, 3:], X["QKVB"][c_ * 128:(c_ + 1) * 128, :], r=[("QKVB", c_)], w=[xp[c_ % 2]])

            load_chunk(0)
            for c_ in range(12):
                b = c_ % 2
                if c_ + 1 < 12:
                    load_chunk(c_ + 1)
                S.op("dve", lambda e: e.tensor_scalar(out=y[b][:], in0=xp[b][:, 0:T], scalar1=cw[:, c_, 0:1], scalar2=None,
                                                       op0=ALU.mult), r=[xp[b], cw], w=[y[b]])
                for j in range(1, 4):
                    S.op("dve", lambda e: e.scalar_tensor_tensor(out=y[b][:], in0=xp[b][:, j:j + T], scalar=cw[:, c_, j:j + 1],
                                                                 in1=y[b][:], op0=ALU.mult, op1=ALU.add), r=[xp[b], cw, y[b]], w=[y[b]])
                S.op("act", lambda e: e.activation(out=y[b][:], in_=y[b][:], func=AF.Silu), r=[y[b]], w=[y[b]])
                if c_ < 8:
                    S.op("pool", lambda e: e.tensor_tensor(out=sq[:], in0=y[b][:], in1=y[b][:], op=ALU.mult), r=[y[b]], w=[sq])
                    for g in range(8):
                        p_ = pn[g % 2]
                        S.op("pe", lambda e: e.matmul(p_[:], self.onesf[:], sq[:, g * 512:(g + 1) * 512], start=True, stop=True),
                             r=[self.onesf, sq], w=[p_])
                        S.op("act", lambda e: e.activation(out=rn[:, g * 512:(g + 1) * 512], in_=p_[:], func=AF.Sqrt,
                                                           bias=self.epsln[:, 1:2]), r=[p_, self.epsln], w=[rn])
                    S.op("dve", lambda e: e.reciprocal(out=rn[:], in_=rn[:]), r=[rn], w=[rn])
                    S.op("dve", lambda e: e.tensor_tensor(out=y[b][:], in0=y[b][:], in1=rn[:], op=ALU.mult), r=[y[b], rn], w=[y[b]])
                S.dma("sp", X["QKVB"][c_ * 128:(c_ + 1) * 128, :], y[b][:], r=[y[b]], w=[("QKVB", c_)])
        S.barrier()
        if cstop <= 1:
            return
        with ExitStack() as es:
            sb = lambda n, s, d: es.enter_context(nc.sbuf_tensor("%s_L%d" % (n, l), s, d))
            ps = lambda n, s, d: es.enter_context(nc.psum_tensor("%s_L%d" % (n, l), s, d))
            ab = sb("c_ab", [128, NT, 8], F32)
            dtb = sb("c_dtb", [128, 4], F32)
            nea = sb("c_nea", [128, 4], F32)
            one1 = sb("c_one1", [128, 1], F32)
            LA = sb("c_LA", [128, NT, 4], F32)
            nbeta = sb("c_nbeta", [128, NT, 4], F32)
            beta = sb("c_beta", [128, NT, 4], F32)
            G = sb("c_G", [128, 128], F32)
            EGn = sb("c_EGn", [128, 128], F32)
            KD = sb("c_KD", [128, 128], F32)
            EGL = [sb("c_EGL%d" % j, [128, 128], F32) for j in range(2)]
            gn = sb("c_gn", [128, 128], F32)
            qkv = [sb("c_qkv%d" % i, [128, 12, 128], F32) for i in range(2)]
            zt = [sb("c_zt%d" % i, [128, 512], F32) for i in range(2)]
            vt = sb("c_vt", [128, 4, 128], F32)
            kd = sb("c_kd", [128, 4, 128], F32)
            O = sb("c_O", [128, 4, 128], F32)
            ob = [sb("c_ob%d" % i, [128, 512], BF16) for i in range(2)]
            ss = sb("c_ss", [128, 4], F32)
            junk = sb("c_junk", [128, 128], F32)
            Sh = [sb("c_S%d" % h, [128, 128], F32) for h in range(4)]
            Dg = [sb("c_Dg%d" % h, [128, 128], F32) for h in range(4)]
            t1 = [sb("c_t1%d" % h, [128, 128], F32) for h in range(4)]
            decT = [sb("c_dec%d" % h, [128, 128], F32) for h in range(4)]
            EGb = [sb("c_EGb%d" % h, [128, 128], F32) for h in range(4)]
            qeg = [sb("c_qeg%d" % h, [128, 128], F32) for h in range(4)]
            aT = [sb("c_aT%d" % h, [128, 128], F32) for h in range(4)]
            AT = [[sb("c_AT%d_%d" % (h, i), [128, 128], F32) for i in range(2)] for h in range(4)]
            A = [[sb("c_A%d_%d" % (h, i), [128, 128], F32) for i in range(2)] for h in range(4)]
            Xh = [sb("c_X%d" % h, [128, 128], F32) for h in range(4)]
            R = [sb("c_R%d" % h, [128, 128], F32) for h in range(4)]
            vn = [sb("c_vn%d" % h, [128, 128], F32) for h in range(4)]
            class Sub:
                def __init__(self, tile, i):
                    self.ap = tile[:, i, :]
                    self.name = tile.name

                def __getitem__(self, k):
                    return self.ap[k]

            ppb = [ps("c_pp%d" % i, [128, 4, 128], F32) for i in range(4)]
            pp = [Sub(ppb[i % 4], i // 4) for i in range(16)]
            pdb = [ps("c_pd%d" % i, [128, 128], F32) for i in range(2)]
            ptv = ps("c_ptv", [128, 4, 128], F32)
            ptk = ps("c_ptk", [128, 4, 128], F32)
            npp = [0]

            def P():
                npp[0] += 1
                return pp[npp[0] % 16]

            S.dma("sp", ab[:], X["AB"].rearrange("(tt p) c -> p tt c", p=128), r=["AB"], w=[ab])
            S.dma("sp", dtb[:], I["dt_bias"][l:l + 1, :].partition_broadcast(128), w=[dtb])
            S.dma("sp", nea[:], I["a_log"][l:l + 1, :].partition_broadcast(128), w=[nea])
            S.dma("sp", gn[:], I["gdn_norm"][l:l + 1, :].partition_broadcast(128), w=[gn])
            S.op("pool", lambda e: e.memset(one1[:], 1.0), w=[one1])
            for h in range(4):
                S.op("pool", lambda e: e.memset(Sh[h][:], 0.0), w=[Sh[h]])
                S.op("pool", lambda e: e.memset(vn[h][:], 0.0), w=[vn[h]])
                S.op("pool", lambda e: e.memset(R[h][:], 0.0), w=[R[h]])
            S.op("act", lambda e: e.activation(out=nea[:], in_=nea[:], func=AF.Exp), r=[nea], w=[nea])
            S.op("dve", lambda e: e.tensor_scalar(out=nea[:], in0=nea[:], scalar1=-1.0, scalar2=None, op0=ALU.mult), r=[nea], w=[nea])
            S.op("dve", lambda e: e.tensor_tensor(out=LA[:], in0=ab[:, :, 0:4], in1=dtb[:].unsqueeze(1).to_broadcast([128, NT, 4]),
                                                   op=ALU.add), r=[ab, dtb], w=[LA])
            S.op("act", lambda e: e.activation(out=LA[:], in_=LA[:], func=AF.Exp), r=[LA], w=[LA])
            S.op("act", lambda e: e.activation(out=LA[:], in_=LA[:], func=AF.Ln, bias=one1[:, 0:1]), r=[LA, one1], w=[LA])
            S.op("dve", lambda e: e.tensor_tensor(out=LA[:], in0=LA[:], in1=nea[:].unsqueeze(1).to_broadcast([128, NT, 4]),
                                                   op=ALU.mult), r=[LA, nea], w=[LA])
            S.op("act", lambda e: e.activation(out=beta[:], in_=ab[:, :, 4:8], func=AF.Sigmoid), r=[ab], w=[beta])
            S.op("dve", lambda e: e.tensor_scalar(out=nbeta[:], in0=beta[:], scalar1=-1.0, scalar2=None, op0=ALU.mult),
                 r=[beta], w=[nbeta])
            LA2 = LA[:].rearrange("p t h -> p (t h)")
            p1, p2, p3, p4 = P(), P(), P(), P()
            S.op("pe", lambda e: e.matmul(p1[:], UTc, LA2, start=True, stop=True), r=[self.gdc, LA], w=[p1])
            S.op("pe", lambda e: e.matmul(p2[:], BON, LA2, start=True, stop=True), r=[self.gdc, LA], w=[p2])
            S.op("pe", lambda e: e.matmul(p3[:], H0, LA2, start=True, stop=True), r=[self.gdc, LA], w=[p3])
            S.op("pe", lambda e: e.matmul(p4[:], H1, LA2, start=True, stop=True), r=[self.gdc, LA], w=[p4])
            S.op("dve", lambda e: e.tensor_copy(out=G[:], in_=p1[:]), r=[p1], w=[G])
            S.op("act", lambda e: e.activation(out=EGn[:], in_=p1[:], func=AF.Exp), r=[p1], w=[EGn])
            S.op("dve", lambda e: e.tensor_scalar(out=EGn[:], in0=EGn[:], scalar1=-1.0, scalar2=None, op0=ALU.mult), r=[EGn], w=[EGn])
            S.op("dve", lambda e: e.tensor_tensor(out=KD[:], in0=p2[:], in1=G[:], op=ALU.subtract), r=[p2, G], w=[KD])
            S.op("act", lambda e: e.activation(out=KD[:], in_=KD[:], func=AF.Exp), r=[KD], w=[KD])
            S.op("act", lambda e: e.activation(out=EGL[0][:], in_=p3[:], func=AF.Exp), r=[p3], w=[EGL[0]])
            S.op("act", lambda e: e.activation(out=EGL[1][:], in_=p4[:], func=AF.Exp), r=[p4], w=[EGL[1]])
            qsrc = X["QKVB"].rearrange("(c p) t -> p c t", p=128)
            for tt in range(NT):
                if tt >= int(os.environ.get("CTILES", "32")) or (cstop <= 2 and tt >= 2):
                    break
                b = tt % 2
                ts = slice(tt * 128, (tt + 1) * 128)
                S.dma("sp", qkv[b][:], qsrc[:, :, ts], r=["QKVB"], w=[qkv[b]])
                S.dma("sp", zt[b][:], X["ZB"][ts, :], r=["ZB"], w=[zt[b]])
                for h in range(4):
                    S.op("pe", lambda e: e.transpose(out=ptv[:, h, :], in_=qkv[b][:, 8 + h, :], identity=self.ident[:]),
                         r=[qkv[b], self.ident], w=[ptv])
                    S.op("pe", lambda e: e.transpose(out=ptk[:, h, :], in_=qkv[b][:, 4 + h, :], identity=self.ident[:]),
                         r=[qkv[b], self.ident], w=[ptk])
                S.op("act", lambda e: e.activation(out=vt[:], in_=ptv[:], func=AF.Copy), r=[ptv], w=[vt])
                for h in range(4):
                    col = tt * 4 + h
                    S.op("dve", lambda e: e.tensor_scalar(out=kd[:, h, :], in0=ptk[:, h, :], scalar1=KD[:, col:col + 1], scalar2=None,
                                                           op0=ALU.mult), r=[ptk, KD], w=[kd])
                H4 = range(4)
                gcol = [G[:, tt * 4 + h:tt * 4 + h + 1] for h in H4]
                kTt = [qkv[b][:, 4 + h, :] for h in H4]
                qTt = [qkv[b][:, h, :] for h in H4]
                for h in H4:
                    S.op("pool", lambda e: e.tensor_scalar(out=Dg[h][:], in0=self.ident[:], scalar1=gcol[h], scalar2=None, op0=ALU.mult),
                         r=[self.ident, G], w=[Dg[h]])
                pg_ = [P() for _ in H4]
                for h in H4:
                    S.op("pe", lambda e: e.matmul(pg_[h][:], self.onesf[:], Dg[h][:], start=True, stop=True), r=[self.onesf, Dg[h]], w=[pg_[h]])
                for h in H4:
                    S.op("dve", lambda e: e.tensor_scalar(out=t1[h][:], in0=pg_[h][:], scalar1=gcol[h], scalar2=0.0, op0=ALU.subtract, op1=ALU.min),
                         r=[pg_[h], G], w=[t1[h]])
                for h in H4:
                    S.op("act", lambda e: e.activation(out=decT[h][:], in_=t1[h][:], func=AF.Exp), r=[t1[h]], w=[decT[h]])
                    S.op("act", lambda e: e.activation(out=EGb[h][:], in_=pg_[h][:], func=AF.Exp), r=[pg_[h]], w=[EGb[h]])
                for h in H4:
                    S.op("dve", lambda e: e.scalar_tensor_tensor(out=qeg[h][:], in0=qTt[h], scalar=QS, in1=EGb[h][:], op0=ALU.mult, op1=ALU.mult),
                         r=[qkv[b], EGb[h]], w=[qeg[h]])
                pkk = [P() for _ in H4]
                pqk = [P() for _ in H4]
                for h in H4:
                    S.op("pe", lambda e: e.matmul(pkk[h][:], kTt[h], kTt[h], start=True, stop=True), r=[qkv[b]], w=[pkk[h]])
                    S.op("pe", lambda e: e.matmul(pqk[h][:], kTt[h], qTt[h], start=True, stop=True), r=[qkv[b]], w=[pqk[h]])
                for h in H4:
                    S.op("dve", lambda e: e.scalar_tensor_tensor(out=AT[h][0][:], in0=pkk[h][:], scalar=nbeta[:, tt, h:h + 1], in1=decT[h][:],
                                                                 op0=ALU.mult, op1=ALU.mult), r=[pkk[h], nbeta, decT[h]], w=[AT[h][0]])
                    S.op("dve", lambda e: e.scalar_tensor_tensor(out=aT[h][:], in0=pqk[h][:], scalar=QS, in1=decT[h][:],
                                                                 op0=ALU.mult, op1=ALU.mult), r=[pqk[h], decT[h]], w=[aT[h]])
                for h in H4:
                    S.op("pool", lambda e: e.tensor_tensor(out=AT[h][0][:], in0=AT[h][0][:], in1=UTs, op=ALU.mult),
                         r=[AT[h][0], self.gdc], w=[AT[h][0]])
                    S.op("pool", lambda e: e.tensor_tensor(out=aT[h][:], in0=aT[h][:], in1=UTc, op=ALU.mult), r=[aT[h], self.gdc], w=[aT[h]])
                pt_ = [P() for _ in H4]
                for h in H4:
                    S.op("pe", lambda e: e.transpose(out=pt_[h][:], in_=AT[h][0][:], identity=self.ident[:]), r=[AT[h][0], self.ident], w=[pt_[h]])
                for h in H4:
                    S.op("act", lambda e: e.activation(out=A[h][0][:], in_=pt_[h][:], func=AF.Copy), r=[pt_[h]], w=[A[h][0]])
                    S.op("pool", lambda e: e.tensor_tensor(out=Xh[h][:], in0=AT[h][0][:], in1=self.ident[:], op=ALU.add),
                         r=[AT[h][0], self.ident], w=[Xh[h]])
                for k in range(5):
                    cu, nx = k % 2, (k + 1) % 2
                    for h in range(4):
                        pa = P()
                        S.op("pe", lambda e: e.matmul(pa[:], AT[h][cu][:], A[h][cu][:], start=True, stop=True), r=[AT[h][cu], A[h][cu]], w=[pa])
                        S.op("act", lambda e: e.activation(out=A[h][nx][:], in_=pa[:], func=AF.Copy), r=[pa], w=[A[h][nx]])
                        if k < 4:
                            pb_ = P()
                            S.op("pe", lambda e: e.matmul(pb_[:], A[h][cu][:], AT[h][cu][:], start=True, stop=True),
                                 r=[AT[h][cu], A[h][cu]], w=[pb_])
                            S.op("dve", lambda e: e.tensor_copy(out=AT[h][nx][:], in_=pb_[:]), r=[pb_], w=[AT[h][nx]])
                    for h in range(4):
                        px = P()
                        S.op("pe", lambda e: e.matmul(px[:], A[h][nx][:], Xh[h][:], start=True, stop=True), r=[A[h][nx], Xh[h]], w=[px])
                        S.op("dve", lambda e: e.tensor_tensor(out=Xh[h][:], in0=px[:], in1=Xh[h][:], op=ALU.add), r=[px, Xh[h]], w=[Xh[h]])
                for j in range(2):
                    rs = slice(64 * j, 64 * j + 64)
                    pk_ = [P() for _ in range(4)]
                    for h in range(4):
                        S.op("pe", lambda e: e.matmul(pk_[h][:], qkv[b][:, 4 + h, :], Sh[h][:], start=True, stop=True),
                             r=[qkv[b], Sh[h]], w=[pk_[h]])
                    for h in range(4):
                        col = tt * 4 + h
                        S.op("dve", lambda e: e.scalar_tensor_tensor(out=R[h][rs, :], in0=pk_[h][rs, :], scalar=EGn[rs, col:col + 1],
                                                                     in1=vt[rs, h, :], op0=ALU.mult, op1=ALU.add),
                             r=[pk_[h], EGn, vt], w=[R[h]])
                    py_ = [P() for _ in range(4)]
                    for h in range(4):
                        S.op("pe", lambda e: e.matmul(py_[h][:], Xh[h][:], R[h][:], start=True, stop=True), r=[Xh[h], R[h]], w=[py_[h]])
                    for h in range(4):
                        S.op("dve", lambda e: e.tensor_scalar(out=vn[h][rs, :], in0=py_[h][rs, :], scalar1=beta[rs, tt, h:h + 1], scalar2=None,
                                                               op0=ALU.mult), r=[py_[h], beta], w=[vn[h]])
                    po_ = [P() for _ in range(4)]
                    for h in range(4):
                        S.op("pe", lambda e: e.matmul(po_[h][:], qeg[h][:], Sh[h][:], start=True, stop=False), r=[qeg[h], Sh[h]], w=[po_[h]])
                        S.op("pe", lambda e: e.matmul(po_[h][:], aT[h][:], vn[h][:], start=False, stop=True), r=[aT[h], vn[h]], w=[po_[h]])
                    for h in range(4):
                        S.op("act", lambda e: e.activation(out=O[rs, h, :], in_=po_[h][rs, :], func=AF.Copy), r=[po_[h]], w=[O])
                    pd_ = [pdb[h % 2] for h in range(4)]
                    for h in range(4):
                        S.op("pe", lambda e: e.matmul(pd_[h][:], kd[rs, h, :], vn[h][rs, :], start=True, stop=True), r=[kd, vn[h]], w=[pd_[h]])
                        col = tt * 4 + h
                        S.op("dve", lambda e: e.scalar_tensor_tensor(out=Sh[h][:], in0=Sh[h][:], scalar=EGL[j][:, col:col + 1], in1=pd_[h][:],
                                                                     op0=ALU.mult, op1=ALU.add), r=[Sh[h], EGL[j], pd_[h]], w=[Sh[h]])
                for h in range(4):
                    S.op("act", lambda e: e.activation(out=junk[:], in_=O[:, h, :], func=AF.Square, accum_out=ss[:, h:h + 1]),
                         r=[O], w=[junk, ss])
                S.op("act", lambda e: e.activation(out=ss[:], in_=ss[:], func=AF.Sqrt, scale=1.0 / 128, bias=self.epsln[:, 1:2]),
                     r=[ss, self.epsln], w=[ss])
                S.op("dve", lambda e: e.reciprocal(out=ss[:], in_=ss[:]), r=[ss], w=[ss])
                S.op("dve", lambda e: e.tensor_tensor(out=O[:], in0=O[:], in1=ss[:].unsqueeze(2).to_broadcast([128, 4, 128]), op=ALU.mult),
                     r=[O, ss], w=[O])
                S.op("dve", lambda e: e.tensor_tensor(out=O[:], in0=O[:], in1=gn[:].unsqueeze(1).to_broadcast([128, 4, 128]), op=ALU.mult),
                     r=[O, gn], w=[O])
                S.op("act", lambda e: e.activation(out=zt[b][:], in_=zt[b][:], func=AF.Silu), r=[zt[b]], w=[zt[b]])
                S.op("dve", lambda e: e.tensor_tensor(out=ob[b][:], in0=O[:].rearrange("p h e -> p (h e)"), in1=zt[b][:], op=ALU.mult),
                     r=[O, zt[b]], w=[ob[b]])
                S.dma("pool", X["OB"][ts, :], ob[b][:], r=[ob[b]], w=["OB"])

_CACHE = {}


def _get_prog(nlayers=DEPTH, debug=(), stop_after=None):
    key = (nlayers, tuple(sorted(debug)), stop_after)
    if key not in _CACHE:
        p = Prog(nlayers, debug, stop_after)
        p.build()
        _CACHE[key] = p
    return _CACHE[key]


def make_in_maps(inputs, ncores=8, nlayers=DEPTH, layer0=0, xs=None):
    consts = host_constants(np.asarray(inputs["rel_bias"], np.float32))
    shared = {}
    for k, v in inputs.items():
        if k in ("x", "rel_bias"):
            continue
        a = np.ascontiguousarray(np.asarray(v, np.float32)[layer0:layer0 + nlayers])
        if k in ("w_uk", "w_uv"):
            a = a.reshape(nlayers, 256, 512)
        shared[k] = a
    shared.update(consts)
    x = np.asarray(inputs["x"], np.float32) if xs is None else xs
    maps = []
    for c in range(ncores):
        m = dict(shared)
        m["x"] = np.ascontiguousarray(x[c])
        maps.append(m)
    return maps


def kernel(**inputs):
    p = _get_prog(DEPTH)
    maps = make_in_maps(inputs, 8, DEPTH)
    res = run_bass_kernel_spmd(p.nc, maps, core_ids=list(range(8)))
    return np.stack([np.asarray(r["out"], np.float32) for r in res.results], axis=0)
```

```python
from contextlib import ExitStack
import math
import numpy as np
import ml_dtypes
import concourse.bass as bass
import concourse.mybir as mybir
from concourse.bass_utils import run_bass_kernel_spmd

F32 = mybir.dt.float32
BF16 = mybir.dt.bfloat16
AF = mybir.ActivationFunctionType
ALU = mybir.AluOpType
AX = mybir.AxisListType

T = 4096
D = 1024
NT = T // 128
DEPTH = 4
INW = 8016
DFF = 4096
ALPHA = (2 * DEPTH) ** 0.25
LN_EPS = 1e-5
RMS_EPS = 1e-6
NEG = -30000.0
IDX_WEIGHT_SCALE = 8 ** -0.5 * 64 ** -0.5

C_QA, C_KA, C_VA, C_QB, C_ZB, C_AB, C_QC, C_CKV, C_QI, C_KI, C_WI, C_GA = (
    0, 512, 1024, 1536, 3072, 3584, 3592, 4104, 4360, 4872, 4936, 4944)


class Sched:
    ENGS = ("pe", "act", "dve", "pool", "sp")
    NDS = 48

    def __init__(self, nc, es):
        self.nc = nc
        self.es = es
        self.epoch = 1
        self.eng = {"pe": nc.tensor, "act": nc.scalar, "dve": nc.vector, "pool": nc.gpsimd, "sp": nc.sync}
        self.sem = {e: es.enter_context(nc.semaphore("s_" + e)) for e in self.ENGS}
        self.dsem = [es.enter_context(nc.semaphore("d_%d" % i)) for i in range(self.NDS)]
        self.cnt = {e: 0 for e in self.ENGS}
        self.seen = {e: {} for e in self.ENGS}
        self.lastw = {}
        self.readers = {}
        self.ndma = 0
        self.ndq = {}
        nsp = (3 * self.NDS) // 4
        self.dpool = {"sp": (0, nsp), "pool": (nsp, self.NDS - nsp)}
        self.dlast = {}
        self.ninst = 0
        self.scratch = es.enter_context(nc.sbuf_tensor("sched_scratch", [128, 1], F32))

    @staticmethod
    def _key(a):
        if isinstance(a, (str, tuple)):
            return a
        return a.name

    def _semof(self, sk):
        if isinstance(sk, tuple):
            return self.dsem[sk[1]]
        return self.sem[sk]

    def _deps(self, rk, wk):
        toks = []
        for k in rk:
            t = self.lastw.get(k)
            if t:
                toks.append(t)
        for k in wk:
            t = self.lastw.get(k)
            if t:
                toks.append(t)
            toks.extend(self.readers.get(k, {}).items())
        return toks

    def _wait(self, eng, toks):
        need = {}
        for (sk, v) in toks:
            if sk == "pe" and eng == "pe":
                continue
            if self.seen[eng].get(sk, 0) >= v:
                continue
            if need.get(sk, 0) < v:
                need[sk] = v
        e = self.eng[eng]
        for sk, v in need.items():
            self.seen[eng][sk] = v
            e.wait_ge(self._semof(sk), v)
            self.ninst += 1

    def _record(self, tok, rk, wk):
        for k in rk:
            d = self.readers.setdefault(k, {})
            if d.get(tok[0], 0) < tok[1]:
                d[tok[0]] = tok[1]
        for k in wk:
            self.lastw[k] = tok
            self.readers[k] = {}

    def op(self, eng, fn, r=(), w=()):
        rk = [self._key(a) for a in r]
        wk = [self._key(a) for a in w]
        self._wait(eng, self._deps(rk, wk))
        ins = fn(self.eng[eng])
        self.cnt[eng] += 1
        ins.then_inc(self.sem[eng], 1)
        self.ninst += 1
        tok = (eng, self.cnt[eng])
        self._record(tok, rk, wk)
        return tok

    def dma(self, qeng, out, in_, r=(), w=(), **kw):
        rk = [self._key(a) for a in r]
        wk = [self._key(a) for a in w]
        lo, n = self.dpool[qeng]
        k = self.ndq.get(qeng, 0)
        self.ndq[qeng] = k + 1
        idx = lo + k % n
        val = 16 * (k // n + 1)
        self.ndma += 1
        toks = self._deps(rk, wk)
        if val > 16:
            toks.append((("d", idx), val - 16))
        self._wait(qeng, toks)
        self.eng[qeng].dma_start(out=out, in_=in_, **kw).then_inc(self.dsem[idx], 16)
        self.ninst += 1
        tok = (("d", idx), val)
        self.dlast[idx] = val
        self._record(tok, rk, wk)
        return tok

    def barrier(self, new_epoch=False):
        toks = [(e, self.cnt[e]) for e in self.ENGS if self.cnt[e] > 0]
        toks += [(("d", i), v) for i, v in self.dlast.items()]
        for e in self.ENGS:
            self._wait(e, toks)
        self.lastw.clear()
        self.readers.clear()
        assert max(self.cnt.values()) < 30000, self.cnt
        if new_epoch:
            self.sem = {e: self.es.enter_context(self.nc.semaphore("s%d_%s" % (self.epoch, e))) for e in self.ENGS}
            self.epoch += 1
            self.cnt = {e: 0 for e in self.ENGS}
            for e in self.ENGS:
                for k in self.ENGS:
                    self.seen[e].pop(k, None)


def host_constants(rel_bias):
    def bucket(d):
        d = np.maximum(d, 0)
        large = 16 + (np.log(np.maximum(d, 16).astype(np.float32) / 16) / math.log(128 / 16) * 16).astype(np.int32)
        return np.where(d < 16, d, np.minimum(large, 31))
    j = np.arange(128)[:, None]
    c = np.arange(1280)[None, :]
    dist = c - j - 512
    b = bucket(dist)
    W = np.empty((128, 16, 1280), np.float32)
    for h in range(16):
        W[:, h, :] = np.where(dist >= 0, rel_bias[b, h], NEG)
    ident = np.eye(128, dtype=np.float32)
    ltri = np.tril(np.ones((128, 128), np.float32))
    s_ = np.arange(128)[:, None]
    c_ = np.arange(128)[None, :]
    same = (s_ // 64) == (c_ // 64)
    gd = np.stack([same & (s_ <= c_), same & (s_ < c_), same, np.broadcast_to(s_ < 64, (128, 128)),
                   np.broadcast_to(s_ >= 64, (128, 128))], axis=1).astype(np.float32)
    return {"c_toep": W.astype(ml_dtypes.bfloat16), "c_ident": ident, "c_ltri": ltri, "c_gdn": np.ascontiguousarray(gd)}


class Prog:
    def __init__(self, nlayers=DEPTH, debug=(), stop_after=None, inject=(), phases="ABCDEF"):
        self.nlayers = nlayers
        self.inject = set(inject)
        self.phases = phases
        self.debug = set(debug)
        self.stop_after = stop_after
        self.nc = bass.Bass("TRN2", target_bir_lowering=False)
        self.out_names = []

    def dram_in(self, name, shape, dt=F32):
        return self.nc.dram_tensor(name, list(shape), dt, kind="ExternalInput").ap()

    def dram_scr(self, name, shape, dt=F32):
        if name in self.inject:
            return self.nc.dram_tensor(name, list(shape), dt, kind="ExternalInput").ap()
        if name in self.debug:
            self.out_names.append(name)
            return self.nc.dram_tensor(name, list(shape), dt, kind="ExternalOutput").ap()
        return self.nc.dram_tensor(name, list(shape), dt).ap()

    def build(self):
        nc = self.nc
        L = self.nlayers
        I = {}
        I["x"] = self.dram_in("x", [T, D])
        I["w_in"] = self.dram_in("w_in", [L, D, INW])
        I["conv_w"] = self.dram_in("conv_w", [L, 4, 1536])
        I["a_log"] = self.dram_in("a_log", [L, 4])
        I["dt_bias"] = self.dram_in("dt_bias", [L, 4])
        I["gdn_norm"] = self.dram_in("gdn_norm", [L, 128])
        I["kv_norm"] = self.dram_in("kv_norm", [L, 256])
        I["idx_k_ln_g"] = self.dram_in("idx_k_ln_g", [L, 64])
        I["idx_k_ln_b"] = self.dram_in("idx_k_ln_b", [L, 64])
        I["w_uk"] = self.dram_in("w_uk", [L, 256, 512])
        I["w_uv"] = self.dram_in("w_uv", [L, 256, 512])
        I["w_branch_a"] = self.dram_in("w_branch_a", [L, 512, D])
        I["w_branch_b"] = self.dram_in("w_branch_b", [L, 512, D])
        I["w_branch_c"] = self.dram_in("w_branch_c", [L, 512, D])
        I["w_out"] = self.dram_in("w_out", [L, D, D])
        I["ln1_g"] = self.dram_in("ln1_g", [L, D])
        I["ln1_b"] = self.dram_in("ln1_b", [L, D])
        I["w_up"] = self.dram_in("w_up", [L, D, DFF])
        I["w_down"] = self.dram_in("w_down", [L, DFF, D])
        I["ln2_g"] = self.dram_in("ln2_g", [L, D])
        I["ln2_b"] = self.dram_in("ln2_b", [L, D])
        I["c_toep"] = self.dram_in("c_toep", [128, 16, 1280], BF16)
        I["c_ident"] = self.dram_in("c_ident", [128, 128])
        I["c_ltri"] = self.dram_in("c_ltri", [128, 128])
        I["c_gdn"] = self.dram_in("c_gdn", [128, 5, 128])
        self.I = I
        self.out = nc.dram_tensor("out", [T, D], F32, kind="ExternalOutput").ap()
        X = {}
        X["QA"] = self.dram_scr("QA", [512, T], BF16)
        X["KA"] = self.dram_scr("KA", [512, T], BF16)
        X["QKVB"] = self.dram_scr("QKVB", [1536, T], F32)
        X["QC"] = self.dram_scr("QC", [512, T], BF16)
        X["QI"] = self.dram_scr("QI", [512, T], BF16)
        X["VA"] = self.dram_scr("VA", [T, 520], BF16)
        X["ZB"] = self.dram_scr("ZB", [T, 512], F32)
        X["AB"] = self.dram_scr("AB", [T, 8], F32)
        X["CKV"] = self.dram_scr("CKV", [T, 256], F32)
        X["KIWI"] = self.dram_scr("KIWI", [T, 72], F32)
        X["GATES"] = self.dram_scr("GATES", [T, 3072], F32)
        X["OA"] = self.dram_scr("OA", [T, 512], BF16)
        X["OB"] = self.dram_scr("OB", [T, 512], BF16)
        X["OC"] = self.dram_scr("OC", [T, 512], BF16)
        X["X1"] = self.dram_scr("X1", [T, D], F32)
        X["XS"] = self.dram_scr("XS", [T, D], F32)
        self.X = X

        with ExitStack() as top:
            S = Sched(nc, top)
            self.S = S
            self.ident = top.enter_context(nc.sbuf_tensor("ident", [128, 128], F32))
            self.identb = top.enter_context(nc.sbuf_tensor("identb", [128, 128], BF16))
            self.i30k = top.enter_context(nc.sbuf_tensor("i30k", [128, 128], BF16))
            S.dma("sp", self.ident[:], I["c_ident"], w=[self.ident])
            S.op("dve", lambda e: e.tensor_copy(out=self.identb[:], in_=self.ident[:]), r=[self.ident], w=[self.identb])
            S.op("dve", lambda e: e.tensor_scalar(out=self.i30k[:], in0=self.ident[:], scalar1=-NEG, scalar2=None,
                                                   op0=ALU.mult), r=[self.ident], w=[self.i30k])
            self.ltri = top.enter_context(nc.sbuf_tensor("ltri", [128, 128], F32))
            S.dma("sp", self.ltri[:], I["c_ltri"], w=[self.ltri])
            self.gdc = top.enter_context(nc.sbuf_tensor("gdc", [128, 5, 128], F32))
            S.dma("sp", self.gdc[:], I["c_gdn"], w=[self.gdc])
            self.onesf = top.enter_context(nc.sbuf_tensor("onesf", [128, 128], F32))
            S.op("pool", lambda e: e.memset(self.onesf[:], 1.0), w=[self.onesf])
            self.onesb = top.enter_context(nc.sbuf_tensor("onesb", [128, 128], BF16))
            S.op("pool", lambda e: e.memset(self.onesb[:], 1.0), w=[self.onesb])
            self.epsln = top.enter_context(nc.sbuf_tensor("epsln", [128, 2], F32))
            S.op("pool", lambda e: e.memset(self.epsln[:, 0:1], LN_EPS), w=[self.epsln])
            S.op("pool", lambda e: e.memset(self.epsln[:, 1:2], RMS_EPS), w=[self.epsln])
            for l in range(self.nlayers):
                xin = I["x"] if l == 0 else X["XS"]
                xout = self.out if l == self.nlayers - 1 else X["XS"]
                for ph in "ABCDEF":
                    if ph not in self.phases:
                        continue
                    if ph == "A":
                        self.phase_a(l, xin)
                    elif ph == "B":
                        self.phase_b(l)
                    elif ph == "C":
                        self.phase_c(l)
                    elif ph == "D":
                        self.phase_d(l)
                    elif ph == "E":
                        self.phase_e(l, xin)
                    elif ph == "F":
                        self.phase_f(l, xout)
                    S.barrier(new_epoch=(ph in "CF"))
            S.barrier()
        return nc

    def phase_a(self, l, xin):
        nc, S, I, X = self.nc, self.S, self.I, self.X
        with ExitStack() as es:
            sb = lambda n, s, d: es.enter_context(nc.sbuf_tensor("%s_L%d" % (n, l), s, d))
            ps = lambda n, s, d: es.enter_context(nc.psum_tensor("%s_L%d" % (n, l), s, d))
            xT = sb("a_xT", [128, 8, T], BF16)
            xs = [sb("a_xs%d" % i, [128, D], F32) for i in range(2)]
            xb = [sb("a_xb%d" % i, [128, D], BF16) for i in range(2)]
            wst = [sb("a_wst%d" % i, [128, 8, 512], F32) for i in range(2)]
            wb = [sb("a_wb%d" % i, [128, 8, 512], BF16) for i in range(2)]
            stf = [sb("a_stf%d" % i, [128, T], F32) for i in range(2)]
            stb = [sb("a_stb%d" % i, [128, T], BF16) for i in range(2)]
            ptr = [ps("a_ptr%d" % i, [128, 8, 128], BF16) for i in range(2)]
            pacc = [ps("a_pacc%d" % i, [128, 512], F32) for i in range(4)]
            stv = [sb("a_stv%d" % i, [128, 8, 65], BF16) for i in range(2)]
            for i in range(2):
                S.op("pool", lambda e: e.memset(stv[i][:], 1.0), w=[stv[i]])

            for tt in range(NT):
                b = tt % 2
                S.dma("sp", xs[b][:], xin[tt * 128:(tt + 1) * 128, :], w=[xs[b]])
                if tt % 2 == 0:
                    S.op("act", lambda e: e.activation(out=xb[b][:], in_=xs[b][:], func=AF.Copy), r=[xs[b]], w=[xb[b]])
                else:
                    S.op("dve", lambda e: e.tensor_copy(out=xb[b][:], in_=xs[b][:]), r=[xs[b]], w=[xb[b]])
                for kc in range(8):
                    S.op("pe", lambda e: e.transpose(out=ptr[b][:, kc, :], in_=xb[b][:, kc * 128:(kc + 1) * 128],
                                                     identity=self.identb[:]), r=[xb[b], self.identb], w=[ptr[b]])
                S.op("dve", lambda e: e.tensor_copy(out=xT[:, :, tt * 128:(tt + 1) * 128], in_=ptr[b][:]),
                     r=[ptr[b]], w=[xT])

            wsrc = I["w_in"][l].rearrange("(kc p) n -> p kc n", p=128)
            blocks = [
                ("FM", C_QA, 512, "QA", 0, 0.125), ("FM", C_KA, 512, "KA", 0, 1.0),
                ("FM", C_QB, 512, "QKVB", 0, 1.0), ("FM", C_QB + 512, 512, "QKVB", 512, 1.0),
                ("FM", C_QB + 1024, 512, "QKVB", 1024, 1.0),
                ("FM", C_QC, 512, "QC", 0, 0.125), ("FM", C_QI, 512, "QI", 0, 1.0),
                ("TM", C_VA, 512, "VA", 0, 1.0), ("TM", C_ZB, 512, "ZB", 0, 1.0), ("TM", C_AB, 8, "AB", 0, 1.0),
                ("TM", C_CKV, 256, "CKV", 0, 1.0), ("TM", C_KI, 72, "KIWI", 0, 1.0),
            ] + [("TM", C_GA + 512 * i, 512, "GATES", 512 * i, 1.0) for i in range(6)]
            ev = 0
            na = 0
            nst = 0
            def load_block(bi):
                _, c0_, ncol_, _, _, _ = blocks[bi]
                b_ = bi % 2
                S.dma("sp", wst[b_][:, :, 0:ncol_], wsrc[:, :, c0_:c0_ + ncol_], w=[wst[b_]])
                S.op("pool", lambda e: e.tensor_copy(out=wb[b_][:, :, 0:ncol_], in_=wst[b_][:, :, 0:ncol_]),
                     r=[wst[b_]], w=[wb[b_]])

            load_block(0)
            for bi, (kind, c0, ncol, dname, doff, scale) in enumerate(blocks):
                b = bi % 2
                dst = X[dname]
                isbf = dname in ("QA", "KA", "QC", "QI", "VA")
                if bi + 1 < len(blocks):
                    load_block(bi + 1)
                if kind == "FM":
                    for j in range(ncol // 128):
                        st = (stb if isbf else stf)[nst % 2]
                        nst += 1
                        for tg in range(8):
                            pa = pacc[na % 4]
                            na += 1
                            for kc in range(8):
                                S.op("pe", lambda e: e.matmul(pa[:], wb[b][:, kc, j * 128:(j + 1) * 128],
                                                              xT[:, kc, tg * 512:(tg + 1) * 512],
                                                              start=(kc == 0), stop=(kc == 7)),
                                     r=[wb[b], xT], w=[pa])
                            eng = "act" if ev % 2 == 0 else "dve"
                            ev += 1
                            if eng == "act":
                                S.op("act", lambda e: e.activation(out=st[:, tg * 512:(tg + 1) * 512], in_=pa[:],
                                                                   func=AF.Copy, scale=scale), r=[pa], w=[st])
                            else:
                                S.op("dve", lambda e: e.tensor_scalar(out=st[:, tg * 512:(tg + 1) * 512], in0=pa[:],
                                                                      scalar1=scale, scalar2=None, op0=ALU.mult),
                                     r=[pa], w=[st])
                        r0 = doff + j * 128
                        S.dma("pool", dst[r0:r0 + 128, :], st[:], r=[st], w=[dname])
                else:
                    for tt in range(NT):
                        pa = pacc[na % 4]
                        na += 1
                        for kc in range(8):
                            S.op("pe", lambda e: e.matmul(pa[:, 0:ncol], xT[:, kc, tt * 128:(tt + 1) * 128],
                                                          wb[b][:, kc, 0:ncol], start=(kc == 0), stop=(kc == 7)),
                                 r=[wb[b], xT], w=[pa])
                        if dname == "VA":
                            sv = stv[nst % 2]
                            nst += 1
                            S.op("act", lambda e: e.activation(out=sv[:, :, 0:64], in_=pa[:].rearrange("p (h d) -> p h d", h=8),
                                                               func=AF.Copy), r=[pa], w=[sv])
                            S.dma("sp", dst[tt * 128:(tt + 1) * 128, :], sv[:].rearrange("p h d -> p (h d)"),
                                  r=[sv], w=[dname])
                            continue
                        st = (stb if isbf else stf)[nst % 2]
                        nst += 1
                        eng = "act" if ev % 2 == 0 else "dve"
                        ev += 1
                        if eng == "act":
                            S.op("act", lambda e: e.activation(out=st[:, 0:ncol], in_=pa[:, 0:ncol], func=AF.Copy),
                                 r=[pa], w=[st])
                        else:
                            S.op("dve", lambda e: e.tensor_copy(out=st[:, 0:ncol], in_=pa[:, 0:ncol]), r=[pa], w=[st])
                        S.dma("sp", dst[tt * 128:(tt + 1) * 128, doff:doff + ncol], st[:, 0:ncol],
                              r=[st], w=[dname])


    def layer_norm_tile(self, z, gt, bt, st1, junk, out):
        S = self.S
        S.op("dve", lambda e: e.tensor_reduce(out=st1[:, 0:1], in_=z[:], axis=AX.X, op=ALU.add), r=[z], w=[st1])
        S.op("dve", lambda e: e.tensor_scalar(out=st1[:, 1:2], in0=st1[:, 0:1], scalar1=-1.0 / D, scalar2=None,
                                               op0=ALU.mult), r=[st1], w=[st1])
        S.op("act", lambda e: e.activation(out=junk[:], in_=z[:], func=AF.Square, bias=st1[:, 1:2],
                                           accum_out=st1[:, 2:3]), r=[z, st1], w=[junk, st1])
        S.op("act", lambda e: e.activation(out=st1[:, 3:4], in_=st1[:, 2:3], func=AF.Sqrt, scale=1.0 / D,
                                           bias=self.epsln[:, 0:1]), r=[st1, self.epsln], w=[st1])
        S.op("dve", lambda e: e.reciprocal(out=st1[:, 4:5], in_=st1[:, 3:4]), r=[st1], w=[st1])
        S.op("dve", lambda e: e.tensor_scalar(out=z[:], in0=z[:], scalar1=st1[:, 1:2], scalar2=st1[:, 4:5],
                                               op0=ALU.add, op1=ALU.mult), r=[z, st1], w=[z])
        S.op("pool", lambda e: e.tensor_tensor(out=z[:], in0=z[:], in1=gt[:], op=ALU.mult), r=[z, gt], w=[z])
        S.op("pool", lambda e: e.tensor_tensor(out=out[:], in0=z[:], in1=bt[:], op=ALU.add), r=[z, bt], w=[out])

    def phase_e(self, l, xin):
        nc, S, I, X = self.nc, self.S, self.I, self.X
        with ExitStack() as es:
            sb = lambda n, s, d: es.enter_context(nc.sbuf_tensor("%s_L%d" % (n, l), s, d))
            ps = lambda n, s, d: es.enter_context(nc.psum_tensor("%s_L%d" % (n, l), s, d))
            wbr = sb("e_wbr", [128, 12, D], BF16)
            wout = sb("e_wout", [128, 8, D], BF16)
            wst = [sb("e_wst%d" % i, [128, 4, D], F32) for i in range(2)]
            gt = sb("e_g", [128, D], F32)
            bt = sb("e_b", [128, D], F32)
            obr = [sb("e_obr%d" % i, [128, 3, 512], BF16) for i in range(2)]
            obT2 = [sb("e_obT%d" % i, [128, 12, 128], BF16) for i in range(2)]
            sg = [sb("e_sg%d" % i, [128, 3072], F32) for i in range(2)]
            xs = [sb("e_xs%d" % i, [128, D], F32) for i in range(2)]
            y2 = [sb("e_y%d" % i, [128, D], F32) for i in range(2)]
            tmp2 = [sb("e_tmp%d" % i, [128, 512], F32) for i in range(2)]
            yb2 = [sb("e_yb%d" % i, [128, D], BF16) for i in range(2)]
            yT2 = [sb("e_yT%d" % i, [128, 8, 128], BF16) for i in range(2)]
            z2 = [sb("e_z%d" % i, [128, D], F32) for i in range(2)]
            junk2 = [sb("e_junk%d" % i, [128, D], F32) for i in range(2)]
            xo = [sb("e_xo%d" % i, [128, D], F32) for i in range(2)]
            st12 = [sb("e_st1%d" % i, [128, 8], F32) for i in range(2)]
            ptr = [ps("e_ptr%d" % i, [128, 8, 128], BF16) for i in range(2)]
            pacc = [ps("e_pacc%d" % i, [128, 512], F32) for i in range(4)]
            k = 0
            for bi, wn in enumerate(("w_branch_a", "w_branch_b", "w_branch_c")):
                S.dma("sp", wst[k % 2][:], I[wn][l].rearrange("(kc p) n -> p kc n", p=128), w=[wst[k % 2]])
                S.op("pool", lambda e: e.tensor_copy(out=wbr[:, bi * 4:(bi + 1) * 4, :], in_=wst[k % 2][:]),
                     r=[wst[k % 2]], w=[wbr])
                k += 1
            wo = I["w_out"][l].rearrange("(kc p) n -> p kc n", p=128)
            for hh in range(2):
                S.dma("sp", wst[k % 2][:], wo[:, hh * 4:(hh + 1) * 4, :], w=[wst[k % 2]])
                S.op("pool", lambda e: e.tensor_copy(out=wout[:, hh * 4:(hh + 1) * 4, :], in_=wst[k % 2][:]),
                     r=[wst[k % 2]], w=[wout])
                k += 1
            S.dma("sp", gt[:], I["ln1_g"][l:l + 1, :].partition_broadcast(128), w=[gt])
            S.dma("sp", bt[:], I["ln1_b"][l:l + 1, :].partition_broadcast(128), w=[bt])
            na = 0
            na_ = [na]

            def bind(tt):
                b = tt % 2
                return (b, slice(tt * 128, (tt + 1) * 128), obT2[b], y2[b], tmp2[b], yb2[b], yT2[b], z2[b], junk2[b], st12[b])

            def stage_x(tt):
                b, rows, obT, y, tmp, yb, yT, z, junk, st1 = bind(tt)
                na = na_[0]
                for bi, nm in enumerate(("OA", "OB", "OC")):
                    S.dma("sp", obr[b][:, bi, :], X[nm][rows, :], r=[nm], w=[obr[b]])
                S.dma("sp", sg[b][:], X["GATES"][rows, :], r=["GATES"], w=[sg[b]])
                S.dma("sp", xs[b][:], xin[rows, :], r=["XS"], w=[xs[b]])
                S.op("act", lambda e: e.activation(out=sg[b][:], in_=sg[b][:], func=AF.Sigmoid), r=[sg[b]], w=[sg[b]])
                for c in range(12):
                    p = ptr[0] if c < 8 else ptr[1]
                    S.op("pe", lambda e: e.transpose(out=p[:, c % 8, :], in_=obr[b][:, c // 4, (c % 4) * 128:(c % 4 + 1) * 128],
                                                     identity=self.identb[:]), r=[obr[b], self.identb], w=[p])
                S.op("dve", lambda e: e.tensor_copy(out=obT[:, 0:8, :], in_=ptr[0][:]), r=[ptr[0]], w=[obT])
                S.op("dve", lambda e: e.tensor_copy(out=obT[:, 8:12, :], in_=ptr[1][:, 0:4, :]), r=[ptr[1]], w=[obT])
                for nh in range(2):
                    cs = slice(nh * 512, (nh + 1) * 512)
                    for bi in range(3):
                        pa = pacc[na % 4]
                        na += 1
                        for kc in range(4):
                            S.op("pe", lambda e: e.matmul(pa[:], obT[:, bi * 4 + kc, :], wbr[:, bi * 4 + kc, cs],
                                                          start=(kc == 0), stop=(kc == 3)), r=[obT, wbr], w=[pa])
                        gsl = sg[b][:, bi * 1024 + nh * 512: bi * 1024 + (nh + 1) * 512]
                        if bi == 0:
                            S.op("dve", lambda e: e.tensor_tensor(out=y[:, cs], in0=pa[:], in1=gsl, op=ALU.mult),
                                 r=[pa, sg[b]], w=[y])
                        else:
                            S.op("dve", lambda e: e.tensor_tensor(out=tmp[:], in0=pa[:], in1=gsl, op=ALU.mult),
                                 r=[pa, sg[b]], w=[tmp])
                            S.op("pool", lambda e: e.tensor_tensor(out=y[:, cs], in0=y[:, cs], in1=tmp[:], op=ALU.add),
                                 r=[y, tmp], w=[y])
                na_[0] = na

            def stage_y(tt):
                b, rows, obT, y, tmp, yb, yT, z, junk, st1 = bind(tt)
                na = na_[0]
                S.op("act", lambda e: e.activation(out=yb[:], in_=y[:], func=AF.Copy), r=[y], w=[yb])
                for kc in range(8):
                    S.op("pe", lambda e: e.transpose(out=ptr[0][:, kc, :], in_=yb[:, kc * 128:(kc + 1) * 128],
                                                     identity=self.identb[:]), r=[yb, self.identb], w=[ptr[0]])
                S.op("dve", lambda e: e.tensor_copy(out=yT[:], in_=ptr[0][:]), r=[ptr[0]], w=[yT])
                for nh in range(2):
                    cs = slice(nh * 512, (nh + 1) * 512)
                    pa = pacc[na % 4]
                    na += 1
                    for kc in range(8):
                        S.op("pe", lambda e: e.matmul(pa[:], yT[:, kc, :], wout[:, kc, cs], start=(kc == 0), stop=(kc == 7)),
                             r=[yT, wout], w=[pa])
                    S.op("dve", lambda e: e.scalar_tensor_tensor(out=z[:, cs], in0=xs[b][:, cs], scalar=ALPHA, in1=pa[:],
                                                                 op0=ALU.mult, op1=ALU.add), r=[xs[b], pa], w=[z])
                self.layer_norm_tile(z, gt, bt, st1, junk, xo[b])
                S.dma("pool", X["X1"][rows, :], xo[b][:], r=[xo[b]], w=["X1"])
                na_[0] = na

            stage_x(0)
            for tt in range(NT):
                if tt + 1 < NT:
                    stage_x(tt + 1)
                stage_y(tt)

    def phase_f(self, l, xout):
        nc, S, I, X = self.nc, self.S, self.I, self.X
        TG = 256
        with ExitStack() as es:
            sb = lambda n, s, d: es.enter_context(nc.sbuf_tensor("%s_L%d" % (n, l), s, d))
            ps = lambda n, s, d: es.enter_context(nc.psum_tensor("%s_L%d" % (n, l), s, d))
            wup = sb("f_wup", [128, 8, DFF], BF16)
            wdn = sb("f_wdn", [128, 32, D], BF16)
            wst = sb("f_wst", [128, 8, 512], F32)
            gt = sb("f_g", [128, D], F32)
            bt = sb("f_b", [128, D], F32)
            x1 = sb("f_x1", [128, 2, D], F32)
            x1b = sb("f_x1b", [128, D], BF16)
            x1T = sb("f_x1T", [128, 8, TG], BF16)
            hT = sb("f_hT", [128, 32, TG], BF16)
            rl = [sb("f_rl%d" % i, [128, TG], F32) for i in range(2)]
            z = sb("f_z", [128, D], F32)
            junk = sb("f_junk", [128, D], F32)
            xo = [sb("f_xo%d" % i, [128, D], F32) for i in range(2)]
            st1 = sb("f_st1", [128, 8], F32)
            ptr = [ps("f_ptr%d" % i, [128, 8, 128], BF16) for i in range(2)]
            pup = [ps("f_pup%d" % i, [128, TG], F32) for i in range(2)]
            pdn = [ps("f_pdn%d" % i, [128, 512], F32) for i in range(2)]
            wu = I["w_up"][l].rearrange("(kc p) n -> p kc n", p=128)
            for c in range(8):
                S.dma("sp", wst[:], wu[:, :, c * 512:(c + 1) * 512], w=[wst])
                S.op("act", lambda e: e.activation(out=wup[:, 0:4, c * 512:(c + 1) * 512], in_=wst[:, 0:4, :], func=AF.Copy), r=[wst], w=[wup])
                S.op("dve", lambda e: e.tensor_copy(out=wup[:, 4:8, c * 512:(c + 1) * 512], in_=wst[:, 4:8, :]), r=[wst], w=[wup])
            wd = I["w_down"][l].rearrange("(fc p) n -> p fc n", p=128)

            def load_wdown():
                for c in range(8):
                    S.dma("sp", wst[:].rearrange("p a n -> p (a n)").rearrange("p (f n) -> p f n", f=4),
                          wd[:, c * 4:(c + 1) * 4, :], w=[wst])
                    wv = wst[:].rearrange("p a n -> p (a n)").rearrange("p (f n) -> p f n", f=4)
                    S.op("act", lambda e: e.activation(out=wdn[:, c * 4:c * 4 + 2, :], in_=wv[:, 0:2, :], func=AF.Copy), r=[wst], w=[wdn])
                    S.op("dve", lambda e: e.tensor_copy(out=wdn[:, c * 4 + 2:c * 4 + 4, :], in_=wv[:, 2:4, :]), r=[wst], w=[wdn])
            S.dma("sp", gt[:], I["ln2_g"][l:l + 1, :].partition_broadcast(128), w=[gt])
            S.dma("sp", bt[:], I["ln2_b"][l:l + 1, :].partition_broadcast(128), w=[bt])
            nu = 0
            nd = 0
            no = 0
            for tg in range(T // TG):
                for u in range(2):
                    tt = tg * 2 + u
                    S.dma("sp", x1[:, u, :], X["X1"][tt * 128:(tt + 1) * 128, :], r=["X1"], w=[x1])
                for u in range(2):
                    S.op("act", lambda e: e.activation(out=x1b[:], in_=x1[:, u, :], func=AF.Copy), r=[x1], w=[x1b])
                    for kc in range(8):
                        S.op("pe", lambda e: e.transpose(out=ptr[u][:, kc, :], in_=x1b[:, kc * 128:(kc + 1) * 128],
                                                         identity=self.identb[:]), r=[x1b, self.identb], w=[ptr[u]])
                    S.op("dve", lambda e: e.tensor_copy(out=x1T[:, :, u * 128:(u + 1) * 128], in_=ptr[u][:]),
                         r=[ptr[u]], w=[x1T])
                for fc in range(32):
                    pu = pup[nu % 2]
                    r_ = rl[nu % 2]
                    nu += 1
                    for kc in range(8):
                        S.op("pe", lambda e: e.matmul(pu[:], wup[:, kc, fc * 128:(fc + 1) * 128], x1T[:, kc, :],
                                                      start=(kc == 0), stop=(kc == 7)), r=[wup, x1T], w=[pu])
                    S.op("act", lambda e: e.activation(out=r_[:], in_=pu[:], func=AF.Relu), r=[pu], w=[r_])
                    S.op("dve", lambda e: e.tensor_tensor(out=hT[:, fc, :], in0=r_[:], in1=r_[:], op=ALU.mult), r=[r_], w=[hT])
                if tg == 0:
                    load_wdown()
                for u in range(2):
                    tt = tg * 2 + u
                    for nh in range(2):
                        cs = slice(nh * 512, (nh + 1) * 512)
                        pd = pdn[nd % 2]
                        nd += 1
                        for fc in range(32):
                            S.op("pe", lambda e: e.matmul(pd[:], hT[:, fc, u * 128:(u + 1) * 128], wdn[:, fc, cs],
                                                          start=(fc == 0), stop=(fc == 31)), r=[hT, wdn], w=[pd])
                        S.op("dve", lambda e: e.scalar_tensor_tensor(out=z[:, cs], in0=x1[:, u, cs], scalar=ALPHA, in1=pd[:],
                                                                     op0=ALU.mult, op1=ALU.add), r=[x1, pd], w=[z])
                    o = xo[no % 2]
                    no += 1
                    self.layer_norm_tile(z, gt, bt, st1, junk, o)
                    oname = "OUT" if xout is self.out else "XS"
                    S.dma("pool", xout[tt * 128:(tt + 1) * 128, :], o[:], r=[o], w=[oname])


    def phase_b(self, l):
        nc, S, I, X = self.nc, self.S, self.I, self.X
        with ExitStack() as es:
            sb = lambda n, s, d: es.enter_context(nc.sbuf_tensor("%s_L%d" % (n, l), s, d))
            ps = lambda n, s, d: es.enter_context(nc.psum_tensor("%s_L%d" % (n, l), s, d))
            kT = sb("b_kT", [128, 4, T], BF16)
            qT = sb("b_qT", [128, 4, T], BF16)
            va = sb("b_va", [128, NT, 520], BF16)
            toep = sb("b_toep", [128, 8, 1280], BF16)
            nsT = sb("b_nsT", [128, T], BF16)
            ksum = sb("b_ksum", [128, 4, 16], F32)
            kmT = sb("b_kmT", [128, 4, 32], BF16)
            gm = sb("b_gm", [128, 8, 16], F32)
            m8 = sb("b_m8", [128, 8, 8], F32)
            ns = sb("b_ns", [128, 8, 16], BF16)
            PT = [sb("b_PT%d" % i, [128, 512], BF16) for i in range(4)]
            oa = [sb("b_oa%d" % i, [128, 4, 512], BF16) for i in range(2)]
            rden = sb("b_rden", [128, 4], F32)
            E = sb("b_E", [128, 128, 128], BF16)
            S.op("dve", lambda e: e.tensor_copy(out=E[:], in_=self.i30k[:].unsqueeze(2).to_broadcast([128, 128, 128])),
                 r=[self.i30k], w=[E])
            pS = [ps("b_pS%d" % i, [128, 512], F32) for i in range(2)]
            pO = [ps("b_pO%d" % i, [128, 65], F32) for i in range(4)]
            es_sel = ExitStack()
            pg = es_sel.enter_context(nc.psum_tensor("b_pg_L%d" % l, [128, 8, 16], F32))
            ptr = es_sel.enter_context(nc.psum_tensor("b_ptr_L%d" % l, [128, 128], BF16))
            for a in range(4):
                S.dma("sp", kT[:, a, :], X["KA"][a * 128:(a + 1) * 128, :], r=["KA"], w=[kT])
                S.dma("sp", qT[:, a, :], X["QA"][a * 128:(a + 1) * 128, :], r=["QA"], w=[qT])
            vsrc = X["VA"].rearrange("(tt p) c -> p tt c", p=128)
            for c in range(4):
                S.dma("sp", va[:, c * 8:(c + 1) * 8, :], vsrc[:, c * 8:(c + 1) * 8, :], r=["VA"], w=[va])
            S.dma("sp", toep[:], I["c_toep"][:, 0:8, :], w=[toep])
            S.op("pool", lambda e: e.memset(nsT[:], 0.0), w=[nsT])
            S.op("pool", lambda e: e.memset(gm[:], -1e30), w=[gm])
            S.op("pool", lambda e: e.memset(ns[:], 0.0), w=[ns])
            import os
            bstop = int(os.environ.get("BSTOP", "99"))
            if bstop <= 1:
                es_sel.close()
                return
            S.op("dve", lambda e: e.tensor_reduce(out=ksum[:], in_=kT[:].rearrange("p a (n s) -> p a n s", s=256),
                                                   axis=AX.X, op=ALU.add), r=[kT], w=[ksum])
            S.op("pool", lambda e: e.memset(kmT[:], 0.0), w=[kmT])
            S.op("dve", lambda e: e.tensor_scalar(out=kmT[0:64, :, 0:16], in0=ksum[0:64, :, :], scalar1=1.0 / 256, scalar2=None,
                                                   op0=ALU.mult), r=[ksum], w=[kmT])
            S.op("dve", lambda e: e.tensor_scalar(out=kmT[64:128, :, 16:32], in0=ksum[64:128, :, :], scalar1=1.0 / 256, scalar2=None,
                                                   op0=ALU.mult), r=[ksum], w=[kmT])
            if bstop <= 2:
                es_sel.close()
                return
            for tt in range(NT):
                cur = tt // 2
                if cur <= 3:
                    continue
                for a in range(4):
                    S.op("pe", lambda e: e.matmul(pg[:, 2 * a:2 * a + 2, :], qT[:, a, tt * 128:(tt + 1) * 128],
                                                  kmT[:, a, :].rearrange("p (h n) -> p h n", h=2), start=True, stop=True),
                         r=[qT, kmT], w=[pg])
                bsub = int(os.environ.get("BSUB", "99"))
                S.op("act", lambda e: e.activation(out=gm[:, :, 0:cur], in_=pg[:, :, 0:cur], func=AF.Copy), r=[pg], w=[gm])
                if bsub <= 1:
                    continue
                for h in range(8):
                    S.op("dve", lambda e: e.max(out=m8[:, h, :], in_=gm[:, h, :]), r=[gm], w=[m8])
                if bsub <= 2:
                    continue
                for h in range(8):
                    S.op("dve", lambda e: e.tensor_scalar(out=ns[:, h, 0:cur], in0=gm[:, h, 0:cur], scalar1=m8[:, h, 2:3],
                                                           scalar2=1.0, op0=ALU.is_ge, op1=ALU.subtract), r=[gm, m8], w=[ns])
                if bsub <= 3:
                    continue
                S.op("pe", lambda e: e.transpose(out=ptr[:], in_=ns[:].rearrange("p h n -> p (h n)"), identity=self.identb[:]),
                     r=[ns, self.identb], w=[ptr])
                S.op("act", lambda e: e.activation(out=nsT[:, tt * 128:(tt + 1) * 128], in_=ptr[:], func=AF.Copy),
                     r=[ptr], w=[nsT])
            S.barrier()
            es_sel.close()
            pS = pS + [ps("b_pS%d" % i, [128, 512], F32) for i in (2, 3)]
            if bstop <= 3:
                return
            nS = [0]
            for qg in range(8):
                if bstop <= 4 and qg >= 1:
                    break
                ob = oa[qg % 2]
                qs = slice(qg * 512, (qg + 1) * 512)
                nkt = 4 * (qg + 1)

                def emit_S(h, kt):
                    hb, a_ = 64 * (h % 2), h // 2
                    m = kt - 4 * qg
                    off = 512 - 128 * m if m >= -1 else 768
                    n = kt // 2
                    need_sel = (qg >= 2) and (n <= 2 * qg)
                    p, pt = pS[nS[0] % 4], PT[nS[0] % 4]
                    nS[0] += 1
                    S.op("pe", lambda e: e.matmul(p[:], kT[hb:hb + 64, a_, kt * 128:(kt + 1) * 128], qT[hb:hb + 64, a_, qs],
                                                  start=True, stop=False), r=[kT, qT], w=[p])
                    S.op("pe", lambda e: e.matmul(p[:], self.identb[:], toep[:, h, off:off + 512],
                                                  start=False, stop=not need_sel), r=[self.identb, toep], w=[p])
                    if need_sel:
                        rr = h * 16 + n
                        S.op("pe", lambda e: e.matmul(p[:], E[:, rr, :], nsT[:, qs], start=False, stop=True), r=[E, nsT], w=[p])
                    return p, pt

                def emit_PV(h, kt, pt):
                    for u in range(4):
                        last = 4 * qg + u
                        if kt <= last:
                            S.op("pe", lambda e: e.matmul(pO[u][:], pt[:, u * 128:(u + 1) * 128], va[:, kt, h * 65:(h + 1) * 65],
                                                          start=(kt == 0), stop=(kt == last)), r=[pt, va], w=[pO[u]])

                def finalize(h):
                    for u in range(4):
                        S.op("dve", lambda e: e.reciprocal(out=rden[:, u:u + 1], in_=pO[u][:, 64:65]), r=[pO[u]], w=[rden])
                        S.op("dve", lambda e: e.tensor_scalar(out=ob[:, u, h * 64:(h + 1) * 64], in0=pO[u][:, 0:64],
                                                               scalar1=rden[:, u:u + 1], scalar2=None, op0=ALU.mult),
                             r=[pO[u], rden], w=[ob])

                steps = [(h, kt) for h in range(8) for kt in range(nkt)]
                pending = emit_S(*steps[0])
                for i, (h, kt) in enumerate(steps):
                    p, pt = pending
                    S.op("act", lambda e: e.activation(out=pt[:], in_=p[:], func=AF.Exp), r=[p], w=[pt])
                    if i + 1 < len(steps):
                        pending = emit_S(*steps[i + 1])
                    emit_PV(h, kt, pt)
                    if kt == nkt - 1:
                        finalize(h)
                S.dma("pool", X["OA"][qg * 512:(qg + 1) * 512, :].rearrange("(u p) c -> p u c", p=128), ob[:],
                      r=[ob], w=["OA"])

    def phase_d(self, l):
        nc, S, I, X = self.nc, self.S, self.I, self.X
        import os
        dstop = int(os.environ.get("DSTOP", "99"))
        with ExitStack() as es:
            sb = lambda n, s, d: es.enter_context(nc.sbuf_tensor("%s_L%d" % (n, l), s, d))
            ps = lambda n, s, d: es.enter_context(nc.psum_tensor("%s_L%d" % (n, l), s, d))
            c = sb("d_c", [128, NT, 256], BF16)
            cT = sb("d_cT", [128, 2, T], BF16)
            kiT2 = sb("d_kiT2", [128, T], BF16)
            toep = sb("d_toep", [128, 8, 1280], BF16)
            absw = sb("d_absw", [128, NT, 8], F32)
            sgn = sb("d_sgn", [128, NT, 8], F32)
            wukT = sb("d_wukT", [128, 4, 256], BF16)
            WBD = sb("d_WBD", [128, 4, 2, 256], BF16)
            wuvb = sb("d_wuvb", [128, 2, 512], BF16)
            pI = [ps("d_pI%d" % i, [128, 512], F32) for i in range(2)]
            pAcc = ps("d_pAcc", [128, 512], F32)
            pS = [ps("d_pS%d" % i, [128, 4, 128], F32) for i in range(2)]
            pOT = [ps("d_pOT%d" % i, [128, 512], F32) for i in range(2)]
            pDen = ps("d_pDen", [128, 512], F32)
            pIb = [p[:].bitcast(BF16).rearrange("p (a n) -> p a n", a=8) for p in pI]

            S.dma("sp", toep[:], I["c_toep"][:, 8:16, :], w=[toep])
            with ExitStack() as es2:
                sb2 = lambda n, s, d: es2.enter_context(nc.sbuf_tensor("%s_L%d" % (n, l), s, d))
                ckv = sb2("d_ckv", [128, NT, 256], F32)
                kiwi = sb2("d_kiwi", [128, NT, 72], F32)
                kin = sb2("d_kin", [128, NT, 128], BF16)
                kif = sb2("d_kif", [128, NT, 64], F32)
                junk = sb2("d_junk", [128, 256], F32)
                wst = sb2("d_wst", [128, 2, 512], F32)
                wukb = sb2("d_wukb", [128, 2, 512], BF16)
                kvn = sb2("d_kvn", [128, 256], F32)
                lng = sb2("d_lng", [128, 64], F32)
                lnb = sb2("d_lnb", [128, 64], F32)
                ss = sb2("d_ss", [128, NT], F32)
                rs = sb2("d_rs", [128, NT], F32)
                mu = sb2("d_mu", [128, NT], F32)
                S.dma("sp", wst[:], I["w_uk"][l].rearrange("(rc p) n -> p rc n", p=128), w=[wst])
                S.op("pool", lambda e: e.tensor_copy(out=wukb[:], in_=wst[:]), r=[wst], w=[wukb])
                for rc in range(2):
                    for a in range(4):
                        S.op("pe", lambda e: e.transpose(out=pIb[0][:, rc * 4 + a, :], in_=wukb[:, rc, a * 128:(a + 1) * 128],
                                                         identity=self.identb[:]), r=[wukb, self.identb], w=[pI[0]])
                S.op("dve", lambda e: e.tensor_copy(out=wukT[:].rearrange("p a (rc r) -> p rc a r", rc=2),
                                                     in_=pIb[0].rearrange("p (rc a) r -> p rc a r", rc=2)), r=[pI[0]], w=[wukT])
                S.op("pool", lambda e: e.memset(WBD[:], 0.0), w=[WBD])
                S.op("dve", lambda e: e.tensor_copy(out=WBD[0:64, :, 0, :], in_=wukT[0:64, :, :]), r=[wukT], w=[WBD])
                S.op("dve", lambda e: e.tensor_copy(out=WBD[64:128, :, 1, :], in_=wukT[64:128, :, :]), r=[wukT], w=[WBD])
                S.dma("sp", wst[:], I["w_uv"][l].rearrange("(rc p) n -> p rc n", p=128), r=[], w=[wst])
                S.op("pool", lambda e: e.tensor_copy(out=wuvb[:], in_=wst[:]), r=[wst], w=[wuvb])
                S.dma("sp", kvn[:], I["kv_norm"][l:l + 1, :].partition_broadcast(128), w=[kvn])
                S.dma("sp", lng[:], I["idx_k_ln_g"][l:l + 1, :].partition_broadcast(128), w=[lng])
                S.dma("sp", lnb[:], I["idx_k_ln_b"][l:l + 1, :].partition_broadcast(128), w=[lnb])
                csrc = X["CKV"].rearrange("(tt p) c -> p tt c", p=128)
                for q4 in range(4):
                    S.dma("sp", ckv[:, q4 * 8:(q4 + 1) * 8, :], csrc[:, q4 * 8:(q4 + 1) * 8, :], r=["CKV"], w=[ckv])
                for tt in range(NT):
                    S.op("act", lambda e: e.activation(out=junk[:], in_=ckv[:, tt, :], func=AF.Square, accum_out=ss[:, tt:tt + 1]),
                         r=[ckv], w=[junk, ss])
                S.op("act", lambda e: e.activation(out=rs[:], in_=ss[:], func=AF.Sqrt, scale=1.0 / 256, bias=self.epsln[:, 1:2]),
                     r=[ss, self.epsln], w=[rs])
                S.op("dve", lambda e: e.reciprocal(out=rs[:], in_=rs[:]), r=[rs], w=[rs])
                S.op("dve", lambda e: e.tensor_tensor(out=ckv[:], in0=ckv[:], in1=rs[:].unsqueeze(2).to_broadcast([128, NT, 256]),
                                                       op=ALU.mult), r=[ckv, rs], w=[ckv])
                S.op("dve", lambda e: e.tensor_tensor(out=c[:], in0=ckv[:], in1=kvn[:].unsqueeze(1).to_broadcast([128, NT, 256]),
                                                       op=ALU.mult), r=[ckv, kvn], w=[c])
                for g in range(NT // 4):
                    pb = pIb[g % 2]
                    for t4 in range(4):
                        for rc in range(2):
                            S.op("pe", lambda e: e.transpose(out=pb[:, t4 * 2 + rc, :], in_=c[:, g * 4 + t4, rc * 128:(rc + 1) * 128],
                                                             identity=self.identb[:]), r=[c, self.identb], w=[pI[g % 2]])
                    S.op("act", lambda e: e.activation(out=cT[:, :, g * 512:(g + 1) * 512].rearrange("p rc (t q) -> p t rc q", t=4),
                                                       in_=pb.rearrange("p (t rc) q -> p t rc q", t=4), func=AF.Copy),
                         r=[pI[g % 2]], w=[cT])
                S.dma("sp", kiwi[:], X["KIWI"].rearrange("(tt p) c -> p tt c", p=128), r=["KIWI"], w=[kiwi])
                S.op("dve", lambda e: e.tensor_reduce(out=mu[:], in_=kiwi[:, :, 0:64], axis=AX.X, op=ALU.add), r=[kiwi], w=[mu])
                S.op("dve", lambda e: e.tensor_scalar(out=mu[:], in0=mu[:], scalar1=-1.0 / 64, scalar2=None, op0=ALU.mult),
                     r=[mu], w=[mu])
                S.op("dve", lambda e: e.tensor_tensor(out=kif[:], in0=kiwi[:, :, 0:64], in1=mu[:].unsqueeze(2).to_broadcast([128, NT, 64]),
                                                       op=ALU.add), r=[kiwi, mu], w=[kif])
                S.op("dve", lambda e: e.tensor_tensor(out=ckv[:, :, 0:64], in0=kif[:], in1=kif[:], op=ALU.mult), r=[kif], w=[ckv])
                S.op("dve", lambda e: e.tensor_reduce(out=ss[:], in_=ckv[:, :, 0:64], axis=AX.X, op=ALU.add), r=[ckv], w=[ss])
                S.op("act", lambda e: e.activation(out=rs[:], in_=ss[:], func=AF.Sqrt, scale=1.0 / 64, bias=self.epsln[:, 0:1]),
                     r=[ss, self.epsln], w=[rs])
                S.op("dve", lambda e: e.reciprocal(out=rs[:], in_=rs[:]), r=[rs], w=[rs])
                S.op("dve", lambda e: e.tensor_tensor(out=kif[:], in0=kif[:], in1=rs[:].unsqueeze(2).to_broadcast([128, NT, 64]),
                                                       op=ALU.mult), r=[kif, rs], w=[kif])
                S.op("dve", lambda e: e.tensor_tensor(out=kif[:], in0=kif[:], in1=lng[:].unsqueeze(1).to_broadcast([128, NT, 64]),
                                                       op=ALU.mult), r=[kif, lng], w=[kif])
                for hf in range(2):
                    S.op("dve", lambda e: e.tensor_tensor(out=kin[:, :, hf * 64:(hf + 1) * 64], in0=kif[:],
                                                           in1=lnb[:].unsqueeze(1).to_broadcast([128, NT, 64]), op=ALU.add),
                         r=[kif, lnb], w=[kin])
                for g in range(NT // 8):
                    pb = pIb[g % 2]
                    for t8 in range(8):
                        S.op("pe", lambda e: e.transpose(out=pb[:, t8, :], in_=kin[:, g * 8 + t8, :], identity=self.identb[:]),
                             r=[kin, self.identb], w=[pI[g % 2]])
                    S.op("act", lambda e: e.activation(out=kiT2[:, g * 1024:(g + 1) * 1024].rearrange("p (t q) -> p t q", t=8),
                                                       in_=pb, func=AF.Copy), r=[pI[g % 2]], w=[kiT2])
                S.op("act", lambda e: e.activation(out=absw[:], in_=kiwi[:, :, 64:72], func=AF.Abs, scale=IDX_WEIGHT_SCALE),
                     r=[kiwi], w=[absw])
                S.op("dve", lambda e: e.tensor_scalar(out=sgn[:], in0=kiwi[:, :, 64:72], scalar1=0.0, scalar2=2.0,
                                                       op0=ALU.is_ge, op1=ALU.mult), r=[kiwi], w=[sgn])
                S.op("dve", lambda e: e.tensor_scalar(out=sgn[:], in0=sgn[:], scalar1=-1.0, scalar2=None, op0=ALU.add),
                     r=[sgn], w=[sgn])
                S.barrier()
            Isc = [sb("d_Isc%d" % i, [128, T], F32) for i in range(2)]
            work = sb("d_work", [128, T], F32)
            negm = [sb("d_negm%d" % i, [128, T], BF16) for i in range(2)]
            nmT = [sb("d_nmT%d" % i, [128, NT, 128], BF16) for i in range(2)]
            qit = [sb("d_qit%d" % i, [128, 4, 128], BF16) for i in range(2)]
            qct = [sb("d_qct%d" % i, [128, 4, 128], BF16) for i in range(2)]
            qlT = [sb("d_qlT%d" % i, [128, 2, 8, 128], BF16) for i in range(3)]
            Dsg = sb("d_Dsg", [128, 8, 128], BF16)
            Ph = [sb("d_Ph%d" % i, [128, 512], BF16) for i in range(2)]
            PT = [sb("d_PT%d" % i, [128, 4, 128], BF16) for i in range(2)]
            rdn = sb("d_rdn", [128, 512], F32)
            OTn = sb("d_OTn", [128, 2, 4, 128], BF16)
            oc = [sb("d_oc%d" % i, [128, 512], BF16) for i in range(2)]
            m8 = sb("d_m8", [128, 8], F32)
            st = sb("d_st", [128, 4], F32)
            if dstop <= 1:
                return
            qisrc = X["QI"].rearrange("(a p) t -> p a t", p=128)
            qcsrc = X["QC"].rearrange("(a p) t -> p a t", p=128)
            nS = [0]

            def stage1(qt):
                L, b = (qt + 1) * 128, qt % 2
                ts = slice(qt * 128, (qt + 1) * 128)
                S.dma("sp", qit[b][:], qisrc[:, :, ts], r=["QI"], w=[qit[b]])
                S.dma("sp", qct[b][:], qcsrc[:, :, ts], r=["QC"], w=[qct[b]])
                if qt >= 2:
                    S.op("pool", lambda e: e.tensor_tensor(out=Dsg[:], in0=self.identb[:].unsqueeze(1).to_broadcast([128, 8, 128]),
                                                           in1=sgn[:, qt, :].unsqueeze(2).to_broadcast([128, 8, 128]), op=ALU.mult),
                         r=[self.identb, sgn], w=[Dsg])
                    for kg in range((L + 511) // 512):
                        w_ = min(512, L - 512 * kg)
                        for h in range(8):
                            hb, a_ = 64 * (h % 2), h // 2
                            pi, ph = pI[h % 2], Ph[h % 2]
                            S.op("pe", lambda e: e.matmul(pi[:, 0:w_], qit[b][hb:hb + 64, a_, :], kiT2[hb:hb + 64, kg * 512:kg * 512 + w_],
                                                          start=True, stop=True), r=[qit[b], kiT2], w=[pi])
                            S.op("act", lambda e: e.activation(out=ph[:, 0:w_], in_=pi[:, 0:w_], func=AF.Relu,
                                                               scale=absw[:, qt, h:h + 1]), r=[pi, absw], w=[ph])
                            S.op("pe", lambda e: e.matmul(pAcc[:, 0:w_], Dsg[:, h, :], ph[:, 0:w_], start=(h == 0), stop=(h == 7)),
                                 r=[Dsg, ph], w=[pAcc])
                        S.op("act", lambda e: e.activation(out=Isc[b][:, kg * 512:kg * 512 + w_], in_=pAcc[:, 0:w_], func=AF.Copy),
                             r=[pAcc], w=[Isc[b]])
                for rc in range(2):
                    for hq in range(2):
                        pq = pI[(rc * 2 + hq) % 2]
                        for hh in range(4):
                            h = hq * 4 + hh
                            S.op("pe", lambda e: e.matmul(pq[:, hh * 128:(hh + 1) * 128], WBD[:, h // 2, h % 2, rc * 128:(rc + 1) * 128],
                                                          qct[b][:, h // 2, :], start=True, stop=True), r=[WBD, qct[b]], w=[pq])
                        S.op("act", lambda e: e.activation(out=qlT[qt % 3][:, rc, hq * 4:(hq + 1) * 4, :],
                                                           in_=pq[:].rearrange("p (h q) -> p h q", h=4), func=AF.Copy), r=[pq], w=[qlT[qt % 3]])

            def stage2(qt):
                L, b = (qt + 1) * 128, qt % 2
                if qt < 2:
                    return
                I_ = Isc[b]
                S.op("dve", lambda e: e.tensor_reduce(out=st[:, 0:1], in_=I_[:, 0:L], axis=AX.X, op=ALU.min), r=[I_], w=[st])
                S.op("dve", lambda e: e.tensor_scalar(out=st[:, 1:2], in0=st[:, 0:1], scalar1=-1.0, scalar2=1.0,
                                                       op0=ALU.mult, op1=ALU.add), r=[st], w=[st])
                S.op("dve", lambda e: e.tensor_scalar(out=I_[:, 0:L], in0=I_[:, 0:L], scalar1=st[:, 1:2], scalar2=None,
                                                       op0=ALU.add), r=[I_, st], w=[I_])
                S.op("dve", lambda e: e.tensor_tensor(out=I_[:, L - 128:L], in0=I_[:, L - 128:L], in1=self.ltri[:], op=ALU.mult),
                     r=[I_, self.ltri], w=[I_])
                for r_ in range(32):
                    src_ = I_ if r_ == 0 else work
                    S.op("dve", lambda e: e.max(out=m8[:], in_=src_[:, 0:L]), r=[src_], w=[m8])
                    if r_ < 31:
                        S.op("dve", lambda e: e.scalar_tensor_tensor(out=work[:, 0:L], in0=src_[:, 0:L], scalar=m8[:, 7:8],
                                                                     in1=src_[:, 0:L], op0=ALU.is_lt, op1=ALU.mult),
                             r=[src_, m8], w=[work])
                S.op("dve", lambda e: e.tensor_scalar(out=negm[b][:, 0:L], in0=I_[:, 0:L], scalar1=m8[:, 7:8], scalar2=1.0,
                                                       op0=ALU.is_ge, op1=ALU.subtract), r=[I_, m8], w=[negm[b]])

            def stage3(qt):
                nk, b = qt + 1, qt % 2
                ts = slice(qt * 128, (qt + 1) * 128)
                masked = qt >= 2
                if masked:
                    for g in range((nk + 7) // 8):
                        n8 = min(8, nk - 8 * g)
                        pb = pIb[g % 2]
                        for t8 in range(n8):
                            kt = g * 8 + t8
                            S.op("pe", lambda e: e.transpose(out=pb[:, t8, :], in_=negm[b][:, kt * 128:(kt + 1) * 128],
                                                             identity=self.identb[:]), r=[negm[b], self.identb], w=[pI[g % 2]])
                        S.op("act", lambda e: e.activation(out=nmT[b][:, g * 8:g * 8 + n8, :], in_=pb[:, 0:n8, :], func=AF.Copy, scale=-NEG),
                             r=[pI[g % 2]], w=[nmT[b]])
                def emit_S(half, kt):
                    hs = slice(4 * half, 4 * half + 4)
                    m = kt - qt
                    off = 512 - 128 * m if m >= -1 else 768
                    p_, pt = pS[nS[0] % 2], PT[nS[0] % 2]
                    nS[0] += 1
                    for rc in range(2):
                        S.op("pe", lambda e: e.matmul(p_[:], cT[:, rc, kt * 128:(kt + 1) * 128], qlT[qt % 3][:, rc, hs, :],
                                                      start=(rc == 0), stop=False), r=[cT, qlT[qt % 3]], w=[p_])
                    S.op("pe", lambda e: e.matmul(p_[:], self.identb[:], toep[:, hs, off:off + 128], start=False, stop=not masked),
                         r=[self.identb, toep], w=[p_])
                    if masked:
                        for hh in range(4):
                            S.op("pe", lambda e: e.matmul(p_[:, hh, :], self.identb[:], nmT[b][:, kt, :], start=False, stop=(hh == 3)),
                                 r=[self.identb, nmT[b]], w=[p_])
                    return p_, pt

                def emit_PV(half, kt, pt):
                    for rc in range(2):
                        S.op("pe", lambda e: e.matmul(pOT[rc][:], c[:, kt, rc * 128:(rc + 1) * 128], pt[:].rearrange("p h q -> p (h q)"),
                                                      start=(kt == 0), stop=(kt == nk - 1)), r=[c, pt], w=[pOT[rc]])
                    S.op("pe", lambda e: e.matmul(pDen[:], self.onesb[:], pt[:].rearrange("p h q -> p (h q)"),
                                                  start=(kt == 0), stop=(kt == nk - 1)), r=[self.onesb, pt], w=[pDen])

                def finalize(half):
                    S.op("dve", lambda e: e.reciprocal(out=rdn[:], in_=pDen[:]), r=[pDen], w=[rdn])
                    for rc in range(2):
                        S.op("dve", lambda e: e.tensor_tensor(out=OTn[:, rc, :, :].rearrange("p h q -> p (h q)"), in0=pOT[rc][:], in1=rdn[:],
                                                               op=ALU.mult), r=[pOT[rc], rdn], w=[OTn])
                    for hh in range(4):
                        h = 4 * half + hh
                        for rc in range(2):
                            S.op("pe", lambda e: e.matmul(pAcc[:, hh * 64:(hh + 1) * 64], OTn[:, rc, hh, :], wuvb[:, rc, h * 64:(h + 1) * 64],
                                                          start=(rc == 0), stop=(rc == 1)), r=[OTn, wuvb], w=[pAcc])
                    S.op("act", lambda e: e.activation(out=oc[b][:, half * 256:(half + 1) * 256], in_=pAcc[:, 0:256], func=AF.Copy),
                         r=[pAcc], w=[oc[b]])

                steps = [(half, kt) for half in range(2) for kt in range(nk)]
                pending = emit_S(*steps[0])
                for i, (half, kt) in enumerate(steps):
                    p_, pt = pending
                    S.op("act", lambda e: e.activation(out=pt[:], in_=p_[:], func=AF.Exp), r=[p_], w=[pt])
                    if i + 1 < len(steps):
                        pending = emit_S(*steps[i + 1])
                    emit_PV(half, kt, pt)
                    if kt == nk - 1:
                        finalize(half)
                S.dma("pool", X["OC"][ts, :], oc[b][:], r=[oc[b]], w=["OC"])

            tiles = [qt for qt in range(NT) if not (dstop <= 2 and qt not in (0, 1, 2, 5))]
            n = len(tiles)
            for i in range(n + 2):
                if i < n:
                    stage1(tiles[i])
                if 0 <= i - 1 < n:
                    stage2(tiles[i - 1])
                if 0 <= i - 2 < n:
                    stage3(tiles[i - 2])

    def phase_c(self, l):
        nc, S, I, X = self.nc, self.S, self.I, self.X
        import os
        cstop = int(os.environ.get("CSTOP", "99"))
        QS = 128 ** -0.5
        UTc, UTs, BON, H0, H1 = (self.gdc[:, i, :] for i in range(5))
        with ExitStack() as es:
            sb = lambda n, s, d: es.enter_context(nc.sbuf_tensor("%s_L%d" % (n, l), s, d))
            ps = lambda n, s, d: es.enter_context(nc.psum_tensor("%s_L%d" % (n, l), s, d))
            cw = sb("c_cw", [128, 12, 4], F32)
            xp = [sb("c_xp%d" % i, [128, T + 3], F32) for i in range(2)]
            y = [sb("c_y%d" % i, [128, T], F32) for i in range(2)]
            sq = sb("c_sq", [128, T], F32)
            rn = sb("c_rn", [128, T], F32)
            pn = [ps("c_pn%d" % i, [128, 512], F32) for i in range(2)]
            cwr = sb("c_cwr", [4, 1536], F32)
            S.dma("sp", cwr[:], I["conv_w"][l], w=[cwr])
            for c_ in range(12):
                S.op("pe", lambda e: e.transpose(out=pn[0][:, c_ * 4:(c_ + 1) * 4], in_=cwr[:, c_ * 128:(c_ + 1) * 128],
                                                 identity=self.ident[0:4, 0:4]), r=[cwr, self.ident], w=[pn[0]])
            S.op("dve", lambda e: e.tensor_copy(out=cw[:].rearrange("p c j -> p (c j)"), in_=pn[0][:, 0:48]), r=[pn[0]], w=[cw])
            for i in range(2):
                S.op("pool", lambda e: e.memset(xp[i][:, 0:3], 0.0), w=[xp[i]])
            def load_chunk(c_):
                S.dma("sp", xp[c_ % 2][:, 3:], X["QKVB"][c_ * 128:(c_ + 1) * 128, :], r=[("QKVB", c_)], w=[xp[c_ % 2]])

            load_chunk(0)
            for c_ in range(12):
                b = c_ % 2
                if c_ + 1 < 12:
                    load_chunk(c_ + 1)
                S.op("dve", lambda e: e.tensor_scalar(out=y[b][:], in0=xp[b][:, 0:T], scalar1=cw[:, c_, 0:1], scalar2=None,
                                                       op0=ALU.mult), r=[xp[b], cw], w=[y[b]])
                for j in range(1, 4):
                    S.op("dve", lambda e: e.scalar_tensor_tensor(out=y[b][:], in0=xp[b][:, j:j + T], scalar=cw[:, c_, j:j + 1],
                                                                 in1=y[b][:], op0=ALU.mult, op1=ALU.add), r=[xp[b], cw, y[b]], w=[y[b]])
                S.op("act", lambda e: e.activation(out=y[b][:], in_=y[b][:], func=AF.Silu), r=[y[b]], w=[y[b]])
                if c_ < 8:
                    S.op("pool", lambda e: e.tensor_tensor(out=sq[:], in0=y[b][:], in1=y[b][:], op=ALU.mult), r=[y[b]], w=[sq])
                    for g in range(8):
                        p_ = pn[g % 2]
                        S.op("pe", lambda e: e.matmul(p_[:], self.onesf[:], sq[:, g * 512:(g + 1) * 512], start=True, stop=True),
                             r=[self.onesf, sq], w=[p_])
                        S.op("act", lambda e: e.activation(out=rn[:, g * 512:(g + 1) * 512], in_=p_[:], func=AF.Sqrt,
                                                           bias=self.epsln[:, 1:2]), r=[p_, self.epsln], w=[rn])
                    S.op("dve", lambda e: e.reciprocal(out=rn[:], in_=rn[:]), r=[rn], w=[rn])
                    S.op("dve", lambda e: e.tensor_tensor(out=y[b][:], in0=y[b][:], in1=rn[:], op=ALU.mult), r=[y[b], rn], w=[y[b]])
                S.dma("sp", X["QKVB"][c_ * 128:(c_ + 1) * 128, :], y[b][:], r=[y[b]], w=[("QKVB", c_)])
        S.barrier()
        if cstop <= 1:
            return
        with ExitStack() as es:
            sb = lambda n, s, d: es.enter_context(nc.sbuf_tensor("%s_L%d" % (n, l), s, d))
            ps = lambda n, s, d: es.enter_context(nc.psum_tensor("%s_L%d" % (n, l), s, d))
            ab = sb("c_ab", [128, NT, 8], F32)
            dtb = sb("c_dtb", [128, 4], F32)
            nea = sb("c_nea", [128, 4], F32)
            one1 = sb("c_one1", [128, 1], F32)
            LA = sb("c_LA", [128, NT, 4], F32)
            nbeta = sb("c_nbeta", [128, NT, 4], F32)
            beta = sb("c_beta", [128, NT, 4], F32)
            G = sb("c_G", [128, 128], F32)
            EGn = sb("c_EGn", [128, 128], F32)
            KD = sb("c_KD", [128, 128], F32)
            EGL = [sb("c_EGL%d" % j, [128, 128], F32) for j in range(2)]
            gn = sb("c_gn", [128, 128], F32)
            qkv = [sb("c_qkv%d" % i, [128, 12, 128], F32) for i in range(2)]
            zt = [sb("c_zt%d" % i, [128, 512], F32) for i in range(2)]
            vt = sb("c_vt", [128, 4, 128], F32)
            kd = sb("c_kd", [128, 4, 128], F32)
            O = sb("c_O", [128, 4, 128], F32)
            ob = [sb("c_ob%d" % i, [128, 512], BF16) for i in range(2)]
            ss = sb("c_ss", [128, 4], F32)
            junk = sb("c_junk", [128, 128], F32)
            Sh = [sb("c_S%d" % h, [128, 128], F32) for h in range(4)]
            Dg = [sb("c_Dg%d" % h, [128, 128], F32) for h in range(4)]
            t1 = [sb("c_t1%d" % h, [128, 128], F32) for h in range(4)]
            decT = [sb("c_dec%d" % h, [128, 128], F32) for h in range(4)]
            EGb = [sb("c_EGb%d" % h, [128, 128], F32) for h in range(4)]
            qeg = [sb("c_qeg%d" % h, [128, 128], F32) for h in range(4)]
            aT = [sb("c_aT%d" % h, [128, 128], F32) for h in range(4)]
            AT = [[sb("c_AT%d_%d" % (h, i), [128, 128], F32) for i in range(2)] for h in range(4)]
            A = [[sb("c_A%d_%d" % (h, i), [128, 128], F32) for i in range(2)] for h in range(4)]
            Xh = [sb("c_X%d" % h, [128, 128], F32) for h in range(4)]
            R = [sb("c_R%d" % h, [128, 128], F32) for h in range(4)]
            vn = [sb("c_vn%d" % h, [128, 128], F32) for h in range(4)]
            class Sub:
                def __init__(self, tile, i):
                    self.ap = tile[:, i, :]
                    self.name = tile.name

                def __getitem__(self, k):
                    return self.ap[k]

            ppb = [ps("c_pp%d" % i, [128, 4, 128], F32) for i in range(4)]
            pp = [Sub(ppb[i % 4], i // 4) for i in range(16)]
            pdb = [ps("c_pd%d" % i, [128, 128], F32) for i in range(2)]
            ptv = ps("c_ptv", [128, 4, 128], F32)
            ptk = ps("c_ptk", [128, 4, 128], F32)
            npp = [0]

            def P():
                npp[0] += 1
                return pp[npp[0] % 16]

            S.dma("sp", ab[:], X["AB"].rearrange("(tt p) c -> p tt c", p=128), r=["AB"], w=[ab])
            S.dma("sp", dtb[:], I["dt_bias"][l:l + 1, :].partition_broadcast(128), w=[dtb])
            S.dma("sp", nea[:], I["a_log"][l:l + 1, :].partition_broadcast(128), w=[nea])
            S.dma("sp", gn[:], I["gdn_norm"][l:l + 1, :].partition_broadcast(128), w=[gn])
            S.op("pool", lambda e: e.memset(one1[:], 1.0), w=[one1])
            for h in range(4):
                S.op("pool", lambda e: e.memset(Sh[h][:], 0.0), w=[Sh[h]])
                S.op("pool", lambda e: e.memset(vn[h][:], 0.0), w=[vn[h]])
                S.op("pool", lambda e: e.memset(R[h][:], 0.0), w=[R[h]])
            S.op("act", lambda e: e.activation(out=nea[:], in_=nea[:], func=AF.Exp), r=[nea], w=[nea])
            S.op("dve", lambda e: e.tensor_scalar(out=nea[:], in0=nea[:], scalar1=-1.0, scalar2=None, op0=ALU.mult), r=[nea], w=[nea])
            S.op("dve", lambda e: e.tensor_tensor(out=LA[:], in0=ab[:, :, 0:4], in1=dtb[:].unsqueeze(1).to_broadcast([128, NT, 4]),
                                                   op=ALU.add), r=[ab, dtb], w=[LA])
            S.op("act", lambda e: e.activation(out=LA[:], in_=LA[:], func=AF.Exp), r=[LA], w=[LA])
            S.op("act", lambda e: e.activation(out=LA[:], in_=LA[:], func=AF.Ln, bias=one1[:, 0:1]), r=[LA, one1], w=[LA])
            S.op("dve", lambda e: e.tensor_tensor(out=LA[:], in0=LA[:], in1=nea[:].unsqueeze(1).to_broadcast([128, NT, 4]),
                                                   op=ALU.mult), r=[LA, nea], w=[LA])
            S.op("act", lambda e: e.activation(out=beta[:], in_=ab[:, :, 4:8], func=AF.Sigmoid), r=[ab], w=[beta])
            S.op("dve", lambda e: e.tensor_scalar(out=nbeta[:], in0=beta[:], scalar1=-1.0, scalar2=None, op0=ALU.mult),
                 r=[beta], w=[nbeta])
            LA2 = LA[:].rearrange("p t h -> p (t h)")
            p1, p2, p3, p4 = P(), P(), P(), P()
            S.op("pe", lambda e: e.matmul(p1[:], UTc, LA2, start=True, stop=True), r=[self.gdc, LA], w=[p1])
            S.op("pe", lambda e: e.matmul(p2[:], BON, LA2, start=True, stop=True), r=[self.gdc, LA], w=[p2])
            S.op("pe", lambda e: e.matmul(p3[:], H0, LA2, start=True, stop=True), r=[self.gdc, LA], w=[p3])
            S.op("pe", lambda e: e.matmul(p4[:], H1, LA2, start=True, stop=True), r=[self.gdc, LA], w=[p4])
            S.op("dve", lambda e: e.tensor_copy(out=G[:], in_=p1[:]), r=[p1], w=[G])
            S.op("act", lambda e: e.activation(out=EGn[:], in_=p1[:], func=AF.Exp), r=[p1], w=[EGn])
            S.op("dve", lambda e: e.tensor_scalar(out=EGn[:], in0=EGn[:], scalar1=-1.0, scalar2=None, op0=ALU.mult), r=[EGn], w=[EGn])
            S.op("dve", lambda e: e.tensor_tensor(out=KD[:], in0=p2[:], in1=G[:], op=ALU.subtract), r=[p2, G], w=[KD])
            S.op("act", lambda e: e.activation(out=KD[:], in_=KD[:], func=AF.Exp), r=[KD], w=[KD])
            S.op("act", lambda e: e.activation(out=EGL[0][:], in_=p3[:], func=AF.Exp), r=[p3], w=[EGL[0]])
            S.op("act", lambda e: e.activation(out=EGL[1][:], in_=p4[:], func=AF.Exp), r=[p4], w=[EGL[1]])
            qsrc = X["QKVB"].rearrange("(c p) t -> p c t", p=128)
            for tt in range(NT):
                if tt >= int(os.environ.get("CTILES", "32")) or (cstop <= 2 and tt >= 2):
                    break
                b = tt % 2
                ts = slice(tt * 128, (tt + 1) * 128)
                S.dma("sp", qkv[b][:], qsrc[:, :, ts], r=["QKVB"], w=[qkv[b]])
                S.dma("sp", zt[b][:], X["ZB"][ts, :], r=["ZB"], w=[zt[b]])
                for h in range(4):
                    S.op("pe", lambda e: e.transpose(out=ptv[:, h, :], in_=qkv[b][:, 8 + h, :], identity=self.ident[:]),
                         r=[qkv[b], self.ident], w=[ptv])
                    S.op("pe", lambda e: e.transpose(out=ptk[:, h, :], in_=qkv[b][:, 4 + h, :], identity=self.ident[:]),
                         r=[qkv[b], self.ident], w=[ptk])
                S.op("act", lambda e: e.activation(out=vt[:], in_=ptv[:], func=AF.Copy), r=[ptv], w=[vt])
                for h in range(4):
                    col = tt * 4 + h
                    S.op("dve", lambda e: e.tensor_scalar(out=kd[:, h, :], in0=ptk[:, h, :], scalar1=KD[:, col:col + 1], scalar2=None,
                                                           op0=ALU.mult), r=[ptk, KD], w=[kd])
                H4 = range(4)
                gcol = [G[:, tt * 4 + h:tt * 4 + h + 1] for h in H4]
                kTt = [qkv[b][:, 4 + h, :] for h in H4]
                qTt = [qkv[b][:, h, :] for h in H4]
                for h in H4:
                    S.op("pool", lambda e: e.tensor_scalar(out=Dg[h][:], in0=self.ident[:], scalar1=gcol[h], scalar2=None, op0=ALU.mult),
                         r=[self.ident, G], w=[Dg[h]])
                pg_ = [P() for _ in H4]
                for h in H4:
                    S.op("pe", lambda e: e.matmul(pg_[h][:], self.onesf[:], Dg[h][:], start=True, stop=True), r=[self.onesf, Dg[h]], w=[pg_[h]])
                for h in H4:
                    S.op("dve", lambda e: e.tensor_scalar(out=t1[h][:], in0=pg_[h][:], scalar1=gcol[h], scalar2=0.0, op0=ALU.subtract, op1=ALU.min),
                         r=[pg_[h], G], w=[t1[h]])
                for h in H4:
                    S.op("act", lambda e: e.activation(out=decT[h][:], in_=t1[h][:], func=AF.Exp), r=[t1[h]], w=[decT[h]])
                    S.op("act", lambda e: e.activation(out=EGb[h][:], in_=pg_[h][:], func=AF.Exp), r=[pg_[h]], w=[EGb[h]])
                for h in H4:
                    S.op("dve", lambda e: e.scalar_tensor_tensor(out=qeg[h][:], in0=qTt[h], scalar=QS, in1=EGb[h][:], op0=ALU.mult, op1=ALU.mult),
                         r=[qkv[b], EGb[h]], w=[qeg[h]])
                pkk = [P() for _ in H4]
                pqk = [P() for _ in H4]
                for h in H4:
                    S.op("pe", lambda e: e.matmul(pkk[h][:], kTt[h], kTt[h], start=True, stop=True), r=[qkv[b]], w=[pkk[h]])
                    S.op("pe", lambda e: e.matmul(pqk[h][:], kTt[h], qTt[h], start=True, stop=True), r=[qkv[b]], w=[pqk[h]])
                for h in H4:
                    S.op("dve", lambda e: e.scalar_tensor_tensor(out=AT[h][0][:], in0=pkk[h][:], scalar=nbeta[:, tt, h:h + 1], in1=decT[h][:],
                                                                 op0=ALU.mult, op1=ALU.mult), r=[pkk[h], nbeta, decT[h]], w=[AT[h][0]])
                    S.op("dve", lambda e: e.scalar_tensor_tensor(out=aT[h][:], in0=pqk[h][:], scalar=QS, in1=decT[h][:],
                                                                 op0=ALU.mult, op1=ALU.mult), r=[pqk[h], decT[h]], w=[aT[h]])
                for h in H4:
                    S.op("pool", lambda e: e.tensor_tensor(out=AT[h][0][:], in0=AT[h][0][:], in1=UTs, op=ALU.mult),
                         r=[AT[h][0], self.gdc], w=[AT[h][0]])
                    S.op("pool", lambda e: e.tensor_tensor(out=aT[h][:], in0=aT[h][:], in1=UTc, op=ALU.mult), r=[aT[h], self.gdc], w=[aT[h]])
                pt_ = [P() for _ in H4]
                for h in H4:
                    S.op("pe", lambda e: e.transpose(out=pt_[h][:], in_=AT[h][0][:], identity=self.ident[:]), r=[AT[h][0], self.ident], w=[pt_[h]])
                for h in H4:
                    S.op("act", lambda e: e.activation(out=A[h][0][:], in_=pt_[h][:], func=AF.Copy), r=[pt_[h]], w=[A[h][0]])
                    S.op("pool", lambda e: e.tensor_tensor(out=Xh[h][:], in0=AT[h][0][:], in1=self.ident[:], op=ALU.add),
                         r=[AT[h][0], self.ident], w=[Xh[h]])
                for k in range(5):
                    cu, nx = k % 2, (k + 1) % 2
                    for h in range(4):
                        pa = P()
                        S.op("pe", lambda e: e.matmul(pa[:], AT[h][cu][:], A[h][cu][:], start=True, stop=True), r=[AT[h][cu], A[h][cu]], w=[pa])
                        S.op("act", lambda e: e.activation(out=A[h][nx][:], in_=pa[:], func=AF.Copy), r=[pa], w=[A[h][nx]])
                        if k < 4:
                            pb_ = P()
                            S.op("pe", lambda e: e.matmul(pb_[:], A[h][cu][:], AT[h][cu][:], start=True, stop=True),
                                 r=[AT[h][cu], A[h][cu]], w=[pb_])
                            S.op("dve", lambda e: e.tensor_copy(out=AT[h][nx][:], in_=pb_[:]), r=[pb_], w=[AT[h][nx]])
                    for h in range(4):
                        px = P()
                        S.op("pe", lambda e: e.matmul(px[:], A[h][nx][:], Xh[h][:], start=True, stop=True), r=[A[h][nx], Xh[h]], w=[px])
                        S.op("dve", lambda e: e.tensor_tensor(out=Xh[h][:], in0=px[:], in1=Xh[h][:], op=ALU.add), r=[px, Xh[h]], w=[Xh[h]])
                for j in range(2):
                    rs = slice(64 * j, 64 * j + 64)
                    pk_ = [P() for _ in range(4)]
                    for h in range(4):
                        S.op("pe", lambda e: e.matmul(pk_[h][:], qkv[b][:, 4 + h, :], Sh[h][:], start=True, stop=True),
                             r=[qkv[b], Sh[h]], w=[pk_[h]])
                    for h in range(4):
                        col = tt * 4 + h
                        S.op("dve", lambda e: e.scalar_tensor_tensor(out=R[h][rs, :], in0=pk_[h][rs, :], scalar=EGn[rs, col:col + 1],
                                                                     in1=vt[rs, h, :], op0=ALU.mult, op1=ALU.add),
                             r=[pk_[h], EGn, vt], w=[R[h]])
                    py_ = [P() for _ in range(4)]
                    for h in range(4):
                        S.op("pe", lambda e: e.matmul(py_[h][:], Xh[h][:], R[h][:], start=True, stop=True), r=[Xh[h], R[h]], w=[py_[h]])
                    for h in range(4):
                        S.op("dve", lambda e: e.tensor_scalar(out=vn[h][rs, :], in0=py_[h][rs, :], scalar1=beta[rs, tt, h:h + 1], scalar2=None,
                                                               op0=ALU.mult), r=[py_[h], beta], w=[vn[h]])
                    po_ = [P() for _ in range(4)]
                    for h in range(4):
                        S.op("pe", lambda e: e.matmul(po_[h][:], qeg[h][:], Sh[h][:], start=True, stop=False), r=[qeg[h], Sh[h]], w=[po_[h]])
                        S.op("pe", lambda e: e.matmul(po_[h][:], aT[h][:], vn[h][:], start=False, stop=True), r=[aT[h], vn[h]], w=[po_[h]])
                    for h in range(4):
                        S.op("act", lambda e: e.activation(out=O[rs, h, :], in_=po_[h][rs, :], func=AF.Copy), r=[po_[h]], w=[O])
                    pd_ = [pdb[h % 2] for h in range(4)]
                    for h in range(4):
                        S.op("pe", lambda e: e.matmul(pd_[h][:], kd[rs, h, :], vn[h][rs, :], start=True, stop=True), r=[kd, vn[h]], w=[pd_[h]])
                        col = tt * 4 + h
                        S.op("dve", lambda e: e.scalar_tensor_tensor(out=Sh[h][:], in0=Sh[h][:], scalar=EGL[j][:, col:col + 1], in1=pd_[h][:],
                                                                     op0=ALU.mult, op1=ALU.add), r=[Sh[h], EGL[j], pd_[h]], w=[Sh[h]])
                for h in range(4):
                    S.op("act", lambda e: e.activation(out=junk[:], in_=O[:, h, :], func=AF.Square, accum_out=ss[:, h:h + 1]),
                         r=[O], w=[junk, ss])
                S.op("act", lambda e: e.activation(out=ss[:], in_=ss[:], func=AF.Sqrt, scale=1.0 / 128, bias=self.epsln[:, 1:2]),
                     r=[ss, self.epsln], w=[ss])
                S.op("dve", lambda e: e.reciprocal(out=ss[:], in_=ss[:]), r=[ss], w=[ss])
                S.op("dve", lambda e: e.tensor_tensor(out=O[:], in0=O[:], in1=ss[:].unsqueeze(2).to_broadcast([128, 4, 128]), op=ALU.mult),
                     r=[O, ss], w=[O])
                S.op("dve", lambda e: e.tensor_tensor(out=O[:], in0=O[:], in1=gn[:].unsqueeze(1).to_broadcast([128, 4, 128]), op=ALU.mult),
                     r=[O, gn], w=[O])
                S.op("act", lambda e: e.activation(out=zt[b][:], in_=zt[b][:], func=AF.Silu), r=[zt[b]], w=[zt[b]])
                S.op("dve", lambda e: e.tensor_tensor(out=ob[b][:], in0=O[:].rearrange("p h e -> p (h e)"), in1=zt[b][:], op=ALU.mult),
                     r=[O, zt[b]], w=[ob[b]])
                S.dma("pool", X["OB"][ts, :], ob[b][:], r=[ob[b]], w=["OB"])

_CACHE = {}


def _get_prog(nlayers=DEPTH, debug=(), stop_after=None):
    key = (nlayers, tuple(sorted(debug)), stop_after)
    if key not in _CACHE:
        p = Prog(nlayers, debug, stop_after)
        p.build()
        _CACHE[key] = p
    return _CACHE[key]


def make_in_maps(inputs, ncores=8, nlayers=DEPTH, layer0=0, xs=None):
    consts = host_constants(np.asarray(inputs["rel_bias"], np.float32))
    shared = {}
    for k, v in inputs.items():
        if k in ("x", "rel_bias"):
            continue
        a = np.ascontiguousarray(np.asarray(v, np.float32)[layer0:layer0 + nlayers])
        if k in ("w_uk", "w_uv"):
            a = a.reshape(nlayers, 256, 512)
        shared[k] = a
    shared.update(consts)
    x = np.asarray(inputs["x"], np.float32) if xs is None else xs
    maps = []
    for c in range(ncores):
        m = dict(shared)
        m["x"] = np.ascontiguousarray(x[c])
        maps.append(m)
    return maps


def kernel(**inputs):
    p = _get_prog(DEPTH)
    maps = make_in_maps(inputs, 8, DEPTH)
    res = run_bass_kernel_spmd(p.nc, maps, core_ids=list(range(8)))
    return np.stack([np.asarray(r["out"], np.float32) for r in res.results], axis=0)
```
